# Optimizing a Trainium2 kernel written in Bass

```python
import jax, jax.numpy as jnp
from jax import lax
import numpy as np

D_MODEL = 1024
BATCH = 8
SEQ = 8192
DEPTH = 2
DEC_BATCH = 8
DEC_SEQ = 32
PAST_LEN = 2048

CHUNK = 64
N_MIXERS = 2
N_RET_LAYERS = (DEPTH + 1) // 2
N_GLA_LAYERS = DEPTH // 2
RET_HEADS = 4
RET_DK = D_MODEL // RET_HEADS
RET_DV = 2 * D_MODEL // RET_HEADS
RET_QK = RET_HEADS * RET_DK
RET_V = RET_HEADS * RET_DV
ROPE_BASE = 10000.0
GLA_HEADS = 4
GLA_DK = D_MODEL // 2 // GLA_HEADS
GLA_DV = D_MODEL // GLA_HEADS
GLA_QK = GLA_HEADS * GLA_DK
GLA_V = GLA_HEADS * GLA_DV
GLA_GATE_RANK = 16
GLA_TAU = 16.0
D_FF = 2816
CONV_W = 3
EPS = 1e-6

kernel_name = "hybrid_retention_gla_convffn_step"


def rmsnorm(x, g):
    xf = x.astype(jnp.float32)
    y = xf * lax.rsqrt(jnp.mean(xf * xf, axis=-1, keepdims=True) + EPS)
    return (y * g.astype(jnp.float32)).astype(x.dtype)


def rotary(x, pos):
    half = x.shape[-1] // 2
    inv = ROPE_BASE ** (-jnp.arange(half, dtype=jnp.float32) / half)
    ang = pos.astype(jnp.float32)[:, None] * inv[None, :]
    cos = jnp.cos(ang)[None, :, None, :]
    sin = jnp.sin(ang)[None, :, None, :]
    x1, x2 = x[..., :half], x[..., half:]
    return jnp.concatenate([x1 * cos - x2 * sin, x1 * sin + x2 * cos], axis=-1)


def run_chunked(step, S0, seqs):
    T = seqs[0].shape[1]
    if T <= CHUNK:
        return step(S0, seqs)
    n = T // CHUNK
    def to_chunks(a):
        return jnp.moveaxis(a.reshape((a.shape[0], n, CHUNK) + a.shape[2:]), 1, 0)
    S, out = lax.scan(step, S0, tuple(to_chunks(a) for a in seqs))
    out = jnp.moveaxis(out, 0, 1)
    return S, out.reshape((out.shape[0], T) + out.shape[3:])


def retention_chunk(S, xs, log_gamma):
    q, k, v = xs
    L = q.shape[1]
    idx = jnp.arange(L, dtype=jnp.float32)
    diff = idx[:, None] - idx[None, :]
    lg = log_gamma[:, None, None]
    decay = jnp.where(diff[None] >= 0.0, jnp.exp(lg * jnp.maximum(diff, 0.0)[None]), 0.0)
    scores = jnp.einsum('bihd,bjhd->bhij', q, k) * decay[None]
    intra = jnp.einsum('bhij,bjhe->bihe', scores, v)
    q_dec = jnp.exp(log_gamma[:, None] * (idx[None, :] + 1.0))
    cross = jnp.einsum('bihd,hi,bhde->bihe', q, q_dec, S)
    k_dec = jnp.exp(log_gamma[:, None] * (L - 1.0 - idx[None, :]))
    S_new = jnp.exp(log_gamma * L)[None, :, None, None] * S + jnp.einsum('bjhd,hj,bjhe->bhde', k, k_dec, v)
    return S_new, intra + cross


def retention_mixer(h, S0, pos, w_in, gn_g, w_out):
    B, T, _ = h.shape
    proj = h @ w_in
    q, k, v, g = jnp.split(proj, [RET_QK, 2 * RET_QK, 2 * RET_QK + RET_V], axis=-1)
    q = rotary(q.astype(jnp.float32).reshape(B, T, RET_HEADS, RET_DK), pos)
    k = rotary(k.astype(jnp.float32).reshape(B, T, RET_HEADS, RET_DK), pos) * (RET_DK ** -0.5)
    v = v.astype(jnp.float32).reshape(B, T, RET_HEADS, RET_DV)
    log_gamma = jnp.log1p(-jnp.exp2(-5.0 - jnp.arange(RET_HEADS, dtype=jnp.float32)))
    step = lambda S, xs: retention_chunk(S, xs, log_gamma)
    S, o = run_chunked(step, S0.astype(jnp.float32), (q, k, v))
    mu = jnp.mean(o, axis=-1, keepdims=True)
    var = jnp.mean(jnp.square(o - mu), axis=-1, keepdims=True)
    o = (o - mu) * lax.rsqrt(var + EPS) * gn_g.astype(jnp.float32)[None, None]
    o = o.reshape(B, T, RET_V).astype(h.dtype)
    return (jax.nn.silu(g) * o) @ w_out, S


def gla_chunk(S, xs):
    q, k, v, lg = xs
    L = q.shape[1]
    b = jnp.cumsum(lg, axis=1)
    causal = jnp.tril(jnp.ones((L, L), dtype=bool))
    expo = b[:, :, None] - b[:, None, :]
    expo = jnp.where(causal[None, :, :, None, None], expo, -jnp.inf)
    A = jnp.einsum('bihd,bjhd,bijhd->bhij', q, k, jnp.exp(expo))
    intra = jnp.einsum('bhij,bjhe->bihe', A, v)
    cross = jnp.einsum('bihd,bhde->bihe', q * jnp.exp(b), S)
    bL = b[:, -1]
    S_new = jnp.exp(bL)[..., None] * S + jnp.einsum('bjhd,bjhe->bhde', k * jnp.exp(bL[:, None] - b), v)
    return S_new, intra + cross


def gla_mixer(h, S0, w_in, w_a2, b_a, norm_g, w_out):
    B, T, _ = h.shape
    proj = h @ w_in
    q, k, v, r, a = jnp.split(proj, [GLA_QK, 2 * GLA_QK, 2 * GLA_QK + GLA_V, 2 * GLA_QK + 2 * GLA_V], axis=-1)
    q = q.astype(jnp.float32).reshape(B, T, GLA_HEADS, GLA_DK)
    k = k.astype(jnp.float32).reshape(B, T, GLA_HEADS, GLA_DK) * (GLA_DK ** -0.5)
    v = v.astype(jnp.float32).reshape(B, T, GLA_HEADS, GLA_DV)
    lg = jax.nn.log_sigmoid((a @ w_a2 + b_a).astype(jnp.float32)) / GLA_TAU
    lg = lg.reshape(B, T, GLA_HEADS, GLA_DK)
    S, o = run_chunked(gla_chunk, S0.astype(jnp.float32), (q, k, v, lg))
    o = o * lax.rsqrt(jnp.mean(o * o, axis=-1, keepdims=True) + EPS) * norm_g.astype(jnp.float32)[None, None]
    o = o.reshape(B, T, GLA_V).astype(h.dtype)
    return (jax.nn.silu(r) * o) @ w_out, S


def conv_ffn(h, conv_state, w_up, conv_w, conv_b, w_down):
    u = h @ w_up
    T = u.shape[1]
    ext = jnp.concatenate([conv_state.astype(u.dtype), u], axis=1)
    c = conv_b + ext[:, 0:T] * conv_w[0]
    for j in range(1, CONV_W):
        c = c + ext[:, j:j + T] * conv_w[j]
    gate, val = jnp.split(c, 2, axis=-1)
    return (jax.nn.silu(gate) * val) @ w_down, ext[:, T:]


def trunk(x, pos, ret_states, gla_states, conv_states,
          norm_mix, norm_ffn, norm_final,
          ret_w_in, ret_gn_g, ret_w_out,
          gla_w_in, gla_w_a2, gla_b_a, gla_norm_g, gla_w_out,
          ffn_w_up, ffn_conv_w, ffn_conv_b, ffn_w_down):
    new_ret, new_gla, new_conv = [], [], []
    for i in range(DEPTH):
        h = rmsnorm(x, norm_mix[i])
        j = i // N_MIXERS
        if i % N_MIXERS == 0:
            y, S = retention_mixer(h, ret_states[j], pos, ret_w_in[j], ret_gn_g[j], ret_w_out[j])
            new_ret.append(S)
        else:
            y, S = gla_mixer(h, gla_states[j], gla_w_in[j], gla_w_a2[j], gla_b_a[j], gla_norm_g[j], gla_w_out[j])
            new_gla.append(S)
        x = x + y
        h = rmsnorm(x, norm_ffn[i])
        y, cs = conv_ffn(h, conv_states[i], ffn_w_up[i], ffn_conv_w[i], ffn_conv_b[i], ffn_w_down[i])
        new_conv.append(cs)
        x = x + y
    return rmsnorm(x, norm_final), jnp.stack(new_ret), jnp.stack(new_gla), jnp.stack(new_conv)


def setup_inputs(seed: int = 0) -> dict:
    key = jax.random.key(seed)
    ks = jax.random.split(key, 24)
    nrm = lambda k, shape, s: jax.random.normal(k, shape, dtype=jnp.float32) * s
    D = D_MODEL
    return {
        "x_prompt": nrm(ks[0], (BATCH, SEQ, D), 1.0),
        "x_sample": nrm(ks[1], (DEC_BATCH, DEC_SEQ, D), 1.0),
        "state_ret": nrm(ks[2], (N_RET_LAYERS, DEC_BATCH, RET_HEADS, RET_DK, RET_DV), 0.5),
        "state_gla": nrm(ks[3], (N_GLA_LAYERS, DEC_BATCH, GLA_HEADS, GLA_DK, GLA_DV), 0.5),
        "cache_conv": nrm(ks[4], (DEPTH, DEC_BATCH, CONV_W - 1, 2 * D_FF), 1.0),
        "norm_mix": 1.0 + nrm(ks[5], (DEPTH, D), 0.02),
        "norm_ffn": 1.0 + nrm(ks[6], (DEPTH, D), 0.02),
        "norm_final": 1.0 + nrm(ks[7], (D,), 0.02),
        "ret_w_in": nrm(ks[8], (N_RET_LAYERS, D, 2 * RET_QK + 2 * RET_V), D ** -0.5),
        "ret_gn_g": 1.0 + nrm(ks[9], (N_RET_LAYERS, RET_HEADS, RET_DV), 0.02),
        "ret_w_out": nrm(ks[10], (N_RET_LAYERS, RET_V, D), RET_V ** -0.5),
        "gla_w_in": nrm(ks[11], (N_GLA_LAYERS, D, 2 * GLA_QK + 2 * GLA_V + GLA_GATE_RANK), D ** -0.5),
        "gla_w_a2": nrm(ks[12], (N_GLA_LAYERS, GLA_GATE_RANK, GLA_QK), GLA_GATE_RANK ** -0.5),
        "gla_b_a": nrm(ks[13], (N_GLA_LAYERS, GLA_QK), 0.01),
        "gla_norm_g": 1.0 + nrm(ks[14], (N_GLA_LAYERS, GLA_HEADS, GLA_DV), 0.02),
        "gla_w_out": nrm(ks[15], (N_GLA_LAYERS, GLA_V, D), GLA_V ** -0.5),
        "ffn_w_up": nrm(ks[16], (DEPTH, D, 2 * D_FF), D ** -0.5),
        "ffn_conv_w": nrm(ks[17], (DEPTH, CONV_W, 2 * D_FF), CONV_W ** -0.5),
        "ffn_conv_b": nrm(ks[18], (DEPTH, 2 * D_FF), 0.01),
        "ffn_w_down": nrm(ks[19], (DEPTH, D_FF, D), D_FF ** -0.5),
    }


def reference(x_prompt, x_sample, state_ret, state_gla, cache_conv,
              norm_mix, norm_ffn, norm_final,
              ret_w_in, ret_gn_g, ret_w_out,
              gla_w_in, gla_w_a2, gla_b_a, gla_norm_g, gla_w_out,
              ffn_w_up, ffn_conv_w, ffn_conv_b, ffn_w_down):
    weights = (norm_mix, norm_ffn, norm_final,
               ret_w_in, ret_gn_g, ret_w_out,
               gla_w_in, gla_w_a2, gla_b_a, gla_norm_g, gla_w_out,
               ffn_w_up, ffn_conv_w, ffn_conv_b, ffn_w_down)
    B = x_prompt.shape[0]
    ret0 = jnp.zeros((N_RET_LAYERS, B, RET_HEADS, RET_DK, RET_DV), jnp.float32)
    gla0 = jnp.zeros((N_GLA_LAYERS, B, GLA_HEADS, GLA_DK, GLA_DV), jnp.float32)
    conv0 = jnp.zeros((DEPTH, B, CONV_W - 1, 2 * D_FF), x_prompt.dtype)
    pos_p = jnp.arange(x_prompt.shape[1], dtype=jnp.int32)
    y_prompt, ret_p, gla_p, conv_p = trunk(x_prompt, pos_p, ret0, gla0, conv0, *weights)
    pos_s = PAST_LEN + jnp.arange(x_sample.shape[1], dtype=jnp.int32)
    y_sample, ret_s, gla_s, conv_s = trunk(x_sample, pos_s, state_ret, state_gla, cache_conv, *weights)
    return (y_prompt, y_sample, ret_p, ret_s, gla_p, gla_s, conv_p, conv_s)
```

```python
import contextlib
import math
import numpy as np
import ml_dtypes
import concourse.bass as bass
import concourse.mybir as mybir
from concourse.bass_utils import run_bass_kernel_spmd

F32 = mybir.dt.float32
BF16 = mybir.dt.bfloat16
U8 = mybir.dt.uint8
AF = mybir.ActivationFunctionType
ALU = mybir.AluOpType
DSZ = {F32: 4, BF16: 2, U8: 1}

D = 1024
DEPTH = 2
RET_H = 4
RET_DK = 256
RET_DV = 512
GLA_H = 4
GLA_DK = 128
GLA_DV = 256
GLA_RANK = 16
GLA_TAU = 16.0
DFF = 2816
NBLK_FF = 2 * DFF // 128
EPS = 1e-6
ROPE_BASE = 10000.0
PAST_LEN = 2048
DEC_SEQ = 32
NCORES = 8

ARENA = 211968


class Prog:
    SBG = 256

    def __init__(self, nc):
        self.nc = nc
        self.q = {e: [] for e in ("pe", "act", "dve", "pool", "sp")}
        self.lanes = {}
        for e in ("pe", "act", "dve", "pool"):
            self.lanes[e] = {"count": 0, "inc": 1, "sem": None}
        self.clock = {e: {} for e in self.q}
        self.snap = {}
        self.gran = {}
        self.maxwait = {}
        self.nops = 0

    def lane(self, name):
        if name not in self.lanes:
            self.lanes[name] = {"count": 0, "inc": 16, "sem": None}
        return name

    def keys(self, ap):
        t = ap.tensor
        name = t.name
        if name not in ("arena", "psum"):
            return [("d", name)]
        esz = DSZ[ap.dtype]
        pairs = list(ap.ap)
        pstep = pairs[0][0]
        off = int(ap.offset)
        inpart = off % pstep if pstep else off
        starts = [inpart]
        free = [(s, c) for (s, c) in pairs[1:] if c > 1 or len(pairs) == 2]
        free = [(s, c) for (s, c) in free if s != 0]
        length = 1
        i = 0
        while i < len(free):
            s, c = free[i]
            inner_ext = sum((cc - 1) * abs(ss) for ss, cc in free[i + 1:]) + 1
            if i == len(free) - 1:
                length = (c - 1) * abs(s) + 1
            elif abs(s) > inner_ext and len(starts) * c <= 128:
                starts = [st + k * s for st in starts for k in range(c)]
            else:
                length = sum((cc - 1) * abs(ss) for ss, cc in free[i:]) + 1
                break
            i += 1
        g = self.SBG if name == "arena" else 2048
        ks = set()
        for st in starts:
            lo = (st * esz) // g
            hi = ((st + length) * esz - 1) // g
            for k in range(lo, hi + 1):
                ks.add((name, k))
        return ks

    def op(self, eng, fn, reads=(), writes=(), sig=True, lane=None, n=1, embed=True):
        self.nops += 1
        mylane = lane if lane is not None else eng
        L = self.lanes[mylane]
        raw = {}
        oth = {}

        def add(d, ls):
            l, s = ls
            if d.get(l, 0) < s:
                d[l] = s

        rkeys = set()
        for ap in reads:
            rkeys |= set(self.keys(ap))
        wkeys = set()
        for ap in writes:
            wkeys |= set(self.keys(ap))
        for k in rkeys:
            g = self.gran.get(k)
            if g is not None and g[0] is not None:
                add(raw, g[0])
        for k in wkeys:
            g = self.gran.get(k)
            if g is not None:
                if g[0] is not None:
                    add(oth, g[0])
                for l, s in g[1].items():
                    add(oth, (l, s))
        deps = {}
        for l, s in raw.items():
            if l == eng and lane is None:
                if eng == "pe":
                    continue
            add(deps, (l, s))
        for l, s in oth.items():
            if l == eng and lane is None:
                continue
            add(deps, (l, s))
        if lane is not None and L["count"] > 0:
            add(deps, (mylane, L["count"]))
        if sig:
            L["count"] += n
            seq = L["count"]
        else:
            seq = L["count"] + 1
        ck = self.clock[eng]
        waits = []
        for l, s in sorted(deps.items()):
            if ck.get(l, 0) < s:
                waits.append((l, s * self.lanes[l]["inc"]))
                if self.maxwait.get(l, 0) < s:
                    self.maxwait[l] = s
                sn = self.snap.get((l, s))
                if sn:
                    for l2, s2 in sn.items():
                        if ck.get(l2, 0) < s2:
                            ck[l2] = s2
                ck[l] = s
        if sig:
            sn = dict(ck)
            sn[mylane] = seq
            self.snap[(mylane, seq)] = sn
        for k in rkeys:
            g = self.gran.get(k)
            if g is None:
                g = [None, {}]
                self.gran[k] = g
            if g[1].get(mylane, 0) < seq:
                g[1][mylane] = seq
        for k in wkeys:
            self.gran[k] = [(mylane, seq), {}]
        self.q[eng].append((waits, fn, (mylane, L["inc"]) if sig else None, embed))

    def wait_all(self, eng, lanes):
        waits = [(l, self.lanes[l]["count"] * self.lanes[l]["inc"]) for l in lanes if self.lanes[l]["count"] > 0]
        self.q[eng].append((waits, None, None, False))

    def emit(self, block):
        for l, s in self.maxwait.items():
            assert s <= self.lanes[l]["count"], (l, s, self.lanes[l]["count"])

        def runner(name):
            items = self.q[name]
            lanes = self.lanes

            def body(e):
                for waits, fn, sig, embed in items:
                    emb = None
                    if embed and fn is not None and waits:
                        emb = waits[-1]
                        waits = waits[:-1]
                    for l, v in waits:
                        e.wait_ge(lanes[l]["sem"], v)
                    if fn is None:
                        continue
                    r = fn(e)
                    if emb is not None:
                        first = r[0] if isinstance(r, (list, tuple)) else r
                        first._wait_ge(lanes[emb[0]]["sem"], emb[1])
                    if sig is not None:
                        sem = lanes[sig[0]]["sem"]
                        if isinstance(r, (list, tuple)):
                            for ins in r:
                                ins.then_inc(sem, sig[1])
                        else:
                            r.then_inc(sem, sig[1])

            return body

        block.tensor(runner("pe"))
        block.scalar(runner("act"))
        block.vector(runner("dve"))
        block.gpsimd(runner("pool"))
        block.sync(runner("sp"))

    def mm(self, out, lhsT, rhs, start, stop, sig=None):
        self.op("pe", lambda e: e.matmul(out, lhsT, rhs, start=start, stop=stop),
                reads=[lhsT, rhs], writes=[out], sig=(stop if sig is None else sig))

    def tr(self, out, in_, ident, sig=True):
        self.op("pe", lambda e: e.transpose(out, in_, ident), reads=[in_, ident], writes=[out], sig=sig)

    def act(self, out, in_, func, bias=None, scale=None, accum=None):
        reads = [in_]
        kw = {}
        if bias is not None:
            kw["bias"] = bias
            if not isinstance(bias, (int, float)):
                reads.append(bias)
        if scale is not None:
            kw["scale"] = scale
            if not isinstance(scale, (int, float)):
                reads.append(scale)
        writes = [out]
        if accum is not None:
            kw["accum_out"] = accum
            writes.append(accum)
        self.op("act", lambda e: e.activation(out, in_, func, **kw), reads=reads, writes=writes,
                embed=(accum is None))

    def tt(self, eng, out, a, b, op):
        self.op(eng, lambda e: e.tensor_tensor(out, a, b, op), reads=[a, b], writes=[out])

    def ts(self, eng, out, a, s1, op0, s2=None, op1=None):
        reads = [a]
        if not isinstance(s1, (int, float)):
            reads.append(s1)
        if s2 is not None and not isinstance(s2, (int, float)):
            reads.append(s2)
        if op1 is None:
            self.op(eng, lambda e: e.tensor_scalar(out, a, s1, None, op0), reads=reads, writes=[out])
        else:
            self.op(eng, lambda e: e.tensor_scalar(out, a, s1, s2, op0, op1), reads=reads, writes=[out])

    def stt(self, out, in0, scalar, in1, op0, op1):
        reads = [in0, in1]
        if not isinstance(scalar, (int, float)):
            reads.append(scalar)
        self.op("dve", lambda e: e.scalar_tensor_tensor(out, in0, scalar, in1, op0, op1),
                reads=reads, writes=[out])

    def copy(self, eng, out, in_):
        if eng == "act":
            self.op("act", lambda e: e.activation(out, in_, AF.Copy), reads=[in_], writes=[out])
        else:
            self.op(eng, lambda e: e.tensor_copy(out, in_), reads=[in_], writes=[out])

    def memset(self, eng, out, val):
        self.op(eng, lambda e: e.memset(out, val), reads=[], writes=[out])

    def dma(self, pairs, lane, eng="sp", **kw):
        self.lane(lane)
        outs = [p[0] for p in pairs]
        ins = [p[1] for p in pairs]

        def fn(e):
            return [e.dma_start(out=o, in_=i, **kw) for o, i in pairs]

        self.op(eng, fn, reads=ins, writes=outs, lane=lane, n=len(pairs))


def _consts(seq):
    c = {}
    c["ident_bf"] = np.eye(128, dtype=np.float32).astype(ml_dtypes.bfloat16)
    c["ones_bf"] = np.ones((128, 128), dtype=np.float32).astype(ml_dtypes.bfloat16)
    lg = np.log1p(-np.exp2(-5.0 - np.arange(RET_H, dtype=np.float64)))
    j = np.arange(128, dtype=np.float64)
    causalT = (j[None, :] >= j[:, None]).astype(np.float64)
    rmask = np.zeros((128, RET_H, 128), np.float64)
    for h in range(RET_H):
        rmask[:, h, :] = np.exp(-lg[h] * (j[:, None] + 1.0)) * (RET_DK ** -0.5) * causalT
    c["rmask"] = rmask.astype(np.float32)
    qd = np.exp(lg[:, None] * (j[None, :] + 1.0))
    c["qdec"] = np.broadcast_to(qd[None], (128, RET_H, 128)).astype(np.float32).copy()
    kd = np.zeros((128, 2, RET_H), np.float64)
    for li, L in enumerate((128, 32)):
        for h in range(RET_H):
            kd[:, li, h] = np.exp(lg[h] * (L - 1.0 - j)) * (RET_DK ** -0.5)
    c["kdec"] = kd.astype(np.float32)
    c["gL"] = {L: [float(np.exp(lg[h] * L)) for h in range(RET_H)] for L in (128, 32)}
    c["gmask"] = (causalT * (GLA_DK ** -0.5)).astype(np.float32)
    c["ucum"] = ((j[:, None] <= j[None, :]) * (-1.0 / GLA_TAU)).astype(np.float32)
    c["neghalf"] = np.full((128, 512), -0.5, np.float32)
    inv = (np.float32(ROPE_BASE) ** (-(np.arange(128, dtype=np.float32) / np.float32(128)))).astype(np.float32)
    pos = np.concatenate([np.arange(seq, dtype=np.float32), PAST_LEN + np.arange(DEC_SEQ, dtype=np.float32)])
    ang = (pos[None, :] * inv[:, None]).astype(np.float32)
    c["cos"] = np.cos(ang).astype(np.float32)
    c["sin"] = np.sin(ang).astype(np.float32)
    return c


def _fm(v):
    v = np.asarray(v, np.float32)
    return np.ascontiguousarray(v.reshape(-1, 128).T)


WEIGHTS = [
    ("ret_w_in", D, 6144), ("ret_w_out", 2048, D), ("gla_w_in", D, 3088), ("gla_w_out", D, D),
    ("ffn_w_up0", D, 2 * DFF), ("ffn_w_up1", D, 2 * DFF), ("ffn_w_down0", DFF, D), ("ffn_w_down1", DFF, D),
]


def build(seq, dbg=()):
    assert seq % 512 == 0
    nc = bass.Bass("TRN2", target_bir_lowering=False)
    P = Prog(nc)
    cst = _consts(seq)
    gL = cst["gL"]
    npos = seq + DEC_SEQ

    def din(name, shape, dt=F32):
        return nc.dram_tensor(name, list(shape), dt, kind="ExternalInput").ap()

    def dout(name, shape, dt=F32):
        return nc.dram_tensor(name, list(shape), dt, kind="ExternalOutput").ap()

    NHALF = -(-seq // 4096)
    HSEQ = seq // NHALF
    assert HSEQ * NHALF == seq and HSEQ % 512 == 0
    xTp = [din("xTp%d" % i, [D, HSEQ]) for i in range(NHALF)]; xTs = din("xTs", [D, DEC_SEQ])
    st_ret = din("st_ret", [RET_H, RET_DK, RET_DV]); st_gla = din("st_gla", [GLA_H, GLA_DK, GLA_DV])
    cconv = din("cconv", [128, DEPTH, NBLK_FF, 2])
    W32 = {n: din(n, [k, m]) for n, k, m in WEIGHTS}
    Wb = {n: nc.dram_tensor(n + "_bf", [k, m], BF16, kind="Internal").ap() for n, k, m in WEIGHTS}
    d_gains = din("gains", [128, 5, 8])
    d_gn = din("gn", [128, 16]); d_ng = din("ng", [128, 8])
    d_cw = din("cw", [128, DEPTH, 3, NBLK_FF]); d_cb = din("cb", [128, DEPTH, NBLK_FF])
    d_wa2 = din("wa2aug", [17, 512])
    d_ident = din("ident_bf", [128, 128], BF16); d_ones = din("ones_bf", [128, 128], BF16)
    d_rmask = din("rmask", [128, RET_H, 128]); d_qdec = din("qdec", [128, RET_H, 128])
    d_kdec = din("kdec", [128, 2, RET_H]); d_gmask = din("gmask", [128, 128]); d_ucum = din("ucum", [128, 128])
    d_neghalf = din("neghalf", [128, 512])
    d_cos = din("cos", [128, npos]); d_sin = din("sin", [128, npos])

    yTp = [dout("yTp%d" % i, [D, HSEQ]) for i in range(NHALF)]; yTs = dout("yTs", [D, DEC_SEQ])
    o_ret = {"p": dout("ret_p", [RET_H, RET_DK, RET_DV]), "s": dout("ret_s", [RET_H, RET_DK, RET_DV])}
    o_gla = {"p": dout("gla_p", [GLA_H, GLA_DK, GLA_DV]), "s": dout("gla_s", [GLA_H, GLA_DK, GLA_DV])}
    o_conv = {"p": dout("conv_p", [128, DEPTH, NBLK_FF, 2]), "s": dout("conv_s", [128, DEPTH, NBLK_FF, 2])}
    dbg_out = {}

    arena_h = nc.alloc_sbuf_tensor("arena", [128, ARENA], U8)
    arena = arena_h.ap()
    psum_h = nc.alloc_psum_tensor("psum", [128, 8, 512], F32)
    psum = psum_h.ap()

    off = {"_": 0}
    reg = {}

    def region(name, nbytes):
        assert nbytes % 256 == 0, name
        reg[name] = (off["_"], nbytes)
        off["_"] += nbytes
        assert off["_"] <= ARENA, (name, off["_"])

    def view(name, dt, shape, boff=0, parts=128):
        o, nb = reg[name]
        n = int(np.prod(shape)) * DSZ[dt]
        assert boff + n <= nb, (name, boff, n, nb)
        ap = arena[0:parts, o + boff:o + boff + n].bitcast(dt)
        if len(shape) == 2:
            ap = ap.rearrange("p (a b) -> p a b", a=shape[0])
        elif len(shape) == 3:
            ap = ap.rearrange("p (a b c) -> p a b c", a=shape[0], b=shape[1])
        return ap

    region("x", 16384); region("hT", 8192); region("ms", 2048); region("rstd", 2048)
    region("qkk", 24576)
    region("vtok", 16384)
    region("sgT", 16384)
    region("ogT", 16384)
    region("mix", 20480)
    region("PT", 1024); region("on", 4096); region("stats", 1024); region("stmp", 2048); region("aT", 1024)
    region("Sret", 16384); region("Sretb", 8192); region("Sgla", 4096); region("Sglab", 2048)
    region("wring", 3 * 8192)
    for nme, nb in [("ident", 256), ("ones", 256), ("rmask", 2048), ("qdec", 2048), ("kdec", 256), ("gmask", 512),
                    ("ucum", 512), ("neghalf", 2048), ("gains", 256), ("gn", 256), ("ng", 256), ("cw", 1280),
                    ("cb", 512), ("wa2", 1024), ("uhb", 512), ("uhf", 768)]:
        region(nme, nb)

    ident = view("ident", BF16, [128]); ones = view("ones", BF16, [128])
    rmask = view("rmask", F32, [RET_H, 128]); qdec = view("qdec", F32, [RET_H, 128])
    kdec = view("kdec", F32, [2, RET_H]); gmask = view("gmask", F32, [128]); ucum = view("ucum", F32, [128])
    neghalf = view("neghalf", F32, [512])
    gains = view("gains", F32, [5, 8]); gnT = view("gn", F32, [16]); ngT = view("ng", F32, [8])
    cw = view("cw", F32, [DEPTH, 3, NBLK_FF]); cb = view("cb", F32, [DEPTH, NBLK_FF])
    wa2 = view("wa2", BF16, [512], parts=32)
    uhb = view("uhb", BF16, [DEPTH, NBLK_FF, 2]); uhf = view("uhf", F32, [DEPTH, NBLK_FF, 2])
    Sret = view("Sret", F32, [RET_H, 2, 512]); Sretb = view("Sretb", BF16, [RET_H, 2, 512])
    Sgla = view("Sgla", F32, [GLA_H, 256]); Sglab = view("Sglab", BF16, [GLA_H, 256])

    psb = [psum[:, b, :] for b in range(8)]
    psb_bf = [psum[:, b, :].bitcast(BF16) for b in range(8)]
    bank_ctr = {"i": 0}

    def bank():
        b = bank_ctr["i"] % 8
        bank_ctr["i"] += 1
        return b

    P.dma([(ident, d_ident), (ones, d_ones), (rmask, d_rmask), (qdec, d_qdec), (kdec, d_kdec), (gmask, d_gmask),
           (ucum, d_ucum), (neghalf, d_neghalf), (gains, d_gains), (gnT, d_gn), (ngT, d_ng), (cw, d_cw), (cb, d_cb)],
          "const")
    P.dma([(wa2[0:17, :], d_wa2)], "wa2c", eng="pool")
    for n, k, m in WEIGHTS:
        P.dma([(Wb[n], W32[n])], "cast_" + n, eng="pool", max_dma_last_dim=4096)

    ring = {"i": 0}

    def slab(wname, kc0, nkc, colgroups):
        s = ring["i"] % 3
        ring["i"] += 1
        ncols = sum(c for _, c in colgroups)
        assert nkc * ncols * 2 <= 8192
        v = view("wring", BF16, [nkc, ncols], boff=s * 8192)
        pairs = []
        co = 0
        for c0, cn in colgroups:
            src = Wb[wname][kc0 * 128:(kc0 + nkc) * 128, c0:c0 + cn].rearrange("(k p) n -> p k n", p=128)
            pairs.append((v[:, :, co:co + cn], src))
            co += cn
        P.dma(pairs, "w%d" % s)
        return v

    aT_all = view("aT", BF16, [512], parts=32)
    P.memset("dve", aT_all, 1.0)

    def run_tile(sk, t0, NT, CL, last):
        NCH = NT // CL
        li = 0 if CL == 128 else 1
        xsrc = xTp[t0 // HSEQ] if sk == "p" else xTs
        ydst = yTp[t0 // HSEQ] if sk == "p" else yTs
        tq = t0 % HSEQ if sk == "p" else t0
        pos0 = t0 if sk == "p" else seq + t0
        x = view("x", F32, [8, NT])
        hT = view("hT", BF16, [8, NT])
        sq = view("ogT", BF16, [8, NT])
        ms = view("ms", F32, [NT]); rstd = view("rstd", F32, [NT])
        qT = view("qkk", BF16, [8, NT]); kT = view("qkk", BF16, [8, NT], boff=8192)
        ktok = view("qkk", BF16, [NCH, 1024], boff=16384)
        vtok = view("vtok", BF16, [NCH, 2048])
        sgT = view("sgT", BF16, [16, NT]); ogT = view("ogT", BF16, [16, NT])
        PTv = [view("PT", BF16, [128], boff=i * 256) for i in range(4)]
        onv = [view("on", BF16, [512], boff=i * 1024) for i in range(3)]
        statv = [(view("stats", F32, [6], boff=i * 256), view("stats", F32, [2], boff=i * 256 + 64),
                  view("stats", F32, [1], boff=i * 256 + 128), view("stats", F32, [1], boff=i * 256 + 192))
                 for i in range(4)]
        stmp = [view("stmp", BF16, [NT], boff=i * 1024) for i in range(2)]
        cnt = {"pt": 0, "on": 0, "st": 0, "sv": 0}

        def tsl(ch):
            return slice(ch * CL, (ch + 1) * CL)

        P.dma([(x, xsrc[:, tq:tq + NT].rearrange("(c p) t -> p c t", p=128))], "xin")

        def norm(gidx, out_hT=True):
            P.act(sq, x, AF.Square)
            b = bank()
            for c in range(8):
                P.mm(psb[b][:, 0:NT], ones, sq[:, c, :], start=(c == 0), stop=(c == 7))
            P.ts("dve", ms, psb[b][:, 0:NT], 1.0 / D, ALU.mult, EPS, ALU.add)
            P.tt("pool", rstd, ms, neghalf[:, 0:NT], ALU.pow)
            for c in range(8):
                dst = hT[:, c, :] if out_hT else x[:, c, :]
                P.stt(dst, x[:, c, :], gains[:, gidx, c:c + 1], rstd, ALU.mult, ALU.mult)

        def resid_proj(wname, nkc, src, kslabs):
            for cg in range(4):
                banks = [bank(), bank()]
                k0 = 0
                while k0 < nkc:
                    nk = min(kslabs, nkc - k0)
                    sl = slab(wname, k0, nk, [(cg * 256, 256)])
                    for j in range(2):
                        for kk in range(nk):
                            kc = k0 + kk
                            P.mm(psb[banks[j]][:, 0:NT], sl[:, kk, j * 128:(j + 1) * 128], src[:, kc, :],
                                 start=(kc == 0), stop=(kc == nkc - 1),
                                 sig=(kc == nkc - 1) or (kk == nk - 1 and j == 1))
                    k0 += nk
                for j in range(2):
                    blk = cg * 2 + j
                    P.tt("dve", x[:, blk, :], x[:, blk, :], psb[banks[j]][:, 0:NT], ALU.add)

        def retention():
            cs = view("mix", F32, [2, NT])
            cqs = [view("mix", F32, [2, NT], boff=4096 + i * 4096) for i in range(2)]
            rt = view("mix", F32, [4, NT], boff=12288)
            P.dma([(cs[:, 0, :], d_cos[:, pos0:pos0 + NT]), (cs[:, 1, :], d_sin[:, pos0:pos0 + NT])], "cs")

            def rotary(pa, pb, ct, st_, o1, o2):
                P.tt("dve", rt[:, 0, :], pa, ct, ALU.mult)
                P.tt("dve", rt[:, 1, :], pb, st_, ALU.mult)
                P.tt("dve", o1, rt[:, 0, :], rt[:, 1, :], ALU.subtract)
                P.tt("dve", rt[:, 2, :], pa, st_, ALU.mult)
                P.tt("dve", rt[:, 3, :], pb, ct, ALU.mult)
                P.tt("dve", o2, rt[:, 2, :], rt[:, 3, :], ALU.add)

            for which in range(2):
                dstT = qT if which == 0 else kT
                for sh in range(2):
                    sl = slab("ret_w_in", 0, 8, [(which * 1024 + sh * 512, 512)])
                    for hh in range(2):
                        h = sh * 2 + hh
                        ba, bb = bank(), bank()
                        for half, bk in ((0, ba), (1, bb)):
                            for kc in range(8):
                                P.mm(psb[bk][:, 0:NT], sl[:, kc, (hh * 2 + half) * 128:(hh * 2 + half + 1) * 128],
                                     hT[:, kc, :], start=(kc == 0), stop=(kc == 7))
                        if which == 0:
                            cq = cqs[h % 2]
                            qd = qdec[:, h, 0:CL].unsqueeze(1).to_broadcast([128, NCH, CL])
                            for t_ in range(2):
                                P.tt("pool", cq[:, t_, :].rearrange("p (a b) -> p a b", a=NCH),
                                     cs[:, t_, :].rearrange("p (a b) -> p a b", a=NCH), qd, ALU.mult)
                            rotary(psb[ba][:, 0:NT], psb[bb][:, 0:NT], cq[:, 0, :], cq[:, 1, :],
                                   dstT[:, 2 * h, :], dstT[:, 2 * h + 1, :])
                        else:
                            rotary(psb[ba][:, 0:NT], psb[bb][:, 0:NT], cs[:, 0, :], cs[:, 1, :],
                                   dstT[:, 2 * h, :], dstT[:, 2 * h + 1, :])
            for h in range(RET_H):
                for ch in range(NCH):
                    b = bank()
                    for dc in range(2):
                        P.tr(psb_bf[b][0:CL, dc * 128:(dc + 1) * 128], kT[:, 2 * h + dc, tsl(ch)], ident,
                             sig=(dc == 1))
                    P.act(ktok[0:CL, ch, h * 256:(h + 1) * 256], psb_bf[b][0:CL, 0:256], AF.Copy,
                          scale=kdec[0:CL, li, h:h + 1])
            for h in range(RET_H):
                sl = slab("ret_w_in", 0, 8, [(2048 + h * 512, 512)])
                for ch in range(NCH):
                    b = bank()
                    for kc in range(8):
                        P.mm(psb[b][0:CL, :], hT[:, kc, tsl(ch)], sl[:, kc, :], start=(kc == 0), stop=(kc == 7))
                    P.copy("act", vtok[0:CL, ch, h * 512:(h + 1) * 512], psb[b][0:CL, :])
            for s4 in range(4):
                sl = slab("ret_w_in", 0, 8, [(4096 + s4 * 512, 512)])
                for j in range(4):
                    blk = s4 * 4 + j
                    b = bank()
                    for kc in range(8):
                        P.mm(psb[b][:, 0:NT], sl[:, kc, j * 128:(j + 1) * 128], hT[:, kc, :],
                             start=(kc == 0), stop=(kc == 7))
                    tmp = stmp[cnt["st"] % 2]; cnt["st"] += 1
                    P.act(tmp, psb[b][:, 0:NT], AF.Silu)
                    P.ts("dve", sgT[:, blk, :], tmp, gnT[:, blk:blk + 1], ALU.mult)
            for ch in range(NCH):
                for h in range(RET_H):
                    bs = bank()
                    for dc in range(2):
                        P.mm(psb[bs][0:CL, 0:CL], kT[:, 2 * h + dc, tsl(ch)], qT[:, 2 * h + dc, tsl(ch)],
                             start=(dc == 0), stop=(dc == 1))
                    PT = PTv[cnt["pt"] % 4]; cnt["pt"] += 1
                    P.tt("dve", PT[0:CL, 0:CL], psb[bs][0:CL, 0:CL], rmask[0:CL, h, 0:CL], ALU.mult)
                    bo = bank()
                    P.mm(psb[bo][0:CL, :], PT[0:CL, 0:CL], vtok[0:CL, ch, h * 512:(h + 1) * 512], start=True, stop=False)
                    for dc in range(2):
                        P.mm(psb[bo][0:CL, :], qT[:, 2 * h + dc, tsl(ch)], Sretb[:, h, dc, :], start=False, stop=(dc == 1))
                    for dc in range(2):
                        bS = bank()
                        P.mm(psb[bS][:, :], ktok[0:CL, ch, h * 256 + dc * 128:h * 256 + (dc + 1) * 128],
                             vtok[0:CL, ch, h * 512:(h + 1) * 512], start=True, stop=True)
                        P.stt(Sret[:, h, dc, :], Sret[:, h, dc, :], gL[CL][h], psb[bS][:, :], ALU.mult, ALU.add)
                        P.copy("act", Sretb[:, h, dc, :], Sret[:, h, dc, :])
                    stats6, mv, ve, rs = statv[cnt["sv"] % 4]; cnt["sv"] += 1
                    P.op("dve", lambda e, o=stats6[0:CL, :], i=psb[bo][0:CL, :]: e.bn_stats(o, i),
                         reads=[psb[bo][0:CL, :]], writes=[stats6[0:CL, :]])
                    P.op("dve", lambda e, o=mv[0:CL, :], i=stats6[0:CL, :]: e.bn_aggr(o, i),
                         reads=[stats6[0:CL, :]], writes=[mv[0:CL, :]])
                    P.ts("dve", ve[0:CL, :], mv[0:CL, 1:2], EPS, ALU.add)
                    P.tt("pool", rs[0:CL, :], ve[0:CL, :], neghalf[0:CL, 0:1], ALU.pow)
                    on = onv[cnt["on"] % 3]; cnt["on"] += 1
                    P.ts("dve", on[0:CL, :], psb[bo][0:CL, :], mv[0:CL, 0:1], ALU.subtract, rs[0:CL, :], ALU.mult)
                    bt = bank()
                    for eb in range(4):
                        P.tr(psb_bf[bt][:, eb * CL:(eb + 1) * CL], on[0:CL, eb * 128:(eb + 1) * 128],
                             ident[0:CL, 0:CL], sig=(eb == 3))
                    P.tt("dve", ogT[:, 4 * h:4 * h + 4, tsl(ch)],
                         psb_bf[bt][:, 0:4 * CL].rearrange("p (a b) -> p a b", a=4),
                         sgT[:, 4 * h:4 * h + 4, tsl(ch)], ALU.mult)
            resid_proj("ret_w_out", 16, ogT, 16)

        def ffn(layer):
            actT = view("qkk", BF16, [22, NT])
            ubuf = [[view("sgT", BF16, [NT + 2], boff=(r * 2 + gv) * 1280) for gv in range(2)] for r in range(2)]
            sgt = [view("sgT", BF16, [NT], boff=5120 + r * 1024) for r in range(2)]
            dg = [[view("sgT", BF16, [3, 128], boff=7168 + (r * 2 + gv) * 768) for gv in range(2)] for r in range(2)]
            wn = "ffn_w_up%d" % layer
            it = 0
            for s in range(11):
                sl = slab(wn, 0, 8, [(s * 256, 256), (DFF + s * 256, 256)])
                for jj in range(2):
                    gb = s * 2 + jj
                    r = it % 2; it += 1
                    bks = []
                    for gv in range(2):
                        blk = gb + gv * 22
                        b = bank(); bks.append(b)
                        for kc in range(8):
                            P.mm(psb[b][:, 0:NT], sl[:, kc, gv * 256 + jj * 128:gv * 256 + (jj + 1) * 128], hT[:, kc, :],
                                 start=(kc == 0), stop=(kc == 7))
                        ub = ubuf[r][gv]
                        P.copy("pool", ub[:, 0:2], uhb[:, layer, blk, :])
                        P.copy("act", ub[:, 2:2 + NT], psb[b][:, 0:NT])
                        P.copy("pool", uhb[:, layer, blk, :], ub[:, NT:NT + 2])
                        if last:
                            P.copy("dve", uhf[:, layer, blk, :], psb[b][:, NT - 2:NT])
                        for j in range(3):
                            P.ts("pool", dg[r][gv][:, j, :], ident, cw[:, layer, j, blk:blk + 1], ALU.mult)
                    cbk = []
                    for gv in range(2):
                        b = bank(); cbk.append(b)
                        for j in range(3):
                            P.mm(psb[b][:, 0:NT], dg[r][gv][:, j, :], ubuf[r][gv][:, j:j + NT], start=(j == 0), stop=(j == 2))
                    P.act(sgt[r], psb[cbk[0]][:, 0:NT], AF.Silu, bias=cb[:, layer, gb:gb + 1])
                    P.stt(actT[:, gb, :], psb[cbk[1]][:, 0:NT], cb[:, layer, gb + 22:gb + 23], sgt[r], ALU.add, ALU.mult)
            resid_proj("ffn_w_down%d" % layer, 22, actT, 11)

        def gla():
            vt = view("vtok", BF16, [NCH, 1024])
            kbar = view("qkk", BF16, [NCH, 512], boff=16384)
            sp = view("mix", F32, [NCH, 512])
            Eq = view("mix", F32, [GLA_H, NT], boff=8192)
            Ek = [view("mix", F32, [NT], boff=16384 + i * 2048) for i in range(2)]
            zt = view("on", F32, [512])
            kbt = view("stmp", BF16, [NT])
            sl = slab("gla_w_in", 0, 8, [(3072, 16)])
            b = bank()
            for kc in range(8):
                P.mm(psb[b][0:16, 0:NT], sl[:, kc, 0:16], hT[:, kc, :], start=(kc == 0), stop=(kc == 7))
            P.copy("act", aT_all[0:16, 0:NT], psb[b][0:16, 0:NT])
            for ch in range(NCH):
                b = bank()
                P.mm(psb[b][0:CL, :], aT_all[0:17, tsl(ch)], wa2[0:17, :], start=True, stop=True)
                P.act(zt[0:CL, :], psb[b][0:CL, :], AF.Exp, scale=-1.0)
                P.act(sp[0:CL, ch, :], zt[0:CL, :], AF.Ln, bias=1.0)
            for h in range(GLA_H):
                b = bank()
                for ch in range(NCH):
                    P.mm(psb[b][:, tsl(ch)], sp[0:CL, ch, h * 128:(h + 1) * 128], ucum[0:CL, 0:CL], start=True, stop=True,
                         sig=(ch == NCH - 1))
                P.act(Eq[:, h, :], psb[b][:, 0:NT], AF.Exp)
                P.act(Ek[h % 2], psb[b][:, 0:NT], AF.Exp, scale=-1.0)
                if h == 0:
                    slq = slab("gla_w_in", 0, 8, [(0, 512)])
                    slk = slab("gla_w_in", 0, 8, [(512, 512)])
                bq = bank()
                for kc in range(8):
                    P.mm(psb[bq][:, 0:NT], slq[:, kc, h * 128:(h + 1) * 128], hT[:, kc, :], start=(kc == 0), stop=(kc == 7))
                P.tt("dve", qT[:, h, :], psb[bq][:, 0:NT], Eq[:, h, :], ALU.mult)
                bk = bank()
                for kc in range(8):
                    P.mm(psb[bk][:, 0:NT], slk[:, kc, h * 128:(h + 1) * 128], hT[:, kc, :], start=(kc == 0), stop=(kc == 7),
                         sig=True)
                P.tt("dve", kT[:, h, :], psb[bk][:, 0:NT], Ek[h % 2], ALU.mult)
                for ch in range(NCH):
                    P.ts("dve", kbt[:, tsl(ch)], kT[:, h, tsl(ch)], Eq[:, h, (ch + 1) * CL - 1:(ch + 1) * CL], ALU.mult,
                         GLA_DK ** -0.5, ALU.mult)
                    bt = bank()
                    P.tr(psb_bf[bt][0:CL, 0:128], kbt[:, tsl(ch)], ident)
                    P.copy("act", kbar[0:CL, ch, h * 128:(h + 1) * 128], psb_bf[bt][0:CL, 0:128])
            for s2 in range(2):
                sl = slab("gla_w_in", 0, 8, [(1024 + s2 * 512, 512)])
                for ch in range(NCH):
                    b = bank()
                    for kc in range(8):
                        P.mm(psb[b][0:CL, :], hT[:, kc, tsl(ch)], sl[:, kc, :], start=(kc == 0), stop=(kc == 7))
                    P.copy("act", vt[0:CL, ch, s2 * 512:(s2 + 1) * 512], psb[b][0:CL, :])
            for s2 in range(2):
                sl = slab("gla_w_in", 0, 8, [(2048 + s2 * 512, 512)])
                for j in range(4):
                    blk = s2 * 4 + j
                    b = bank()
                    for kc in range(8):
                        P.mm(psb[b][:, 0:NT], sl[:, kc, j * 128:(j + 1) * 128], hT[:, kc, :], start=(kc == 0), stop=(kc == 7))
                    tmp = stmp[cnt["st"] % 2]; cnt["st"] += 1
                    P.act(tmp, psb[b][:, 0:NT], AF.Silu)
                    P.ts("dve", sgT[:, blk, :], tmp, ngT[:, blk:blk + 1], ALU.mult)
            junk = view("on", BF16, [256], boff=2048)
            for ch in range(NCH):
                for h in range(GLA_H):
                    ba = bank()
                    P.mm(psb[ba][0:CL, 0:CL], kT[:, h, tsl(ch)], qT[:, h, tsl(ch)], start=True, stop=True)
                    PT = PTv[cnt["pt"] % 4]; cnt["pt"] += 1
                    P.tt("dve", PT[0:CL, 0:CL], psb[ba][0:CL, 0:CL], gmask[0:CL, 0:CL], ALU.mult)
                    bo = bank()
                    P.mm(psb[bo][0:CL, 0:256], PT[0:CL, 0:CL], vt[0:CL, ch, h * 256:(h + 1) * 256], start=True, stop=False)
                    P.mm(psb[bo][0:CL, 0:256], qT[:, h, tsl(ch)], Sglab[:, h, :], start=False, stop=True)
                    bS = bank()
                    P.mm(psb[bS][:, 0:256], kbar[0:CL, ch, h * 128:(h + 1) * 128], vt[0:CL, ch, h * 256:(h + 1) * 256],
                         start=True, stop=True)
                    P.stt(Sgla[:, h, :], Sgla[:, h, :], Eq[:, h, (ch + 1) * CL - 1:(ch + 1) * CL], psb[bS][:, 0:256],
                          ALU.mult, ALU.add)
                    P.copy("act", Sglab[:, h, :], Sgla[:, h, :])
                    stats6, mv, ve, rs = statv[cnt["sv"] % 4]; cnt["sv"] += 1
                    P.act(junk[0:CL, :], psb[bo][0:CL, 0:256], AF.Square, accum=ve[0:CL, :])
                    P.ts("dve", mv[0:CL, 0:1], ve[0:CL, :], 1.0 / GLA_DV, ALU.mult, EPS, ALU.add)
                    P.tt("pool", rs[0:CL, :], mv[0:CL, 0:1], neghalf[0:CL, 0:1], ALU.pow)
                    on = view("on", BF16, [256], boff=2560 + (cnt["on"] % 2) * 512); cnt["on"] += 1
                    P.ts("dve", on[0:CL, :], psb[bo][0:CL, 0:256], rs[0:CL, :], ALU.mult)
                    bt = bank()
                    for eb in range(2):
                        P.tr(psb_bf[bt][:, eb * CL:(eb + 1) * CL], on[0:CL, eb * 128:(eb + 1) * 128], ident[0:CL, 0:CL],
                             sig=(eb == 1))
                    P.tt("dve", ogT[:, 2 * h:2 * h + 2, tsl(ch)],
                         psb_bf[bt][:, 0:2 * CL].rearrange("p (a b) -> p a b", a=2),
                         sgT[:, 2 * h:2 * h + 2, tsl(ch)], ALU.mult)
            resid_proj("gla_w_out", 8, ogT, 8)

        norm(0); retention()
        norm(2); ffn(0)
        norm(1); gla()
        norm(3); ffn(1)
        norm(4, out_hT=False)
        P.dma([(ydst[:, tq:tq + NT].rearrange("(c p) t -> p c t", p=128), x)], "yout")

    def seq_end(sk):
        P.dma([(o_ret[sk].rearrange("h (dc p) e -> p h dc e", p=128), Sret)], "so_ret")
        P.dma([(o_gla[sk].rearrange("h p e -> p h e"), Sgla)], "so_gla")
        P.dma([(o_conv[sk], uhf)], "so_conv")

    P.dma([(Sret, st_ret.rearrange("h (dc p) e -> p h dc e", p=128)), (Sgla, st_gla.rearrange("h p e -> p h e")),
           (uhf, cconv)], "stin")
    P.copy("act", Sretb, Sret); P.copy("act", Sglab, Sgla); P.copy("dve", uhb, uhf)
    run_tile("s", 0, DEC_SEQ, DEC_SEQ, True)
    seq_end("s")
    P.memset("pool", Sret, 0.0); P.memset("pool", Sretb, 0.0); P.memset("pool", Sgla, 0.0); P.memset("pool", Sglab, 0.0)
    P.memset("pool", uhb, 0.0)
    ntile = seq // 512
    for ti in range(ntile):
        run_tile("p", ti * 512, 512, 128, ti == ntile - 1)
    seq_end("p")

    P.wait_all("sp", ("yout", "so_ret", "so_gla", "so_conv"))

    with contextlib.ExitStack() as es:
        for lname, L in P.lanes.items():
            L["sem"] = es.enter_context(nc.semaphore("s_" + lname))
        block = es.enter_context(nc.Block())
        P.emit(block)
    return nc, cst, P


_CACHE = {}


def _prep_inputs(inp, seq, cst):
    f32 = lambda a: np.ascontiguousarray(np.asarray(a, np.float32))
    shared = {
        "ret_w_in": f32(inp["ret_w_in"][0]), "ret_w_out": f32(inp["ret_w_out"][0]),
        "gla_w_in": f32(inp["gla_w_in"][0]), "gla_w_out": f32(inp["gla_w_out"][0]),
        "ffn_w_up0": f32(inp["ffn_w_up"][0]), "ffn_w_up1": f32(inp["ffn_w_up"][1]),
        "ffn_w_down0": f32(inp["ffn_w_down"][0]), "ffn_w_down1": f32(inp["ffn_w_down"][1]),
    }
    nm, nf = np.asarray(inp["norm_mix"], np.float32), np.asarray(inp["norm_ffn"], np.float32)
    gains = np.stack([_fm(nm[0]), _fm(nm[1]), _fm(nf[0]), _fm(nf[1]), _fm(inp["norm_final"])], axis=1)
    shared["gains"] = np.ascontiguousarray(gains)
    shared["gn"] = _fm(np.asarray(inp["ret_gn_g"], np.float32)[0].reshape(-1))
    shared["ng"] = _fm(np.asarray(inp["gla_norm_g"], np.float32)[0].reshape(-1))
    cwv = np.asarray(inp["ffn_conv_w"], np.float32)
    shared["cw"] = np.ascontiguousarray(cwv.reshape(DEPTH, 3, NBLK_FF, 128).transpose(3, 0, 1, 2))
    cbv = np.asarray(inp["ffn_conv_b"], np.float32)
    shared["cb"] = np.ascontiguousarray(cbv.reshape(DEPTH, NBLK_FF, 128).transpose(2, 0, 1))
    shared["wa2aug"] = np.ascontiguousarray(np.concatenate(
        [np.asarray(inp["gla_w_a2"], np.float32)[0], np.asarray(inp["gla_b_a"], np.float32)[0][None, :]], axis=0))
    for k in ("ident_bf", "ones_bf", "rmask", "qdec", "kdec", "gmask", "ucum", "neghalf", "cos", "sin"):
        shared[k] = cst[k]
    xp = np.asarray(inp["x_prompt"], np.float32)
    xs = np.asarray(inp["x_sample"], np.float32)
    cc = np.asarray(inp["cache_conv"], np.float32)
    maps = []
    for b in range(NCORES):
        m = dict(shared)
        hs = seq // (-(-seq // 4096))
        for i in range(seq // hs):
            m["xTp%d" % i] = np.ascontiguousarray(xp[b, i * hs:(i + 1) * hs].T)
        m["xTs"] = np.ascontiguousarray(xs[b].T)
        m["st_ret"] = f32(inp["state_ret"][0, b])
        m["st_gla"] = f32(inp["state_gla"][0, b])
        m["cconv"] = np.ascontiguousarray(cc[:, b].reshape(DEPTH, 2, NBLK_FF, 128).transpose(3, 0, 2, 1))
        maps.append(m)
    return maps


def _run(inp, seq):
    if seq not in _CACHE:
        _CACHE[seq] = build(seq)
    nc, cst, _ = _CACHE[seq]
    maps = _prep_inputs(inp, seq, cst)
    res = run_bass_kernel_spmd(nc, maps, core_ids=list(range(NCORES)))
    R = res.results
    B = NCORES
    hs = seq // (-(-seq // 4096))
    y_p = np.stack([np.concatenate([R[b]["yTp%d" % i].T for i in range(seq // hs)], axis=0)
                    for b in range(B)]).astype(np.float32)
    y_s = np.stack([R[b]["yTs"].T for b in range(B)]).astype(np.float32)
    ret_p = np.stack([R[b]["ret_p"] for b in range(B)])[None].astype(np.float32)
    ret_s = np.stack([R[b]["ret_s"] for b in range(B)])[None].astype(np.float32)
    gla_p = np.stack([R[b]["gla_p"] for b in range(B)])[None].astype(np.float32)
    gla_s = np.stack([R[b]["gla_s"] for b in range(B)])[None].astype(np.float32)

    def conv(k):
        a = np.stack([R[b][k] for b in range(B)])
        return np.ascontiguousarray(a.transpose(2, 0, 4, 3, 1).reshape(DEPTH, B, 2, 2 * DFF)).astype(np.float32)

    return (y_p, y_s, ret_p, ret_s, gla_p, gla_s, conv("conv_p"), conv("conv_s"))


def kernel(**inputs):
    seq = int(np.asarray(inputs["x_prompt"]).shape[1])
    return _run(inputs, seq)
```

```python
import contextlib
import math
import numpy as np
import ml_dtypes
import concourse.bass as bass
import concourse.mybir as mybir
from concourse.bass_utils import run_bass_kernel_spmd

F32 = mybir.dt.float32
BF16 = mybir.dt.bfloat16
U8 = mybir.dt.uint8
AF = mybir.ActivationFunctionType
ALU = mybir.AluOpType
DSZ = {F32: 4, BF16: 2, U8: 1}

D = 1024
DEPTH = 2
RET_H = 4
RET_DK = 256
RET_DV = 512
GLA_H = 4
GLA_DK = 128
GLA_DV = 256
GLA_RANK = 16
GLA_TAU = 16.0
DFF = 2816
NBLK_FF = 2 * DFF // 128
EPS = 1e-6
ROPE_BASE = 10000.0
PAST_LEN = 2048
DEC_SEQ = 32
NCORES = 8

ARENA = 211968


class Prog:
    SBG = 256

    def __init__(self, nc):
        self.nc = nc
        self.q = {e: [] for e in ("pe", "act", "dve", "pool", "sp")}
        self.lanes = {}
        for e in ("pe", "act", "dve", "pool"):
            self.lanes[e] = {"count": 0, "inc": 1, "sem": None}
        self.clock = {e: {} for e in self.q}
        self.snap = {}
        self.gran = {}
        self.maxwait = {}
        self.nops = 0

    def lane(self, name):
        if name not in self.lanes:
            self.lanes[name] = {"count": 0, "inc": 16, "sem": None}
        return name

    def keys(self, ap):
        t = ap.tensor
        name = t.name
        if name not in ("arena", "psum"):
            return [("d", name)]
        esz = DSZ[ap.dtype]
        pairs = list(ap.ap)
        pstep = pairs[0][0]
        off = int(ap.offset)
        inpart = off % pstep if pstep else off
        starts = [inpart]
        free = [(s, c) for (s, c) in pairs[1:] if c > 1 or len(pairs) == 2]
        free = [(s, c) for (s, c) in free if s != 0]
        length = 1
        i = 0
        while i < len(free):
            s, c = free[i]
            inner_ext = sum((cc - 1) * abs(ss) for ss, cc in free[i + 1:]) + 1
            if i == len(free) - 1:
                length = (c - 1) * abs(s) + 1
            elif abs(s) > inner_ext and len(starts) * c <= 128:
                starts = [st + k * s for st in starts for k in range(c)]
            else:
                length = sum((cc - 1) * abs(ss) for ss, cc in free[i:]) + 1
                break
            i += 1
        g = self.SBG if name == "arena" else 2048
        ks = set()
        for st in starts:
            lo = (st * esz) // g
            hi = ((st + length) * esz - 1) // g
            for k in range(lo, hi + 1):
                ks.add((name, k))
        return ks

    def op(self, eng, fn, reads=(), writes=(), sig=True, lane=None, n=1, embed=True):
        self.nops += 1
        mylane = lane if lane is not None else eng
        L = self.lanes[mylane]
        raw = {}
        oth = {}

        def add(d, ls):
            l, s = ls
            if d.get(l, 0) < s:
                d[l] = s

        rkeys = set()
        for ap in reads:
            rkeys |= set(self.keys(ap))
        wkeys = set()
        for ap in writes:
            wkeys |= set(self.keys(ap))
        for k in rkeys:
            g = self.gran.get(k)
            if g is not None and g[0] is not None:
                add(raw, g[0])
        for k in wkeys:
            g = self.gran.get(k)
            if g is not None:
                if g[0] is not None:
                    add(oth, g[0])
                for l, s in g[1].items():
                    add(oth, (l, s))
        deps = {}
        for l, s in raw.items():
            if l == eng and lane is None:
                if eng == "pe":
                    continue
            add(deps, (l, s))
        for l, s in oth.items():
            if l == eng and lane is None:
                continue
            add(deps, (l, s))
        if lane is not None and L["count"] > 0:
            add(deps, (mylane, L["count"]))
        if sig:
            L["count"] += n
            seq = L["count"]
        else:
            seq = L["count"] + 1
        ck = self.clock[eng]
        waits = []
        for l, s in sorted(deps.items()):
            if ck.get(l, 0) < s:
                waits.append((l, s * self.lanes[l]["inc"]))
                if self.maxwait.get(l, 0) < s:
                    self.maxwait[l] = s
                sn = self.snap.get((l, s))
                if sn:
                    for l2, s2 in sn.items():
                        if ck.get(l2, 0) < s2:
                            ck[l2] = s2
                ck[l] = s
        if sig:
            sn = dict(ck)
            sn[mylane] = seq
            self.snap[(mylane, seq)] = sn
        for k in rkeys:
            g = self.gran.get(k)
            if g is None:
                g = [None, {}]
                self.gran[k] = g
            if g[1].get(mylane, 0) < seq:
                g[1][mylane] = seq
        for k in wkeys:
            self.gran[k] = [(mylane, seq), {}]
        self.q[eng].append((waits, fn, (mylane, L["inc"]) if sig else None, embed))

    def wait_all(self, eng, lanes):
        waits = [(l, self.lanes[l]["count"] * self.lanes[l]["inc"]) for l in lanes if self.lanes[l]["count"] > 0]
        self.q[eng].append((waits, None, None, False))

    def emit(self, block):
        for l, s in self.maxwait.items():
            assert s <= self.lanes[l]["count"], (l, s, self.lanes[l]["count"])

        def runner(name):
            items = self.q[name]
            lanes = self.lanes

            def body(e):
                for waits, fn, sig, embed in items:
                    emb = None
                    if embed and fn is not None and waits:
                        emb = waits[-1]
                        waits = waits[:-1]
                    for l, v in waits:
                        e.wait_ge(lanes[l]["sem"], v)
                    if fn is None:
                        continue
                    r = fn(e)
                    if emb is not None:
                        first = r[0] if isinstance(r, (list, tuple)) else r
                        first._wait_ge(lanes[emb[0]]["sem"], emb[1])
                    if sig is not None:
                        sem = lanes[sig[0]]["sem"]
                        if isinstance(r, (list, tuple)):
                            for ins in r:
                                ins.then_inc(sem, sig[1])
                        else:
                            r.then_inc(sem, sig[1])

            return body

        block.tensor(runner("pe"))
        block.scalar(runner("act"))
        block.vector(runner("dve"))
        block.gpsimd(runner("pool"))
        block.sync(runner("sp"))

    def mm(self, out, lhsT, rhs, start, stop, sig=None):
        self.op("pe", lambda e: e.matmul(out, lhsT, rhs, start=start, stop=stop),
                reads=[lhsT, rhs], writes=[out], sig=(stop if sig is None else sig))

    def tr(self, out, in_, ident, sig=True):
        self.op("pe", lambda e: e.transpose(out, in_, ident), reads=[in_, ident], writes=[out], sig=sig)

    def act(self, out, in_, func, bias=None, scale=None, accum=None):
        reads = [in_]
        kw = {}
        if bias is not None:
            kw["bias"] = bias
            if not isinstance(bias, (int, float)):
                reads.append(bias)
        if scale is not None:
            kw["scale"] = scale
            if not isinstance(scale, (int, float)):
                reads.append(scale)
        writes = [out]
        if accum is not None:
            kw["accum_out"] = accum
            writes.append(accum)
        self.op("act", lambda e: e.activation(out, in_, func, **kw), reads=reads, writes=writes,
                embed=(accum is None))

    def tt(self, eng, out, a, b, op):
        self.op(eng, lambda e: e.tensor_tensor(out, a, b, op), reads=[a, b], writes=[out])

    def ts(self, eng, out, a, s1, op0, s2=None, op1=None):
        reads = [a]
        if not isinstance(s1, (int, float)):
            reads.append(s1)
        if s2 is not None and not isinstance(s2, (int, float)):
            reads.append(s2)
        if op1 is None:
            self.op(eng, lambda e: e.tensor_scalar(out, a, s1, None, op0), reads=reads, writes=[out])
        else:
            self.op(eng, lambda e: e.tensor_scalar(out, a, s1, s2, op0, op1), reads=reads, writes=[out])

    def stt(self, out, in0, scalar, in1, op0, op1):
        reads = [in0, in1]
        if not isinstance(scalar, (int, float)):
            reads.append(scalar)
        self.op("dve", lambda e: e.scalar_tensor_tensor(out, in0, scalar, in1, op0, op1),
                reads=reads, writes=[out])

    def copy(self, eng, out, in_):
        if eng == "act":
            self.op("act", lambda e: e.activation(out, in_, AF.Copy), reads=[in_], writes=[out])
        else:
            self.op(eng, lambda e: e.tensor_copy(out, in_), reads=[in_], writes=[out])

    def memset(self, eng, out, val):
        self.op(eng, lambda e: e.memset(out, val), reads=[], writes=[out])

    def dma(self, pairs, lane, eng="sp", **kw):
        self.lane(lane)
        outs = [p[0] for p in pairs]
        ins = [p[1] for p in pairs]

        def fn(e):
            return [e.dma_start(out=o, in_=i, **kw) for o, i in pairs]

        self.op(eng, fn, reads=ins, writes=outs, lane=lane, n=len(pairs))


def _consts(seq):
    c = {}
    c["ident_bf"] = np.eye(128, dtype=np.float32).astype(ml_dtypes.bfloat16)
    c["ones_bf"] = np.ones((128, 128), dtype=np.float32).astype(ml_dtypes.bfloat16)
    c["ident_f"] = np.eye(128, dtype=np.float32)
    c["ones_f"] = np.ones((128, 128), dtype=np.float32)
    lg = np.log1p(-np.exp2(-5.0 - np.arange(RET_H, dtype=np.float64)))
    j = np.arange(128, dtype=np.float64)
    causalT = (j[None, :] >= j[:, None]).astype(np.float64)
    rmask = np.zeros((128, RET_H, 128), np.float64)
    for h in range(RET_H):
        rmask[:, h, :] = np.exp(-lg[h] * (j[:, None] + 1.0)) * (RET_DK ** -0.5) * causalT
    c["rmask"] = rmask.astype(np.float32)
    qd = np.exp(lg[:, None] * (j[None, :] + 1.0))
    c["qdec"] = np.broadcast_to(qd[None], (128, RET_H, 128)).astype(np.float32).copy()
    kd = np.zeros((128, 2, RET_H), np.float64)
    for li, L in enumerate((128, 32)):
        for h in range(RET_H):
            kd[:, li, h] = np.exp(lg[h] * (L - 1.0 - j)) * (RET_DK ** -0.5)
    c["kdec"] = kd.astype(np.float32)
    c["gL"] = {L: [float(np.exp(lg[h] * L)) for h in range(RET_H)] for L in (128, 32)}
    c["gmask"] = (causalT * (GLA_DK ** -0.5)).astype(np.float32)
    c["ucum"] = ((j[:, None] <= j[None, :]) * (-1.0 / GLA_TAU)).astype(np.float32)
    c["neghalf"] = np.full((128, 512), -0.5, np.float32)
    inv = (np.float32(ROPE_BASE) ** (-(np.arange(128, dtype=np.float32) / np.float32(128)))).astype(np.float32)
    pos = np.concatenate([np.arange(seq, dtype=np.float32), PAST_LEN + np.arange(DEC_SEQ, dtype=np.float32)])
    ang = (pos[None, :] * inv[:, None]).astype(np.float32)
    c["cos"] = np.cos(ang).astype(np.float32)
    c["sin"] = np.sin(ang).astype(np.float32)
    return c


def _fm(v):
    v = np.asarray(v, np.float32)
    return np.ascontiguousarray(v.reshape(-1, 128).T)


WEIGHTS = [
    ("ret_w_in", D, 6144), ("ret_w_out", 2048, D), ("gla_w_in", D, 3088), ("gla_w_out", D, D),
    ("ffn_w_up0", D, 2 * DFF), ("ffn_w_up1", D, 2 * DFF), ("ffn_w_down0", DFF, D), ("ffn_w_down1", DFF, D),
]


def build(seq, dbg=()):
    assert seq % 512 == 0
    nc = bass.Bass("TRN2", target_bir_lowering=False)
    P = Prog(nc)
    cst = _consts(seq)
    gL = cst["gL"]
    npos = seq + DEC_SEQ

    def din(name, shape, dt=F32):
        return nc.dram_tensor(name, list(shape), dt, kind="ExternalInput").ap()

    def dout(name, shape, dt=F32):
        return nc.dram_tensor(name, list(shape), dt, kind="ExternalOutput").ap()

    NHALF = -(-seq // 4096)
    HSEQ = seq // NHALF
    assert HSEQ * NHALF == seq and HSEQ % 512 == 0
    xTp = [din("xTp%d" % i, [D, HSEQ]) for i in range(NHALF)]; xTs = din("xTs", [D, DEC_SEQ])
    st_ret = din("st_ret", [RET_H, RET_DK, RET_DV]); st_gla = din("st_gla", [GLA_H, GLA_DK, GLA_DV])
    cconv = din("cconv", [128, DEPTH, NBLK_FF, 2])
    W32 = {n: din(n, [k, m]) for n, k, m in WEIGHTS}
    Wb = {n: nc.dram_tensor(n + "_bf", [k, m], BF16, kind="Internal").ap() for n, k, m in WEIGHTS}
    d_gains = din("gains", [128, 5, 8])
    d_gn = din("gn", [128, 16]); d_ng = din("ng", [128, 8])
    d_cw = din("cw", [128, DEPTH, 3, NBLK_FF]); d_cb = din("cb", [128, DEPTH, NBLK_FF])
    d_wa2 = din("wa2aug", [17, 512])
    d_ident = din("ident_bf", [128, 128], BF16); d_ones = din("ones_bf", [128, 128], BF16)
    d_rmask = din("rmask", [128, RET_H, 128]); d_qdec = din("qdec", [128, RET_H, 128])
    d_kdec = din("kdec", [128, 2, RET_H]); d_gmask = din("gmask", [128, 128]); d_ucum = din("ucum", [128, 128])
    d_neghalf = din("neghalf", [128, 512])
    d_identf = din("ident_f", [128, 128]); d_onesf = din("ones_f", [128, 128])
    d_cos = din("cos", [128, npos]); d_sin = din("sin", [128, npos])

    yTp = [dout("yTp%d" % i, [D, HSEQ]) for i in range(NHALF)]; yTs = dout("yTs", [D, DEC_SEQ])
    o_ret = {"p": dout("ret_p", [RET_H, RET_DK, RET_DV]), "s": dout("ret_s", [RET_H, RET_DK, RET_DV])}
    o_gla = {"p": dout("gla_p", [GLA_H, GLA_DK, GLA_DV]), "s": dout("gla_s", [GLA_H, GLA_DK, GLA_DV])}
    o_conv = {"p": dout("conv_p", [128, DEPTH, NBLK_FF, 2]), "s": dout("conv_s", [128, DEPTH, NBLK_FF, 2])}
    dbg_out = {}

    arena_h = nc.alloc_sbuf_tensor("arena", [128, ARENA], U8)
    arena = arena_h.ap()
    psum_h = nc.alloc_psum_tensor("psum", [128, 8, 512], F32)
    psum = psum_h.ap()

    off = {"_": 0}
    reg = {}

    def region(name, nbytes):
        assert nbytes % 256 == 0, name
        reg[name] = (off["_"], nbytes)
        off["_"] += nbytes
        assert off["_"] <= ARENA, (name, off["_"])

    def view(name, dt, shape, boff=0, parts=128):
        o, nb = reg[name]
        n = int(np.prod(shape)) * DSZ[dt]
        assert boff + n <= nb, (name, boff, n, nb)
        ap = arena[0:parts, o + boff:o + boff + n].bitcast(dt)
        if len(shape) == 2:
            ap = ap.rearrange("p (a b) -> p a b", a=shape[0])
        elif len(shape) == 3:
            ap = ap.rearrange("p (a b c) -> p a b c", a=shape[0], b=shape[1])
        return ap

    region("x", 16384); region("hT", 8192); region("ms", 2048); region("rstd", 2048)
    region("identf", 512); region("onesf", 512)
    region("qkk", 24576)
    region("vtok", 16384)
    region("sgT", 16384)
    region("ogT", 16384)
    region("mix", 20480)
    region("PT", 1024); region("on", 4096); region("stats", 1024); region("stmp", 2048); region("aT", 1024)
    region("Sret", 16384); region("Sretb", 8192); region("Sgla", 4096); region("Sglab", 2048)
    region("wring", 3 * 8192)
    for nme, nb in [("ident", 256), ("ones", 256), ("rmask", 2048), ("qdec", 2048), ("kdec", 256), ("gmask", 512),
                    ("ucum", 512), ("neghalf", 2048), ("gains", 256), ("gn", 256), ("ng", 256), ("cw", 1280),
                    ("cb", 512), ("wa2", 1024), ("uhb", 512), ("uhf", 768)]:
        region(nme, nb)

    ident = view("ident", BF16, [128]); ones = view("ones", BF16, [128])
    identf = view("identf", F32, [128]); onesf = view("onesf", F32, [128])
    rmask = view("rmask", F32, [RET_H, 128]); qdec = view("qdec", F32, [RET_H, 128])
    kdec = view("kdec", F32, [2, RET_H]); gmask = view("gmask", F32, [128]); ucum = view("ucum", F32, [128])
    neghalf = view("neghalf", F32, [512])
    gains = view("gains", F32, [5, 8]); gnT = view("gn", F32, [16]); ngT = view("ng", F32, [8])
    cw = view("cw", F32, [DEPTH, 3, NBLK_FF]); cb = view("cb", F32, [DEPTH, NBLK_FF])
    wa2 = view("wa2", BF16, [512], parts=32)
    uhb = view("uhb", BF16, [DEPTH, NBLK_FF, 2]); uhf = view("uhf", F32, [DEPTH, NBLK_FF, 2])
    Sret = view("Sret", F32, [RET_H, 2, 512]); Sretb = view("Sretb", BF16, [RET_H, 2, 512])
    Sgla = view("Sgla", F32, [GLA_H, 256]); Sglab = view("Sglab", BF16, [GLA_H, 256])

    psb = [psum[:, b, :] for b in range(8)]
    psb_bf = [psum[:, b, :].bitcast(BF16) for b in range(8)]
    bank_ctr = {"i": 0}

    def bank():
        b = bank_ctr["i"] % 8
        bank_ctr["i"] += 1
        return b

    P.dma([(ident, d_ident), (ones, d_ones), (rmask, d_rmask), (qdec, d_qdec), (kdec, d_kdec), (gmask, d_gmask),
           (ucum, d_ucum), (neghalf, d_neghalf), (identf, d_identf), (onesf, d_onesf), (gains, d_gains), (gnT, d_gn), (ngT, d_ng), (cw, d_cw), (cb, d_cb)],
          "const")
    P.dma([(wa2[0:17, :], d_wa2)], "wa2c", eng="pool")
    for n, k, m in WEIGHTS:
        P.dma([(Wb[n], W32[n])], "cast_" + n, eng="pool", max_dma_last_dim=4096)

    ring = {"i": 0}

    def slab(wname, kc0, nkc, colgroups):
        s = ring["i"] % 3
        ring["i"] += 1
        ncols = sum(c for _, c in colgroups)
        assert nkc * ncols * 2 <= 8192
        v = view("wring", BF16, [nkc, ncols], boff=s * 8192)
        pairs = []
        co = 0
        for c0, cn in colgroups:
            src = Wb[wname][kc0 * 128:(kc0 + nkc) * 128, c0:c0 + cn].rearrange("(k p) n -> p k n", p=128)
            pairs.append((v[:, :, co:co + cn], src))
            co += cn
        P.dma(pairs, "w%d" % s)
        return v

    aT_all = view("aT", BF16, [512], parts=32)
    P.memset("dve", aT_all, 1.0)

    def run_tile(sk, t0, NT, CL, last):
        NCH = NT // CL
        li = 0 if CL == 128 else 1
        xsrc = xTp[t0 // HSEQ] if sk == "p" else xTs
        ydst = yTp[t0 // HSEQ] if sk == "p" else yTs
        tq = t0 % HSEQ if sk == "p" else t0
        pos0 = t0 if sk == "p" else seq + t0
        x = view("x", F32, [8, NT])
        hT = view("hT", BF16, [8, NT])
        sq = view("ogT", BF16, [8, NT])
        ms = view("ms", F32, [4]); rstd = view("ms", F32, [4], boff=256)
        rbc = view("rstd", F32, [4, 128])
        qT = view("qkk", BF16, [8, NT]); kT = view("qkk", BF16, [8, NT], boff=8192)
        ktok = view("qkk", BF16, [NCH, 1024], boff=16384)
        vtok = view("vtok", BF16, [NCH, 2048])
        sgT = view("sgT", BF16, [16, NT]); ogT = view("ogT", BF16, [16, NT])
        PTv = [view("PT", BF16, [128], boff=i * 256) for i in range(4)]
        onv = [view("on", BF16, [512], boff=i * 1024) for i in range(3)]
        statv = [(view("stats", F32, [6], boff=i * 256), view("stats", F32, [2], boff=i * 256 + 64),
                  view("stats", F32, [1], boff=i * 256 + 128), view("stats", F32, [1], boff=i * 256 + 192))
                 for i in range(4)]
        stmp = [view("stmp", BF16, [NT], boff=i * 1024) for i in range(2)]
        cnt = {"pt": 0, "on": 0, "st": 0, "sv": 0}

        def tsl(ch):
            return slice(ch * CL, (ch + 1) * CL)

        P.dma([(x, xsrc[:, tq:tq + NT].rearrange("(c p) t -> p c t", p=128))], "xin")

        def norm(gidx, out_hT=True):
            TB = min(128, NT)
            NTB = NT // TB
            P.act(sq, x, AF.Square)
            b = bank()
            for tb in range(NTB):
                for c in range(8):
                    P.mm(psb[b][0:TB, tb:tb + 1], sq[:, c, tb * TB:(tb + 1) * TB], ones[:, 0:1],
                         start=(c == 0), stop=(c == 7), sig=(c == 7 and tb == NTB - 1))
            P.ts("dve", ms[0:TB, 0:NTB], psb[b][0:TB, 0:NTB], 1.0 / D, ALU.mult, EPS, ALU.add)
            P.tt("pool", rstd[0:TB, 0:NTB], ms[0:TB, 0:NTB], neghalf[0:TB, 0:NTB], ALU.pow)
            b2 = bank()
            for tb in range(NTB):
                P.ts("dve", rbc[0:TB, tb, :], onesf[0:TB, :], rstd[0:TB, tb:tb + 1], ALU.mult)
                P.mm(psb[b2][:, tb * TB:(tb + 1) * TB], rbc[0:TB, tb, :], identf[0:TB, 0:TB],
                     start=True, stop=True, sig=(tb == NTB - 1))
            for c in range(8):
                dst = hT[:, c, :] if out_hT else x[:, c, :]
                P.stt(dst, x[:, c, :], gains[:, gidx, c:c + 1], psb[b2][:, 0:NT], ALU.mult, ALU.mult)

        def resid_proj(wname, nkc, src, kslabs):
            for cg in range(4):
                banks = [bank(), bank()]
                k0 = 0
                while k0 < nkc:
                    nk = min(kslabs, nkc - k0)
                    sl = slab(wname, k0, nk, [(cg * 256, 256)])
                    for j in range(2):
                        for kk in range(nk):
                            kc = k0 + kk
                            P.mm(psb[banks[j]][:, 0:NT], sl[:, kk, j * 128:(j + 1) * 128], src[:, kc, :],
                                 start=(kc == 0), stop=(kc == nkc - 1),
                                 sig=(kc == nkc - 1) or (kk == nk - 1 and j == 1))
                    k0 += nk
                for j in range(2):
                    blk = cg * 2 + j
                    P.tt("dve", x[:, blk, :], x[:, blk, :], psb[banks[j]][:, 0:NT], ALU.add)

        def retention():
            cs = view("mix", F32, [2, NT])
            cqs = [view("mix", F32, [2, NT], boff=4096 + i * 4096) for i in range(2)]
            rt = view("mix", F32, [4, NT], boff=12288)
            P.dma([(cs[:, 0, :], d_cos[:, pos0:pos0 + NT]), (cs[:, 1, :], d_sin[:, pos0:pos0 + NT])], "cs")

            def rotary(pa, pb, ct, st_, o1, o2):
                P.tt("dve", rt[:, 0, :], pa, ct, ALU.mult)
                P.tt("dve", rt[:, 1, :], pb, st_, ALU.mult)
                P.tt("dve", o1, rt[:, 0, :], rt[:, 1, :], ALU.subtract)
                P.tt("dve", rt[:, 2, :], pa, st_, ALU.mult)
                P.tt("dve", rt[:, 3, :], pb, ct, ALU.mult)
                P.tt("dve", o2, rt[:, 2, :], rt[:, 3, :], ALU.add)

            for which in range(2):
                dstT = qT if which == 0 else kT
                for sh in range(2):
                    sl = slab("ret_w_in", 0, 8, [(which * 1024 + sh * 512, 512)])
                    for hh in range(2):
                        h = sh * 2 + hh
                        ba, bb = bank(), bank()
                        for half, bk in ((0, ba), (1, bb)):
                            for kc in range(8):
                                P.mm(psb[bk][:, 0:NT], sl[:, kc, (hh * 2 + half) * 128:(hh * 2 + half + 1) * 128],
                                     hT[:, kc, :], start=(kc == 0), stop=(kc == 7))
                        if which == 0:
                            cq = cqs[h % 2]
                            qd = qdec[:, h, 0:CL].unsqueeze(1).to_broadcast([128, NCH, CL])
                            for t_ in range(2):
                                P.tt("pool", cq[:, t_, :].rearrange("p (a b) -> p a b", a=NCH),
                                     cs[:, t_, :].rearrange("p (a b) -> p a b", a=NCH), qd, ALU.mult)
                            rotary(psb[ba][:, 0:NT], psb[bb][:, 0:NT], cq[:, 0, :], cq[:, 1, :],
                                   dstT[:, 2 * h, :], dstT[:, 2 * h + 1, :])
                        else:
                            rotary(psb[ba][:, 0:NT], psb[bb][:, 0:NT], cs[:, 0, :], cs[:, 1, :],
                                   dstT[:, 2 * h, :], dstT[:, 2 * h + 1, :])
            for h in range(RET_H):
                for ch in range(NCH):
                    b = bank()
                    for dc in range(2):
                        P.tr(psb_bf[b][0:CL, dc * 128:(dc + 1) * 128], kT[:, 2 * h + dc, tsl(ch)], ident,
                             sig=(dc == 1))
                    P.act(ktok[0:CL, ch, h * 256:(h + 1) * 256], psb_bf[b][0:CL, 0:256], AF.Copy,
                          scale=kdec[0:CL, li, h:h + 1])
            for h in range(RET_H):
                sl = slab("ret_w_in", 0, 8, [(2048 + h * 512, 512)])
                for ch in range(NCH):
                    b = bank()
                    for kc in range(8):
                        P.mm(psb[b][0:CL, :], hT[:, kc, tsl(ch)], sl[:, kc, :], start=(kc == 0), stop=(kc == 7))
                    P.copy("act", vtok[0:CL, ch, h * 512:(h + 1) * 512], psb[b][0:CL, :])
            for s4 in range(4):
                sl = slab("ret_w_in", 0, 8, [(4096 + s4 * 512, 512)])
                for j in range(4):
                    blk = s4 * 4 + j
                    b = bank()
                    for kc in range(8):
                        P.mm(psb[b][:, 0:NT], sl[:, kc, j * 128:(j + 1) * 128], hT[:, kc, :],
                             start=(kc == 0), stop=(kc == 7))
                    tmp = stmp[cnt["st"] % 2]; cnt["st"] += 1
                    P.act(tmp, psb[b][:, 0:NT], AF.Silu)
                    P.ts("dve", sgT[:, blk, :], tmp, gnT[:, blk:blk + 1], ALU.mult)
            for ch in range(NCH):
                for h in range(RET_H):
                    bs = bank()
                    for dc in range(2):
                        P.mm(psb[bs][0:CL, 0:CL], kT[:, 2 * h + dc, tsl(ch)], qT[:, 2 * h + dc, tsl(ch)],
                             start=(dc == 0), stop=(dc == 1))
                    PT = PTv[cnt["pt"] % 4]; cnt["pt"] += 1
                    P.tt("dve", PT[0:CL, 0:CL], psb[bs][0:CL, 0:CL], rmask[0:CL, h, 0:CL], ALU.mult)
                    bo = bank()
                    P.mm(psb[bo][0:CL, :], PT[0:CL, 0:CL], vtok[0:CL, ch, h * 512:(h + 1) * 512], start=True, stop=False)
                    for dc in range(2):
                        P.mm(psb[bo][0:CL, :], qT[:, 2 * h + dc, tsl(ch)], Sretb[:, h, dc, :], start=False, stop=(dc == 1))
                    for dc in range(2):
                        bS = bank()
                        P.mm(psb[bS][:, :], ktok[0:CL, ch, h * 256 + dc * 128:h * 256 + (dc + 1) * 128],
                             vtok[0:CL, ch, h * 512:(h + 1) * 512], start=True, stop=True)
                        P.stt(Sret[:, h, dc, :], Sret[:, h, dc, :], gL[CL][h], psb[bS][:, :], ALU.mult, ALU.add)
                        P.copy("act", Sretb[:, h, dc, :], Sret[:, h, dc, :])
                    stats6, mv, ve, rs = statv[cnt["sv"] % 4]; cnt["sv"] += 1
                    P.op("dve", lambda e, o=stats6[0:CL, :], i=psb[bo][0:CL, :]: e.bn_stats(o, i),
                         reads=[psb[bo][0:CL, :]], writes=[stats6[0:CL, :]])
                    P.op("dve", lambda e, o=mv[0:CL, :], i=stats6[0:CL, :]: e.bn_aggr(o, i),
                         reads=[stats6[0:CL, :]], writes=[mv[0:CL, :]])
                    P.ts("dve", ve[0:CL, :], mv[0:CL, 1:2], EPS, ALU.add)
                    P.tt("pool", rs[0:CL, :], ve[0:CL, :], neghalf[0:CL, 0:1], ALU.pow)
                    on = onv[cnt["on"] % 3]; cnt["on"] += 1
                    P.ts("dve", on[0:CL, :], psb[bo][0:CL, :], mv[0:CL, 0:1], ALU.subtract, rs[0:CL, :], ALU.mult)
                    bt = bank()
                    for eb in range(4):
                        P.tr(psb_bf[bt][:, eb * CL:(eb + 1) * CL], on[0:CL, eb * 128:(eb + 1) * 128],
                             ident[0:CL, 0:CL], sig=(eb == 3))
                    P.tt("dve", ogT[:, 4 * h:4 * h + 4, tsl(ch)],
                         psb_bf[bt][:, 0:4 * CL].rearrange("p (a b) -> p a b", a=4),
                         sgT[:, 4 * h:4 * h + 4, tsl(ch)], ALU.mult)
            resid_proj("ret_w_out", 16, ogT, 16)

        def ffn(layer):
            actT = view("qkk", BF16, [22, NT])
            ubuf = [[view("sgT", BF16, [NT + 2], boff=(r * 2 + gv) * 1280) for gv in range(2)] for r in range(2)]
            sgt = [view("sgT", BF16, [NT], boff=5120 + r * 1024) for r in range(2)]
            dg = [[view("sgT", BF16, [3, 128], boff=7168 + (r * 2 + gv) * 768) for gv in range(2)] for r in range(2)]
            wn = "ffn_w_up%d" % layer
            it = 0
            for s in range(11):
                sl = slab(wn, 0, 8, [(s * 256, 256), (DFF + s * 256, 256)])
                for jj in range(2):
                    gb = s * 2 + jj
                    r = it % 2; it += 1
                    bks = []
                    for gv in range(2):
                        blk = gb + gv * 22
                        b = bank(); bks.append(b)
                        for kc in range(8):
                            P.mm(psb[b][:, 0:NT], sl[:, kc, gv * 256 + jj * 128:gv * 256 + (jj + 1) * 128], hT[:, kc, :],
                                 start=(kc == 0), stop=(kc == 7))
                        ub = ubuf[r][gv]
                        P.copy("pool", ub[:, 0:2], uhb[:, layer, blk, :])
                        P.copy("act", ub[:, 2:2 + NT], psb[b][:, 0:NT])
                        P.copy("pool", uhb[:, layer, blk, :], ub[:, NT:NT + 2])
                        if last:
                            P.copy("dve", uhf[:, layer, blk, :], psb[b][:, NT - 2:NT])
                        for j in range(3):
                            P.ts("pool", dg[r][gv][:, j, :], ident, cw[:, layer, j, blk:blk + 1], ALU.mult, 1.0, ALU.mult)
                    cbk = []
                    for gv in range(2):
                        b = bank(); cbk.append(b)
                        for j in range(3):
                            P.mm(psb[b][:, 0:NT], dg[r][gv][:, j, :], ubuf[r][gv][:, j:j + NT], start=(j == 0), stop=(j == 2))
                    P.act(sgt[r], psb[cbk[0]][:, 0:NT], AF.Silu, bias=cb[:, layer, gb:gb + 1])
                    P.stt(actT[:, gb, :], psb[cbk[1]][:, 0:NT], cb[:, layer, gb + 22:gb + 23], sgt[r], ALU.add, ALU.mult)
            resid_proj("ffn_w_down%d" % layer, 22, actT, 11)

        def gla():
            vt = view("vtok", BF16, [NCH, 1024])
            kbar = view("qkk", BF16, [NCH, 512], boff=16384)
            sp = view("mix", F32, [NCH, 512])
            Eq = view("mix", F32, [GLA_H, NT], boff=8192)
            Ek = [view("mix", F32, [NT], boff=16384 + i * 2048) for i in range(2)]
            zt = view("on", F32, [512])
            kbt = view("stmp", BF16, [NT])
            sl = slab("gla_w_in", 0, 8, [(3072, 16)])
            b = bank()
            for kc in range(8):
                P.mm(psb[b][0:16, 0:NT], sl[:, kc, 0:16], hT[:, kc, :], start=(kc == 0), stop=(kc == 7))
            P.copy("act", aT_all[0:16, 0:NT], psb[b][0:16, 0:NT])
            for ch in range(NCH):
                b = bank()
                P.mm(psb[b][0:CL, :], aT_all[0:17, tsl(ch)], wa2[0:17, :], start=True, stop=True)
                P.act(zt[0:CL, :], psb[b][0:CL, :], AF.Exp, scale=-1.0)
                P.act(sp[0:CL, ch, :], zt[0:CL, :], AF.Ln, bias=1.0)
            for h in range(GLA_H):
                b = bank()
                for ch in range(NCH):
                    P.mm(psb[b][:, tsl(ch)], sp[0:CL, ch, h * 128:(h + 1) * 128], ucum[0:CL, 0:CL], start=True, stop=True,
                         sig=(ch == NCH - 1))
                P.act(Eq[:, h, :], psb[b][:, 0:NT], AF.Exp)
                P.act(Ek[h % 2], psb[b][:, 0:NT], AF.Exp, scale=-1.0)
                if h == 0:
                    slq = slab("gla_w_in", 0, 8, [(0, 512)])
                    slk = slab("gla_w_in", 0, 8, [(512, 512)])
                bq = bank()
                for kc in range(8):
                    P.mm(psb[bq][:, 0:NT], slq[:, kc, h * 128:(h + 1) * 128], hT[:, kc, :], start=(kc == 0), stop=(kc == 7))
                P.tt("dve", qT[:, h, :], psb[bq][:, 0:NT], Eq[:, h, :], ALU.mult)
                bk = bank()
                for kc in range(8):
                    P.mm(psb[bk][:, 0:NT], slk[:, kc, h * 128:(h + 1) * 128], hT[:, kc, :], start=(kc == 0), stop=(kc == 7),
                         sig=True)
                P.tt("dve", kT[:, h, :], psb[bk][:, 0:NT], Ek[h % 2], ALU.mult)
                for ch in range(NCH):
                    P.ts("dve", kbt[:, tsl(ch)], kT[:, h, tsl(ch)], Eq[:, h, (ch + 1) * CL - 1:(ch + 1) * CL], ALU.mult,
                         GLA_DK ** -0.5, ALU.mult)
                    bt = bank()
                    P.tr(psb_bf[bt][0:CL, 0:128], kbt[:, tsl(ch)], ident)
                    P.copy("act", kbar[0:CL, ch, h * 128:(h + 1) * 128], psb_bf[bt][0:CL, 0:128])
            for s2 in range(2):
                sl = slab("gla_w_in", 0, 8, [(1024 + s2 * 512, 512)])
                for ch in range(NCH):
                    b = bank()
                    for kc in range(8):
                        P.mm(psb[b][0:CL, :], hT[:, kc, tsl(ch)], sl[:, kc, :], start=(kc == 0), stop=(kc == 7))
                    P.copy("act", vt[0:CL, ch, s2 * 512:(s2 + 1) * 512], psb[b][0:CL, :])
            for s2 in range(2):
                sl = slab("gla_w_in", 0, 8, [(2048 + s2 * 512, 512)])
                for j in range(4):
                    blk = s2 * 4 + j
                    b = bank()
                    for kc in range(8):
                        P.mm(psb[b][:, 0:NT], sl[:, kc, j * 128:(j + 1) * 128], hT[:, kc, :], start=(kc == 0), stop=(kc == 7))
                    tmp = stmp[cnt["st"] % 2]; cnt["st"] += 1
                    P.act(tmp, psb[b][:, 0:NT], AF.Silu)
                    P.ts("dve", sgT[:, blk, :], tmp, ngT[:, blk:blk + 1], ALU.mult)
            junk = view("on", BF16, [256], boff=2048)
            for ch in range(NCH):
                for h in range(GLA_H):
                    ba = bank()
                    P.mm(psb[ba][0:CL, 0:CL], kT[:, h, tsl(ch)], qT[:, h, tsl(ch)], start=True, stop=True)
                    PT = PTv[cnt["pt"] % 4]; cnt["pt"] += 1
                    P.tt("dve", PT[0:CL, 0:CL], psb[ba][0:CL, 0:CL], gmask[0:CL, 0:CL], ALU.mult)
                    bo = bank()
                    P.mm(psb[bo][0:CL, 0:256], PT[0:CL, 0:CL], vt[0:CL, ch, h * 256:(h + 1) * 256], start=True, stop=False)
                    P.mm(psb[bo][0:CL, 0:256], qT[:, h, tsl(ch)], Sglab[:, h, :], start=False, stop=True)
                    bS = bank()
                    P.mm(psb[bS][:, 0:256], kbar[0:CL, ch, h * 128:(h + 1) * 128], vt[0:CL, ch, h * 256:(h + 1) * 256],
                         start=True, stop=True)
                    P.stt(Sgla[:, h, :], Sgla[:, h, :], Eq[:, h, (ch + 1) * CL - 1:(ch + 1) * CL], psb[bS][:, 0:256],
                          ALU.mult, ALU.add)
                    P.copy("act", Sglab[:, h, :], Sgla[:, h, :])
                    stats6, mv, ve, rs = statv[cnt["sv"] % 4]; cnt["sv"] += 1
                    P.act(junk[0:CL, :], psb[bo][0:CL, 0:256], AF.Square, accum=ve[0:CL, :])
                    P.ts("dve", mv[0:CL, 0:1], ve[0:CL, :], 1.0 / GLA_DV, ALU.mult, EPS, ALU.add)
                    P.tt("pool", rs[0:CL, :], mv[0:CL, 0:1], neghalf[0:CL, 0:1], ALU.pow)
                    on = view("on", BF16, [256], boff=2560 + (cnt["on"] % 2) * 512); cnt["on"] += 1
                    P.ts("dve", on[0:CL, :], psb[bo][0:CL, 0:256], rs[0:CL, :], ALU.mult)
                    bt = bank()
                    for eb in range(2):
                        P.tr(psb_bf[bt][:, eb * CL:(eb + 1) * CL], on[0:CL, eb * 128:(eb + 1) * 128], ident[0:CL, 0:CL],
                             sig=(eb == 1))
                    P.tt("dve", ogT[:, 2 * h:2 * h + 2, tsl(ch)],
                         psb_bf[bt][:, 0:2 * CL].rearrange("p (a b) -> p a b", a=2),
                         sgT[:, 2 * h:2 * h + 2, tsl(ch)], ALU.mult)
            resid_proj("gla_w_out", 8, ogT, 8)

        norm(0); retention()
        norm(2); ffn(0)
        norm(1); gla()
        norm(3); ffn(1)
        norm(4, out_hT=False)
        P.dma([(ydst[:, tq:tq + NT].rearrange("(c p) t -> p c t", p=128), x)], "yout")

    def seq_end(sk):
        P.dma([(o_ret[sk].rearrange("h (dc p) e -> p h dc e", p=128), Sret)], "so_ret")
        P.dma([(o_gla[sk].rearrange("h p e -> p h e"), Sgla)], "so_gla")
        P.dma([(o_conv[sk], uhf)], "so_conv")

    P.dma([(Sret, st_ret.rearrange("h (dc p) e -> p h dc e", p=128)), (Sgla, st_gla.rearrange("h p e -> p h e")),
           (uhf, cconv)], "stin")
    P.copy("act", Sretb, Sret); P.copy("act", Sglab, Sgla); P.copy("dve", uhb, uhf)
    run_tile("s", 0, DEC_SEQ, DEC_SEQ, True)
    seq_end("s")
    P.memset("pool", Sret, 0.0); P.memset("pool", Sretb, 0.0); P.memset("pool", Sgla, 0.0); P.memset("pool", Sglab, 0.0)
    P.memset("pool", uhb, 0.0)
    ntile = seq // 512
    for ti in range(ntile):
        run_tile("p", ti * 512, 512, 128, ti == ntile - 1)
    seq_end("p")

    P.wait_all("sp", ("yout", "so_ret", "so_gla", "so_conv"))

    with contextlib.ExitStack() as es:
        for lname, L in P.lanes.items():
            L["sem"] = es.enter_context(nc.semaphore("s_" + lname))
        block = es.enter_context(nc.Block())
        P.emit(block)
    return nc, cst, P


_CACHE = {}


def _prep_inputs(inp, seq, cst):
    f32 = lambda a: np.ascontiguousarray(np.asarray(a, np.float32))
    shared = {
        "ret_w_in": f32(inp["ret_w_in"][0]), "ret_w_out": f32(inp["ret_w_out"][0]),
        "gla_w_in": f32(inp["gla_w_in"][0]), "gla_w_out": f32(inp["gla_w_out"][0]),
        "ffn_w_up0": f32(inp["ffn_w_up"][0]), "ffn_w_up1": f32(inp["ffn_w_up"][1]),
        "ffn_w_down0": f32(inp["ffn_w_down"][0]), "ffn_w_down1": f32(inp["ffn_w_down"][1]),
    }
    nm, nf = np.asarray(inp["norm_mix"], np.float32), np.asarray(inp["norm_ffn"], np.float32)
    gains = np.stack([_fm(nm[0]), _fm(nm[1]), _fm(nf[0]), _fm(nf[1]), _fm(inp["norm_final"])], axis=1)
    shared["gains"] = np.ascontiguousarray(gains)
    shared["gn"] = _fm(np.asarray(inp["ret_gn_g"], np.float32)[0].reshape(-1))
    shared["ng"] = _fm(np.asarray(inp["gla_norm_g"], np.float32)[0].reshape(-1))
    cwv = np.asarray(inp["ffn_conv_w"], np.float32)
    shared["cw"] = np.ascontiguousarray(cwv.reshape(DEPTH, 3, NBLK_FF, 128).transpose(3, 0, 1, 2))
    cbv = np.asarray(inp["ffn_conv_b"], np.float32)
    shared["cb"] = np.ascontiguousarray(cbv.reshape(DEPTH, NBLK_FF, 128).transpose(2, 0, 1))
    shared["wa2aug"] = np.ascontiguousarray(np.concatenate(
        [np.asarray(inp["gla_w_a2"], np.float32)[0], np.asarray(inp["gla_b_a"], np.float32)[0][None, :]], axis=0))
    for k in ("ident_bf", "ones_bf", "ident_f", "ones_f", "rmask", "qdec", "kdec", "gmask", "ucum", "neghalf", "cos", "sin"):
        shared[k] = cst[k]
    xp = np.asarray(inp["x_prompt"], np.float32)
    xs = np.asarray(inp["x_sample"], np.float32)
    cc = np.asarray(inp["cache_conv"], np.float32)
    maps = []
    for b in range(NCORES):
        m = dict(shared)
        hs = seq // (-(-seq // 4096))
        for i in range(seq // hs):
            m["xTp%d" % i] = np.ascontiguousarray(xp[b, i * hs:(i + 1) * hs].T)
        m["xTs"] = np.ascontiguousarray(xs[b].T)
        m["st_ret"] = f32(inp["state_ret"][0, b])
        m["st_gla"] = f32(inp["state_gla"][0, b])
        m["cconv"] = np.ascontiguousarray(cc[:, b].reshape(DEPTH, 2, NBLK_FF, 128).transpose(3, 0, 2, 1))
        maps.append(m)
    return maps


def _run(inp, seq):
    if seq not in _CACHE:
        _CACHE[seq] = build(seq)
    nc, cst, _ = _CACHE[seq]
    maps = _prep_inputs(inp, seq, cst)
    res = run_bass_kernel_spmd(nc, maps, core_ids=list(range(NCORES)))
    R = res.results
    B = NCORES
    hs = seq // (-(-seq // 4096))
    y_p = np.stack([np.concatenate([R[b]["yTp%d" % i].T for i in range(seq // hs)], axis=0)
                    for b in range(B)]).astype(np.float32)
    y_s = np.stack([R[b]["yTs"].T for b in range(B)]).astype(np.float32)
    ret_p = np.stack([R[b]["ret_p"] for b in range(B)])[None].astype(np.float32)
    ret_s = np.stack([R[b]["ret_s"] for b in range(B)])[None].astype(np.float32)
    gla_p = np.stack([R[b]["gla_p"] for b in range(B)])[None].astype(np.float32)
    gla_s = np.stack([R[b]["gla_s"] for b in range(B)])[None].astype(np.float32)

    def conv(k):
        a = np.stack([R[b][k] for b in range(B)])
        return np.ascontiguousarray(a.transpose(2, 0, 4, 3, 1).reshape(DEPTH, B, 2, 2 * DFF)).astype(np.float32)

    return (y_p, y_s, ret_p, ret_s, gla_p, gla_s, conv("conv_p"), conv("conv_s"))


def kernel(**inputs):
    seq = int(np.asarray(inputs["x_prompt"]).shape[1])
    return _run(inputs, seq)
```

```python
import contextlib
import math
import numpy as np
import ml_dtypes
import concourse.bass as bass
import concourse.mybir as mybir
from concourse.bass_utils import run_bass_kernel_spmd

F32 = mybir.dt.float32
BF16 = mybir.dt.bfloat16
U8 = mybir.dt.uint8
AF = mybir.ActivationFunctionType
ALU = mybir.AluOpType
DSZ = {F32: 4, BF16: 2, U8: 1}

D = 1024
DEPTH = 2
RET_H = 4
RET_DK = 256
RET_DV = 512
GLA_H = 4
GLA_DK = 128
GLA_DV = 256
GLA_RANK = 16
GLA_TAU = 16.0
DFF = 2816
NBLK_FF = 2 * DFF // 128
EPS = 1e-6
ROPE_BASE = 10000.0
PAST_LEN = 2048
DEC_SEQ = 32
NCORES = 8

ARENA = 211968


class Prog:
    SBG = 256

    def __init__(self, nc):
        self.nc = nc
        self.q = {e: [] for e in ("pe", "act", "dve", "pool", "sp")}
        self.lanes = {}
        for e in ("pe", "act", "dve", "pool"):
            self.lanes[e] = {"count": 0, "inc": 1, "sem": None}
        self.clock = {e: {} for e in self.q}
        self.snap = {}
        self.gran = {}
        self.maxwait = {}
        self.nops = 0

    def lane(self, name):
        if name not in self.lanes:
            self.lanes[name] = {"count": 0, "inc": 16, "sem": None}
        return name

    def keys(self, ap):
        t = ap.tensor
        name = t.name
        if name not in ("arena", "psum"):
            return [("d", name)]
        esz = DSZ[ap.dtype]
        pairs = list(ap.ap)
        pstep = pairs[0][0]
        off = int(ap.offset)
        inpart = off % pstep if pstep else off
        starts = [inpart]
        free = [(s, c) for (s, c) in pairs[1:] if c > 1 or len(pairs) == 2]
        free = [(s, c) for (s, c) in free if s != 0]
        length = 1
        i = 0
        while i < len(free):
            s, c = free[i]
            inner_ext = sum((cc - 1) * abs(ss) for ss, cc in free[i + 1:]) + 1
            if i == len(free) - 1:
                length = (c - 1) * abs(s) + 1
            elif abs(s) > inner_ext and len(starts) * c <= 128:
                starts = [st + k * s for st in starts for k in range(c)]
            else:
                length = sum((cc - 1) * abs(ss) for ss, cc in free[i:]) + 1
                break
            i += 1
        g = self.SBG if name == "arena" else 2048
        ks = set()
        for st in starts:
            lo = (st * esz) // g
            hi = ((st + length) * esz - 1) // g
            for k in range(lo, hi + 1):
                ks.add((name, k))
        return ks

    def op(self, eng, fn, reads=(), writes=(), sig=True, lane=None, n=1, embed=True):
        self.nops += 1
        mylane = lane if lane is not None else eng
        L = self.lanes[mylane]
        raw = {}
        oth = {}

        def add(d, ls):
            l, s = ls
            if d.get(l, 0) < s:
                d[l] = s

        rkeys = set()
        for ap in reads:
            rkeys |= set(self.keys(ap))
        wkeys = set()
        for ap in writes:
            wkeys |= set(self.keys(ap))
        for k in rkeys:
            g = self.gran.get(k)
            if g is not None and g[0] is not None:
                add(raw, g[0])
        for k in wkeys:
            g = self.gran.get(k)
            if g is not None:
                if g[0] is not None:
                    add(oth, g[0])
                for l, s in g[1].items():
                    add(oth, (l, s))
        deps = {}
        for l, s in raw.items():
            if l == eng and lane is None:
                if eng == "pe":
                    continue
            add(deps, (l, s))
        for l, s in oth.items():
            if l == eng and lane is None:
                continue
            add(deps, (l, s))
        if lane is not None and L["count"] > 0:
            add(deps, (mylane, L["count"]))
        if sig:
            L["count"] += n
            seq = L["count"]
        else:
            seq = L["count"] + 1
        ck = self.clock[eng]
        waits = []
        for l, s in sorted(deps.items()):
            if ck.get(l, 0) < s:
                waits.append((l, s * self.lanes[l]["inc"]))
                if self.maxwait.get(l, 0) < s:
                    self.maxwait[l] = s
                sn = self.snap.get((l, s))
                if sn:
                    for l2, s2 in sn.items():
                        if ck.get(l2, 0) < s2:
                            ck[l2] = s2
                ck[l] = s
        if sig:
            sn = dict(ck)
            sn[mylane] = seq
            self.snap[(mylane, seq)] = sn
        for k in rkeys:
            g = self.gran.get(k)
            if g is None:
                g = [None, {}]
                self.gran[k] = g
            if g[1].get(mylane, 0) < seq:
                g[1][mylane] = seq
        for k in wkeys:
            self.gran[k] = [(mylane, seq), {}]
        self.q[eng].append((waits, fn, (mylane, L["inc"]) if sig else None, embed))

    def wait_all(self, eng, lanes):
        waits = [(l, self.lanes[l]["count"] * self.lanes[l]["inc"]) for l in lanes if self.lanes[l]["count"] > 0]
        self.q[eng].append((waits, None, None, False))

    def emit(self, block):
        for l, s in self.maxwait.items():
            assert s <= self.lanes[l]["count"], (l, s, self.lanes[l]["count"])

        def runner(name):
            items = self.q[name]
            lanes = self.lanes

            def body(e):
                for waits, fn, sig, embed in items:
                    emb = None
                    if embed and fn is not None and waits:
                        emb = waits[-1]
                        waits = waits[:-1]
                    for l, v in waits:
                        e.wait_ge(lanes[l]["sem"], v)
                    if fn is None:
                        continue
                    r = fn(e)
                    if emb is not None:
                        first = r[0] if isinstance(r, (list, tuple)) else r
                        first._wait_ge(lanes[emb[0]]["sem"], emb[1])
                    if sig is not None:
                        sem = lanes[sig[0]]["sem"]
                        if isinstance(r, (list, tuple)):
                            for ins in r:
                                ins.then_inc(sem, sig[1])
                        else:
                            r.then_inc(sem, sig[1])

            return body

        block.tensor(runner("pe"))
        block.scalar(runner("act"))
        block.vector(runner("dve"))
        block.gpsimd(runner("pool"))
        block.sync(runner("sp"))

    def mm(self, out, lhsT, rhs, start, stop, sig=None):
        self.op("pe", lambda e: e.matmul(out, lhsT, rhs, start=start, stop=stop),
                reads=[lhsT, rhs], writes=[out], sig=(stop if sig is None else sig))

    def tr(self, out, in_, ident, sig=True):
        self.op("pe", lambda e: e.transpose(out, in_, ident), reads=[in_, ident], writes=[out], sig=sig)

    def act(self, out, in_, func, bias=None, scale=None, accum=None):
        reads = [in_]
        kw = {}
        if bias is not None:
            kw["bias"] = bias
            if not isinstance(bias, (int, float)):
                reads.append(bias)
        if scale is not None:
            kw["scale"] = scale
            if not isinstance(scale, (int, float)):
                reads.append(scale)
        writes = [out]
        if accum is not None:
            kw["accum_out"] = accum
            writes.append(accum)
        self.op("act", lambda e: e.activation(out, in_, func, **kw), reads=reads, writes=writes,
                embed=(accum is None))

    def tt(self, eng, out, a, b, op):
        self.op(eng, lambda e: e.tensor_tensor(out, a, b, op), reads=[a, b], writes=[out])

    def ts(self, eng, out, a, s1, op0, s2=None, op1=None):
        reads = [a]
        if not isinstance(s1, (int, float)):
            reads.append(s1)
        if s2 is not None and not isinstance(s2, (int, float)):
            reads.append(s2)
        if op1 is None:
            self.op(eng, lambda e: e.tensor_scalar(out, a, s1, None, op0), reads=reads, writes=[out])
        else:
            self.op(eng, lambda e: e.tensor_scalar(out, a, s1, s2, op0, op1), reads=reads, writes=[out])

    def stt(self, out, in0, scalar, in1, op0, op1):
        reads = [in0, in1]
        if not isinstance(scalar, (int, float)):
            reads.append(scalar)
        self.op("dve", lambda e: e.scalar_tensor_tensor(out, in0, scalar, in1, op0, op1),
                reads=reads, writes=[out])

    def copy(self, eng, out, in_):
        if eng == "act":
            self.op("act", lambda e: e.activation(out, in_, AF.Copy), reads=[in_], writes=[out])
        else:
            self.op(eng, lambda e: e.tensor_copy(out, in_), reads=[in_], writes=[out])

    def memset(self, eng, out, val):
        self.op(eng, lambda e: e.memset(out, val), reads=[], writes=[out])

    def dma(self, pairs, lane, eng="sp", **kw):
        self.lane(lane)
        outs = [p[0] for p in pairs]
        ins = [p[1] for p in pairs]

        def fn(e):
            return [e.dma_start(out=o, in_=i, **kw) for o, i in pairs]

        self.op(eng, fn, reads=ins, writes=outs, lane=lane, n=len(pairs))


def _consts(seq):
    c = {}
    c["ident_bf"] = np.eye(128, dtype=np.float32).astype(ml_dtypes.bfloat16)
    c["ones_bf"] = np.ones((128, 128), dtype=np.float32).astype(ml_dtypes.bfloat16)
    c["ident_f"] = np.eye(128, dtype=np.float32)
    c["ones_f"] = np.ones((128, 128), dtype=np.float32)
    lg = np.log1p(-np.exp2(-5.0 - np.arange(RET_H, dtype=np.float64)))
    j = np.arange(128, dtype=np.float64)
    causalT = (j[None, :] >= j[:, None]).astype(np.float64)
    rmask = np.zeros((128, RET_H, 128), np.float64)
    for h in range(RET_H):
        rmask[:, h, :] = np.exp(-lg[h] * (j[:, None] + 1.0)) * (RET_DK ** -0.5) * causalT
    c["rmask"] = rmask.astype(np.float32)
    qd = np.exp(lg[:, None] * (j[None, :] + 1.0))
    c["qdec"] = np.broadcast_to(qd[None], (128, RET_H, 128)).astype(np.float32).copy()
    kd = np.zeros((128, 2, RET_H), np.float64)
    for li, L in enumerate((128, 32)):
        for h in range(RET_H):
            kd[:, li, h] = np.exp(lg[h] * (L - 1.0 - j)) * (RET_DK ** -0.5)
    c["kdec"] = kd.astype(np.float32)
    c["gL"] = {L: [float(np.exp(lg[h] * L)) for h in range(RET_H)] for L in (128, 32)}
    c["gmask"] = (causalT * (GLA_DK ** -0.5)).astype(np.float32)
    c["ucum"] = ((j[:, None] <= j[None, :]) * (-1.0 / GLA_TAU)).astype(np.float32)
    c["neghalf"] = np.full((128, 512), -0.5, np.float32)
    inv = (np.float32(ROPE_BASE) ** (-(np.arange(128, dtype=np.float32) / np.float32(128)))).astype(np.float32)
    pos = np.concatenate([np.arange(seq, dtype=np.float32), PAST_LEN + np.arange(DEC_SEQ, dtype=np.float32)])
    ang = (pos[None, :] * inv[:, None]).astype(np.float32)
    c["cos"] = np.cos(ang).astype(np.float32)
    c["sin"] = np.sin(ang).astype(np.float32)
    return c


def _fm(v):
    v = np.asarray(v, np.float32)
    return np.ascontiguousarray(v.reshape(-1, 128).T)


WEIGHTS = [
    ("ret_w_in", D, 6144), ("ret_w_out", 2048, D), ("gla_w_in", D, 3088), ("gla_w_out", D, D),
    ("ffn_w_up0", D, 2 * DFF), ("ffn_w_up1", D, 2 * DFF), ("ffn_w_down0", DFF, D), ("ffn_w_down1", DFF, D),
]


def build(seq, dbg=()):
    assert seq % 512 == 0
    nc = bass.Bass("TRN2", target_bir_lowering=False)
    P = Prog(nc)
    cst = _consts(seq)
    gL = cst["gL"]
    npos = seq + DEC_SEQ

    def din(name, shape, dt=F32):
        return nc.dram_tensor(name, list(shape), dt, kind="ExternalInput").ap()

    def dout(name, shape, dt=F32):
        return nc.dram_tensor(name, list(shape), dt, kind="ExternalOutput").ap()

    NHALF = -(-seq // 4096)
    HSEQ = seq // NHALF
    assert HSEQ * NHALF == seq and HSEQ % 512 == 0
    xTp = [din("xTp%d" % i, [D, HSEQ]) for i in range(NHALF)]; xTs = din("xTs", [D, DEC_SEQ])
    st_ret = din("st_ret", [RET_H, RET_DK, RET_DV]); st_gla = din("st_gla", [GLA_H, GLA_DK, GLA_DV])
    cconv = din("cconv", [128, DEPTH, NBLK_FF, 2])
    W32 = {n: din(n, [k, m]) for n, k, m in WEIGHTS}
    Wb = {n: nc.dram_tensor(n + "_bf", [k, m], BF16, kind="Internal").ap() for n, k, m in WEIGHTS}
    d_gains = din("gains", [128, 5, 8])
    d_gn = din("gn", [128, 16]); d_ng = din("ng", [128, 8])
    d_cw = din("cw", [128, DEPTH, 3, NBLK_FF]); d_cb = din("cb", [128, DEPTH, NBLK_FF])
    d_wa2 = din("wa2aug", [17, 512])
    d_ident = din("ident_bf", [128, 128], BF16); d_ones = din("ones_bf", [128, 128], BF16)
    d_rmask = din("rmask", [128, RET_H, 128]); d_qdec = din("qdec", [128, RET_H, 128])
    d_kdec = din("kdec", [128, 2, RET_H]); d_gmask = din("gmask", [128, 128]); d_ucum = din("ucum", [128, 128])
    d_neghalf = din("neghalf", [128, 512])
    d_identf = din("ident_f", [128, 128]); d_onesf = din("ones_f", [128, 128])
    d_cos = din("cos", [128, npos]); d_sin = din("sin", [128, npos])

    yTp = [dout("yTp%d" % i, [D, HSEQ]) for i in range(NHALF)]; yTs = dout("yTs", [D, DEC_SEQ])
    o_ret = {"p": dout("ret_p", [RET_H, RET_DK, RET_DV]), "s": dout("ret_s", [RET_H, RET_DK, RET_DV])}
    o_gla = {"p": dout("gla_p", [GLA_H, GLA_DK, GLA_DV]), "s": dout("gla_s", [GLA_H, GLA_DK, GLA_DV])}
    o_conv = {"p": dout("conv_p", [128, DEPTH, NBLK_FF, 2]), "s": dout("conv_s", [128, DEPTH, NBLK_FF, 2])}
    dbg_out = {}

    arena_h = nc.alloc_sbuf_tensor("arena", [128, ARENA], U8)
    arena = arena_h.ap()
    psum_h = nc.alloc_psum_tensor("psum", [128, 8, 512], F32)
    psum = psum_h.ap()

    off = {"_": 0}
    reg = {}

    def region(name, nbytes):
        assert nbytes % 256 == 0, name
        reg[name] = (off["_"], nbytes)
        off["_"] += nbytes
        assert off["_"] <= ARENA, (name, off["_"])

    def view(name, dt, shape, boff=0, parts=128):
        o, nb = reg[name]
        n = int(np.prod(shape)) * DSZ[dt]
        assert boff + n <= nb, (name, boff, n, nb)
        ap = arena[0:parts, o + boff:o + boff + n].bitcast(dt)
        if len(shape) == 2:
            ap = ap.rearrange("p (a b) -> p a b", a=shape[0])
        elif len(shape) == 3:
            ap = ap.rearrange("p (a b c) -> p a b c", a=shape[0], b=shape[1])
        return ap

    region("x", 16384); region("hT", 8192); region("ms", 2048); region("rstd", 2048)
    region("identf", 512); region("onesf", 512)
    region("qkk", 24576)
    region("vtok", 16384)
    region("sgT", 16384)
    region("ogT", 16384)
    region("mix", 20480)
    region("PT", 1024); region("on", 4096); region("stats", 1024); region("stmp", 2048); region("aT", 1024)
    region("Sret", 16384); region("Sretb", 8192); region("Sgla", 4096); region("Sglab", 2048)
    region("wring", 3 * 8192)
    for nme, nb in [("ident", 256), ("ones", 256), ("rmask", 2048), ("qdec", 2048), ("kdec", 256), ("gmask", 512),
                    ("ucum", 512), ("neghalf", 2048), ("gains", 256), ("gn", 256), ("ng", 256), ("cw", 1280),
                    ("cb", 512), ("wa2", 1024), ("uhb", 512), ("uhf", 768)]:
        region(nme, nb)

    ident = view("ident", BF16, [128]); ones = view("ones", BF16, [128])
    identf = view("identf", F32, [128]); onesf = view("onesf", F32, [128])
    rmask = view("rmask", F32, [RET_H, 128]); qdec = view("qdec", F32, [RET_H, 128])
    kdec = view("kdec", F32, [2, RET_H]); gmask = view("gmask", F32, [128]); ucum = view("ucum", F32, [128])
    neghalf = view("neghalf", F32, [512])
    gains = view("gains", F32, [5, 8]); gnT = view("gn", F32, [16]); ngT = view("ng", F32, [8])
    cw = view("cw", F32, [DEPTH, 3, NBLK_FF]); cb = view("cb", F32, [DEPTH, NBLK_FF])
    wa2 = view("wa2", BF16, [512], parts=32)
    uhb = view("uhb", BF16, [DEPTH, NBLK_FF, 2]); uhf = view("uhf", F32, [DEPTH, NBLK_FF, 2])
    Sret = view("Sret", F32, [RET_H, 2, 512]); Sretb = view("Sretb", BF16, [RET_H, 2, 512])
    Sgla = view("Sgla", F32, [GLA_H, 256]); Sglab = view("Sglab", BF16, [GLA_H, 256])

    psb = [psum[:, b, :] for b in range(8)]
    psb_bf = [psum[:, b, :].bitcast(BF16) for b in range(8)]
    bank_ctr = {"i": 0}

    def bank():
        b = bank_ctr["i"] % 8
        bank_ctr["i"] += 1
        return b

    P.dma([(ident, d_ident), (ones, d_ones), (rmask, d_rmask), (qdec, d_qdec), (kdec, d_kdec), (gmask, d_gmask),
           (ucum, d_ucum), (neghalf, d_neghalf), (identf, d_identf), (onesf, d_onesf), (gains, d_gains), (gnT, d_gn), (ngT, d_ng), (cw, d_cw), (cb, d_cb)],
          "const")
    P.dma([(wa2[0:17, :], d_wa2)], "wa2c", eng="pool")
    for n, k, m in WEIGHTS:
        if n in ("ret_w_out", "gla_w_out"):
            continue
        P.dma([(Wb[n], W32[n])], "cast_" + n, eng="pool", max_dma_last_dim=4096)
    fi = 0
    for n, gv_, nch in (("ret_w_out", gnT, 16), ("gla_w_out", ngT, 8)):
        for ec in range(nch):
            wi = view("mix", F32, [1024], boff=(fi % 2) * 4096)
            wo = view("mix", BF16, [1024], boff=8192 + (fi % 2) * 2048)
            fi += 1
            P.dma([(wi, W32[n][ec * 128:(ec + 1) * 128, :])], "foldin")
            P.ts("dve", wo, wi, gv_[:, ec:ec + 1], ALU.mult)
            P.dma([(Wb[n][ec * 128:(ec + 1) * 128, :], wo)], "foldout")

    ring = {"i": 0}

    def slab(wname, kc0, nkc, colgroups):
        s = ring["i"] % 3
        ring["i"] += 1
        ncols = sum(c for _, c in colgroups)
        assert nkc * ncols * 2 <= 8192
        v = view("wring", BF16, [nkc, ncols], boff=s * 8192)
        pairs = []
        co = 0
        for c0, cn in colgroups:
            src = Wb[wname][kc0 * 128:(kc0 + nkc) * 128, c0:c0 + cn].rearrange("(k p) n -> p k n", p=128)
            pairs.append((v[:, :, co:co + cn], src))
            co += cn
        P.dma(pairs, "w%d" % s)
        return v

    aT_all = view("aT", BF16, [512], parts=32)
    P.memset("dve", aT_all, 1.0)

    def run_tile(sk, t0, NT, CL, last):
        NCH = NT // CL
        li = 0 if CL == 128 else 1
        xsrc = xTp[t0 // HSEQ] if sk == "p" else xTs
        ydst = yTp[t0 // HSEQ] if sk == "p" else yTs
        tq = t0 % HSEQ if sk == "p" else t0
        pos0 = t0 if sk == "p" else seq + t0
        x = view("x", F32, [8, NT])
        hT = view("hT", BF16, [8, NT])
        ybuf = view("vtok", F32, [8, NT])
        sq = view("ogT", BF16, [8, NT])
        ms = view("ms", F32, [4]); rstd = view("ms", F32, [4], boff=256)
        rbc = view("rstd", F32, [4, 128])
        qT = view("qkk", BF16, [8, NT]); kT = view("qkk", BF16, [8, NT], boff=8192)
        ktok = view("qkk", BF16, [NCH, 1024], boff=16384)
        vtok = view("vtok", BF16, [NCH, 2048])
        sgT = view("sgT", BF16, [16, NT]); ogT = view("ogT", BF16, [16, NT])
        PTv = [view("PT", BF16, [128], boff=i * 256) for i in range(4)]
        onv = [view("on", BF16, [512], boff=i * 1024) for i in range(3)]
        statv = [(view("stats", F32, [6], boff=i * 256), view("stats", F32, [2], boff=i * 256 + 64),
                  view("stats", F32, [1], boff=i * 256 + 128), view("stats", F32, [1], boff=i * 256 + 192))
                 for i in range(4)]
        stmp = [view("stmp", BF16, [NT], boff=i * 1024) for i in range(2)]
        cnt = {"pt": 0, "on": 0, "st": 0, "sv": 0}

        def tsl(ch):
            return slice(ch * CL, (ch + 1) * CL)

        P.dma([(x, xsrc[:, tq:tq + NT].rearrange("(c p) t -> p c t", p=128))], "xin")

        def norm(gidx, out_hT=True):
            TB = min(128, NT)
            NTB = NT // TB
            P.act(sq, x, AF.Square)
            b = bank()
            for tb in range(NTB):
                for c in range(8):
                    P.mm(psb[b][0:TB, tb:tb + 1], sq[:, c, tb * TB:(tb + 1) * TB], ones[:, 0:1],
                         start=(c == 0), stop=(c == 7), sig=(c == 7 and tb == NTB - 1))
            P.ts("dve", ms[0:TB, 0:NTB], psb[b][0:TB, 0:NTB], 1.0 / D, ALU.mult, EPS, ALU.add)
            P.tt("pool", rstd[0:TB, 0:NTB], ms[0:TB, 0:NTB], neghalf[0:TB, 0:NTB], ALU.pow)
            b2 = bank()
            for tb in range(NTB):
                P.ts("dve", rbc[0:TB, tb, :], onesf[0:TB, :], rstd[0:TB, tb:tb + 1], ALU.mult)
                P.mm(psb[b2][:, tb * TB:(tb + 1) * TB], rbc[0:TB, tb, :], identf[0:TB, 0:TB],
                     start=True, stop=True, sig=(tb == NTB - 1))
            for c in range(8):
                dst = hT[:, c, :] if out_hT else ybuf[:, c, :]
                P.stt(dst, x[:, c, :], gains[:, gidx, c:c + 1], psb[b2][:, 0:NT], ALU.mult, ALU.mult)

        def resid_proj(wname, nkc, src, kslabs):
            for cg in range(4):
                banks = [bank(), bank()]
                k0 = 0
                while k0 < nkc:
                    nk = min(kslabs, nkc - k0)
                    sl = slab(wname, k0, nk, [(cg * 256, 256)])
                    for j in range(2):
                        for kk in range(nk):
                            kc = k0 + kk
                            P.mm(psb[banks[j]][:, 0:NT], sl[:, kk, j * 128:(j + 1) * 128], src[:, kc, :],
                                 start=(kc == 0), stop=(kc == nkc - 1),
                                 sig=(kc == nkc - 1) or (kk == nk - 1 and j == 1))
                    k0 += nk
                for j in range(2):
                    blk = cg * 2 + j
                    P.tt("dve", x[:, blk, :], x[:, blk, :], psb[banks[j]][:, 0:NT], ALU.add)

        def retention():
            cs = view("mix", F32, [2, NT])
            cqs = [view("mix", F32, [2, NT], boff=4096 + i * 4096) for i in range(2)]
            rt = view("mix", F32, [4, NT], boff=12288)
            P.dma([(cs[:, 0, :], d_cos[:, pos0:pos0 + NT]), (cs[:, 1, :], d_sin[:, pos0:pos0 + NT])], "cs")

            def rotary(pa, pb, ct, st_, o1, o2):
                P.tt("dve", rt[:, 0, :], pa, ct, ALU.mult)
                P.tt("dve", rt[:, 1, :], pb, st_, ALU.mult)
                P.tt("dve", o1, rt[:, 0, :], rt[:, 1, :], ALU.subtract)
                P.tt("dve", rt[:, 2, :], pa, st_, ALU.mult)
                P.tt("dve", rt[:, 3, :], pb, ct, ALU.mult)
                P.tt("dve", o2, rt[:, 2, :], rt[:, 3, :], ALU.add)

            sg_tok = view("sgT", BF16, [NCH, 2048])

            def mk_groups(h):
                st = {}
                gl = []
                for kind in (0, 1):
                    for ch in range(NCH):
                        def grp(kind=kind, ch=ch):
                            if ch == 0:
                                st[kind] = slab("ret_w_in", 0, 8, [(2048 + kind * 2048 + h * 512, 512)])
                            sl_ = st[kind]
                            b_ = bank()
                            for kc in range(8):
                                P.mm(psb[b_][0:CL, :], hT[:, kc, tsl(ch)], sl_[:, kc, :], start=(kc == 0), stop=(kc == 7))
                            if kind == 0:
                                P.copy("act", vtok[0:CL, ch, h * 512:(h + 1) * 512], psb[b_][0:CL, :])
                            else:
                                P.act(sg_tok[0:CL, ch, h * 512:(h + 1) * 512], psb[b_][0:CL, :], AF.Silu)
                        gl.append(grp)
                return gl

            Q = []
            for h in range(RET_H):
                Q += mk_groups(h)

            def emit_groups(n):
                for _ in range(n):
                    if Q:
                        Q.pop(0)()

            for which in range(2):
                dstT = qT if which == 0 else kT
                for sh in range(2):
                    sl = slab("ret_w_in", 0, 8, [(which * 1024 + sh * 512, 512)])
                    for hh in range(2):
                        h = sh * 2 + hh
                        ba, bb = bank(), bank()
                        for half, bk in ((0, ba), (1, bb)):
                            for kc in range(8):
                                P.mm(psb[bk][:, 0:NT], sl[:, kc, (hh * 2 + half) * 128:(hh * 2 + half + 1) * 128],
                                     hT[:, kc, :], start=(kc == 0), stop=(kc == 7))
                        if which == 0:
                            cq = cqs[h % 2]
                            qd = qdec[:, h, 0:CL].unsqueeze(1).to_broadcast([128, NCH, CL])
                            for t_ in range(2):
                                P.tt("pool", cq[:, t_, :].rearrange("p (a b) -> p a b", a=NCH),
                                     cs[:, t_, :].rearrange("p (a b) -> p a b", a=NCH), qd, ALU.mult)
                            rotary(psb[ba][:, 0:NT], psb[bb][:, 0:NT], cq[:, 0, :], cq[:, 1, :],
                                   dstT[:, 2 * h, :], dstT[:, 2 * h + 1, :])
                        else:
                            rotary(psb[ba][:, 0:NT], psb[bb][:, 0:NT], cs[:, 0, :], cs[:, 1, :],
                                   dstT[:, 2 * h, :], dstT[:, 2 * h + 1, :])
                        emit_groups(1)
            for h in range(RET_H):
                for ch in range(NCH):
                    b = bank()
                    for dc in range(2):
                        P.tr(psb_bf[b][0:CL, dc * 128:(dc + 1) * 128], kT[:, 2 * h + dc, tsl(ch)], ident,
                             sig=(dc == 1))
                    P.act(ktok[0:CL, ch, h * 256:(h + 1) * 256], psb_bf[b][0:CL, 0:256], AF.Copy,
                          scale=kdec[0:CL, li, h:h + 1])
            for h in range(RET_H):
                for ch in range(NCH):
                    bs = bank()
                    for dc in range(2):
                        P.mm(psb[bs][0:CL, 0:CL], kT[:, 2 * h + dc, tsl(ch)], qT[:, 2 * h + dc, tsl(ch)],
                             start=(dc == 0), stop=(dc == 1))
                    PT = PTv[cnt["pt"] % 4]; cnt["pt"] += 1
                    P.tt("dve", PT[0:CL, 0:CL], psb[bs][0:CL, 0:CL], rmask[0:CL, h, 0:CL], ALU.mult)
                    emit_groups(1)
                    bo = bank()
                    P.mm(psb[bo][0:CL, :], PT[0:CL, 0:CL], vtok[0:CL, ch, h * 512:(h + 1) * 512], start=True, stop=False)
                    for dc in range(2):
                        P.mm(psb[bo][0:CL, :], qT[:, 2 * h + dc, tsl(ch)], Sretb[:, h, dc, :], start=False, stop=(dc == 1))
                    bS = [bank(), bank()]
                    for dc in range(2):
                        P.mm(psb[bS[dc]][:, :], ktok[0:CL, ch, h * 256 + dc * 128:h * 256 + (dc + 1) * 128],
                             vtok[0:CL, ch, h * 512:(h + 1) * 512], start=True, stop=True)
                    stats6, mv, ve, rs = statv[cnt["sv"] % 4]; cnt["sv"] += 1
                    P.op("dve", lambda e, o=stats6[0:CL, :], i=psb[bo][0:CL, :]: e.bn_stats(o, i),
                         reads=[psb[bo][0:CL, :]], writes=[stats6[0:CL, :]])
                    P.op("dve", lambda e, o=mv[0:CL, :], i=stats6[0:CL, :]: e.bn_aggr(o, i),
                         reads=[stats6[0:CL, :]], writes=[mv[0:CL, :]])
                    P.ts("dve", ve[0:CL, :], mv[0:CL, 1:2], EPS, ALU.add)
                    P.tt("pool", rs[0:CL, :], ve[0:CL, :], neghalf[0:CL, 0:1], ALU.pow)
                    for dc in range(2):
                        P.stt(Sret[:, h, dc, :], Sret[:, h, dc, :], gL[CL][h], psb[bS[dc]][:, :], ALU.mult, ALU.add)
                        P.copy("act", Sretb[:, h, dc, :], Sret[:, h, dc, :])
                    emit_groups(1)
                    on = onv[cnt["on"] % 3]; cnt["on"] += 1
                    P.ts("dve", on[0:CL, :], psb[bo][0:CL, :], mv[0:CL, 0:1], ALU.subtract, rs[0:CL, :], ALU.mult)
                    P.tt("dve", on[0:CL, :], on[0:CL, :], sg_tok[0:CL, ch, h * 512:(h + 1) * 512], ALU.mult)
                    bt = bank()
                    for eb in range(4):
                        P.tr(psb_bf[bt][:, eb * CL:(eb + 1) * CL], on[0:CL, eb * 128:(eb + 1) * 128],
                             ident[0:CL, 0:CL], sig=(eb == 3))
                    P.copy("act", ogT[:, 4 * h:4 * h + 4, tsl(ch)],
                           psb_bf[bt][:, 0:4 * CL].rearrange("p (a b) -> p a b", a=4))
            emit_groups(len(Q))
            resid_proj("ret_w_out", 16, ogT, 16)

        def ffn(layer):
            actT = view("qkk", BF16, [22, NT])
            ubuf = [[view("sgT", BF16, [NT + 2], boff=(r * 2 + gv) * 1280) for gv in range(2)] for r in range(2)]
            sgt = [view("sgT", BF16, [NT], boff=5120 + r * 1024) for r in range(2)]
            dg = [[view("sgT", BF16, [3, 128], boff=7168 + (r * 2 + gv) * 768) for gv in range(2)] for r in range(2)]
            wn = "ffn_w_up%d" % layer
            it = 0
            pend = None

            def conv_stage(r, gb):
                cbk = []
                for gv in range(2):
                    b = bank(); cbk.append(b)
                    for j in range(3):
                        P.mm(psb[b][:, 0:NT], dg[r][gv][:, j, :], ubuf[r][gv][:, j:j + NT], start=(j == 0), stop=(j == 2))
                P.act(sgt[r], psb[cbk[0]][:, 0:NT], AF.Silu, bias=cb[:, layer, gb:gb + 1])
                P.stt(actT[:, gb, :], psb[cbk[1]][:, 0:NT], cb[:, layer, gb + 22:gb + 23], sgt[r], ALU.add, ALU.mult)

            for s in range(11):
                sl = slab(wn, 0, 8, [(s * 256, 256), (DFF + s * 256, 256)])
                for jj in range(2):
                    gb = s * 2 + jj
                    r = it % 2; it += 1
                    for gv in range(2):
                        blk = gb + gv * 22
                        b = bank()
                        for kc in range(8):
                            P.mm(psb[b][:, 0:NT], sl[:, kc, gv * 256 + jj * 128:gv * 256 + (jj + 1) * 128], hT[:, kc, :],
                                 start=(kc == 0), stop=(kc == 7))
                        ub = ubuf[r][gv]
                        P.copy("pool", ub[:, 0:2], uhb[:, layer, blk, :])
                        P.copy("act", ub[:, 2:2 + NT], psb[b][:, 0:NT])
                        P.copy("pool", uhb[:, layer, blk, :], ub[:, NT:NT + 2])
                        if last:
                            P.copy("dve", uhf[:, layer, blk, :], psb[b][:, NT - 2:NT])
                        for j in range(3):
                            P.ts("pool", dg[r][gv][:, j, :], ident, cw[:, layer, j, blk:blk + 1], ALU.mult, 1.0, ALU.mult)
                    if pend is not None:
                        conv_stage(*pend)
                    pend = (r, gb)
            conv_stage(*pend)
            resid_proj("ffn_w_down%d" % layer, 22, actT, 11)

        def gla():
            vt = view("vtok", BF16, [NCH, 1024])
            kbar = view("qkk", BF16, [NCH, 512], boff=16384)
            sp = view("mix", F32, [NCH, 512])
            Eq = view("mix", F32, [GLA_H, NT], boff=8192)
            Ek = [view("mix", F32, [NT], boff=16384 + i * 2048) for i in range(2)]
            zt = view("on", F32, [512])
            kbt = view("stmp", BF16, [NT])
            sl = slab("gla_w_in", 0, 8, [(3072, 16)])
            b = bank()
            for kc in range(8):
                P.mm(psb[b][0:16, 0:NT], sl[:, kc, 0:16], hT[:, kc, :], start=(kc == 0), stop=(kc == 7))
            P.copy("act", aT_all[0:16, 0:NT], psb[b][0:16, 0:NT])
            for ch in range(NCH):
                b = bank()
                P.mm(psb[b][0:CL, :], aT_all[0:17, tsl(ch)], wa2[0:17, :], start=True, stop=True)
                P.act(zt[0:CL, :], psb[b][0:CL, :], AF.Exp, scale=-1.0)
                P.act(sp[0:CL, ch, :], zt[0:CL, :], AF.Ln, bias=1.0)
            for h in range(GLA_H):
                b = bank()
                for ch in range(NCH):
                    P.mm(psb[b][:, tsl(ch)], sp[0:CL, ch, h * 128:(h + 1) * 128], ucum[0:CL, 0:CL], start=True, stop=True,
                         sig=(ch == NCH - 1))
                P.act(Eq[:, h, :], psb[b][:, 0:NT], AF.Exp)
                P.act(Ek[h % 2], psb[b][:, 0:NT], AF.Exp, scale=-1.0)
                if h == 0:
                    slq = slab("gla_w_in", 0, 8, [(0, 512)])
                    slk = slab("gla_w_in", 0, 8, [(512, 512)])
                bq = bank()
                for kc in range(8):
                    P.mm(psb[bq][:, 0:NT], slq[:, kc, h * 128:(h + 1) * 128], hT[:, kc, :], start=(kc == 0), stop=(kc == 7))
                P.tt("dve", qT[:, h, :], psb[bq][:, 0:NT], Eq[:, h, :], ALU.mult)
                bk = bank()
                for kc in range(8):
                    P.mm(psb[bk][:, 0:NT], slk[:, kc, h * 128:(h + 1) * 128], hT[:, kc, :], start=(kc == 0), stop=(kc == 7),
                         sig=True)
                P.tt("dve", kT[:, h, :], psb[bk][:, 0:NT], Ek[h % 2], ALU.mult)
                for ch in range(NCH):
                    P.ts("dve", kbt[:, tsl(ch)], kT[:, h, tsl(ch)], Eq[:, h, (ch + 1) * CL - 1:(ch + 1) * CL], ALU.mult,
                         GLA_DK ** -0.5, ALU.mult)
                    bt = bank()
                    P.tr(psb_bf[bt][0:CL, 0:128], kbt[:, tsl(ch)], ident)
                    P.copy("act", kbar[0:CL, ch, h * 128:(h + 1) * 128], psb_bf[bt][0:CL, 0:128])
            for s2 in range(2):
                sl = slab("gla_w_in", 0, 8, [(1024 + s2 * 512, 512)])
                for ch in range(NCH):
                    b = bank()
                    for kc in range(8):
                        P.mm(psb[b][0:CL, :], hT[:, kc, tsl(ch)], sl[:, kc, :], start=(kc == 0), stop=(kc == 7))
                    P.copy("act", vt[0:CL, ch, s2 * 512:(s2 + 1) * 512], psb[b][0:CL, :])
            for s2 in range(2):
                sl = slab("gla_w_in", 0, 8, [(2048 + s2 * 512, 512)])
                for j in range(4):
                    blk = s2 * 4 + j
                    b = bank()
                    for kc in range(8):
                        P.mm(psb[b][:, 0:NT], sl[:, kc, j * 128:(j + 1) * 128], hT[:, kc, :], start=(kc == 0), stop=(kc == 7))
                    P.act(sgT[:, blk, :], psb[b][:, 0:NT], AF.Silu)
            junk = view("on", BF16, [256], boff=2048)
            for ch in range(NCH):
                for h in range(GLA_H):
                    ba = bank()
                    P.mm(psb[ba][0:CL, 0:CL], kT[:, h, tsl(ch)], qT[:, h, tsl(ch)], start=True, stop=True)
                    PT = PTv[cnt["pt"] % 4]; cnt["pt"] += 1
                    P.tt("dve", PT[0:CL, 0:CL], psb[ba][0:CL, 0:CL], gmask[0:CL, 0:CL], ALU.mult)
                    bo = bank()
                    P.mm(psb[bo][0:CL, 0:256], PT[0:CL, 0:CL], vt[0:CL, ch, h * 256:(h + 1) * 256], start=True, stop=False)
                    P.mm(psb[bo][0:CL, 0:256], qT[:, h, tsl(ch)], Sglab[:, h, :], start=False, stop=True)
                    bS = bank()
                    P.mm(psb[bS][:, 0:256], kbar[0:CL, ch, h * 128:(h + 1) * 128], vt[0:CL, ch, h * 256:(h + 1) * 256],
                         start=True, stop=True)
                    P.stt(Sgla[:, h, :], Sgla[:, h, :], Eq[:, h, (ch + 1) * CL - 1:(ch + 1) * CL], psb[bS][:, 0:256],
                          ALU.mult, ALU.add)
                    P.copy("act", Sglab[:, h, :], Sgla[:, h, :])
                    stats6, mv, ve, rs = statv[cnt["sv"] % 4]; cnt["sv"] += 1
                    P.act(junk[0:CL, :], psb[bo][0:CL, 0:256], AF.Square, accum=ve[0:CL, :])
                    P.ts("dve", mv[0:CL, 0:1], ve[0:CL, :], 1.0 / GLA_DV, ALU.mult, EPS, ALU.add)
                    P.tt("pool", rs[0:CL, :], mv[0:CL, 0:1], neghalf[0:CL, 0:1], ALU.pow)
                    on = view("on", BF16, [256], boff=2560 + (cnt["on"] % 2) * 512); cnt["on"] += 1
                    P.ts("dve", on[0:CL, :], psb[bo][0:CL, 0:256], rs[0:CL, :], ALU.mult)
                    bt = bank()
                    for eb in range(2):
                        P.tr(psb_bf[bt][:, eb * CL:(eb + 1) * CL], on[0:CL, eb * 128:(eb + 1) * 128], ident[0:CL, 0:CL],
                             sig=(eb == 1))
                    P.tt("dve", ogT[:, 2 * h:2 * h + 2, tsl(ch)],
                         psb_bf[bt][:, 0:2 * CL].rearrange("p (a b) -> p a b", a=2),
                         sgT[:, 2 * h:2 * h + 2, tsl(ch)], ALU.mult)
            resid_proj("gla_w_out", 8, ogT, 8)

        norm(0); retention()
        norm(2); ffn(0)
        norm(1); gla()
        norm(3); ffn(1)
        norm(4, out_hT=False)
        P.dma([(ydst[:, tq:tq + NT].rearrange("(c p) t -> p c t", p=128), ybuf)], "yout")

    def seq_end(sk):
        P.dma([(o_ret[sk].rearrange("h (dc p) e -> p h dc e", p=128), Sret)], "so_ret")
        P.dma([(o_gla[sk].rearrange("h p e -> p h e"), Sgla)], "so_gla")
        P.dma([(o_conv[sk], uhf)], "so_conv")

    P.dma([(Sret, st_ret.rearrange("h (dc p) e -> p h dc e", p=128)), (Sgla, st_gla.rearrange("h p e -> p h e")),
           (uhf, cconv)], "stin")
    P.copy("act", Sretb, Sret); P.copy("act", Sglab, Sgla); P.copy("dve", uhb, uhf)
    run_tile("s", 0, DEC_SEQ, DEC_SEQ, True)
    seq_end("s")
    P.memset("pool", Sret, 0.0); P.memset("pool", Sretb, 0.0); P.memset("pool", Sgla, 0.0); P.memset("pool", Sglab, 0.0)
    P.memset("pool", uhb, 0.0)
    ntile = seq // 512
    for ti in range(ntile):
        run_tile("p", ti * 512, 512, 128, ti == ntile - 1)
    seq_end("p")

    P.wait_all("sp", ("yout", "so_ret", "so_gla", "so_conv"))

    with contextlib.ExitStack() as es:
        for lname, L in P.lanes.items():
            L["sem"] = es.enter_context(nc.semaphore("s_" + lname))
        block = es.enter_context(nc.Block())
        P.emit(block)
    return nc, cst, P


_CACHE = {}


def _prep_inputs(inp, seq, cst):
    f32 = lambda a: np.ascontiguousarray(np.asarray(a, np.float32))
    shared = {
        "ret_w_in": f32(inp["ret_w_in"][0]), "ret_w_out": f32(inp["ret_w_out"][0]),
        "gla_w_in": f32(inp["gla_w_in"][0]), "gla_w_out": f32(inp["gla_w_out"][0]),
        "ffn_w_up0": f32(inp["ffn_w_up"][0]), "ffn_w_up1": f32(inp["ffn_w_up"][1]),
        "ffn_w_down0": f32(inp["ffn_w_down"][0]), "ffn_w_down1": f32(inp["ffn_w_down"][1]),
    }
    nm, nf = np.asarray(inp["norm_mix"], np.float32), np.asarray(inp["norm_ffn"], np.float32)
    gains = np.stack([_fm(nm[0]), _fm(nm[1]), _fm(nf[0]), _fm(nf[1]), _fm(inp["norm_final"])], axis=1)
    shared["gains"] = np.ascontiguousarray(gains)
    shared["gn"] = _fm(np.asarray(inp["ret_gn_g"], np.float32)[0].reshape(-1))
    shared["ng"] = _fm(np.asarray(inp["gla_norm_g"], np.float32)[0].reshape(-1))
    cwv = np.asarray(inp["ffn_conv_w"], np.float32)
    shared["cw"] = np.ascontiguousarray(cwv.reshape(DEPTH, 3, NBLK_FF, 128).transpose(3, 0, 1, 2))
    cbv = np.asarray(inp["ffn_conv_b"], np.float32)
    shared["cb"] = np.ascontiguousarray(cbv.reshape(DEPTH, NBLK_FF, 128).transpose(2, 0, 1))
    shared["wa2aug"] = np.ascontiguousarray(np.concatenate(
        [np.asarray(inp["gla_w_a2"], np.float32)[0], np.asarray(inp["gla_b_a"], np.float32)[0][None, :]], axis=0))
    for k in ("ident_bf", "ones_bf", "ident_f", "ones_f", "rmask", "qdec", "kdec", "gmask", "ucum", "neghalf", "cos", "sin"):
        shared[k] = cst[k]
    xp = np.asarray(inp["x_prompt"], np.float32)
    xs = np.asarray(inp["x_sample"], np.float32)
    cc = np.asarray(inp["cache_conv"], np.float32)
    maps = []
    for b in range(NCORES):
        m = dict(shared)
        hs = seq // (-(-seq // 4096))
        for i in range(seq // hs):
            m["xTp%d" % i] = np.ascontiguousarray(xp[b, i * hs:(i + 1) * hs].T)
        m["xTs"] = np.ascontiguousarray(xs[b].T)
        m["st_ret"] = f32(inp["state_ret"][0, b])
        m["st_gla"] = f32(inp["state_gla"][0, b])
        m["cconv"] = np.ascontiguousarray(cc[:, b].reshape(DEPTH, 2, NBLK_FF, 128).transpose(3, 0, 2, 1))
        maps.append(m)
    return maps


def _run(inp, seq):
    if seq not in _CACHE:
        _CACHE[seq] = build(seq)
    nc, cst, _ = _CACHE[seq]
    maps = _prep_inputs(inp, seq, cst)
    res = run_bass_kernel_spmd(nc, maps, core_ids=list(range(NCORES)))
    R = res.results
    B = NCORES
    hs = seq // (-(-seq // 4096))
    y_p = np.stack([np.concatenate([R[b]["yTp%d" % i].T for i in range(seq // hs)], axis=0)
                    for b in range(B)]).astype(np.float32)
    y_s = np.stack([R[b]["yTs"].T for b in range(B)]).astype(np.float32)
    ret_p = np.stack([R[b]["ret_p"] for b in range(B)])[None].astype(np.float32)
    ret_s = np.stack([R[b]["ret_s"] for b in range(B)])[None].astype(np.float32)
    gla_p = np.stack([R[b]["gla_p"] for b in range(B)])[None].astype(np.float32)
    gla_s = np.stack([R[b]["gla_s"] for b in range(B)])[None].astype(np.float32)

    def conv(k):
        a = np.stack([R[b][k] for b in range(B)])
        return np.ascontiguousarray(a.transpose(2, 0, 4, 3, 1).reshape(DEPTH, B, 2, 2 * DFF)).astype(np.float32)

    return (y_p, y_s, ret_p, ret_s, gla_p, gla_s, conv("conv_p"), conv("conv_s"))


def kernel(**inputs):
    seq = int(np.asarray(inputs["x_prompt"]).shape[1])
    return _run(inputs, seq)
```

```python
import contextlib
import math
import numpy as np
import ml_dtypes
import concourse.bass as bass
import concourse.mybir as mybir
from concourse.bass_utils import run_bass_kernel_spmd

F32 = mybir.dt.float32
BF16 = mybir.dt.bfloat16
U8 = mybir.dt.uint8
AF = mybir.ActivationFunctionType
ALU = mybir.AluOpType
DSZ = {F32: 4, BF16: 2, U8: 1}

D = 1024
DEPTH = 2
RET_H = 4
RET_DK = 256
RET_DV = 512
GLA_H = 4
GLA_DK = 128
GLA_DV = 256
GLA_RANK = 16
GLA_TAU = 16.0
DFF = 2816
NBLK_FF = 2 * DFF // 128
EPS = 1e-6
ROPE_BASE = 10000.0
PAST_LEN = 2048
DEC_SEQ = 32
NCORES = 8

ARENA = 211968


class Prog:
    SBG = 256

    def __init__(self, nc):
        self.nc = nc
        self.q = {e: [] for e in ("pe", "act", "dve", "pool", "sp")}
        self.lanes = {}
        for e in ("pe", "act", "dve", "pool"):
            self.lanes[e] = {"count": 0, "inc": 1, "sem": None}
        self.clock = {e: {} for e in self.q}
        self.snap = {}
        self.gran = {}
        self.maxwait = {}
        self.nops = 0

    def lane(self, name):
        if name not in self.lanes:
            self.lanes[name] = {"count": 0, "inc": 16, "sem": None}
        return name

    def keys(self, ap):
        t = ap.tensor
        name = t.name
        if name not in ("arena", "psum"):
            return [("d", name)]
        esz = DSZ[ap.dtype]
        pairs = list(ap.ap)
        pstep = pairs[0][0]
        off = int(ap.offset)
        inpart = off % pstep if pstep else off
        starts = [inpart]
        free = [(s, c) for (s, c) in pairs[1:] if c > 1 or len(pairs) == 2]
        free = [(s, c) for (s, c) in free if s != 0]
        length = 1
        i = 0
        while i < len(free):
            s, c = free[i]
            inner_ext = sum((cc - 1) * abs(ss) for ss, cc in free[i + 1:]) + 1
            if i == len(free) - 1:
                length = (c - 1) * abs(s) + 1
            elif abs(s) > inner_ext and len(starts) * c <= 128:
                starts = [st + k * s for st in starts for k in range(c)]
            else:
                length = sum((cc - 1) * abs(ss) for ss, cc in free[i:]) + 1
                break
            i += 1
        g = self.SBG if name == "arena" else 2048
        ks = set()
        for st in starts:
            lo = (st * esz) // g
            hi = ((st + length) * esz - 1) // g
            for k in range(lo, hi + 1):
                ks.add((name, k))
        return ks

    def op(self, eng, fn, reads=(), writes=(), sig=True, lane=None, n=1, embed=True):
        self.nops += 1
        mylane = lane if lane is not None else eng
        L = self.lanes[mylane]
        raw = {}
        oth = {}

        def add(d, ls):
            l, s = ls
            if d.get(l, 0) < s:
                d[l] = s

        rkeys = set()
        for ap in reads:
            rkeys |= set(self.keys(ap))
        wkeys = set()
        for ap in writes:
            wkeys |= set(self.keys(ap))
        for k in rkeys:
            g = self.gran.get(k)
            if g is not None and g[0] is not None:
                add(raw, g[0])
        for k in wkeys:
            g = self.gran.get(k)
            if g is not None:
                if g[0] is not None:
                    add(oth, g[0])
                for l, s in g[1].items():
                    add(oth, (l, s))
        deps = {}
        for l, s in raw.items():
            if l == eng and lane is None:
                if eng == "pe":
                    continue
            add(deps, (l, s))
        for l, s in oth.items():
            if l == eng and lane is None and eng == "pe":
                continue
            add(deps, (l, s))
        if lane is not None and L["count"] > 0:
            add(deps, (mylane, L["count"]))
        if sig:
            L["count"] += n
            seq = L["count"]
        else:
            seq = L["count"] + 1
        ck = self.clock[eng]
        waits = []
        for l, s in sorted(deps.items()):
            if ck.get(l, 0) < s:
                waits.append((l, s * self.lanes[l]["inc"]))
                if self.maxwait.get(l, 0) < s:
                    self.maxwait[l] = s
                sn = self.snap.get((l, s))
                if sn:
                    for l2, s2 in sn.items():
                        if ck.get(l2, 0) < s2:
                            ck[l2] = s2
                ck[l] = s
        if sig:
            sn = dict(ck)
            sn[mylane] = seq
            self.snap[(mylane, seq)] = sn
        for k in rkeys:
            g = self.gran.get(k)
            if g is None:
                g = [None, {}]
                self.gran[k] = g
            if g[1].get(mylane, 0) < seq:
                g[1][mylane] = seq
        for k in wkeys:
            self.gran[k] = [(mylane, seq), {}]
        self.q[eng].append((waits, fn, (mylane, L["inc"]) if sig else None, embed))

    def wait_all(self, eng, lanes):
        waits = [(l, self.lanes[l]["count"] * self.lanes[l]["inc"]) for l in lanes if self.lanes[l]["count"] > 0]
        self.q[eng].append((waits, None, None, False))

    def emit(self, block):
        for l, s in self.maxwait.items():
            assert s <= self.lanes[l]["count"], (l, s, self.lanes[l]["count"])

        def runner(name):
            items = self.q[name]
            lanes = self.lanes

            def body(e):
                for waits, fn, sig, embed in items:
                    emb = None
                    if embed and fn is not None and waits:
                        emb = waits[-1]
                        waits = waits[:-1]
                    for l, v in waits:
                        e.wait_ge(lanes[l]["sem"], v)
                    if fn is None:
                        continue
                    r = fn(e)
                    if emb is not None:
                        first = r[0] if isinstance(r, (list, tuple)) else r
                        first._wait_ge(lanes[emb[0]]["sem"], emb[1])
                    if sig is not None:
                        sem = lanes[sig[0]]["sem"]
                        if isinstance(r, (list, tuple)):
                            for ins in r:
                                ins.then_inc(sem, sig[1])
                        else:
                            r.then_inc(sem, sig[1])

            return body

        block.tensor(runner("pe"))
        block.scalar(runner("act"))
        block.vector(runner("dve"))
        block.gpsimd(runner("pool"))
        block.sync(runner("sp"))

    def mm(self, out, lhsT, rhs, start, stop, sig=None):
        self.op("pe", lambda e: e.matmul(out, lhsT, rhs, start=start, stop=stop),
                reads=[lhsT, rhs], writes=[out], sig=(stop if sig is None else sig))

    def tr(self, out, in_, ident, sig=True):
        self.op("pe", lambda e: e.transpose(out, in_, ident), reads=[in_, ident], writes=[out], sig=sig)

    def act(self, out, in_, func, bias=None, scale=None, accum=None):
        reads = [in_]
        kw = {}
        if bias is not None:
            kw["bias"] = bias
            if not isinstance(bias, (int, float)):
                reads.append(bias)
        if scale is not None:
            kw["scale"] = scale
            if not isinstance(scale, (int, float)):
                reads.append(scale)
        writes = [out]
        if accum is not None:
            kw["accum_out"] = accum
            writes.append(accum)
        self.op("act", lambda e: e.activation(out, in_, func, **kw), reads=reads, writes=writes,
                embed=(accum is None))

    def tt(self, eng, out, a, b, op):
        self.op(eng, lambda e: e.tensor_tensor(out, a, b, op), reads=[a, b], writes=[out])

    def ts(self, eng, out, a, s1, op0, s2=None, op1=None):
        reads = [a]
        if not isinstance(s1, (int, float)):
            reads.append(s1)
        if s2 is not None and not isinstance(s2, (int, float)):
            reads.append(s2)
        if op1 is None:
            self.op(eng, lambda e: e.tensor_scalar(out, a, s1, None, op0), reads=reads, writes=[out])
        else:
            self.op(eng, lambda e: e.tensor_scalar(out, a, s1, s2, op0, op1), reads=reads, writes=[out])

    def stt(self, out, in0, scalar, in1, op0, op1):
        reads = [in0, in1]
        if not isinstance(scalar, (int, float)):
            reads.append(scalar)
        self.op("dve", lambda e: e.scalar_tensor_tensor(out, in0, scalar, in1, op0, op1),
                reads=reads, writes=[out])

    def copy(self, eng, out, in_):
        if eng == "act":
            self.op("act", lambda e: e.activation(out, in_, AF.Copy), reads=[in_], writes=[out])
        else:
            self.op(eng, lambda e: e.tensor_copy(out, in_), reads=[in_], writes=[out])

    def memset(self, eng, out, val):
        self.op(eng, lambda e: e.memset(out, val), reads=[], writes=[out])

    def dma(self, pairs, lane, eng="sp", **kw):
        self.lane(lane)
        outs = [p[0] for p in pairs]
        ins = [p[1] for p in pairs]

        def fn(e):
            return [e.dma_start(out=o, in_=i, **kw) for o, i in pairs]

        self.op(eng, fn, reads=ins, writes=outs, lane=lane, n=len(pairs))


def _consts(seq):
    c = {}
    c["ident_bf"] = np.eye(128, dtype=np.float32).astype(ml_dtypes.bfloat16)
    c["ones_bf"] = np.ones((128, 128), dtype=np.float32).astype(ml_dtypes.bfloat16)
    c["ident_f"] = np.eye(128, dtype=np.float32)
    c["ones_f"] = np.ones((128, 128), dtype=np.float32)
    lg = np.log1p(-np.exp2(-5.0 - np.arange(RET_H, dtype=np.float64)))
    j = np.arange(128, dtype=np.float64)
    causalT = (j[None, :] >= j[:, None]).astype(np.float64)
    rmask = np.zeros((128, RET_H, 128), np.float64)
    for h in range(RET_H):
        rmask[:, h, :] = np.exp(-lg[h] * (j[:, None] + 1.0)) * (RET_DK ** -0.5) * causalT
    c["rmask"] = rmask.astype(np.float32)
    qd = np.exp(lg[:, None] * (j[None, :] + 1.0))
    c["qdec"] = np.broadcast_to(qd[None], (128, RET_H, 128)).astype(np.float32).copy()
    kd = np.zeros((128, 2, RET_H), np.float64)
    for li, L in enumerate((128, 32)):
        for h in range(RET_H):
            kd[:, li, h] = np.exp(lg[h] * (L - 1.0 - j)) * (RET_DK ** -0.5)
    c["kdec"] = kd.astype(np.float32)
    c["gL"] = {L: [float(np.exp(lg[h] * L)) for h in range(RET_H)] for L in (128, 32)}
    c["gmask"] = (causalT * (GLA_DK ** -0.5)).astype(np.float32)
    c["ucum"] = ((j[:, None] <= j[None, :]) * (-1.0 / GLA_TAU)).astype(np.float32)
    c["neghalf"] = np.full((128, 512), -0.5, np.float32)
    c["epsv"] = np.full((128, 8), EPS, np.float32)
    inv = (np.float32(ROPE_BASE) ** (-(np.arange(128, dtype=np.float32) / np.float32(128)))).astype(np.float32)
    pos = np.concatenate([np.arange(seq, dtype=np.float32), PAST_LEN + np.arange(DEC_SEQ, dtype=np.float32)])
    ang = (pos[None, :] * inv[:, None]).astype(np.float32)
    c["cos"] = np.cos(ang).astype(np.float32)
    c["sin"] = np.sin(ang).astype(np.float32)
    return c


def _fm(v):
    v = np.asarray(v, np.float32)
    return np.ascontiguousarray(v.reshape(-1, 128).T)


WEIGHTS = [
    ("ret_w_in", D, 6144), ("ret_w_out", 2048, D), ("gla_w_in", D, 3088), ("gla_w_out", D, D),
    ("ffn_w_up0", D, 2 * DFF), ("ffn_w_up1", D, 2 * DFF), ("ffn_w_down0", DFF, D), ("ffn_w_down1", DFF, D),
]


def build(seq, dbg=()):
    assert seq % 512 == 0
    nc = bass.Bass("TRN2", target_bir_lowering=False)
    P = Prog(nc)
    cst = _consts(seq)
    gL = cst["gL"]
    npos = seq + DEC_SEQ

    def din(name, shape, dt=F32):
        return nc.dram_tensor(name, list(shape), dt, kind="ExternalInput").ap()

    def dout(name, shape, dt=F32):
        return nc.dram_tensor(name, list(shape), dt, kind="ExternalOutput").ap()

    NHALF = -(-seq // 4096)
    HSEQ = seq // NHALF
    assert HSEQ * NHALF == seq and HSEQ % 512 == 0
    xTp = [din("xTp%d" % i, [D, HSEQ]) for i in range(NHALF)]; xTs = din("xTs", [D, DEC_SEQ])
    st_ret = din("st_ret", [RET_H, RET_DK, RET_DV]); st_gla = din("st_gla", [GLA_H, GLA_DK, GLA_DV])
    cconv = din("cconv", [128, DEPTH, NBLK_FF, 2])
    W32 = {n: din(n, [k, m]) for n, k, m in WEIGHTS}
    Wb = {n: nc.dram_tensor(n + "_bf", [k, m], BF16, kind="Internal").ap() for n, k, m in WEIGHTS}
    d_gains = din("gains", [128, 5, 8])
    d_gn = din("gn", [128, 16]); d_ng = din("ng", [128, 8])
    d_cw = din("cw", [128, DEPTH, 3, NBLK_FF]); d_cb = din("cb", [128, DEPTH, NBLK_FF])
    d_wa2 = din("wa2aug", [17, 512])
    d_ident = din("ident_bf", [128, 128], BF16); d_ones = din("ones_bf", [128, 128], BF16)
    d_rmask = din("rmask", [128, RET_H, 128]); d_qdec = din("qdec", [128, RET_H, 128])
    d_kdec = din("kdec", [128, 2, RET_H]); d_gmask = din("gmask", [128, 128]); d_ucum = din("ucum", [128, 128])
    d_neghalf = din("neghalf", [128, 512]); d_epsv = din("epsv", [128, 8])
    d_identf = din("ident_f", [128, 128]); d_onesf = din("ones_f", [128, 128])
    d_cos = din("cos", [128, npos]); d_sin = din("sin", [128, npos])

    yTp = [dout("yTp%d" % i, [D, HSEQ]) for i in range(NHALF)]; yTs = dout("yTs", [D, DEC_SEQ])
    o_ret = {"p": dout("ret_p", [RET_H, RET_DK, RET_DV]), "s": dout("ret_s", [RET_H, RET_DK, RET_DV])}
    o_gla = {"p": dout("gla_p", [GLA_H, GLA_DK, GLA_DV]), "s": dout("gla_s", [GLA_H, GLA_DK, GLA_DV])}
    o_conv = {"p": dout("conv_p", [128, DEPTH, NBLK_FF, 2]), "s": dout("conv_s", [128, DEPTH, NBLK_FF, 2])}
    dbg_out = {}

    arena_h = nc.alloc_sbuf_tensor("arena", [128, ARENA], U8)
    arena = arena_h.ap()
    psum_h = nc.alloc_psum_tensor("psum", [128, 8, 512], F32)
    psum = psum_h.ap()

    off = {"_": 0}
    reg = {}

    def region(name, nbytes):
        assert nbytes % 256 == 0, name
        reg[name] = (off["_"], nbytes)
        off["_"] += nbytes
        assert off["_"] <= ARENA, (name, off["_"])

    def view(name, dt, shape, boff=0, parts=128):
        o, nb = reg[name]
        n = int(np.prod(shape)) * DSZ[dt]
        assert boff + n <= nb, (name, boff, n, nb)
        ap = arena[0:parts, o + boff:o + boff + n].bitcast(dt)
        if len(shape) == 2:
            ap = ap.rearrange("p (a b) -> p a b", a=shape[0])
        elif len(shape) == 3:
            ap = ap.rearrange("p (a b c) -> p a b c", a=shape[0], b=shape[1])
        return ap

    region("x", 16384); region("hT", 8192); region("ms", 2048); region("rstd", 2048)
    region("identf", 512); region("onesf", 512)
    region("qkk", 24576)
    region("vtok", 16384)
    region("sgT", 16384)
    region("ogT", 16384)
    region("mix", 20480)
    region("PT", 1024); region("on", 4096); region("stats", 1024); region("stmp", 2048); region("aT", 1024)
    region("Sret", 16384); region("Sretb", 8192); region("Sgla", 4096); region("Sglab", 2048)
    region("wring", 3 * 8192)
    for nme, nb in [("ident", 256), ("ones", 256), ("rmask", 2048), ("qdec", 2048), ("kdec", 256), ("gmask", 512),
                    ("ucum", 512), ("neghalf", 2048), ("epsv", 256), ("gains", 256), ("gn", 256), ("ng", 256), ("cw", 1280),
                    ("cb", 512), ("wa2", 1024), ("uhb", 512), ("uhf", 768)]:
        region(nme, nb)

    ident = view("ident", BF16, [128]); ones = view("ones", BF16, [128])
    identf = view("identf", F32, [128]); onesf = view("onesf", F32, [128])
    rmask = view("rmask", F32, [RET_H, 128]); qdec = view("qdec", F32, [RET_H, 128])
    kdec = view("kdec", F32, [2, RET_H]); gmask = view("gmask", F32, [128]); ucum = view("ucum", F32, [128])
    neghalf = view("neghalf", F32, [512]); epsv = view("epsv", F32, [8])
    gains = view("gains", F32, [5, 8]); gnT = view("gn", F32, [16]); ngT = view("ng", F32, [8])
    cw = view("cw", F32, [DEPTH, 3, NBLK_FF]); cb = view("cb", F32, [DEPTH, NBLK_FF])
    wa2 = view("wa2", BF16, [512], parts=32)
    uhb = view("uhb", BF16, [DEPTH, NBLK_FF, 2]); uhf = view("uhf", F32, [DEPTH, NBLK_FF, 2])
    Sret = view("Sret", F32, [RET_H, 2, 512]); Sretb = view("Sretb", BF16, [RET_H, 2, 512])
    Sgla = view("Sgla", F32, [GLA_H, 256]); Sglab = view("Sglab", BF16, [GLA_H, 256])

    psb = [psum[:, b, :] for b in range(8)]
    psb_bf = [psum[:, b, :].bitcast(BF16) for b in range(8)]
    bank_ctr = {"i": 0}

    reserved = set()

    def bank():
        while True:
            b = bank_ctr["i"] % 8
            bank_ctr["i"] += 1
            if b not in reserved:
                return b

    P.dma([(ident, d_ident), (ones, d_ones), (rmask, d_rmask), (qdec, d_qdec), (kdec, d_kdec), (gmask, d_gmask),
           (ucum, d_ucum), (neghalf, d_neghalf), (epsv, d_epsv), (identf, d_identf), (onesf, d_onesf), (gains, d_gains), (gnT, d_gn), (ngT, d_ng), (cw, d_cw), (cb, d_cb)],
          "const")
    P.dma([(wa2[0:17, :], d_wa2)], "wa2c", eng="pool")
    for n, k, m in WEIGHTS:
        if n in ("ret_w_out", "gla_w_out"):
            continue
        P.dma([(Wb[n], W32[n])], "cast_" + n, eng="pool", max_dma_last_dim=4096)
    fi = 0
    for n, gv_, nch in (("ret_w_out", gnT, 16), ("gla_w_out", ngT, 8)):
        for ec in range(nch):
            wi = view("mix", F32, [1024], boff=(fi % 2) * 4096)
            wo = view("mix", BF16, [1024], boff=8192 + (fi % 2) * 2048)
            fi += 1
            P.dma([(wi, W32[n][ec * 128:(ec + 1) * 128, :])], "foldin")
            P.ts("dve", wo, wi, gv_[:, ec:ec + 1], ALU.mult)
            P.dma([(Wb[n][ec * 128:(ec + 1) * 128, :], wo)], "foldout")

    ring = {"i": 0}

    def slab(wname, kc0, nkc, colgroups):
        s = ring["i"] % 3
        ring["i"] += 1
        ncols = sum(c for _, c in colgroups)
        assert nkc * ncols * 2 <= 8192
        v = view("wring", BF16, [nkc, ncols], boff=s * 8192)
        pairs = []
        co = 0
        for c0, cn in colgroups:
            src = Wb[wname][kc0 * 128:(kc0 + nkc) * 128, c0:c0 + cn].rearrange("(k p) n -> p k n", p=128)
            pairs.append((v[:, :, co:co + cn], src))
            co += cn
        P.dma(pairs, "w%d" % s)
        return v

    aT_all = view("aT", BF16, [512], parts=32)
    P.memset("dve", aT_all, 1.0)

    def run_tile(sk, t0, NT, CL, last):
        NCH = NT // CL
        li = 0 if CL == 128 else 1
        xsrc = xTp[t0 // HSEQ] if sk == "p" else xTs
        ydst = yTp[t0 // HSEQ] if sk == "p" else yTs
        tq = t0 % HSEQ if sk == "p" else t0
        pos0 = t0 if sk == "p" else seq + t0
        x = view("x", F32, [8, NT])
        hT = view("hT", BF16, [8, NT])
        ybuf = view("vtok", F32, [8, NT])
        sq = view("sgT", BF16, [8, NT], boff=8192)
        ms = view("ms", F32, [4]); rstd = view("ms", F32, [4], boff=256)
        rbc = view("rstd", F32, [4, 128])
        qT = view("qkk", BF16, [8, NT]); kT = view("qkk", BF16, [8, NT], boff=8192)
        ktok = view("qkk", BF16, [NCH, 1024], boff=16384)
        vtok = view("vtok", BF16, [NCH, 2048])
        sgT = view("sgT", BF16, [16, NT]); ogT = view("ogT", BF16, [16, NT])
        PTv = [view("PT", BF16, [128], boff=i * 256) for i in range(4)]
        onv = [view("on", BF16, [512], boff=i * 1024) for i in range(3)]
        statv = [(view("stats", F32, [6], boff=i * 256), view("stats", F32, [2], boff=i * 256 + 64),
                  view("stats", F32, [1], boff=i * 256 + 128), view("stats", F32, [1], boff=i * 256 + 192))
                 for i in range(4)]
        stmp = [view("stmp", BF16, [NT], boff=i * 1024) for i in range(2)]
        cnt = {"pt": 0, "on": 0, "st": 0, "sv": 0}

        def tsl(ch):
            return slice(ch * CL, (ch + 1) * CL)

        P.dma([(x, xsrc[:, tq:tq + NT].rearrange("(c p) t -> p c t", p=128))], "xin")

        def norm(gidx, out_hT=True):
            TB = min(128, NT)
            NTB = NT // TB
            P.act(sq, x, AF.Square)
            b = bank()
            for tb in range(NTB):
                for c in range(8):
                    P.mm(psb[b][0:TB, tb:tb + 1], sq[:, c, tb * TB:(tb + 1) * TB], ones[:, 0:1],
                         start=(c == 0), stop=(c == 7), sig=(c == 7 and tb == NTB - 1))
            P.ts("dve", ms[0:TB, 0:NTB], psb[b][0:TB, 0:NTB], 1.0 / D, ALU.mult, EPS, ALU.add)
            P.tt("pool", rstd[0:TB, 0:NTB], ms[0:TB, 0:NTB], neghalf[0:TB, 0:NTB], ALU.pow)
            b2 = bank()
            for tb in range(NTB):
                P.ts("dve", rbc[0:TB, tb, :], onesf[0:TB, :], rstd[0:TB, tb:tb + 1], ALU.mult)
                P.mm(psb[b2][:, tb * TB:(tb + 1) * TB], rbc[0:TB, tb, :], identf[0:TB, 0:TB],
                     start=True, stop=True, sig=(tb == NTB - 1))
            for c in range(8):
                dst = hT[:, c, :] if out_hT else ybuf[:, c, :]
                P.stt(dst, x[:, c, :], gains[:, gidx, c:c + 1], psb[b2][:, 0:NT], ALU.mult, ALU.mult)

        def resid_proj(wname, nkc, src, kslabs):
            for cg in range(4):
                banks = [bank(), bank()]
                k0 = 0
                while k0 < nkc:
                    nk = min(kslabs, nkc - k0)
                    sl = slab(wname, k0, nk, [(cg * 256, 256)])
                    for j in range(2):
                        for kk in range(nk):
                            kc = k0 + kk
                            P.mm(psb[banks[j]][:, 0:NT], sl[:, kk, j * 128:(j + 1) * 128], src[:, kc, :],
                                 start=(kc == 0), stop=(kc == nkc - 1),
                                 sig=(kc == nkc - 1) or (kk == nk - 1 and j == 1))
                    k0 += nk
                for j in range(2):
                    blk = cg * 2 + j
                    P.tt("dve", x[:, blk, :], x[:, blk, :], psb[banks[j]][:, 0:NT], ALU.add)

        def retention():
            cs = view("mix", F32, [2, NT])
            cqs = [view("mix", F32, [2, NT], boff=4096 + i * 4096) for i in range(2)]
            rt = view("mix", F32, [4, NT], boff=12288)
            P.dma([(cs[:, 0, :], d_cos[:, pos0:pos0 + NT]), (cs[:, 1, :], d_sin[:, pos0:pos0 + NT])], "cs")

            def rotary(pa, pb, ct, st_, o1, o2):
                P.tt("dve", rt[:, 0, :], pa, ct, ALU.mult)
                P.tt("dve", rt[:, 1, :], pb, st_, ALU.mult)
                P.tt("dve", o1, rt[:, 0, :], rt[:, 1, :], ALU.subtract)
                P.tt("dve", rt[:, 2, :], pa, st_, ALU.mult)
                P.tt("dve", rt[:, 3, :], pb, ct, ALU.mult)
                P.tt("dve", o2, rt[:, 2, :], rt[:, 3, :], ALU.add)

            sg_tok = view("sgT", BF16, [NCH, 2048])

            def mk_groups(h):
                st = {}
                gl = []
                for kind in (0, 1):
                    for ch in range(NCH):
                        def grp(kind=kind, ch=ch):
                            if ch == 0:
                                st[kind] = slab("ret_w_in", 0, 8, [(2048 + kind * 2048 + h * 512, 512)])
                            sl_ = st[kind]
                            b_ = bank()
                            for kc in range(8):
                                P.mm(psb[b_][0:CL, :], hT[:, kc, tsl(ch)], sl_[:, kc, :], start=(kc == 0), stop=(kc == 7))
                            if kind == 0:
                                P.copy("act", vtok[0:CL, ch, h * 512:(h + 1) * 512], psb[b_][0:CL, :])
                            else:
                                P.act(sg_tok[0:CL, ch, h * 512:(h + 1) * 512], psb[b_][0:CL, :], AF.Silu)
                        gl.append(grp)
                return gl

            Q = []
            for h in range(RET_H):
                Q += mk_groups(h)

            def emit_groups(n):
                for _ in range(n):
                    if Q:
                        Q.pop(0)()

            for which in range(2):
                dstT = qT if which == 0 else kT
                for sh in range(2):
                    sl = slab("ret_w_in", 0, 8, [(which * 1024 + sh * 512, 512)])
                    for hh in range(2):
                        h = sh * 2 + hh
                        ba, bb = bank(), bank()
                        for half, bk in ((0, ba), (1, bb)):
                            for kc in range(8):
                                P.mm(psb[bk][:, 0:NT], sl[:, kc, (hh * 2 + half) * 128:(hh * 2 + half + 1) * 128],
                                     hT[:, kc, :], start=(kc == 0), stop=(kc == 7))
                        if which == 0:
                            cq = cqs[h % 2]
                            qd = qdec[:, h, 0:CL].unsqueeze(1).to_broadcast([128, NCH, CL])
                            for t_ in range(2):
                                P.tt("pool", cq[:, t_, :].rearrange("p (a b) -> p a b", a=NCH),
                                     cs[:, t_, :].rearrange("p (a b) -> p a b", a=NCH), qd, ALU.mult)
                            rotary(psb[ba][:, 0:NT], psb[bb][:, 0:NT], cq[:, 0, :], cq[:, 1, :],
                                   dstT[:, 2 * h, :], dstT[:, 2 * h + 1, :])
                        else:
                            rotary(psb[ba][:, 0:NT], psb[bb][:, 0:NT], cs[:, 0, :], cs[:, 1, :],
                                   dstT[:, 2 * h, :], dstT[:, 2 * h + 1, :])
                        emit_groups(1)
            for h in range(RET_H):
                for ch in range(NCH):
                    b = bank()
                    for dc in range(2):
                        P.tr(psb_bf[b][0:CL, dc * 128:(dc + 1) * 128], kT[:, 2 * h + dc, tsl(ch)], ident,
                             sig=(dc == 1))
                    P.act(ktok[0:CL, ch, h * 256:(h + 1) * 256], psb_bf[b][0:CL, 0:256], AF.Copy,
                          scale=kdec[0:CL, li, h:h + 1])
            for h in range(RET_H):
                for ch in range(NCH):
                    bs = bank()
                    for dc in range(2):
                        P.mm(psb[bs][0:CL, 0:CL], kT[:, 2 * h + dc, tsl(ch)], qT[:, 2 * h + dc, tsl(ch)],
                             start=(dc == 0), stop=(dc == 1))
                    PT = PTv[cnt["pt"] % 4]; cnt["pt"] += 1
                    P.tt("dve", PT[0:CL, 0:CL], psb[bs][0:CL, 0:CL], rmask[0:CL, h, 0:CL], ALU.mult)
                    emit_groups(1)
                    bo = bank()
                    P.mm(psb[bo][0:CL, :], PT[0:CL, 0:CL], vtok[0:CL, ch, h * 512:(h + 1) * 512], start=True, stop=False)
                    for dc in range(2):
                        P.mm(psb[bo][0:CL, :], qT[:, 2 * h + dc, tsl(ch)], Sretb[:, h, dc, :], start=False, stop=(dc == 1))
                    bS = [bank(), bank()]
                    for dc in range(2):
                        P.mm(psb[bS[dc]][:, :], ktok[0:CL, ch, h * 256 + dc * 128:h * 256 + (dc + 1) * 128],
                             vtok[0:CL, ch, h * 512:(h + 1) * 512], start=True, stop=True)
                    stats6, mv, ve, rs = statv[cnt["sv"] % 4]; cnt["sv"] += 1
                    P.op("dve", lambda e, o=stats6[0:CL, :], i=psb[bo][0:CL, :]: e.bn_stats(o, i),
                         reads=[psb[bo][0:CL, :]], writes=[stats6[0:CL, :]])
                    P.op("dve", lambda e, o=mv[0:CL, :], i=stats6[0:CL, :]: e.bn_aggr(o, i),
                         reads=[stats6[0:CL, :]], writes=[mv[0:CL, :]])
                    P.ts("dve", ve[0:CL, :], mv[0:CL, 1:2], EPS, ALU.add)
                    P.tt("pool", rs[0:CL, :], ve[0:CL, :], neghalf[0:CL, 0:1], ALU.pow)
                    for dc in range(2):
                        P.stt(Sret[:, h, dc, :], Sret[:, h, dc, :], gL[CL][h], psb[bS[dc]][:, :], ALU.mult, ALU.add)
                        P.copy("act", Sretb[:, h, dc, :], Sret[:, h, dc, :])
                    emit_groups(1)
                    on = onv[cnt["on"] % 3]; cnt["on"] += 1
                    P.ts("dve", on[0:CL, :], psb[bo][0:CL, :], mv[0:CL, 0:1], ALU.subtract, rs[0:CL, :], ALU.mult)
                    P.tt("dve", on[0:CL, :], on[0:CL, :], sg_tok[0:CL, ch, h * 512:(h + 1) * 512], ALU.mult)
                    bt = bank()
                    for eb in range(4):
                        P.tr(psb_bf[bt][:, eb * CL:(eb + 1) * CL], on[0:CL, eb * 128:(eb + 1) * 128],
                             ident[0:CL, 0:CL], sig=(eb == 3))
                    P.copy("act", ogT[:, 4 * h:4 * h + 4, tsl(ch)],
                           psb_bf[bt][:, 0:4 * CL].rearrange("p (a b) -> p a b", a=4))
            emit_groups(len(Q))
            resid_proj("ret_w_out", 16, ogT, 16)

        def ffn(layer):
            actT = view("qkk", BF16, [22, NT])
            ubuf = [[view("sgT", BF16, [NT + 2], boff=(r * 2 + gv) * 1280) for gv in range(2)] for r in range(2)]
            sgt = [view("sgT", BF16, [NT], boff=5120 + r * 1024) for r in range(2)]
            dg = [[view("on", BF16, [3, 128], boff=(r * 2 + gv) * 768) for gv in range(2)] for r in range(2)]
            wn = "ffn_w_up%d" % layer
            it = 0
            pend = None

            def conv_stage(r, gb):
                cbk = []
                for gv in range(2):
                    b = bank(); cbk.append(b)
                    for j in range(3):
                        P.mm(psb[b][:, 0:NT], dg[r][gv][:, j, :], ubuf[r][gv][:, j:j + NT], start=(j == 0), stop=(j == 2))
                P.act(sgt[r], psb[cbk[0]][:, 0:NT], AF.Silu, bias=cb[:, layer, gb:gb + 1])
                P.stt(actT[:, gb, :], psb[cbk[1]][:, 0:NT], cb[:, layer, gb + 22:gb + 23], sgt[r], ALU.add, ALU.mult)

            for s in range(11):
                sl = slab(wn, 0, 8, [(s * 256, 256), (DFF + s * 256, 256)])
                for jj in range(2):
                    gb = s * 2 + jj
                    r = it % 2; it += 1
                    for gv in range(2):
                        blk = gb + gv * 22
                        b = bank()
                        for kc in range(8):
                            P.mm(psb[b][:, 0:NT], sl[:, kc, gv * 256 + jj * 128:gv * 256 + (jj + 1) * 128], hT[:, kc, :],
                                 start=(kc == 0), stop=(kc == 7))
                        ub = ubuf[r][gv]
                        P.copy("pool", ub[:, 0:2], uhb[:, layer, blk, :])
                        P.copy("act", ub[:, 2:2 + NT], psb[b][:, 0:NT])
                        P.copy("pool", uhb[:, layer, blk, :], ub[:, NT:NT + 2])
                        if last:
                            P.copy("dve", uhf[:, layer, blk, :], psb[b][:, NT - 2:NT])
                        for j in range(3):
                            P.ts("pool", dg[r][gv][:, j, :], ident, cw[:, layer, j, blk:blk + 1], ALU.mult, 1.0, ALU.mult)
                    if pend is not None:
                        conv_stage(*pend)
                    pend = (r, gb)
            conv_stage(*pend)
            resid_proj("ffn_w_down%d" % layer, 22, actT, 11)

        def gla():
            vt = view("vtok", BF16, [NCH, 1024])
            sr_tok = view("sgT", BF16, [NCH, 1024])
            kbar = view("qkk", BF16, [NCH, 512], boff=16384)
            sp = view("mix", F32, [NCH, 512])
            Eq = view("mix", F32, [GLA_H, NT], boff=8192)
            Ek = [view("mix", F32, [NT], boff=16384 + i * 2048) for i in range(2)]
            zt = view("on", F32, [512])
            kbt = view("stmp", BF16, [NT])
            junk = view("on", BF16, [256], boff=2048)

            def mk_groups(kind, s2):
                st = {}
                gl = []
                for ch in range(NCH):
                    def grp(ch=ch):
                        if ch == 0:
                            st["s"] = slab("gla_w_in", 0, 8, [(1024 + kind * 1024 + s2 * 512, 512)])
                        b_ = bank()
                        for kc in range(8):
                            P.mm(psb[b_][0:CL, :], hT[:, kc, tsl(ch)], st["s"][:, kc, :], start=(kc == 0), stop=(kc == 7))
                        if kind == 0:
                            P.copy("act", vt[0:CL, ch, s2 * 512:(s2 + 1) * 512], psb[b_][0:CL, :])
                        else:
                            P.act(sr_tok[0:CL, ch, s2 * 512:(s2 + 1) * 512], psb[b_][0:CL, :], AF.Silu)
                    gl.append(grp)
                return gl

            Q = mk_groups(0, 0) + mk_groups(1, 0) + mk_groups(0, 1) + mk_groups(1, 1)

            def emit_groups(n):
                for _ in range(n):
                    if Q:
                        Q.pop(0)()

            sl = slab("gla_w_in", 0, 8, [(3072, 16)])
            b = bank()
            for kc in range(8):
                P.mm(psb[b][0:16, 0:NT], sl[:, kc, 0:16], hT[:, kc, :], start=(kc == 0), stop=(kc == 7))
            P.copy("act", aT_all[0:16, 0:NT], psb[b][0:16, 0:NT])
            for ch in range(NCH):
                b = bank()
                P.mm(psb[b][0:CL, :], aT_all[0:17, tsl(ch)], wa2[0:17, :], start=True, stop=True)
                P.act(zt[0:CL, :], psb[b][0:CL, :], AF.Exp, scale=-1.0)
                P.act(sp[0:CL, ch, :], zt[0:CL, :], AF.Ln, bias=1.0)
            for h in range(GLA_H):
                b = bank()
                for ch in range(NCH):
                    P.mm(psb[b][:, tsl(ch)], sp[0:CL, ch, h * 128:(h + 1) * 128], ucum[0:CL, 0:CL], start=True, stop=True,
                         sig=(ch == NCH - 1))
                P.act(Eq[:, h, :], psb[b][:, 0:NT], AF.Exp)
            slq = slab("gla_w_in", 0, 8, [(0, 512)])
            for h in range(GLA_H):
                bq = bank()
                for kc in range(8):
                    P.mm(psb[bq][:, 0:NT], slq[:, kc, h * 128:(h + 1) * 128], hT[:, kc, :], start=(kc == 0), stop=(kc == 7))
                P.tt("dve", qT[:, h, :], psb[bq][:, 0:NT], Eq[:, h, :], ALU.mult)
                if NCH >= 4:
                    emit_groups(1)
            slk = slab("gla_w_in", 0, 8, [(512, 512)])
            for h in range(GLA_H):
                P.op("dve", lambda e, o=Ek[h % 2], i=Eq[:, h, :]: e.reciprocal(o, i), reads=[Eq[:, h, :]], writes=[Ek[h % 2]])
                bk = bank()
                for kc in range(8):
                    P.mm(psb[bk][:, 0:NT], slk[:, kc, h * 128:(h + 1) * 128], hT[:, kc, :], start=(kc == 0), stop=(kc == 7))
                P.tt("dve", kT[:, h, :], psb[bk][:, 0:NT], Ek[h % 2], ALU.mult)
                for ch in range(NCH):
                    P.ts("dve", kbt[:, tsl(ch)], kT[:, h, tsl(ch)], Eq[:, h, (ch + 1) * CL - 1:(ch + 1) * CL], ALU.mult,
                         GLA_DK ** -0.5, ALU.mult)
                    bt = bank()
                    P.tr(psb_bf[bt][0:CL, 0:128], kbt[:, tsl(ch)], ident)
                    P.copy("act", kbar[0:CL, ch, h * 128:(h + 1) * 128], psb_bf[bt][0:CL, 0:128])
                if NCH >= 4:
                    emit_groups(1)
            if NCH < 4:
                emit_groups(len(Q))
            for h in range(GLA_H):
                for ch in range(NCH):
                    ba = bank()
                    P.mm(psb[ba][0:CL, 0:CL], kT[:, h, tsl(ch)], qT[:, h, tsl(ch)], start=True, stop=True)
                    PT = PTv[cnt["pt"] % 4]; cnt["pt"] += 1
                    P.tt("dve", PT[0:CL, 0:CL], psb[ba][0:CL, 0:CL], gmask[0:CL, 0:CL], ALU.mult)
                    emit_groups(1)
                    bo = bank()
                    P.mm(psb[bo][0:CL, 0:256], PT[0:CL, 0:CL], vt[0:CL, ch, h * 256:(h + 1) * 256], start=True, stop=False)
                    P.mm(psb[bo][0:CL, 0:256], qT[:, h, tsl(ch)], Sglab[:, h, :], start=False, stop=True)
                    bS = bank()
                    P.mm(psb[bS][:, 0:256], kbar[0:CL, ch, h * 128:(h + 1) * 128], vt[0:CL, ch, h * 256:(h + 1) * 256],
                         start=True, stop=True)
                    stats6, mv, ve, rs = statv[cnt["sv"] % 4]; cnt["sv"] += 1
                    P.act(junk[0:CL, :], psb[bo][0:CL, 0:256], AF.Square, accum=ve[0:CL, :])
                    P.ts("dve", mv[0:CL, 0:1], ve[0:CL, :], 1.0 / GLA_DV, ALU.mult, EPS, ALU.add)
                    P.tt("pool", rs[0:CL, :], mv[0:CL, 0:1], neghalf[0:CL, 0:1], ALU.pow)
                    P.stt(Sgla[:, h, :], Sgla[:, h, :], Eq[:, h, (ch + 1) * CL - 1:(ch + 1) * CL], psb[bS][:, 0:256],
                          ALU.mult, ALU.add)
                    P.copy("act", Sglab[:, h, :], Sgla[:, h, :])
                    on = view("on", BF16, [256], boff=2560 + (cnt["on"] % 2) * 512); cnt["on"] += 1
                    P.ts("dve", on[0:CL, :], psb[bo][0:CL, 0:256], rs[0:CL, :], ALU.mult)
                    P.tt("dve", on[0:CL, :], on[0:CL, :], sr_tok[0:CL, ch, h * 256:(h + 1) * 256], ALU.mult)
                    bt = bank()
                    for eb in range(2):
                        P.tr(psb_bf[bt][:, eb * CL:(eb + 1) * CL], on[0:CL, eb * 128:(eb + 1) * 128], ident[0:CL, 0:CL],
                             sig=(eb == 1))
                    P.copy("act", ogT[:, 2 * h:2 * h + 2, tsl(ch)],
                           psb_bf[bt][:, 0:2 * CL].rearrange("p (a b) -> p a b", a=2))
            emit_groups(len(Q))
            resid_proj("gla_w_out", 8, ogT, 8)

        norm(0); retention()
        norm(2); ffn(0)
        norm(1); gla()
        norm(3); ffn(1)
        norm(4, out_hT=False)
        P.dma([(ydst[:, tq:tq + NT].rearrange("(c p) t -> p c t", p=128), ybuf)], "yout")

    def seq_end(sk):
        P.dma([(o_ret[sk].rearrange("h (dc p) e -> p h dc e", p=128), Sret)], "so_ret")
        P.dma([(o_gla[sk].rearrange("h p e -> p h e"), Sgla)], "so_gla")
        P.dma([(o_conv[sk], uhf)], "so_conv")

    P.dma([(Sret, st_ret.rearrange("h (dc p) e -> p h dc e", p=128)), (Sgla, st_gla.rearrange("h p e -> p h e")),
           (uhf, cconv)], "stin")
    P.copy("act", Sretb, Sret); P.copy("act", Sglab, Sgla); P.copy("dve", uhb, uhf)
    run_tile("s", 0, DEC_SEQ, DEC_SEQ, True)
    seq_end("s")
    P.memset("pool", Sret, 0.0); P.memset("pool", Sretb, 0.0); P.memset("pool", Sgla, 0.0); P.memset("pool", Sglab, 0.0)
    P.memset("pool", uhb, 0.0)
    ntile = seq // 512
    for ti in range(ntile):
        run_tile("p", ti * 512, 512, 128, ti == ntile - 1)
    seq_end("p")

    P.wait_all("sp", ("yout", "so_ret", "so_gla", "so_conv"))

    with contextlib.ExitStack() as es:
        for lname, L in P.lanes.items():
            L["sem"] = es.enter_context(nc.semaphore("s_" + lname))
        block = es.enter_context(nc.Block())
        P.emit(block)
    return nc, cst, P


_CACHE = {}


def _prep_inputs(inp, seq, cst):
    f32 = lambda a: np.ascontiguousarray(np.asarray(a, np.float32))
    shared = {
        "ret_w_in": f32(inp["ret_w_in"][0]), "ret_w_out": f32(inp["ret_w_out"][0]),
        "gla_w_in": f32(inp["gla_w_in"][0]), "gla_w_out": f32(inp["gla_w_out"][0]),
        "ffn_w_up0": f32(inp["ffn_w_up"][0]), "ffn_w_up1": f32(inp["ffn_w_up"][1]),
        "ffn_w_down0": f32(inp["ffn_w_down"][0]), "ffn_w_down1": f32(inp["ffn_w_down"][1]),
    }
    nm, nf = np.asarray(inp["norm_mix"], np.float32), np.asarray(inp["norm_ffn"], np.float32)
    gains = np.stack([_fm(nm[0]), _fm(nm[1]), _fm(nf[0]), _fm(nf[1]), _fm(inp["norm_final"])], axis=1)
    shared["gains"] = np.ascontiguousarray(gains)
    shared["gn"] = _fm(np.asarray(inp["ret_gn_g"], np.float32)[0].reshape(-1))
    shared["ng"] = _fm(np.asarray(inp["gla_norm_g"], np.float32)[0].reshape(-1))
    cwv = np.asarray(inp["ffn_conv_w"], np.float32)
    shared["cw"] = np.ascontiguousarray(cwv.reshape(DEPTH, 3, NBLK_FF, 128).transpose(3, 0, 1, 2))
    cbv = np.asarray(inp["ffn_conv_b"], np.float32)
    shared["cb"] = np.ascontiguousarray(cbv.reshape(DEPTH, NBLK_FF, 128).transpose(2, 0, 1))
    shared["wa2aug"] = np.ascontiguousarray(np.concatenate(
        [np.asarray(inp["gla_w_a2"], np.float32)[0], np.asarray(inp["gla_b_a"], np.float32)[0][None, :]], axis=0))
    for k in ("ident_bf", "ones_bf", "ident_f", "ones_f", "rmask", "qdec", "kdec", "gmask", "ucum", "neghalf", "epsv", "cos", "sin"):
        shared[k] = cst[k]
    xp = np.asarray(inp["x_prompt"], np.float32)
    xs = np.asarray(inp["x_sample"], np.float32)
    cc = np.asarray(inp["cache_conv"], np.float32)
    maps = []
    for b in range(NCORES):
        m = dict(shared)
        hs = seq // (-(-seq // 4096))
        for i in range(seq // hs):
            m["xTp%d" % i] = np.ascontiguousarray(xp[b, i * hs:(i + 1) * hs].T)
        m["xTs"] = np.ascontiguousarray(xs[b].T)
        m["st_ret"] = f32(inp["state_ret"][0, b])
        m["st_gla"] = f32(inp["state_gla"][0, b])
        m["cconv"] = np.ascontiguousarray(cc[:, b].reshape(DEPTH, 2, NBLK_FF, 128).transpose(3, 0, 2, 1))
        maps.append(m)
    return maps


def _run(inp, seq):
    if seq not in _CACHE:
        _CACHE[seq] = build(seq)
    nc, cst, _ = _CACHE[seq]
    maps = _prep_inputs(inp, seq, cst)
    res = run_bass_kernel_spmd(nc, maps, core_ids=list(range(NCORES)))
    R = res.results
    B = NCORES
    hs = seq // (-(-seq // 4096))
    y_p = np.stack([np.concatenate([R[b]["yTp%d" % i].T for i in range(seq // hs)], axis=0)
                    for b in range(B)]).astype(np.float32)
    y_s = np.stack([R[b]["yTs"].T for b in range(B)]).astype(np.float32)
    ret_p = np.stack([R[b]["ret_p"] for b in range(B)])[None].astype(np.float32)
    ret_s = np.stack([R[b]["ret_s"] for b in range(B)])[None].astype(np.float32)
    gla_p = np.stack([R[b]["gla_p"] for b in range(B)])[None].astype(np.float32)
    gla_s = np.stack([R[b]["gla_s"] for b in range(B)])[None].astype(np.float32)

    def conv(k):
        a = np.stack([R[b][k] for b in range(B)])
        return np.ascontiguousarray(a.transpose(2, 0, 4, 3, 1).reshape(DEPTH, B, 2, 2 * DFF)).astype(np.float32)

    return (y_p, y_s, ret_p, ret_s, gla_p, gla_s, conv("conv_p"), conv("conv_s"))


def kernel(**inputs):
    seq = int(np.asarray(inputs["x_prompt"]).shape[1])
    return _run(inputs, seq)
```

```python
import contextlib
import math
import numpy as np
import ml_dtypes
import concourse.bass as bass
import concourse.mybir as mybir
from concourse.bass_utils import run_bass_kernel_spmd

F32 = mybir.dt.float32
BF16 = mybir.dt.bfloat16
U8 = mybir.dt.uint8
AF = mybir.ActivationFunctionType
ALU = mybir.AluOpType
DSZ = {F32: 4, BF16: 2, U8: 1}

D = 1024
DEPTH = 2
RET_H = 4
RET_DK = 256
RET_DV = 512
GLA_H = 4
GLA_DK = 128
GLA_DV = 256
GLA_RANK = 16
GLA_TAU = 16.0
DFF = 2816
NBLK_FF = 2 * DFF // 128
EPS = 1e-6
ROPE_BASE = 10000.0
PAST_LEN = 2048
DEC_SEQ = 32
NCORES = 8

ARENA = 211968


class Prog:
    SBG = 256

    def __init__(self, nc):
        self.nc = nc
        self.q = {e: [] for e in ("pe", "act", "dve", "pool", "sp")}
        self.lanes = {}
        for e in ("pe", "act", "dve", "pool"):
            self.lanes[e] = {"count": 0, "inc": 1, "sem": None}
        self.clock = {e: {} for e in self.q}
        self.snap = {}
        self.gran = {}
        self.maxwait = {}
        self.nops = 0

    def lane(self, name):
        if name not in self.lanes:
            self.lanes[name] = {"count": 0, "inc": 16, "sem": None}
        return name

    def keys(self, ap):
        t = ap.tensor
        name = t.name
        if name not in ("arena", "psum"):
            return [("d", name)]
        esz = DSZ[ap.dtype]
        pairs = list(ap.ap)
        pstep = pairs[0][0]
        off = int(ap.offset)
        inpart = off % pstep if pstep else off
        starts = [inpart]
        free = [(s, c) for (s, c) in pairs[1:] if c > 1 or len(pairs) == 2]
        free = [(s, c) for (s, c) in free if s != 0]
        length = 1
        i = 0
        while i < len(free):
            s, c = free[i]
            inner_ext = sum((cc - 1) * abs(ss) for ss, cc in free[i + 1:]) + 1
            if i == len(free) - 1:
                length = (c - 1) * abs(s) + 1
            elif abs(s) > inner_ext and len(starts) * c <= 128:
                starts = [st + k * s for st in starts for k in range(c)]
            else:
                length = sum((cc - 1) * abs(ss) for ss, cc in free[i:]) + 1
                break
            i += 1
        g = self.SBG if name == "arena" else 2048
        ks = set()
        for st in starts:
            lo = (st * esz) // g
            hi = ((st + length) * esz - 1) // g
            for k in range(lo, hi + 1):
                ks.add((name, k))
        return ks

    def op(self, eng, fn, reads=(), writes=(), sig=True, lane=None, n=1, embed=True):
        self.nops += 1
        mylane = lane if lane is not None else eng
        L = self.lanes[mylane]
        raw = {}
        oth = {}

        def add(d, ls):
            l, s = ls
            if d.get(l, 0) < s:
                d[l] = s

        rkeys = set()
        for ap in reads:
            rkeys |= set(self.keys(ap))
        wkeys = set()
        for ap in writes:
            wkeys |= set(self.keys(ap))
        for k in rkeys:
            g = self.gran.get(k)
            if g is not None and g[0] is not None:
                add(raw, g[0])
        for k in wkeys:
            g = self.gran.get(k)
            if g is not None:
                if g[0] is not None:
                    add(oth, g[0])
                for l, s in g[1].items():
                    add(oth, (l, s))
        deps = {}
        for l, s in raw.items():
            if l == eng and lane is None:
                if eng == "pe":
                    continue
            add(deps, (l, s))
        for l, s in oth.items():
            if l == eng and lane is None and eng == "pe":
                continue
            add(deps, (l, s))
        if lane is not None and L["count"] > 0:
            add(deps, (mylane, L["count"]))
        if sig:
            L["count"] += n
            seq = L["count"]
        else:
            seq = L["count"] + 1
        ck = self.clock[eng]
        waits = []
        for l, s in sorted(deps.items()):
            if ck.get(l, 0) < s:
                waits.append((l, s * self.lanes[l]["inc"]))
                if self.maxwait.get(l, 0) < s:
                    self.maxwait[l] = s
                sn = self.snap.get((l, s))
                if sn:
                    for l2, s2 in sn.items():
                        if ck.get(l2, 0) < s2:
                            ck[l2] = s2
                ck[l] = s
        if sig:
            sn = dict(ck)
            sn[mylane] = seq
            self.snap[(mylane, seq)] = sn
        for k in rkeys:
            g = self.gran.get(k)
            if g is None:
                g = [None, {}]
                self.gran[k] = g
            if g[1].get(mylane, 0) < seq:
                g[1][mylane] = seq
        for k in wkeys:
            self.gran[k] = [(mylane, seq), {}]
        self.q[eng].append((waits, fn, (mylane, L["inc"]) if sig else None, embed))

    def wait_all(self, eng, lanes):
        waits = [(l, self.lanes[l]["count"] * self.lanes[l]["inc"]) for l in lanes if self.lanes[l]["count"] > 0]
        self.q[eng].append((waits, None, None, False))

    def emit(self, block):
        for l, s in self.maxwait.items():
            assert s <= self.lanes[l]["count"], (l, s, self.lanes[l]["count"])

        def runner(name):
            items = self.q[name]
            lanes = self.lanes

            def body(e):
                for waits, fn, sig, embed in items:
                    emb = None
                    if embed and fn is not None and waits:
                        emb = waits[-1]
                        waits = waits[:-1]
                    for l, v in waits:
                        e.wait_ge(lanes[l]["sem"], v)
                    if fn is None:
                        continue
                    r = fn(e)
                    if emb is not None:
                        first = r[0] if isinstance(r, (list, tuple)) else r
                        first._wait_ge(lanes[emb[0]]["sem"], emb[1])
                    if sig is not None:
                        sem = lanes[sig[0]]["sem"]
                        if isinstance(r, (list, tuple)):
                            for ins in r:
                                ins.then_inc(sem, sig[1])
                        else:
                            r.then_inc(sem, sig[1])

            return body

        block.tensor(runner("pe"))
        block.scalar(runner("act"))
        block.vector(runner("dve"))
        block.gpsimd(runner("pool"))
        block.sync(runner("sp"))

    def mm(self, out, lhsT, rhs, start, stop, sig=None):
        self.op("pe", lambda e: e.matmul(out, lhsT, rhs, start=start, stop=stop),
                reads=[lhsT, rhs], writes=[out], sig=(stop if sig is None else sig))

    def tr(self, out, in_, ident, sig=True):
        self.op("pe", lambda e: e.transpose(out, in_, ident), reads=[in_, ident], writes=[out], sig=sig)

    def act(self, out, in_, func, bias=None, scale=None, accum=None):
        reads = [in_]
        kw = {}
        if bias is not None:
            kw["bias"] = bias
            if not isinstance(bias, (int, float)):
                reads.append(bias)
        if scale is not None:
            kw["scale"] = scale
            if not isinstance(scale, (int, float)):
                reads.append(scale)
        writes = [out]
        if accum is not None:
            kw["accum_out"] = accum
            writes.append(accum)
        self.op("act", lambda e: e.activation(out, in_, func, **kw), reads=reads, writes=writes,
                embed=(accum is None))

    def tt(self, eng, out, a, b, op):
        self.op(eng, lambda e: e.tensor_tensor(out, a, b, op), reads=[a, b], writes=[out])

    def ts(self, eng, out, a, s1, op0, s2=None, op1=None):
        reads = [a]
        if not isinstance(s1, (int, float)):
            reads.append(s1)
        if s2 is not None and not isinstance(s2, (int, float)):
            reads.append(s2)
        if op1 is None:
            self.op(eng, lambda e: e.tensor_scalar(out, a, s1, None, op0), reads=reads, writes=[out])
        else:
            self.op(eng, lambda e: e.tensor_scalar(out, a, s1, s2, op0, op1), reads=reads, writes=[out])

    def stt(self, out, in0, scalar, in1, op0, op1):
        reads = [in0, in1]
        if not isinstance(scalar, (int, float)):
            reads.append(scalar)
        self.op("dve", lambda e: e.scalar_tensor_tensor(out, in0, scalar, in1, op0, op1),
                reads=reads, writes=[out])

    def copy(self, eng, out, in_):
        if eng == "act":
            self.op("act", lambda e: e.activation(out, in_, AF.Copy), reads=[in_], writes=[out])
        else:
            self.op(eng, lambda e: e.tensor_copy(out, in_), reads=[in_], writes=[out])

    def memset(self, eng, out, val):
        self.op(eng, lambda e: e.memset(out, val), reads=[], writes=[out])

    def dma(self, pairs, lane, eng="sp", **kw):
        self.lane(lane)
        outs = [p[0] for p in pairs]
        ins = [p[1] for p in pairs]

        def fn(e):
            return [e.dma_start(out=o, in_=i, **kw) for o, i in pairs]

        self.op(eng, fn, reads=ins, writes=outs, lane=lane, n=len(pairs))


def _consts(seq):
    c = {}
    c["ident_bf"] = np.eye(128, dtype=np.float32).astype(ml_dtypes.bfloat16)
    c["ones_bf"] = np.ones((128, 128), dtype=np.float32).astype(ml_dtypes.bfloat16)
    c["ident_f"] = np.eye(128, dtype=np.float32)
    c["ones_f"] = np.ones((128, 128), dtype=np.float32)
    lg = np.log1p(-np.exp2(-5.0 - np.arange(RET_H, dtype=np.float64)))
    j = np.arange(128, dtype=np.float64)
    causalT = (j[None, :] >= j[:, None]).astype(np.float64)
    rmask = np.zeros((128, RET_H, 128), np.float64)
    for h in range(RET_H):
        rmask[:, h, :] = np.exp(-lg[h] * (j[:, None] + 1.0)) * (RET_DK ** -0.5) * causalT
    c["rmask"] = rmask.astype(np.float32)
    qd = np.exp(lg[:, None] * (j[None, :] + 1.0))
    c["qdec"] = np.broadcast_to(qd[None], (128, RET_H, 128)).astype(np.float32).copy()
    kd = np.zeros((128, 2, RET_H), np.float64)
    for li, L in enumerate((128, 32)):
        for h in range(RET_H):
            kd[:, li, h] = np.exp(lg[h] * (L - 1.0 - j)) * (RET_DK ** -0.5)
    c["kdec"] = kd.astype(np.float32)
    c["gL"] = {L: [float(np.exp(lg[h] * L)) for h in range(RET_H)] for L in (128, 32)}
    c["gmask"] = (causalT * (GLA_DK ** -0.5)).astype(np.float32)
    c["ucum"] = ((j[:, None] <= j[None, :]) * (-1.0 / GLA_TAU)).astype(np.float32)
    c["neghalf"] = np.full((128, 512), -0.5, np.float32)
    c["epsv"] = np.full((128, 8), EPS, np.float32)
    inv = (np.float32(ROPE_BASE) ** (-(np.arange(128, dtype=np.float32) / np.float32(128)))).astype(np.float32)
    pos = np.concatenate([np.arange(seq, dtype=np.float32), PAST_LEN + np.arange(DEC_SEQ, dtype=np.float32)])
    ang = (pos[None, :] * inv[:, None]).astype(np.float32)
    c["cos"] = np.cos(ang).astype(np.float32)
    c["sin"] = np.sin(ang).astype(np.float32)
    return c


def _fm(v):
    v = np.asarray(v, np.float32)
    return np.ascontiguousarray(v.reshape(-1, 128).T)


WEIGHTS = [
    ("ret_w_in", D, 6144), ("ret_w_out", 2048, D), ("gla_w_in", D, 3088), ("gla_w_out", D, D),
    ("ffn_w_up0", D, 2 * DFF), ("ffn_w_up1", D, 2 * DFF), ("ffn_w_down0", DFF, D), ("ffn_w_down1", DFF, D),
]


def build(seq, dbg=()):
    assert seq % 512 == 0
    nc = bass.Bass("TRN2", target_bir_lowering=False)
    P = Prog(nc)
    cst = _consts(seq)
    gL = cst["gL"]
    npos = seq + DEC_SEQ

    def din(name, shape, dt=F32):
        return nc.dram_tensor(name, list(shape), dt, kind="ExternalInput").ap()

    def dout(name, shape, dt=F32):
        return nc.dram_tensor(name, list(shape), dt, kind="ExternalOutput").ap()

    NHALF = -(-seq // 4096)
    HSEQ = seq // NHALF
    assert HSEQ * NHALF == seq and HSEQ % 512 == 0
    xTp = [din("xTp%d" % i, [D, HSEQ]) for i in range(NHALF)]; xTs = din("xTs", [D, DEC_SEQ])
    st_ret = din("st_ret", [RET_H, RET_DK, RET_DV]); st_gla = din("st_gla", [GLA_H, GLA_DK, GLA_DV])
    cconv = din("cconv", [128, DEPTH, NBLK_FF, 2])
    W32 = {n: din(n, [k, m]) for n, k, m in WEIGHTS}
    Wb = {n: nc.dram_tensor(n + "_bf", [k, m], BF16, kind="Internal").ap() for n, k, m in WEIGHTS}
    d_gains = din("gains", [128, 5, 8])
    d_gn = din("gn", [128, 16]); d_ng = din("ng", [128, 8])
    d_cw = din("cw", [128, DEPTH, 3, NBLK_FF]); d_cb = din("cb", [128, DEPTH, NBLK_FF])
    d_wa2 = din("wa2aug", [17, 512])
    d_ident = din("ident_bf", [128, 128], BF16); d_ones = din("ones_bf", [128, 128], BF16)
    d_rmask = din("rmask", [128, RET_H, 128]); d_qdec = din("qdec", [128, RET_H, 128])
    d_kdec = din("kdec", [128, 2, RET_H]); d_gmask = din("gmask", [128, 128]); d_ucum = din("ucum", [128, 128])
    d_neghalf = din("neghalf", [128, 512]); d_epsv = din("epsv", [128, 8])
    d_identf = din("ident_f", [128, 128]); d_onesf = din("ones_f", [128, 128])
    d_cos = din("cos", [128, npos]); d_sin = din("sin", [128, npos])

    yTp = [dout("yTp%d" % i, [D, HSEQ]) for i in range(NHALF)]; yTs = dout("yTs", [D, DEC_SEQ])
    o_ret = {"p": dout("ret_p", [RET_H, RET_DK, RET_DV]), "s": dout("ret_s", [RET_H, RET_DK, RET_DV])}
    o_gla = {"p": dout("gla_p", [GLA_H, GLA_DK, GLA_DV]), "s": dout("gla_s", [GLA_H, GLA_DK, GLA_DV])}
    o_conv = {"p": dout("conv_p", [128, DEPTH, NBLK_FF, 2]), "s": dout("conv_s", [128, DEPTH, NBLK_FF, 2])}
    dbg_out = {}

    arena_h = nc.alloc_sbuf_tensor("arena", [128, ARENA], U8)
    arena = arena_h.ap()
    psum_h = nc.alloc_psum_tensor("psum", [128, 8, 512], F32)
    psum = psum_h.ap()

    off = {"_": 0}
    reg = {}

    def region(name, nbytes):
        assert nbytes % 256 == 0, name
        reg[name] = (off["_"], nbytes)
        off["_"] += nbytes
        assert off["_"] <= ARENA, (name, off["_"])

    def view(name, dt, shape, boff=0, parts=128):
        o, nb = reg[name]
        n = int(np.prod(shape)) * DSZ[dt]
        assert boff + n <= nb, (name, boff, n, nb)
        ap = arena[0:parts, o + boff:o + boff + n].bitcast(dt)
        if len(shape) == 2:
            ap = ap.rearrange("p (a b) -> p a b", a=shape[0])
        elif len(shape) == 3:
            ap = ap.rearrange("p (a b c) -> p a b c", a=shape[0], b=shape[1])
        return ap

    region("x", 16384); region("hT", 8192); region("ms", 2048); region("rstd", 2048)
    region("identf", 512); region("onesf", 512)
    region("qkk", 24576)
    region("vtok", 16384)
    region("sgT", 16384)
    region("ogT", 16384)
    region("mix", 20480)
    region("PT", 1024); region("on", 4096); region("stats", 1024); region("stmp", 2048); region("aT", 1024)
    region("Sret", 16384); region("Sretb", 8192); region("Sgla", 4096); region("Sglab", 2048)
    region("wring", 3 * 8192)
    for nme, nb in [("ident", 256), ("ones", 256), ("rmask", 2048), ("qdec", 2048), ("kdec", 256), ("gmask", 512),
                    ("ucum", 512), ("neghalf", 2048), ("epsv", 256), ("gains", 256), ("gn", 256), ("ng", 256), ("cw", 1280),
                    ("cb", 512), ("wa2", 1024), ("uhb", 512), ("uhf", 768)]:
        region(nme, nb)

    ident = view("ident", BF16, [128]); ones = view("ones", BF16, [128])
    identf = view("identf", F32, [128]); onesf = view("onesf", F32, [128])
    rmask = view("rmask", F32, [RET_H, 128]); qdec = view("qdec", F32, [RET_H, 128])
    kdec = view("kdec", F32, [2, RET_H]); gmask = view("gmask", F32, [128]); ucum = view("ucum", F32, [128])
    neghalf = view("neghalf", F32, [512]); epsv = view("epsv", F32, [8])
    gains = view("gains", F32, [5, 8]); gnT = view("gn", F32, [16]); ngT = view("ng", F32, [8])
    cw = view("cw", F32, [DEPTH, 3, NBLK_FF]); cb = view("cb", F32, [DEPTH, NBLK_FF])
    wa2 = view("wa2", BF16, [512], parts=32)
    uhb = view("uhb", BF16, [DEPTH, NBLK_FF, 2]); uhf = view("uhf", F32, [DEPTH, NBLK_FF, 2])
    Sret = view("Sret", F32, [RET_H, 2, 512]); Sretb = view("Sretb", BF16, [RET_H, 2, 512])
    Sgla = view("Sgla", F32, [GLA_H, 256]); Sglab = view("Sglab", BF16, [GLA_H, 256])

    psb = [psum[:, b, :] for b in range(8)]
    psb_bf = [psum[:, b, :].bitcast(BF16) for b in range(8)]
    bank_ctr = {"i": 0}

    reserved = set()

    def bank():
        while True:
            b = bank_ctr["i"] % 8
            bank_ctr["i"] += 1
            if b not in reserved:
                return b

    P.dma([(ident, d_ident), (ones, d_ones), (rmask, d_rmask), (qdec, d_qdec), (kdec, d_kdec), (gmask, d_gmask),
           (ucum, d_ucum), (neghalf, d_neghalf), (epsv, d_epsv), (identf, d_identf), (onesf, d_onesf), (gains, d_gains), (gnT, d_gn), (ngT, d_ng), (cw, d_cw), (cb, d_cb)],
          "const")
    P.dma([(wa2[0:17, :], d_wa2)], "wa2c", eng="pool")
    for n, k, m in WEIGHTS:
        if n in ("ret_w_out", "gla_w_out"):
            continue
        P.dma([(Wb[n], W32[n])], "cast_" + n, eng="pool", max_dma_last_dim=4096)
    fi = 0
    for n, gv_, nch in (("ret_w_out", gnT, 16), ("gla_w_out", ngT, 8)):
        for ec in range(nch):
            wi = view("mix", F32, [1024], boff=(fi % 2) * 4096)
            wo = view("mix", BF16, [1024], boff=8192 + (fi % 2) * 2048)
            fi += 1
            P.dma([(wi, W32[n][ec * 128:(ec + 1) * 128, :])], "foldin")
            P.ts("dve", wo, wi, gv_[:, ec:ec + 1], ALU.mult)
            P.dma([(Wb[n][ec * 128:(ec + 1) * 128, :], wo)], "foldout")

    ring = {"i": 0}

    def slab(wname, kc0, nkc, colgroups):
        s = ring["i"] % 3
        ring["i"] += 1
        ncols = sum(c for _, c in colgroups)
        assert nkc * ncols * 2 <= 8192
        v = view("wring", BF16, [nkc, ncols], boff=s * 8192)
        pairs = []
        co = 0
        for c0, cn in colgroups:
            src = Wb[wname][kc0 * 128:(kc0 + nkc) * 128, c0:c0 + cn].rearrange("(k p) n -> p k n", p=128)
            pairs.append((v[:, :, co:co + cn], src))
            co += cn
        P.dma(pairs, "w%d" % s)
        return v

    aT_all = view("aT", BF16, [512], parts=32)
    P.memset("dve", aT_all, 1.0)

    def run_tile(sk, t0, NT, CL, last):
        NCH = NT // CL
        li = 0 if CL == 128 else 1
        xsrc = xTp[t0 // HSEQ] if sk == "p" else xTs
        ydst = yTp[t0 // HSEQ] if sk == "p" else yTs
        tq = t0 % HSEQ if sk == "p" else t0
        pos0 = t0 if sk == "p" else seq + t0
        x = view("x", F32, [8, NT])
        hT = view("hT", BF16, [8, NT])
        ybuf = view("vtok", F32, [8, NT])
        sq = view("sgT", BF16, [8, NT], boff=8192)
        ms = view("ms", F32, [4]); rstd = view("ms", F32, [4], boff=256)
        rbc = view("rstd", F32, [4, 128])
        qT = view("qkk", BF16, [8, NT]); kT = view("qkk", BF16, [8, NT], boff=8192)
        ktok = view("qkk", BF16, [NCH, 1024], boff=16384)
        vtok = view("vtok", BF16, [NCH, 2048])
        sgT = view("sgT", BF16, [16, NT]); ogT = view("ogT", BF16, [16, NT])
        PTv = [view("PT", BF16, [128], boff=i * 256) for i in range(4)]
        onv = [view("on", BF16, [512], boff=i * 1024) for i in range(3)]
        statv = [(view("stats", F32, [6], boff=i * 256), view("stats", F32, [2], boff=i * 256 + 64),
                  view("stats", F32, [1], boff=i * 256 + 128), view("stats", F32, [1], boff=i * 256 + 192))
                 for i in range(4)]
        stmp = [view("stmp", BF16, [NT], boff=i * 1024) for i in range(2)]
        cnt = {"pt": 0, "on": 0, "st": 0, "sv": 0}

        def tsl(ch):
            return slice(ch * CL, (ch + 1) * CL)

        P.dma([(x, xsrc[:, tq:tq + NT].rearrange("(c p) t -> p c t", p=128))], "xin")

        def norm(gidx, out_hT=True, presq=False):
            TB = min(128, NT)
            NTB = NT // TB
            if not presq:
                for c in range(8):
                    P.act(sq[:, c, :], x[:, c, :], AF.Square)
            b = bank()
            for tb in range(NTB):
                for c in range(8):
                    P.mm(psb[b][0:TB, tb:tb + 1], sq[:, c, tb * TB:(tb + 1) * TB], ones[:, 0:1],
                         start=(c == 0), stop=(c == 7), sig=(c == 7 and tb == NTB - 1))
            P.ts("dve", ms[0:TB, 0:NTB], psb[b][0:TB, 0:NTB], 1.0 / D, ALU.mult, EPS, ALU.add)
            P.tt("pool", rstd[0:TB, 0:NTB], ms[0:TB, 0:NTB], neghalf[0:TB, 0:NTB], ALU.pow)
            b2 = bank()
            for tb in range(NTB):
                P.ts("dve", rbc[0:TB, tb, :], onesf[0:TB, :], rstd[0:TB, tb:tb + 1], ALU.mult)
                P.mm(psb[b2][:, tb * TB:(tb + 1) * TB], rbc[0:TB, tb, :], identf[0:TB, 0:TB],
                     start=True, stop=True, sig=(tb == NTB - 1))
            for c in range(8):
                dst = hT[:, c, :] if out_hT else ybuf[:, c, :]
                P.stt(dst, x[:, c, :], gains[:, gidx, c:c + 1], psb[b2][:, 0:NT], ALU.mult, ALU.mult)

        def resid_proj(wname, nkc, src, kslabs):
            for cg in range(4):
                banks = [bank(), bank()]
                k0 = 0
                while k0 < nkc:
                    nk = min(kslabs, nkc - k0)
                    sl = slab(wname, k0, nk, [(cg * 256, 256)])
                    for j in range(2):
                        for kk in range(nk):
                            kc = k0 + kk
                            P.mm(psb[banks[j]][:, 0:NT], sl[:, kk, j * 128:(j + 1) * 128], src[:, kc, :],
                                 start=(kc == 0), stop=(kc == nkc - 1),
                                 sig=(kc == nkc - 1) or (kk == nk - 1 and j == 1))
                    k0 += nk
                for j in range(2):
                    blk = cg * 2 + j
                    P.tt("dve", x[:, blk, :], x[:, blk, :], psb[banks[j]][:, 0:NT], ALU.add)
                    P.act(sq[:, blk, :], x[:, blk, :], AF.Square)

        def retention():
            cs = view("mix", F32, [2, NT])
            cqs = [view("mix", F32, [2, NT], boff=4096 + i * 4096) for i in range(2)]
            rt = view("mix", F32, [4, NT], boff=12288)
            P.dma([(cs[:, 0, :], d_cos[:, pos0:pos0 + NT]), (cs[:, 1, :], d_sin[:, pos0:pos0 + NT])], "cs")

            def rotary(pa, pb, ct, st_, o1, o2):
                P.tt("dve", rt[:, 0, :], pa, ct, ALU.mult)
                P.tt("dve", rt[:, 1, :], pb, st_, ALU.mult)
                P.tt("dve", o1, rt[:, 0, :], rt[:, 1, :], ALU.subtract)
                P.tt("dve", rt[:, 2, :], pa, st_, ALU.mult)
                P.tt("dve", rt[:, 3, :], pb, ct, ALU.mult)
                P.tt("dve", o2, rt[:, 2, :], rt[:, 3, :], ALU.add)

            sg_tok = view("sgT", BF16, [NCH, 2048])

            def mk_groups(h):
                st = {}
                gl = []
                for kind in (0, 1):
                    for ch in range(NCH):
                        def grp(kind=kind, ch=ch):
                            if ch == 0:
                                st[kind] = slab("ret_w_in", 0, 8, [(2048 + kind * 2048 + h * 512, 512)])
                            sl_ = st[kind]
                            b_ = bank()
                            for kc in range(8):
                                P.mm(psb[b_][0:CL, :], hT[:, kc, tsl(ch)], sl_[:, kc, :], start=(kc == 0), stop=(kc == 7))
                            if kind == 0:
                                P.copy("act", vtok[0:CL, ch, h * 512:(h + 1) * 512], psb[b_][0:CL, :])
                            else:
                                P.act(sg_tok[0:CL, ch, h * 512:(h + 1) * 512], psb[b_][0:CL, :], AF.Silu)
                        gl.append(grp)
                return gl

            Q = []
            for h in range(RET_H):
                Q += mk_groups(h)

            def emit_groups(n):
                for _ in range(n):
                    if Q:
                        Q.pop(0)()

            for which in range(2):
                dstT = qT if which == 0 else kT
                for sh in range(2):
                    sl = slab("ret_w_in", 0, 8, [(which * 1024 + sh * 512, 512)])
                    for hh in range(2):
                        h = sh * 2 + hh
                        ba, bb = bank(), bank()
                        for half, bk in ((0, ba), (1, bb)):
                            for kc in range(8):
                                P.mm(psb[bk][:, 0:NT], sl[:, kc, (hh * 2 + half) * 128:(hh * 2 + half + 1) * 128],
                                     hT[:, kc, :], start=(kc == 0), stop=(kc == 7))
                        if which == 0:
                            cq = cqs[h % 2]
                            qd = qdec[:, h, 0:CL].unsqueeze(1).to_broadcast([128, NCH, CL])
                            for t_ in range(2):
                                P.tt("pool", cq[:, t_, :].rearrange("p (a b) -> p a b", a=NCH),
                                     cs[:, t_, :].rearrange("p (a b) -> p a b", a=NCH), qd, ALU.mult)
                            rotary(psb[ba][:, 0:NT], psb[bb][:, 0:NT], cq[:, 0, :], cq[:, 1, :],
                                   dstT[:, 2 * h, :], dstT[:, 2 * h + 1, :])
                        else:
                            rotary(psb[ba][:, 0:NT], psb[bb][:, 0:NT], cs[:, 0, :], cs[:, 1, :],
                                   dstT[:, 2 * h, :], dstT[:, 2 * h + 1, :])
                        emit_groups(1)
            for h in range(RET_H):
                for ch in range(NCH):
                    b = bank()
                    for dc in range(2):
                        P.tr(psb_bf[b][0:CL, dc * 128:(dc + 1) * 128], kT[:, 2 * h + dc, tsl(ch)], ident,
                             sig=(dc == 1))
                    P.act(ktok[0:CL, ch, h * 256:(h + 1) * 256], psb_bf[b][0:CL, 0:256], AF.Copy,
                          scale=kdec[0:CL, li, h:h + 1])
            for h in range(RET_H):
                for ch in range(NCH):
                    bs = bank()
                    for dc in range(2):
                        P.mm(psb[bs][0:CL, 0:CL], kT[:, 2 * h + dc, tsl(ch)], qT[:, 2 * h + dc, tsl(ch)],
                             start=(dc == 0), stop=(dc == 1))
                    PT = PTv[cnt["pt"] % 4]; cnt["pt"] += 1
                    P.tt("dve", PT[0:CL, 0:CL], psb[bs][0:CL, 0:CL], rmask[0:CL, h, 0:CL], ALU.mult)
                    emit_groups(1)
                    bo = bank()
                    P.mm(psb[bo][0:CL, :], PT[0:CL, 0:CL], vtok[0:CL, ch, h * 512:(h + 1) * 512], start=True, stop=False)
                    for dc in range(2):
                        P.mm(psb[bo][0:CL, :], qT[:, 2 * h + dc, tsl(ch)], Sretb[:, h, dc, :], start=False, stop=(dc == 1))
                    bS = [bank(), bank()]
                    for dc in range(2):
                        P.mm(psb[bS[dc]][:, :], ktok[0:CL, ch, h * 256 + dc * 128:h * 256 + (dc + 1) * 128],
                             vtok[0:CL, ch, h * 512:(h + 1) * 512], start=True, stop=True)
                    stats6, mv, ve, rs = statv[cnt["sv"] % 4]; cnt["sv"] += 1
                    P.op("dve", lambda e, o=stats6[0:CL, :], i=psb[bo][0:CL, :]: e.bn_stats(o, i),
                         reads=[psb[bo][0:CL, :]], writes=[stats6[0:CL, :]])
                    P.op("dve", lambda e, o=mv[0:CL, :], i=stats6[0:CL, :]: e.bn_aggr(o, i),
                         reads=[stats6[0:CL, :]], writes=[mv[0:CL, :]])
                    P.ts("dve", ve[0:CL, :], mv[0:CL, 1:2], EPS, ALU.add)
                    P.tt("pool", rs[0:CL, :], ve[0:CL, :], neghalf[0:CL, 0:1], ALU.pow)
                    for dc in range(2):
                        P.stt(Sret[:, h, dc, :], Sret[:, h, dc, :], gL[CL][h], psb[bS[dc]][:, :], ALU.mult, ALU.add)
                        P.copy("act", Sretb[:, h, dc, :], Sret[:, h, dc, :])
                    emit_groups(1)
                    on = onv[cnt["on"] % 3]; cnt["on"] += 1
                    P.ts("dve", on[0:CL, :], psb[bo][0:CL, :], mv[0:CL, 0:1], ALU.subtract, rs[0:CL, :], ALU.mult)
                    P.tt("dve", on[0:CL, :], on[0:CL, :], sg_tok[0:CL, ch, h * 512:(h + 1) * 512], ALU.mult)
                    bt = bank()
                    for eb in range(4):
                        P.tr(psb_bf[bt][:, eb * CL:(eb + 1) * CL], on[0:CL, eb * 128:(eb + 1) * 128],
                             ident[0:CL, 0:CL], sig=(eb == 3))
                    P.copy("act", ogT[:, 4 * h:4 * h + 4, tsl(ch)],
                           psb_bf[bt][:, 0:4 * CL].rearrange("p (a b) -> p a b", a=4))
            emit_groups(len(Q))
            resid_proj("ret_w_out", 16, ogT, 16)

        def ffn(layer):
            actT = view("qkk", BF16, [22, NT])
            ubuf = [[view("sgT", BF16, [NT + 2], boff=(r * 2 + gv) * 1280) for gv in range(2)] for r in range(2)]
            sgt = [view("sgT", BF16, [NT], boff=5120 + r * 1024) for r in range(2)]
            dg = [[view("on", BF16, [3, 128], boff=(r * 2 + gv) * 768) for gv in range(2)] for r in range(2)]
            wn = "ffn_w_up%d" % layer
            it = 0
            pend = None

            def conv_stage(r, gb):
                cbk = []
                for gv in range(2):
                    b = bank(); cbk.append(b)
                    for j in range(3):
                        P.mm(psb[b][:, 0:NT], dg[r][gv][:, j, :], ubuf[r][gv][:, j:j + NT], start=(j == 0), stop=(j == 2))
                P.act(sgt[r], psb[cbk[0]][:, 0:NT], AF.Silu, bias=cb[:, layer, gb:gb + 1])
                P.stt(actT[:, gb, :], psb[cbk[1]][:, 0:NT], cb[:, layer, gb + 22:gb + 23], sgt[r], ALU.add, ALU.mult)

            for s in range(11):
                sl = slab(wn, 0, 8, [(s * 256, 256), (DFF + s * 256, 256)])
                for jj in range(2):
                    gb = s * 2 + jj
                    r = it % 2; it += 1
                    for gv in range(2):
                        blk = gb + gv * 22
                        b = bank()
                        for kc in range(8):
                            P.mm(psb[b][:, 0:NT], sl[:, kc, gv * 256 + jj * 128:gv * 256 + (jj + 1) * 128], hT[:, kc, :],
                                 start=(kc == 0), stop=(kc == 7))
                        ub = ubuf[r][gv]
                        P.copy("pool", ub[:, 0:2], uhb[:, layer, blk, :])
                        P.copy("act", ub[:, 2:2 + NT], psb[b][:, 0:NT])
                        P.copy("pool", uhb[:, layer, blk, :], ub[:, NT:NT + 2])
                        if last:
                            P.copy("dve", uhf[:, layer, blk, :], psb[b][:, NT - 2:NT])
                        for j in range(3):
                            P.ts("pool", dg[r][gv][:, j, :], ident, cw[:, layer, j, blk:blk + 1], ALU.mult, 1.0, ALU.mult)
                    if pend is not None:
                        conv_stage(*pend)
                    pend = (r, gb)
            conv_stage(*pend)
            resid_proj("ffn_w_down%d" % layer, 22, actT, 11)

        def gla():
            vt = view("vtok", BF16, [NCH, 1024])
            sr_tok = view("sgT", BF16, [NCH, 1024])
            kbar = view("qkk", BF16, [NCH, 512], boff=16384)
            sp = view("mix", F32, [NCH, 512])
            Eq = view("mix", F32, [GLA_H, NT], boff=8192)
            Ek = [view("mix", F32, [NT], boff=16384 + i * 2048) for i in range(2)]
            zt = view("on", F32, [512])
            kbt = view("stmp", BF16, [NT])
            junk = view("on", BF16, [256], boff=2048)

            def mk_groups(kind, s2):
                st = {}
                gl = []
                for ch in range(NCH):
                    def grp(ch=ch):
                        if ch == 0:
                            st["s"] = slab("gla_w_in", 0, 8, [(1024 + kind * 1024 + s2 * 512, 512)])
                        b_ = bank()
                        for kc in range(8):
                            P.mm(psb[b_][0:CL, :], hT[:, kc, tsl(ch)], st["s"][:, kc, :], start=(kc == 0), stop=(kc == 7))
                        if kind == 0:
                            P.copy("act", vt[0:CL, ch, s2 * 512:(s2 + 1) * 512], psb[b_][0:CL, :])
                        else:
                            P.act(sr_tok[0:CL, ch, s2 * 512:(s2 + 1) * 512], psb[b_][0:CL, :], AF.Silu)
                    gl.append(grp)
                return gl

            Q = mk_groups(0, 0) + mk_groups(1, 0) + mk_groups(0, 1) + mk_groups(1, 1)

            def emit_groups(n):
                for _ in range(n):
                    if Q:
                        Q.pop(0)()

            sl = slab("gla_w_in", 0, 8, [(3072, 16)])
            b = bank()
            for kc in range(8):
                P.mm(psb[b][0:16, 0:NT], sl[:, kc, 0:16], hT[:, kc, :], start=(kc == 0), stop=(kc == 7))
            P.copy("act", aT_all[0:16, 0:NT], psb[b][0:16, 0:NT])
            for ch in range(NCH):
                b = bank()
                P.mm(psb[b][0:CL, :], aT_all[0:17, tsl(ch)], wa2[0:17, :], start=True, stop=True)
                P.act(zt[0:CL, :], psb[b][0:CL, :], AF.Exp, scale=-1.0)
                P.act(sp[0:CL, ch, :], zt[0:CL, :], AF.Ln, bias=1.0)
            for h in range(GLA_H):
                b = bank()
                for ch in range(NCH):
                    P.mm(psb[b][:, tsl(ch)], sp[0:CL, ch, h * 128:(h + 1) * 128], ucum[0:CL, 0:CL], start=True, stop=True,
                         sig=(ch == NCH - 1))
                P.act(Eq[:, h, :], psb[b][:, 0:NT], AF.Exp)
            slq = slab("gla_w_in", 0, 8, [(0, 512)])
            for h in range(GLA_H):
                bq = bank()
                for kc in range(8):
                    P.mm(psb[bq][:, 0:NT], slq[:, kc, h * 128:(h + 1) * 128], hT[:, kc, :], start=(kc == 0), stop=(kc == 7))
                P.tt("dve", qT[:, h, :], psb[bq][:, 0:NT], Eq[:, h, :], ALU.mult)
                if NCH >= 4:
                    emit_groups(1)
            slk = slab("gla_w_in", 0, 8, [(512, 512)])
            for h in range(GLA_H):
                P.op("dve", lambda e, o=Ek[h % 2], i=Eq[:, h, :]: e.reciprocal(o, i), reads=[Eq[:, h, :]], writes=[Ek[h % 2]])
                bk = bank()
                for kc in range(8):
                    P.mm(psb[bk][:, 0:NT], slk[:, kc, h * 128:(h + 1) * 128], hT[:, kc, :], start=(kc == 0), stop=(kc == 7))
                P.tt("dve", kT[:, h, :], psb[bk][:, 0:NT], Ek[h % 2], ALU.mult)
                for ch in range(NCH):
                    P.ts("dve", kbt[:, tsl(ch)], kT[:, h, tsl(ch)], Eq[:, h, (ch + 1) * CL - 1:(ch + 1) * CL], ALU.mult,
                         GLA_DK ** -0.5, ALU.mult)
                    bt = bank()
                    P.tr(psb_bf[bt][0:CL, 0:128], kbt[:, tsl(ch)], ident)
                    P.copy("act", kbar[0:CL, ch, h * 128:(h + 1) * 128], psb_bf[bt][0:CL, 0:128])
                if NCH >= 4:
                    emit_groups(1)
            if NCH < 4:
                emit_groups(len(Q))
            for h in range(GLA_H):
                for ch in range(NCH):
                    ba = bank()
                    P.mm(psb[ba][0:CL, 0:CL], kT[:, h, tsl(ch)], qT[:, h, tsl(ch)], start=True, stop=True)
                    PT = PTv[cnt["pt"] % 4]; cnt["pt"] += 1
                    P.tt("dve", PT[0:CL, 0:CL], psb[ba][0:CL, 0:CL], gmask[0:CL, 0:CL], ALU.mult)
                    emit_groups(1)
                    bo = bank()
                    P.mm(psb[bo][0:CL, 0:256], PT[0:CL, 0:CL], vt[0:CL, ch, h * 256:(h + 1) * 256], start=True, stop=False)
                    P.mm(psb[bo][0:CL, 0:256], qT[:, h, tsl(ch)], Sglab[:, h, :], start=False, stop=True)
                    bS = bank()
                    P.mm(psb[bS][:, 0:256], kbar[0:CL, ch, h * 128:(h + 1) * 128], vt[0:CL, ch, h * 256:(h + 1) * 256],
                         start=True, stop=True)
                    stats6, mv, ve, rs = statv[cnt["sv"] % 4]; cnt["sv"] += 1
                    P.act(junk[0:CL, :], psb[bo][0:CL, 0:256], AF.Square, accum=ve[0:CL, :])
                    P.ts("dve", mv[0:CL, 0:1], ve[0:CL, :], 1.0 / GLA_DV, ALU.mult, EPS, ALU.add)
                    P.tt("pool", rs[0:CL, :], mv[0:CL, 0:1], neghalf[0:CL, 0:1], ALU.pow)
                    P.stt(Sgla[:, h, :], Sgla[:, h, :], Eq[:, h, (ch + 1) * CL - 1:(ch + 1) * CL], psb[bS][:, 0:256],
                          ALU.mult, ALU.add)
                    P.copy("act", Sglab[:, h, :], Sgla[:, h, :])
                    on = view("on", BF16, [256], boff=2560 + (cnt["on"] % 2) * 512); cnt["on"] += 1
                    P.ts("dve", on[0:CL, :], psb[bo][0:CL, 0:256], rs[0:CL, :], ALU.mult)
                    P.tt("dve", on[0:CL, :], on[0:CL, :], sr_tok[0:CL, ch, h * 256:(h + 1) * 256], ALU.mult)
                    bt = bank()
                    for eb in range(2):
                        P.tr(psb_bf[bt][:, eb * CL:(eb + 1) * CL], on[0:CL, eb * 128:(eb + 1) * 128], ident[0:CL, 0:CL],
                             sig=(eb == 1))
                    P.copy("act", ogT[:, 2 * h:2 * h + 2, tsl(ch)],
                           psb_bf[bt][:, 0:2 * CL].rearrange("p (a b) -> p a b", a=2))
            emit_groups(len(Q))
            resid_proj("gla_w_out", 8, ogT, 8)

        norm(0); retention()
        norm(2, presq=True); ffn(0)
        norm(1, presq=True); gla()
        norm(3, presq=True); ffn(1)
        norm(4, out_hT=False, presq=True)
        P.dma([(ydst[:, tq:tq + NT].rearrange("(c p) t -> p c t", p=128), ybuf)], "yout")

    def seq_end(sk):
        P.dma([(o_ret[sk].rearrange("h (dc p) e -> p h dc e", p=128), Sret)], "so_ret")
        P.dma([(o_gla[sk].rearrange("h p e -> p h e"), Sgla)], "so_gla")
        P.dma([(o_conv[sk], uhf)], "so_conv")

    P.dma([(Sret, st_ret.rearrange("h (dc p) e -> p h dc e", p=128)), (Sgla, st_gla.rearrange("h p e -> p h e")),
           (uhf, cconv)], "stin")
    P.copy("act", Sretb, Sret); P.copy("act", Sglab, Sgla); P.copy("dve", uhb, uhf)
    run_tile("s", 0, DEC_SEQ, DEC_SEQ, True)
    seq_end("s")
    P.memset("pool", Sret, 0.0); P.memset("pool", Sretb, 0.0); P.memset("pool", Sgla, 0.0); P.memset("pool", Sglab, 0.0)
    P.memset("pool", uhb, 0.0)
    ntile = seq // 512
    for ti in range(ntile):
        run_tile("p", ti * 512, 512, 128, ti == ntile - 1)
    seq_end("p")

    P.wait_all("sp", ("yout", "so_ret", "so_gla", "so_conv"))

    with contextlib.ExitStack() as es:
        for lname, L in P.lanes.items():
            L["sem"] = es.enter_context(nc.semaphore("s_" + lname))
        block = es.enter_context(nc.Block())
        P.emit(block)
    return nc, cst, P


_CACHE = {}


def _prep_inputs(inp, seq, cst):
    f32 = lambda a: np.ascontiguousarray(np.asarray(a, np.float32))
    shared = {
        "ret_w_in": f32(inp["ret_w_in"][0]), "ret_w_out": f32(inp["ret_w_out"][0]),
        "gla_w_in": f32(inp["gla_w_in"][0]), "gla_w_out": f32(inp["gla_w_out"][0]),
        "ffn_w_up0": f32(inp["ffn_w_up"][0]), "ffn_w_up1": f32(inp["ffn_w_up"][1]),
        "ffn_w_down0": f32(inp["ffn_w_down"][0]), "ffn_w_down1": f32(inp["ffn_w_down"][1]),
    }
    nm, nf = np.asarray(inp["norm_mix"], np.float32), np.asarray(inp["norm_ffn"], np.float32)
    gains = np.stack([_fm(nm[0]), _fm(nm[1]), _fm(nf[0]), _fm(nf[1]), _fm(inp["norm_final"])], axis=1)
    shared["gains"] = np.ascontiguousarray(gains)
    shared["gn"] = _fm(np.asarray(inp["ret_gn_g"], np.float32)[0].reshape(-1))
    shared["ng"] = _fm(np.asarray(inp["gla_norm_g"], np.float32)[0].reshape(-1))
    cwv = np.asarray(inp["ffn_conv_w"], np.float32)
    shared["cw"] = np.ascontiguousarray(cwv.reshape(DEPTH, 3, NBLK_FF, 128).transpose(3, 0, 1, 2))
    cbv = np.asarray(inp["ffn_conv_b"], np.float32)
    shared["cb"] = np.ascontiguousarray(cbv.reshape(DEPTH, NBLK_FF, 128).transpose(2, 0, 1))
    shared["wa2aug"] = np.ascontiguousarray(np.concatenate(
        [np.asarray(inp["gla_w_a2"], np.float32)[0], np.asarray(inp["gla_b_a"], np.float32)[0][None, :]], axis=0))
    for k in ("ident_bf", "ones_bf", "ident_f", "ones_f", "rmask", "qdec", "kdec", "gmask", "ucum", "neghalf", "epsv", "cos", "sin"):
        shared[k] = cst[k]
    xp = np.asarray(inp["x_prompt"], np.float32)
    xs = np.asarray(inp["x_sample"], np.float32)
    cc = np.asarray(inp["cache_conv"], np.float32)
    maps = []
    for b in range(NCORES):
        m = dict(shared)
        hs = seq // (-(-seq // 4096))
        for i in range(seq // hs):
            m["xTp%d" % i] = np.ascontiguousarray(xp[b, i * hs:(i + 1) * hs].T)
        m["xTs"] = np.ascontiguousarray(xs[b].T)
        m["st_ret"] = f32(inp["state_ret"][0, b])
        m["st_gla"] = f32(inp["state_gla"][0, b])
        m["cconv"] = np.ascontiguousarray(cc[:, b].reshape(DEPTH, 2, NBLK_FF, 128).transpose(3, 0, 2, 1))
        maps.append(m)
    return maps


def _run(inp, seq):
    if seq not in _CACHE:
        _CACHE[seq] = build(seq)
    nc, cst, _ = _CACHE[seq]
    maps = _prep_inputs(inp, seq, cst)
    res = run_bass_kernel_spmd(nc, maps, core_ids=list(range(NCORES)))
    R = res.results
    B = NCORES
    hs = seq // (-(-seq // 4096))
    y_p = np.stack([np.concatenate([R[b]["yTp%d" % i].T for i in range(seq // hs)], axis=0)
                    for b in range(B)]).astype(np.float32)
    y_s = np.stack([R[b]["yTs"].T for b in range(B)]).astype(np.float32)
    ret_p = np.stack([R[b]["ret_p"] for b in range(B)])[None].astype(np.float32)
    ret_s = np.stack([R[b]["ret_s"] for b in range(B)])[None].astype(np.float32)
    gla_p = np.stack([R[b]["gla_p"] for b in range(B)])[None].astype(np.float32)
    gla_s = np.stack([R[b]["gla_s"] for b in range(B)])[None].astype(np.float32)

    def conv(k):
        a = np.stack([R[b][k] for b in range(B)])
        return np.ascontiguousarray(a.transpose(2, 0, 4, 3, 1).reshape(DEPTH, B, 2, 2 * DFF)).astype(np.float32)

    return (y_p, y_s, ret_p, ret_s, gla_p, gla_s, conv("conv_p"), conv("conv_s"))


def kernel(**inputs):
    seq = int(np.asarray(inputs["x_prompt"]).shape[1])
    return _run(inputs, seq)
```

```python
import contextlib
import math
import numpy as np
import ml_dtypes
import concourse.bass as bass
import concourse.mybir as mybir
from concourse.bass_utils import run_bass_kernel_spmd

F32 = mybir.dt.float32
BF16 = mybir.dt.bfloat16
U8 = mybir.dt.uint8
AF = mybir.ActivationFunctionType
ALU = mybir.AluOpType
DSZ = {F32: 4, BF16: 2, U8: 1}

D = 1024
DEPTH = 2
RET_H = 4
RET_DK = 256
RET_DV = 512
GLA_H = 4
GLA_DK = 128
GLA_DV = 256
GLA_RANK = 16
GLA_TAU = 16.0
DFF = 2816
NBLK_FF = 2 * DFF // 128
EPS = 1e-6
ROPE_BASE = 10000.0
PAST_LEN = 2048
DEC_SEQ = 32
NCORES = 8

ARENA = 211968


class Prog:
    SBG = 256

    def __init__(self, nc):
        self.nc = nc
        self.q = {e: [] for e in ("pe", "act", "dve", "pool", "sp")}
        self.lanes = {}
        for e in ("pe", "act", "dve", "pool"):
            self.lanes[e] = {"count": 0, "inc": 1, "sem": None}
        self.clock = {e: {} for e in self.q}
        self.snap = {}
        self.gran = {}
        self.maxwait = {}
        self.nops = 0

    def lane(self, name):
        if name not in self.lanes:
            self.lanes[name] = {"count": 0, "inc": 16, "sem": None}
        return name

    def keys(self, ap):
        t = ap.tensor
        name = t.name
        if name not in ("arena", "psum"):
            return [("d", name)]
        esz = DSZ[ap.dtype]
        pairs = list(ap.ap)
        pstep = pairs[0][0]
        off = int(ap.offset)
        inpart = off % pstep if pstep else off
        starts = [inpart]
        free = [(s, c) for (s, c) in pairs[1:] if c > 1 or len(pairs) == 2]
        free = [(s, c) for (s, c) in free if s != 0]
        length = 1
        i = 0
        while i < len(free):
            s, c = free[i]
            inner_ext = sum((cc - 1) * abs(ss) for ss, cc in free[i + 1:]) + 1
            if i == len(free) - 1:
                length = (c - 1) * abs(s) + 1
            elif abs(s) > inner_ext and len(starts) * c <= 128:
                starts = [st + k * s for st in starts for k in range(c)]
            else:
                length = sum((cc - 1) * abs(ss) for ss, cc in free[i:]) + 1
                break
            i += 1
        g = self.SBG if name == "arena" else 2048
        ks = set()
        for st in starts:
            lo = (st * esz) // g
            hi = ((st + length) * esz - 1) // g
            for k in range(lo, hi + 1):
                ks.add((name, k))
        return ks

    def op(self, eng, fn, reads=(), writes=(), sig=True, lane=None, n=1, embed=True):
        self.nops += 1
        mylane = lane if lane is not None else eng
        L = self.lanes[mylane]
        raw = {}
        oth = {}

        def add(d, ls):
            l, s = ls
            if d.get(l, 0) < s:
                d[l] = s

        rkeys = set()
        for ap in reads:
            rkeys |= set(self.keys(ap))
        wkeys = set()
        for ap in writes:
            wkeys |= set(self.keys(ap))
        for k in rkeys:
            g = self.gran.get(k)
            if g is not None and g[0] is not None:
                add(raw, g[0])
        for k in wkeys:
            g = self.gran.get(k)
            if g is not None:
                if g[0] is not None:
                    add(oth, g[0])
                for l, s in g[1].items():
                    add(oth, (l, s))
        deps = {}
        for l, s in raw.items():
            if l == eng and lane is None:
                if eng == "pe":
                    continue
            add(deps, (l, s))
        for l, s in oth.items():
            if l == eng and lane is None and eng == "pe":
                continue
            add(deps, (l, s))
        if lane is not None and L["count"] > 0:
            add(deps, (mylane, L["count"]))
        if sig:
            L["count"] += n
            seq = L["count"]
        else:
            seq = L["count"] + 1
        ck = self.clock[eng]
        waits = []
        for l, s in sorted(deps.items()):
            if ck.get(l, 0) < s:
                waits.append((l, s * self.lanes[l]["inc"]))
                if self.maxwait.get(l, 0) < s:
                    self.maxwait[l] = s
                sn = self.snap.get((l, s))
                if sn:
                    for l2, s2 in sn.items():
                        if ck.get(l2, 0) < s2:
                            ck[l2] = s2
                ck[l] = s
        if sig:
            sn = dict(ck)
            sn[mylane] = seq
            self.snap[(mylane, seq)] = sn
        for k in rkeys:
            g = self.gran.get(k)
            if g is None:
                g = [None, {}]
                self.gran[k] = g
            if g[1].get(mylane, 0) < seq:
                g[1][mylane] = seq
        for k in wkeys:
            self.gran[k] = [(mylane, seq), {}]
        self.q[eng].append((waits, fn, (mylane, L["inc"]) if sig else None, embed))

    def wait_all(self, eng, lanes):
        waits = [(l, self.lanes[l]["count"] * self.lanes[l]["inc"]) for l in lanes if self.lanes[l]["count"] > 0]
        self.q[eng].append((waits, None, None, False))

    def emit(self, block):
        for l, s in self.maxwait.items():
            assert s <= self.lanes[l]["count"], (l, s, self.lanes[l]["count"])

        def runner(name):
            items = self.q[name]
            lanes = self.lanes

            def body(e):
                for waits, fn, sig, embed in items:
                    emb = None
                    if embed and fn is not None and waits:
                        emb = waits[-1]
                        waits = waits[:-1]
                    for l, v in waits:
                        e.wait_ge(lanes[l]["sem"], v)
                    if fn is None:
                        continue
                    r = fn(e)
                    if emb is not None:
                        first = r[0] if isinstance(r, (list, tuple)) else r
                        first._wait_ge(lanes[emb[0]]["sem"], emb[1])
                    if sig is not None:
                        sem = lanes[sig[0]]["sem"]
                        if isinstance(r, (list, tuple)):
                            for ins in r:
                                ins.then_inc(sem, sig[1])
                        else:
                            r.then_inc(sem, sig[1])

            return body

        block.tensor(runner("pe"))
        block.scalar(runner("act"))
        block.vector(runner("dve"))
        block.gpsimd(runner("pool"))
        block.sync(runner("sp"))

    def mm(self, out, lhsT, rhs, start, stop, sig=None):
        self.op("pe", lambda e: e.matmul(out, lhsT, rhs, start=start, stop=stop),
                reads=[lhsT, rhs], writes=[out], sig=(stop if sig is None else sig))

    def tr(self, out, in_, ident, sig=True):
        self.op("pe", lambda e: e.transpose(out, in_, ident), reads=[in_, ident], writes=[out], sig=sig)

    def act(self, out, in_, func, bias=None, scale=None, accum=None):
        reads = [in_]
        kw = {}
        if bias is not None:
            kw["bias"] = bias
            if not isinstance(bias, (int, float)):
                reads.append(bias)
        if scale is not None:
            kw["scale"] = scale
            if not isinstance(scale, (int, float)):
                reads.append(scale)
        writes = [out]
        if accum is not None:
            kw["accum_out"] = accum
            writes.append(accum)
        self.op("act", lambda e: e.activation(out, in_, func, **kw), reads=reads, writes=writes,
                embed=(accum is None))

    def tt(self, eng, out, a, b, op):
        self.op(eng, lambda e: e.tensor_tensor(out, a, b, op), reads=[a, b], writes=[out])

    def ts(self, eng, out, a, s1, op0, s2=None, op1=None):
        reads = [a]
        if not isinstance(s1, (int, float)):
            reads.append(s1)
        if s2 is not None and not isinstance(s2, (int, float)):
            reads.append(s2)
        if op1 is None:
            self.op(eng, lambda e: e.tensor_scalar(out, a, s1, None, op0), reads=reads, writes=[out])
        else:
            self.op(eng, lambda e: e.tensor_scalar(out, a, s1, s2, op0, op1), reads=reads, writes=[out])

    def stt(self, out, in0, scalar, in1, op0, op1):
        reads = [in0, in1]
        if not isinstance(scalar, (int, float)):
            reads.append(scalar)
        self.op("dve", lambda e: e.scalar_tensor_tensor(out, in0, scalar, in1, op0, op1),
                reads=reads, writes=[out])

    def copy(self, eng, out, in_):
        if eng == "act":
            self.op("act", lambda e: e.activation(out, in_, AF.Copy), reads=[in_], writes=[out])
        else:
            self.op(eng, lambda e: e.tensor_copy(out, in_), reads=[in_], writes=[out])

    def memset(self, eng, out, val):
        self.op(eng, lambda e: e.memset(out, val), reads=[], writes=[out])

    def dma(self, pairs, lane, eng="sp", **kw):
        self.lane(lane)
        outs = [p[0] for p in pairs]
        ins = [p[1] for p in pairs]

        def fn(e):
            return [e.dma_start(out=o, in_=i, **kw) for o, i in pairs]

        self.op(eng, fn, reads=ins, writes=outs, lane=lane, n=len(pairs))


def _consts(seq):
    c = {}
    c["ident_bf"] = np.eye(128, dtype=np.float32).astype(ml_dtypes.bfloat16)
    c["ones_bf"] = np.ones((128, 128), dtype=np.float32).astype(ml_dtypes.bfloat16)
    c["ident_f"] = np.eye(128, dtype=np.float32)
    c["ones_f"] = np.ones((128, 128), dtype=np.float32)
    lg = np.log1p(-np.exp2(-5.0 - np.arange(RET_H, dtype=np.float64)))
    j = np.arange(128, dtype=np.float64)
    causalT = (j[None, :] >= j[:, None]).astype(np.float64)
    rmask = np.zeros((128, RET_H, 128), np.float64)
    for h in range(RET_H):
        rmask[:, h, :] = np.exp(-lg[h] * (j[:, None] + 1.0)) * (RET_DK ** -0.5) * causalT
    c["rmask"] = rmask.astype(np.float32)
    qd = np.exp(lg[:, None] * (j[None, :] + 1.0))
    c["qdec"] = np.broadcast_to(qd[None], (128, RET_H, 128)).astype(np.float32).copy()
    kd = np.zeros((128, 2, RET_H), np.float64)
    for li, L in enumerate((128, 32)):
        for h in range(RET_H):
            kd[:, li, h] = np.exp(lg[h] * (L - 1.0 - j)) * (RET_DK ** -0.5)
    c["kdec"] = kd.astype(np.float32)
    c["gL"] = {L: [float(np.exp(lg[h] * L)) for h in range(RET_H)] for L in (128, 32)}
    c["gmask"] = (causalT * (GLA_DK ** -0.5)).astype(np.float32)
    c["ucum"] = ((j[:, None] <= j[None, :]) * (-1.0 / GLA_TAU)).astype(np.float32)
    c["neghalf"] = np.full((128, 512), -0.5, np.float32)
    c["epsv"] = np.full((128, 8), EPS, np.float32)
    inv = (np.float32(ROPE_BASE) ** (-(np.arange(128, dtype=np.float32) / np.float32(128)))).astype(np.float32)
    pos = np.concatenate([np.arange(seq, dtype=np.float32), PAST_LEN + np.arange(DEC_SEQ, dtype=np.float32)])
    ang = (pos[None, :] * inv[:, None]).astype(np.float32)
    c["cos"] = np.cos(ang).astype(np.float32)
    c["sin"] = np.sin(ang).astype(np.float32)
    return c


def _fm(v):
    v = np.asarray(v, np.float32)
    return np.ascontiguousarray(v.reshape(-1, 128).T)


WEIGHTS = [
    ("ret_w_in", D, 6144), ("ret_w_out", 2048, D), ("gla_w_in", D, 3088), ("gla_w_out", D, D),
    ("ffn_w_up0", D, 2 * DFF), ("ffn_w_up1", D, 2 * DFF), ("ffn_w_down0", DFF, D), ("ffn_w_down1", DFF, D),
]


def build(seq, dbg=()):
    assert seq % 512 == 0
    nc = bass.Bass("TRN2", target_bir_lowering=False)
    P = Prog(nc)
    cst = _consts(seq)
    gL = cst["gL"]
    npos = seq + DEC_SEQ

    def din(name, shape, dt=F32):
        return nc.dram_tensor(name, list(shape), dt, kind="ExternalInput").ap()

    def dout(name, shape, dt=F32):
        return nc.dram_tensor(name, list(shape), dt, kind="ExternalOutput").ap()

    NHALF = -(-seq // 4096)
    HSEQ = seq // NHALF
    assert HSEQ * NHALF == seq and HSEQ % 512 == 0
    xTp = [din("xTp%d" % i, [D, HSEQ]) for i in range(NHALF)]; xTs = din("xTs", [D, DEC_SEQ])
    st_ret = din("st_ret", [RET_H, RET_DK, RET_DV]); st_gla = din("st_gla", [GLA_H, GLA_DK, GLA_DV])
    cconv = din("cconv", [128, DEPTH, NBLK_FF, 2])
    W32 = {n: din(n, [k, m]) for n, k, m in WEIGHTS}
    Wb = {n: nc.dram_tensor(n + "_bf", [k, m], BF16, kind="Internal").ap() for n, k, m in WEIGHTS}
    d_gains = din("gains", [128, 5, 8])
    d_gn = din("gn", [128, 16]); d_ng = din("ng", [128, 8])
    d_cw = din("cw", [128, DEPTH, 3, NBLK_FF]); d_cb = din("cb", [128, DEPTH, NBLK_FF])
    d_wa2 = din("wa2aug", [17, 512])
    d_ident = din("ident_bf", [128, 128], BF16); d_ones = din("ones_bf", [128, 128], BF16)
    d_rmask = din("rmask", [128, RET_H, 128]); d_qdec = din("qdec", [128, RET_H, 128])
    d_kdec = din("kdec", [128, 2, RET_H]); d_gmask = din("gmask", [128, 128]); d_ucum = din("ucum", [128, 128])
    d_neghalf = din("neghalf", [128, 512]); d_epsv = din("epsv", [128, 8])
    d_identf = din("ident_f", [128, 128]); d_onesf = din("ones_f", [128, 128])
    d_cos = din("cos", [128, npos]); d_sin = din("sin", [128, npos])

    yTp = [dout("yTp%d" % i, [D, HSEQ]) for i in range(NHALF)]; yTs = dout("yTs", [D, DEC_SEQ])
    o_ret = {"p": dout("ret_p", [RET_H, RET_DK, RET_DV]), "s": dout("ret_s", [RET_H, RET_DK, RET_DV])}
    o_gla = {"p": dout("gla_p", [GLA_H, GLA_DK, GLA_DV]), "s": dout("gla_s", [GLA_H, GLA_DK, GLA_DV])}
    o_conv = {"p": dout("conv_p", [128, DEPTH, NBLK_FF, 2]), "s": dout("conv_s", [128, DEPTH, NBLK_FF, 2])}
    dbg_out = {}

    arena_h = nc.alloc_sbuf_tensor("arena", [128, ARENA], U8)
    arena = arena_h.ap()
    psum_h = nc.alloc_psum_tensor("psum", [128, 8, 512], F32)
    psum = psum_h.ap()

    off = {"_": 0}
    reg = {}

    def region(name, nbytes):
        assert nbytes % 256 == 0, name
        reg[name] = (off["_"], nbytes)
        off["_"] += nbytes
        assert off["_"] <= ARENA, (name, off["_"])

    def view(name, dt, shape, boff=0, parts=128):
        o, nb = reg[name]
        n = int(np.prod(shape)) * DSZ[dt]
        assert boff + n <= nb, (name, boff, n, nb)
        ap = arena[0:parts, o + boff:o + boff + n].bitcast(dt)
        if len(shape) == 2:
            ap = ap.rearrange("p (a b) -> p a b", a=shape[0])
        elif len(shape) == 3:
            ap = ap.rearrange("p (a b c) -> p a b c", a=shape[0], b=shape[1])
        return ap

    region("x", 16384); region("hT", 8192); region("ms", 2048); region("rstd", 2048)
    region("identf", 512); region("onesf", 512)
    region("qkk", 24576)
    region("vtok", 16384)
    region("sgT", 16384)
    region("ogT", 16384)
    region("mix", 20480)
    region("PT", 1024); region("on", 4096); region("stats", 1024); region("stmp", 2048); region("aT", 1024)
    region("Sret", 16384); region("Sretb", 8192); region("Sgla", 4096); region("Sglab", 2048)
    region("wring", 3 * 8192)
    for nme, nb in [("ident", 256), ("ones", 256), ("rmask", 2048), ("qdec", 2048), ("kdec", 256), ("gmask", 512),
                    ("ucum", 512), ("neghalf", 2048), ("epsv", 256), ("gains", 256), ("gn", 256), ("ng", 256), ("cw", 1280),
                    ("cb", 512), ("wa2", 1024), ("uhb", 512), ("uhf", 768)]:
        region(nme, nb)

    ident = view("ident", BF16, [128]); ones = view("ones", BF16, [128])
    identf = view("identf", F32, [128]); onesf = view("onesf", F32, [128])
    rmask = view("rmask", F32, [RET_H, 128]); qdec = view("qdec", F32, [RET_H, 128])
    kdec = view("kdec", F32, [2, RET_H]); gmask = view("gmask", F32, [128]); ucum = view("ucum", F32, [128])
    neghalf = view("neghalf", F32, [512]); epsv = view("epsv", F32, [8])
    gains = view("gains", F32, [5, 8]); gnT = view("gn", F32, [16]); ngT = view("ng", F32, [8])
    cw = view("cw", F32, [DEPTH, 3, NBLK_FF]); cb = view("cb", F32, [DEPTH, NBLK_FF])
    wa2 = view("wa2", BF16, [512], parts=32)
    uhb = view("uhb", BF16, [DEPTH, NBLK_FF, 2]); uhf = view("uhf", F32, [DEPTH, NBLK_FF, 2])
    Sret = view("Sret", F32, [RET_H, 2, 512]); Sretb = view("Sretb", BF16, [RET_H, 2, 512])
    Sgla = view("Sgla", F32, [GLA_H, 256]); Sglab = view("Sglab", BF16, [GLA_H, 256])

    psb = [psum[:, b, :] for b in range(8)]
    psb_bf = [psum[:, b, :].bitcast(BF16) for b in range(8)]
    bank_ctr = {"i": 0}

    reserved = set()

    def bank():
        while True:
            b = bank_ctr["i"] % 8
            bank_ctr["i"] += 1
            if b not in reserved:
                return b

    P.dma([(ident, d_ident), (ones, d_ones), (rmask, d_rmask), (qdec, d_qdec), (kdec, d_kdec), (gmask, d_gmask),
           (ucum, d_ucum), (neghalf, d_neghalf), (epsv, d_epsv), (identf, d_identf), (onesf, d_onesf), (gains, d_gains), (gnT, d_gn), (ngT, d_ng), (cw, d_cw), (cb, d_cb)],
          "const")
    P.dma([(wa2[0:17, :], d_wa2)], "wa2c", eng="pool")
    for n, k, m in WEIGHTS:
        if n in ("ret_w_out", "gla_w_out"):
            continue
        P.dma([(Wb[n], W32[n])], "cast_" + n, eng="pool", max_dma_last_dim=4096)
    fi = 0
    for n, gv_, nch in (("ret_w_out", gnT, 16), ("gla_w_out", ngT, 8)):
        for ec in range(nch):
            wi = view("mix", F32, [1024], boff=(fi % 2) * 4096)
            wo = view("mix", BF16, [1024], boff=8192 + (fi % 2) * 2048)
            fi += 1
            P.dma([(wi, W32[n][ec * 128:(ec + 1) * 128, :])], "foldin")
            P.ts("dve", wo, wi, gv_[:, ec:ec + 1], ALU.mult)
            P.dma([(Wb[n][ec * 128:(ec + 1) * 128, :], wo)], "foldout")

    ring = {"i": 0}

    def slab(wname, kc0, nkc, colgroups):
        s = ring["i"] % 3
        ring["i"] += 1
        ncols = sum(c for _, c in colgroups)
        assert nkc * ncols * 2 <= 8192
        v = view("wring", BF16, [nkc, ncols], boff=s * 8192)
        pairs = []
        co = 0
        for c0, cn in colgroups:
            src = Wb[wname][kc0 * 128:(kc0 + nkc) * 128, c0:c0 + cn].rearrange("(k p) n -> p k n", p=128)
            pairs.append((v[:, :, co:co + cn], src))
            co += cn
        P.dma(pairs, "w%d" % s)
        return v

    aT_all = view("aT", BF16, [512], parts=32)
    P.memset("dve", aT_all, 1.0)

    def run_tile(sk, t0, NT, CL, last):
        NCH = NT // CL
        li = 0 if CL == 128 else 1
        xsrc = xTp[t0 // HSEQ] if sk == "p" else xTs
        ydst = yTp[t0 // HSEQ] if sk == "p" else yTs
        tq = t0 % HSEQ if sk == "p" else t0
        pos0 = t0 if sk == "p" else seq + t0
        x = view("x", F32, [8, NT])
        hT = view("hT", BF16, [8, NT])
        ybuf = view("vtok", F32, [8, NT])
        sq = view("sgT", BF16, [8, NT], boff=8192)
        ms = view("ms", F32, [4]); rstd = view("ms", F32, [4], boff=256)
        rbc = view("rstd", F32, [4, 128])
        qT = view("qkk", BF16, [8, NT]); kT = view("qkk", BF16, [8, NT], boff=8192)
        ktok = view("qkk", BF16, [NCH, 1024], boff=16384)
        vtok = view("vtok", BF16, [NCH, 2048])
        sgT = view("sgT", BF16, [16, NT]); ogT = view("ogT", BF16, [16, NT])
        PTv = [view("PT", BF16, [128], boff=i * 256) for i in range(4)]
        onv = [view("on", BF16, [512], boff=i * 1024) for i in range(3)]
        statv = [(view("stats", F32, [6], boff=i * 256), view("stats", F32, [2], boff=i * 256 + 64),
                  view("stats", F32, [1], boff=i * 256 + 128), view("stats", F32, [1], boff=i * 256 + 192))
                 for i in range(4)]
        stmp = [view("stmp", BF16, [NT], boff=i * 1024) for i in range(2)]
        cnt = {"pt": 0, "on": 0, "st": 0, "sv": 0}

        def tsl(ch):
            return slice(ch * CL, (ch + 1) * CL)

        P.dma([(x, xsrc[:, tq:tq + NT].rearrange("(c p) t -> p c t", p=128))], "xin")

        def norm(gidx, out_hT=True):
            TB = min(128, NT)
            NTB = NT // TB
            P.act(sq, x, AF.Square)
            b = bank()
            for tb in range(NTB):
                for c in range(8):
                    P.mm(psb[b][0:TB, tb:tb + 1], sq[:, c, tb * TB:(tb + 1) * TB], ones[:, 0:1],
                         start=(c == 0), stop=(c == 7), sig=(c == 7 and tb == NTB - 1))
            P.ts("dve", ms[0:TB, 0:NTB], psb[b][0:TB, 0:NTB], 1.0 / D, ALU.mult, EPS, ALU.add)
            P.tt("pool", rstd[0:TB, 0:NTB], ms[0:TB, 0:NTB], neghalf[0:TB, 0:NTB], ALU.pow)
            b2 = bank()
            for tb in range(NTB):
                P.ts("dve", rbc[0:TB, tb, :], onesf[0:TB, :], rstd[0:TB, tb:tb + 1], ALU.mult)
                P.mm(psb[b2][:, tb * TB:(tb + 1) * TB], rbc[0:TB, tb, :], identf[0:TB, 0:TB],
                     start=True, stop=True, sig=(tb == NTB - 1))
            for c in range(8):
                dst = hT[:, c, :] if out_hT else ybuf[:, c, :]
                P.stt(dst, x[:, c, :], gains[:, gidx, c:c + 1], psb[b2][:, 0:NT], ALU.mult, ALU.mult)

        def resid_proj(wname, nkc, src, kslabs):
            for cg in range(4):
                banks = [bank(), bank()]
                k0 = 0
                while k0 < nkc:
                    nk = min(kslabs, nkc - k0)
                    sl = slab(wname, k0, nk, [(cg * 256, 256)])
                    for j in range(2):
                        for kk in range(nk):
                            kc = k0 + kk
                            P.mm(psb[banks[j]][:, 0:NT], sl[:, kk, j * 128:(j + 1) * 128], src[:, kc, :],
                                 start=(kc == 0), stop=(kc == nkc - 1),
                                 sig=(kc == nkc - 1) or (kk == nk - 1 and j == 1))
                    k0 += nk
                for j in range(2):
                    blk = cg * 2 + j
                    P.tt("dve", x[:, blk, :], x[:, blk, :], psb[banks[j]][:, 0:NT], ALU.add)

        def retention():
            cs = view("mix", F32, [2, NT])
            cqs = [view("mix", F32, [2, NT], boff=4096 + i * 4096) for i in range(2)]
            rt = view("mix", F32, [4, NT], boff=12288)
            P.dma([(cs[:, 0, :], d_cos[:, pos0:pos0 + NT]), (cs[:, 1, :], d_sin[:, pos0:pos0 + NT])], "cs")

            def rotary(pa, pb, ct, st_, o1, o2):
                P.tt("dve", rt[:, 0, :], pa, ct, ALU.mult)
                P.tt("dve", rt[:, 1, :], pb, st_, ALU.mult)
                P.tt("dve", o1, rt[:, 0, :], rt[:, 1, :], ALU.subtract)
                P.tt("dve", rt[:, 2, :], pa, st_, ALU.mult)
                P.tt("dve", rt[:, 3, :], pb, ct, ALU.mult)
                P.tt("dve", o2, rt[:, 2, :], rt[:, 3, :], ALU.add)

            sg_tok = view("sgT", BF16, [NCH, 2048])

            def mk_groups(h):
                st = {}
                gl = []
                for kind in (0, 1):
                    for ch in range(NCH):
                        def grp(kind=kind, ch=ch):
                            if ch == 0:
                                st[kind] = slab("ret_w_in", 0, 8, [(2048 + kind * 2048 + h * 512, 512)])
                            sl_ = st[kind]
                            b_ = bank()
                            for kc in range(8):
                                P.mm(psb[b_][0:CL, :], hT[:, kc, tsl(ch)], sl_[:, kc, :], start=(kc == 0), stop=(kc == 7))
                            if kind == 0:
                                P.copy("act", vtok[0:CL, ch, h * 512:(h + 1) * 512], psb[b_][0:CL, :])
                            else:
                                P.act(sg_tok[0:CL, ch, h * 512:(h + 1) * 512], psb[b_][0:CL, :], AF.Silu)
                        gl.append(grp)
                return gl

            Q = []
            for h in range(RET_H):
                Q += mk_groups(h)

            def emit_groups(n):
                for _ in range(n):
                    if Q:
                        Q.pop(0)()

            for which in range(2):
                dstT = qT if which == 0 else kT
                for sh in range(2):
                    sl = slab("ret_w_in", 0, 8, [(which * 1024 + sh * 512, 512)])
                    for hh in range(2):
                        h = sh * 2 + hh
                        ba, bb = bank(), bank()
                        for half, bk in ((0, ba), (1, bb)):
                            for kc in range(8):
                                P.mm(psb[bk][:, 0:NT], sl[:, kc, (hh * 2 + half) * 128:(hh * 2 + half + 1) * 128],
                                     hT[:, kc, :], start=(kc == 0), stop=(kc == 7))
                        if which == 0:
                            cq = cqs[h % 2]
                            qd = qdec[:, h, 0:CL].unsqueeze(1).to_broadcast([128, NCH, CL])
                            for t_ in range(2):
                                P.tt("pool", cq[:, t_, :].rearrange("p (a b) -> p a b", a=NCH),
                                     cs[:, t_, :].rearrange("p (a b) -> p a b", a=NCH), qd, ALU.mult)
                            rotary(psb[ba][:, 0:NT], psb[bb][:, 0:NT], cq[:, 0, :], cq[:, 1, :],
                                   dstT[:, 2 * h, :], dstT[:, 2 * h + 1, :])
                        else:
                            rotary(psb[ba][:, 0:NT], psb[bb][:, 0:NT], cs[:, 0, :], cs[:, 1, :],
                                   dstT[:, 2 * h, :], dstT[:, 2 * h + 1, :])
                        emit_groups(1)
            for h in range(RET_H):
                for ch in range(NCH):
                    b = bank()
                    for dc in range(2):
                        P.tr(psb_bf[b][0:CL, dc * 128:(dc + 1) * 128], kT[:, 2 * h + dc, tsl(ch)], ident,
                             sig=(dc == 1))
                    P.act(ktok[0:CL, ch, h * 256:(h + 1) * 256], psb_bf[b][0:CL, 0:256], AF.Copy,
                          scale=kdec[0:CL, li, h:h + 1])
            for h in range(RET_H):
                for ch in range(NCH):
                    bs = bank()
                    for dc in range(2):
                        P.mm(psb[bs][0:CL, 0:CL], kT[:, 2 * h + dc, tsl(ch)], qT[:, 2 * h + dc, tsl(ch)],
                             start=(dc == 0), stop=(dc == 1))
                    PT = PTv[cnt["pt"] % 4]; cnt["pt"] += 1
                    P.tt("dve", PT[0:CL, 0:CL], psb[bs][0:CL, 0:CL], rmask[0:CL, h, 0:CL], ALU.mult)
                    emit_groups(1)
                    bo = bank()
                    P.mm(psb[bo][0:CL, :], PT[0:CL, 0:CL], vtok[0:CL, ch, h * 512:(h + 1) * 512], start=True, stop=False)
                    for dc in range(2):
                        P.mm(psb[bo][0:CL, :], qT[:, 2 * h + dc, tsl(ch)], Sretb[:, h, dc, :], start=False, stop=(dc == 1))
                    bS = [bank(), bank()]
                    for dc in range(2):
                        P.mm(psb[bS[dc]][:, :], ktok[0:CL, ch, h * 256 + dc * 128:h * 256 + (dc + 1) * 128],
                             vtok[0:CL, ch, h * 512:(h + 1) * 512], start=True, stop=True)
                    stats6, mv, ve, rs = statv[cnt["sv"] % 4]; cnt["sv"] += 1
                    P.op("dve", lambda e, o=stats6[0:CL, :], i=psb[bo][0:CL, :]: e.bn_stats(o, i),
                         reads=[psb[bo][0:CL, :]], writes=[stats6[0:CL, :]])
                    P.op("dve", lambda e, o=mv[0:CL, :], i=stats6[0:CL, :]: e.bn_aggr(o, i),
                         reads=[stats6[0:CL, :]], writes=[mv[0:CL, :]])
                    P.ts("dve", ve[0:CL, :], mv[0:CL, 1:2], EPS, ALU.add)
                    P.tt("pool", rs[0:CL, :], ve[0:CL, :], neghalf[0:CL, 0:1], ALU.pow)
                    for dc in range(2):
                        P.stt(Sret[:, h, dc, :], Sret[:, h, dc, :], gL[CL][h], psb[bS[dc]][:, :], ALU.mult, ALU.add)
                        P.copy("act", Sretb[:, h, dc, :], Sret[:, h, dc, :])
                    emit_groups(1)
                    on = onv[cnt["on"] % 3]; cnt["on"] += 1
                    P.ts("dve", on[0:CL, :], psb[bo][0:CL, :], mv[0:CL, 0:1], ALU.subtract, rs[0:CL, :], ALU.mult)
                    P.tt("dve", on[0:CL, :], on[0:CL, :], sg_tok[0:CL, ch, h * 512:(h + 1) * 512], ALU.mult)
                    bt = bank()
                    for eb in range(4):
                        P.tr(psb_bf[bt][:, eb * CL:(eb + 1) * CL], on[0:CL, eb * 128:(eb + 1) * 128],
                             ident[0:CL, 0:CL], sig=(eb == 3))
                    P.copy("act", ogT[:, 4 * h:4 * h + 4, tsl(ch)],
                           psb_bf[bt][:, 0:4 * CL].rearrange("p (a b) -> p a b", a=4))
            emit_groups(len(Q))
            resid_proj("ret_w_out", 16, ogT, 16)

        def ffn(layer):
            actT = view("qkk", BF16, [22, NT])
            ubuf = [[view("sgT", F32, [NT + 2], boff=(r * 2 + gv) * 2304) for gv in range(2)] for r in range(2)]
            cbuf = [[view("sgT", F32, [NT], boff=9216 + gv * 2048), view("on", F32, [NT], boff=gv * 2048)][r]
                    for r in range(2) for gv in range(2)]
            cbuf = [[cbuf[r * 2 + gv] for gv in range(2)] for r in range(2)]
            sgt = [view("sgT", BF16, [NT], boff=13312 + r * 1024) for r in range(2)]
            wn = "ffn_w_up%d" % layer
            it = 0
            pend = None

            def gate_stage(r, gb):
                P.act(sgt[r], cbuf[r][0], AF.Silu)
                P.tt("dve", actT[:, gb, :], cbuf[r][1], sgt[r], ALU.mult)

            for s in range(11):
                sl = slab(wn, 0, 8, [(s * 256, 256), (DFF + s * 256, 256)])
                for jj in range(2):
                    gb = s * 2 + jj
                    r = it % 2; it += 1
                    for gv in range(2):
                        blk = gb + gv * 22
                        b = bank()
                        for kc in range(8):
                            P.mm(psb[b][:, 0:NT], sl[:, kc, gv * 256 + jj * 128:gv * 256 + (jj + 1) * 128], hT[:, kc, :],
                                 start=(kc == 0), stop=(kc == 7))
                        ub = ubuf[r][gv]
                        c = cbuf[r][gv]
                        P.copy("pool", ub[:, 0:2], uhf[:, layer, blk, :])
                        P.copy("act", ub[:, 2:2 + NT], psb[b][:, 0:NT])
                        P.copy("pool", uhf[:, layer, blk, :], ub[:, NT:NT + 2])
                        P.act(c, ub[:, 0:NT], AF.Identity, bias=cb[:, layer, blk:blk + 1], scale=cw[:, layer, 0, blk:blk + 1])
                        P.stt(c, ub[:, 1:NT + 1], cw[:, layer, 1, blk:blk + 1], c, ALU.mult, ALU.add)
                        P.stt(c, ub[:, 2:NT + 2], cw[:, layer, 2, blk:blk + 1], c, ALU.mult, ALU.add)
                    if pend is not None:
                        gate_stage(*pend)
                    pend = (r, gb)
            gate_stage(*pend)
            resid_proj("ffn_w_down%d" % layer, 22, actT, 11)

        def gla():
            vt = view("vtok", BF16, [NCH, 1024])
            sr_tok = view("sgT", BF16, [NCH, 1024])
            kbar = view("qkk", BF16, [NCH, 512], boff=16384)
            sp = view("mix", F32, [NCH, 512])
            Eq = view("mix", F32, [GLA_H, NT], boff=8192)
            Ek = [view("mix", F32, [NT], boff=16384 + i * 2048) for i in range(2)]
            zt = view("on", F32, [512])
            kbt = view("stmp", BF16, [NT])
            junk = view("on", BF16, [256], boff=2048)

            def mk_groups(kind, s2):
                st = {}
                gl = []
                for ch in range(NCH):
                    def grp(ch=ch):
                        if ch == 0:
                            st["s"] = slab("gla_w_in", 0, 8, [(1024 + kind * 1024 + s2 * 512, 512)])
                        b_ = bank()
                        for kc in range(8):
                            P.mm(psb[b_][0:CL, :], hT[:, kc, tsl(ch)], st["s"][:, kc, :], start=(kc == 0), stop=(kc == 7))
                        if kind == 0:
                            P.copy("act", vt[0:CL, ch, s2 * 512:(s2 + 1) * 512], psb[b_][0:CL, :])
                        else:
                            P.act(sr_tok[0:CL, ch, s2 * 512:(s2 + 1) * 512], psb[b_][0:CL, :], AF.Silu)
                    gl.append(grp)
                return gl

            Q = mk_groups(0, 0) + mk_groups(1, 0) + mk_groups(0, 1) + mk_groups(1, 1)

            def emit_groups(n):
                for _ in range(n):
                    if Q:
                        Q.pop(0)()

            sl = slab("gla_w_in", 0, 8, [(3072, 16)])
            b = bank()
            for kc in range(8):
                P.mm(psb[b][0:16, 0:NT], sl[:, kc, 0:16], hT[:, kc, :], start=(kc == 0), stop=(kc == 7))
            P.copy("act", aT_all[0:16, 0:NT], psb[b][0:16, 0:NT])
            for ch in range(NCH):
                b = bank()
                P.mm(psb[b][0:CL, :], aT_all[0:17, tsl(ch)], wa2[0:17, :], start=True, stop=True)
                P.act(zt[0:CL, :], psb[b][0:CL, :], AF.Exp, scale=-1.0)
                P.act(sp[0:CL, ch, :], zt[0:CL, :], AF.Ln, bias=1.0)
            for h in range(GLA_H):
                b = bank()
                for ch in range(NCH):
                    P.mm(psb[b][:, tsl(ch)], sp[0:CL, ch, h * 128:(h + 1) * 128], ucum[0:CL, 0:CL], start=True, stop=True,
                         sig=(ch == NCH - 1))
                P.act(Eq[:, h, :], psb[b][:, 0:NT], AF.Exp)
            slq = slab("gla_w_in", 0, 8, [(0, 512)])
            for h in range(GLA_H):
                bq = bank()
                for kc in range(8):
                    P.mm(psb[bq][:, 0:NT], slq[:, kc, h * 128:(h + 1) * 128], hT[:, kc, :], start=(kc == 0), stop=(kc == 7))
                P.tt("dve", qT[:, h, :], psb[bq][:, 0:NT], Eq[:, h, :], ALU.mult)
                if NCH >= 4:
                    emit_groups(1)
            slk = slab("gla_w_in", 0, 8, [(512, 512)])
            for h in range(GLA_H):
                P.op("dve", lambda e, o=Ek[h % 2], i=Eq[:, h, :]: e.reciprocal(o, i), reads=[Eq[:, h, :]], writes=[Ek[h % 2]])
                bk = bank()
                for kc in range(8):
                    P.mm(psb[bk][:, 0:NT], slk[:, kc, h * 128:(h + 1) * 128], hT[:, kc, :], start=(kc == 0), stop=(kc == 7))
                P.tt("dve", kT[:, h, :], psb[bk][:, 0:NT], Ek[h % 2], ALU.mult)
                for ch in range(NCH):
                    P.ts("dve", kbt[:, tsl(ch)], kT[:, h, tsl(ch)], Eq[:, h, (ch + 1) * CL - 1:(ch + 1) * CL], ALU.mult,
                         GLA_DK ** -0.5, ALU.mult)
                    bt = bank()
                    P.tr(psb_bf[bt][0:CL, 0:128], kbt[:, tsl(ch)], ident)
                    P.copy("act", kbar[0:CL, ch, h * 128:(h + 1) * 128], psb_bf[bt][0:CL, 0:128])
                if NCH >= 4:
                    emit_groups(1)
            if NCH < 4:
                emit_groups(len(Q))
            for h in range(GLA_H):
                for ch in range(NCH):
                    ba = bank()
                    P.mm(psb[ba][0:CL, 0:CL], kT[:, h, tsl(ch)], qT[:, h, tsl(ch)], start=True, stop=True)
                    PT = PTv[cnt["pt"] % 4]; cnt["pt"] += 1
                    P.tt("dve", PT[0:CL, 0:CL], psb[ba][0:CL, 0:CL], gmask[0:CL, 0:CL], ALU.mult)
                    emit_groups(1)
                    bo = bank()
                    P.mm(psb[bo][0:CL, 0:256], PT[0:CL, 0:CL], vt[0:CL, ch, h * 256:(h + 1) * 256], start=True, stop=False)
                    P.mm(psb[bo][0:CL, 0:256], qT[:, h, tsl(ch)], Sglab[:, h, :], start=False, stop=True)
                    bS = bank()
                    P.mm(psb[bS][:, 0:256], kbar[0:CL, ch, h * 128:(h + 1) * 128], vt[0:CL, ch, h * 256:(h + 1) * 256],
                         start=True, stop=True)
                    stats6, mv, ve, rs = statv[cnt["sv"] % 4]; cnt["sv"] += 1
                    P.act(junk[0:CL, :], psb[bo][0:CL, 0:256], AF.Square, accum=ve[0:CL, :])
                    P.ts("dve", mv[0:CL, 0:1], ve[0:CL, :], 1.0 / GLA_DV, ALU.mult, EPS, ALU.add)
                    P.tt("pool", rs[0:CL, :], mv[0:CL, 0:1], neghalf[0:CL, 0:1], ALU.pow)
                    P.stt(Sgla[:, h, :], Sgla[:, h, :], Eq[:, h, (ch + 1) * CL - 1:(ch + 1) * CL], psb[bS][:, 0:256],
                          ALU.mult, ALU.add)
                    P.copy("act", Sglab[:, h, :], Sgla[:, h, :])
                    on = view("on", BF16, [256], boff=2560 + (cnt["on"] % 2) * 512); cnt["on"] += 1
                    P.ts("dve", on[0:CL, :], psb[bo][0:CL, 0:256], rs[0:CL, :], ALU.mult)
                    P.tt("dve", on[0:CL, :], on[0:CL, :], sr_tok[0:CL, ch, h * 256:(h + 1) * 256], ALU.mult)
                    bt = bank()
                    for eb in range(2):
                        P.tr(psb_bf[bt][:, eb * CL:(eb + 1) * CL], on[0:CL, eb * 128:(eb + 1) * 128], ident[0:CL, 0:CL],
                             sig=(eb == 1))
                    P.copy("act", ogT[:, 2 * h:2 * h + 2, tsl(ch)],
                           psb_bf[bt][:, 0:2 * CL].rearrange("p (a b) -> p a b", a=2))
            emit_groups(len(Q))
            resid_proj("gla_w_out", 8, ogT, 8)

        norm(0); retention()
        norm(2); ffn(0)
        norm(1); gla()
        norm(3); ffn(1)
        norm(4, out_hT=False)
        P.dma([(ydst[:, tq:tq + NT].rearrange("(c p) t -> p c t", p=128), ybuf)], "yout")

    def seq_end(sk):
        P.dma([(o_ret[sk].rearrange("h (dc p) e -> p h dc e", p=128), Sret)], "so_ret")
        P.dma([(o_gla[sk].rearrange("h p e -> p h e"), Sgla)], "so_gla")
        P.dma([(o_conv[sk], uhf)], "so_conv")

    P.dma([(Sret, st_ret.rearrange("h (dc p) e -> p h dc e", p=128)), (Sgla, st_gla.rearrange("h p e -> p h e")),
           (uhf, cconv)], "stin")
    P.copy("act", Sretb, Sret); P.copy("act", Sglab, Sgla); P.copy("dve", uhb, uhf)
    run_tile("s", 0, DEC_SEQ, DEC_SEQ, True)
    seq_end("s")
    P.memset("pool", Sret, 0.0); P.memset("pool", Sretb, 0.0); P.memset("pool", Sgla, 0.0); P.memset("pool", Sglab, 0.0)
    P.memset("pool", uhb, 0.0); P.memset("pool", uhf, 0.0)
    ntile = seq // 512
    for ti in range(ntile):
        run_tile("p", ti * 512, 512, 128, ti == ntile - 1)
    seq_end("p")

    P.wait_all("sp", ("yout", "so_ret", "so_gla", "so_conv"))

    with contextlib.ExitStack() as es:
        for lname, L in P.lanes.items():
            L["sem"] = es.enter_context(nc.semaphore("s_" + lname))
        block = es.enter_context(nc.Block())
        P.emit(block)
    return nc, cst, P


_CACHE = {}


def _prep_inputs(inp, seq, cst):
    f32 = lambda a: np.ascontiguousarray(np.asarray(a, np.float32))
    shared = {
        "ret_w_in": f32(inp["ret_w_in"][0]), "ret_w_out": f32(inp["ret_w_out"][0]),
        "gla_w_in": f32(inp["gla_w_in"][0]), "gla_w_out": f32(inp["gla_w_out"][0]),
        "ffn_w_up0": f32(inp["ffn_w_up"][0]), "ffn_w_up1": f32(inp["ffn_w_up"][1]),
        "ffn_w_down0": f32(inp["ffn_w_down"][0]), "ffn_w_down1": f32(inp["ffn_w_down"][1]),
    }
    nm, nf = np.asarray(inp["norm_mix"], np.float32), np.asarray(inp["norm_ffn"], np.float32)
    gains = np.stack([_fm(nm[0]), _fm(nm[1]), _fm(nf[0]), _fm(nf[1]), _fm(inp["norm_final"])], axis=1)
    shared["gains"] = np.ascontiguousarray(gains)
    shared["gn"] = _fm(np.asarray(inp["ret_gn_g"], np.float32)[0].reshape(-1))
    shared["ng"] = _fm(np.asarray(inp["gla_norm_g"], np.float32)[0].reshape(-1))
    cwv = np.asarray(inp["ffn_conv_w"], np.float32)
    shared["cw"] = np.ascontiguousarray(cwv.reshape(DEPTH, 3, NBLK_FF, 128).transpose(3, 0, 1, 2))
    cbv = np.asarray(inp["ffn_conv_b"], np.float32)
    shared["cb"] = np.ascontiguousarray(cbv.reshape(DEPTH, NBLK_FF, 128).transpose(2, 0, 1))
    shared["wa2aug"] = np.ascontiguousarray(np.concatenate(
        [np.asarray(inp["gla_w_a2"], np.float32)[0], np.asarray(inp["gla_b_a"], np.float32)[0][None, :]], axis=0))
    for k in ("ident_bf", "ones_bf", "ident_f", "ones_f", "rmask", "qdec", "kdec", "gmask", "ucum", "neghalf", "epsv", "cos", "sin"):
        shared[k] = cst[k]
    xp = np.asarray(inp["x_prompt"], np.float32)
    xs = np.asarray(inp["x_sample"], np.float32)
    cc = np.asarray(inp["cache_conv"], np.float32)
    maps = []
    for b in range(NCORES):
        m = dict(shared)
        hs = seq // (-(-seq // 4096))
        for i in range(seq // hs):
            m["xTp%d" % i] = np.ascontiguousarray(xp[b, i * hs:(i + 1) * hs].T)
        m["xTs"] = np.ascontiguousarray(xs[b].T)
        m["st_ret"] = f32(inp["state_ret"][0, b])
        m["st_gla"] = f32(inp["state_gla"][0, b])
        m["cconv"] = np.ascontiguousarray(cc[:, b].reshape(DEPTH, 2, NBLK_FF, 128).transpose(3, 0, 2, 1))
        maps.append(m)
    return maps


def _run(inp, seq):
    if seq not in _CACHE:
        _CACHE[seq] = build(seq)
    nc, cst, _ = _CACHE[seq]
    maps = _prep_inputs(inp, seq, cst)
    res = run_bass_kernel_spmd(nc, maps, core_ids=list(range(NCORES)))
    R = res.results
    B = NCORES
    hs = seq // (-(-seq // 4096))
    y_p = np.stack([np.concatenate([R[b]["yTp%d" % i].T for i in range(seq // hs)], axis=0)
                    for b in range(B)]).astype(np.float32)
    y_s = np.stack([R[b]["yTs"].T for b in range(B)]).astype(np.float32)
    ret_p = np.stack([R[b]["ret_p"] for b in range(B)])[None].astype(np.float32)
    ret_s = np.stack([R[b]["ret_s"] for b in range(B)])[None].astype(np.float32)
    gla_p = np.stack([R[b]["gla_p"] for b in range(B)])[None].astype(np.float32)
    gla_s = np.stack([R[b]["gla_s"] for b in range(B)])[None].astype(np.float32)

    def conv(k):
        a = np.stack([R[b][k] for b in range(B)])
        return np.ascontiguousarray(a.transpose(2, 0, 4, 3, 1).reshape(DEPTH, B, 2, 2 * DFF)).astype(np.float32)

    return (y_p, y_s, ret_p, ret_s, gla_p, gla_s, conv("conv_p"), conv("conv_s"))


def kernel(**inputs):
    seq = int(np.asarray(inputs["x_prompt"]).shape[1])
    return _run(inputs, seq)
```

```python
import contextlib
import math
import numpy as np
import ml_dtypes
import concourse.bass as bass
import concourse.mybir as mybir
from concourse.bass_utils import run_bass_kernel_spmd

F32 = mybir.dt.float32
BF16 = mybir.dt.bfloat16
U8 = mybir.dt.uint8
AF = mybir.ActivationFunctionType
ALU = mybir.AluOpType
DSZ = {F32: 4, BF16: 2, U8: 1}

D = 1024
DEPTH = 2
RET_H = 4
RET_DK = 256
RET_DV = 512
GLA_H = 4
GLA_DK = 128
GLA_DV = 256
GLA_RANK = 16
GLA_TAU = 16.0
DFF = 2816
NBLK_FF = 2 * DFF // 128
EPS = 1e-6
ROPE_BASE = 10000.0
PAST_LEN = 2048
DEC_SEQ = 32
NCORES = 8

ARENA = 211968


class Prog:
    SBG = 256

    def __init__(self, nc):
        self.nc = nc
        self.q = {e: [] for e in ("pe", "act", "dve", "pool", "sp")}
        self.lanes = {}
        for e in ("pe", "act", "dve", "pool"):
            self.lanes[e] = {"count": 0, "inc": 1, "sem": None}
        self.clock = {e: {} for e in self.q}
        self.snap = {}
        self.gran = {}
        self.maxwait = {}
        self.nops = 0

    def lane(self, name):
        if name not in self.lanes:
            self.lanes[name] = {"count": 0, "inc": 16, "sem": None}
        return name

    def keys(self, ap):
        t = ap.tensor
        name = t.name
        if name not in ("arena", "psum"):
            return [("d", name)]
        esz = DSZ[ap.dtype]
        pairs = list(ap.ap)
        pstep = pairs[0][0]
        off = int(ap.offset)
        inpart = off % pstep if pstep else off
        starts = [inpart]
        free = [(s, c) for (s, c) in pairs[1:] if c > 1 or len(pairs) == 2]
        free = [(s, c) for (s, c) in free if s != 0]
        length = 1
        i = 0
        while i < len(free):
            s, c = free[i]
            inner_ext = sum((cc - 1) * abs(ss) for ss, cc in free[i + 1:]) + 1
            if i == len(free) - 1:
                length = (c - 1) * abs(s) + 1
            elif abs(s) > inner_ext and len(starts) * c <= 128:
                starts = [st + k * s for st in starts for k in range(c)]
            else:
                length = sum((cc - 1) * abs(ss) for ss, cc in free[i:]) + 1
                break
            i += 1
        g = self.SBG if name == "arena" else 2048
        ks = set()
        for st in starts:
            lo = (st * esz) // g
            hi = ((st + length) * esz - 1) // g
            for k in range(lo, hi + 1):
                ks.add((name, k))
        return ks

    def op(self, eng, fn, reads=(), writes=(), sig=True, lane=None, n=1, embed=True):
        self.nops += 1
        mylane = lane if lane is not None else eng
        L = self.lanes[mylane]
        raw = {}
        oth = {}

        def add(d, ls):
            l, s = ls
            if d.get(l, 0) < s:
                d[l] = s

        rkeys = set()
        for ap in reads:
            rkeys |= set(self.keys(ap))
        wkeys = set()
        for ap in writes:
            wkeys |= set(self.keys(ap))
        for k in rkeys:
            g = self.gran.get(k)
            if g is not None and g[0] is not None:
                add(raw, g[0])
        for k in wkeys:
            g = self.gran.get(k)
            if g is not None:
                if g[0] is not None:
                    add(oth, g[0])
                for l, s in g[1].items():
                    add(oth, (l, s))
        deps = {}
        for l, s in raw.items():
            if l == eng and lane is None:
                if eng == "pe":
                    continue
            add(deps, (l, s))
        for l, s in oth.items():
            if l == eng and lane is None and eng == "pe":
                continue
            add(deps, (l, s))
        if lane is not None and L["count"] > 0:
            add(deps, (mylane, L["count"]))
        if sig:
            L["count"] += n
            seq = L["count"]
        else:
            seq = L["count"] + 1
        ck = self.clock[eng]
        waits = []
        for l, s in sorted(deps.items()):
            if ck.get(l, 0) < s:
                waits.append((l, s * self.lanes[l]["inc"]))
                if self.maxwait.get(l, 0) < s:
                    self.maxwait[l] = s
                sn = self.snap.get((l, s))
                if sn:
                    for l2, s2 in sn.items():
                        if ck.get(l2, 0) < s2:
                            ck[l2] = s2
                ck[l] = s
        if sig:
            sn = dict(ck)
            sn[mylane] = seq
            self.snap[(mylane, seq)] = sn
        for k in rkeys:
            g = self.gran.get(k)
            if g is None:
                g = [None, {}]
                self.gran[k] = g
            if g[1].get(mylane, 0) < seq:
                g[1][mylane] = seq
        for k in wkeys:
            self.gran[k] = [(mylane, seq), {}]
        self.q[eng].append((waits, fn, (mylane, L["inc"]) if sig else None, embed))

    def wait_all(self, eng, lanes):
        waits = [(l, self.lanes[l]["count"] * self.lanes[l]["inc"]) for l in lanes if self.lanes[l]["count"] > 0]
        self.q[eng].append((waits, None, None, False))

    def emit(self, block):
        for l, s in self.maxwait.items():
            assert s <= self.lanes[l]["count"], (l, s, self.lanes[l]["count"])

        def runner(name):
            items = self.q[name]
            lanes = self.lanes

            def body(e):
                for waits, fn, sig, embed in items:
                    emb = None
                    if embed and fn is not None and waits:
                        emb = waits[-1]
                        waits = waits[:-1]
                    for l, v in waits:
                        e.wait_ge(lanes[l]["sem"], v)
                    if fn is None:
                        continue
                    r = fn(e)
                    if emb is not None:
                        first = r[0] if isinstance(r, (list, tuple)) else r
                        first._wait_ge(lanes[emb[0]]["sem"], emb[1])
                    if sig is not None:
                        sem = lanes[sig[0]]["sem"]
                        if isinstance(r, (list, tuple)):
                            for ins in r:
                                ins.then_inc(sem, sig[1])
                        else:
                            r.then_inc(sem, sig[1])

            return body

        block.tensor(runner("pe"))
        block.scalar(runner("act"))
        block.vector(runner("dve"))
        block.gpsimd(runner("pool"))
        block.sync(runner("sp"))

    def mm(self, out, lhsT, rhs, start, stop, sig=None):
        self.op("pe", lambda e: e.matmul(out, lhsT, rhs, start=start, stop=stop),
                reads=[lhsT, rhs], writes=[out], sig=(stop if sig is None else sig))

    def tr(self, out, in_, ident, sig=True):
        self.op("pe", lambda e: e.transpose(out, in_, ident), reads=[in_, ident], writes=[out], sig=sig)

    def act(self, out, in_, func, bias=None, scale=None, accum=None):
        reads = [in_]
        kw = {}
        if bias is not None:
            kw["bias"] = bias
            if not isinstance(bias, (int, float)):
                reads.append(bias)
        if scale is not None:
            kw["scale"] = scale
            if not isinstance(scale, (int, float)):
                reads.append(scale)
        writes = [out]
        if accum is not None:
            kw["accum_out"] = accum
            writes.append(accum)
        self.op("act", lambda e: e.activation(out, in_, func, **kw), reads=reads, writes=writes,
                embed=(accum is None))

    def tt(self, eng, out, a, b, op):
        self.op(eng, lambda e: e.tensor_tensor(out, a, b, op), reads=[a, b], writes=[out])

    def ts(self, eng, out, a, s1, op0, s2=None, op1=None):
        reads = [a]
        if not isinstance(s1, (int, float)):
            reads.append(s1)
        if s2 is not None and not isinstance(s2, (int, float)):
            reads.append(s2)
        if op1 is None:
            self.op(eng, lambda e: e.tensor_scalar(out, a, s1, None, op0), reads=reads, writes=[out])
        else:
            self.op(eng, lambda e: e.tensor_scalar(out, a, s1, s2, op0, op1), reads=reads, writes=[out])

    def stt(self, out, in0, scalar, in1, op0, op1):
        reads = [in0, in1]
        if not isinstance(scalar, (int, float)):
            reads.append(scalar)
        self.op("dve", lambda e: e.scalar_tensor_tensor(out, in0, scalar, in1, op0, op1),
                reads=reads, writes=[out])

    def copy(self, eng, out, in_):
        if eng == "act":
            self.op("act", lambda e: e.activation(out, in_, AF.Copy), reads=[in_], writes=[out])
        else:
            self.op(eng, lambda e: e.tensor_copy(out, in_), reads=[in_], writes=[out])

    def memset(self, eng, out, val):
        self.op(eng, lambda e: e.memset(out, val), reads=[], writes=[out])

    def dma(self, pairs, lane, eng="sp", **kw):
        self.lane(lane)
        outs = [p[0] for p in pairs]
        ins = [p[1] for p in pairs]

        def fn(e):
            return [e.dma_start(out=o, in_=i, **kw) for o, i in pairs]

        self.op(eng, fn, reads=ins, writes=outs, lane=lane, n=len(pairs))


def _consts(seq):
    c = {}
    c["ident_bf"] = np.eye(128, dtype=np.float32).astype(ml_dtypes.bfloat16)
    c["ones_bf"] = np.ones((128, 128), dtype=np.float32).astype(ml_dtypes.bfloat16)
    c["ident_f"] = np.eye(128, dtype=np.float32)
    c["ones_f"] = np.ones((128, 128), dtype=np.float32)
    lg = np.log1p(-np.exp2(-5.0 - np.arange(RET_H, dtype=np.float64)))
    j = np.arange(128, dtype=np.float64)
    causalT = (j[None, :] >= j[:, None]).astype(np.float64)
    rmask = np.zeros((128, RET_H, 128), np.float64)
    for h in range(RET_H):
        rmask[:, h, :] = np.exp(-lg[h] * (j[:, None] + 1.0)) * (RET_DK ** -0.5) * causalT
    c["rmask"] = rmask.astype(np.float32)
    qd = np.exp(lg[:, None] * (j[None, :] + 1.0))
    c["qdec"] = np.broadcast_to(qd[None], (128, RET_H, 128)).astype(np.float32).copy()
    kd = np.zeros((128, 2, RET_H), np.float64)
    for li, L in enumerate((128, 32)):
        for h in range(RET_H):
            kd[:, li, h] = np.exp(lg[h] * (L - 1.0 - j)) * (RET_DK ** -0.5)
    c["kdec"] = kd.astype(np.float32)
    c["gL"] = {L: [float(np.exp(lg[h] * L)) for h in range(RET_H)] for L in (128, 32)}
    c["gmask"] = (causalT * (GLA_DK ** -0.5)).astype(np.float32)
    c["ucum"] = ((j[:, None] <= j[None, :]) * (-1.0 / GLA_TAU)).astype(np.float32)
    c["neghalf"] = np.full((128, 512), -0.5, np.float32)
    c["epsv"] = np.full((128, 8), EPS, np.float32)
    inv = (np.float32(ROPE_BASE) ** (-(np.arange(128, dtype=np.float32) / np.float32(128)))).astype(np.float32)
    pos = np.concatenate([np.arange(seq, dtype=np.float32), PAST_LEN + np.arange(DEC_SEQ, dtype=np.float32)])
    ang = (pos[None, :] * inv[:, None]).astype(np.float32)
    c["cos"] = np.cos(ang).astype(np.float32)
    c["sin"] = np.sin(ang).astype(np.float32)
    return c


def _fm(v):
    v = np.asarray(v, np.float32)
    return np.ascontiguousarray(v.reshape(-1, 128).T)


WEIGHTS = [
    ("ret_w_in", D, 6144), ("ret_w_out", 2048, D), ("gla_w_in", D, 3088), ("gla_w_out", D, D),
    ("ffn_w_up0", D, 2 * DFF), ("ffn_w_up1", D, 2 * DFF), ("ffn_w_down0", DFF, D), ("ffn_w_down1", DFF, D),
]


def build(seq, dbg=()):
    assert seq % 512 == 0
    nc = bass.Bass("TRN2", target_bir_lowering=False)
    P = Prog(nc)
    cst = _consts(seq)
    gL = cst["gL"]
    npos = seq + DEC_SEQ

    def din(name, shape, dt=F32):
        return nc.dram_tensor(name, list(shape), dt, kind="ExternalInput").ap()

    def dout(name, shape, dt=F32):
        return nc.dram_tensor(name, list(shape), dt, kind="ExternalOutput").ap()

    NHALF = -(-seq // 4096)
    HSEQ = seq // NHALF
    assert HSEQ * NHALF == seq and HSEQ % 512 == 0
    xTp = [din("xTp%d" % i, [D, HSEQ]) for i in range(NHALF)]; xTs = din("xTs", [D, DEC_SEQ])
    st_ret = din("st_ret", [RET_H, RET_DK, RET_DV]); st_gla = din("st_gla", [GLA_H, GLA_DK, GLA_DV])
    cconv = din("cconv", [128, DEPTH, NBLK_FF, 2])
    W32 = {n: din(n, [k, m]) for n, k, m in WEIGHTS}
    Wb = {n: nc.dram_tensor(n + "_bf", [k, m], BF16, kind="Internal").ap() for n, k, m in WEIGHTS}
    d_gains = din("gains", [128, 5, 8])
    d_gn = din("gn", [128, 16]); d_ng = din("ng", [128, 8])
    d_cw = din("cw", [128, DEPTH, 3, NBLK_FF]); d_cb = din("cb", [128, DEPTH, NBLK_FF])
    d_wa2 = din("wa2aug", [17, 512])
    d_ident = din("ident_bf", [128, 128], BF16); d_ones = din("ones_bf", [128, 128], BF16)
    d_rmask = din("rmask", [128, RET_H, 128]); d_qdec = din("qdec", [128, RET_H, 128])
    d_kdec = din("kdec", [128, 2, RET_H]); d_gmask = din("gmask", [128, 128]); d_ucum = din("ucum", [128, 128])
    d_neghalf = din("neghalf", [128, 512]); d_epsv = din("epsv", [128, 8])
    d_identf = din("ident_f", [128, 128]); d_onesf = din("ones_f", [128, 128])
    d_cos = din("cos", [128, npos]); d_sin = din("sin", [128, npos])

    yTp = [dout("yTp%d" % i, [D, HSEQ]) for i in range(NHALF)]; yTs = dout("yTs", [D, DEC_SEQ])
    o_ret = {"p": dout("ret_p", [RET_H, RET_DK, RET_DV]), "s": dout("ret_s", [RET_H, RET_DK, RET_DV])}
    o_gla = {"p": dout("gla_p", [GLA_H, GLA_DK, GLA_DV]), "s": dout("gla_s", [GLA_H, GLA_DK, GLA_DV])}
    o_conv = {"p": dout("conv_p", [128, DEPTH, NBLK_FF, 2]), "s": dout("conv_s", [128, DEPTH, NBLK_FF, 2])}
    dbg_out = {}

    arena_h = nc.alloc_sbuf_tensor("arena", [128, ARENA], U8)
    arena = arena_h.ap()
    psum_h = nc.alloc_psum_tensor("psum", [128, 8, 512], F32)
    psum = psum_h.ap()

    off = {"_": 0}
    reg = {}

    def region(name, nbytes):
        assert nbytes % 256 == 0, name
        reg[name] = (off["_"], nbytes)
        off["_"] += nbytes
        assert off["_"] <= ARENA, (name, off["_"])

    def view(name, dt, shape, boff=0, parts=128):
        o, nb = reg[name]
        n = int(np.prod(shape)) * DSZ[dt]
        assert boff + n <= nb, (name, boff, n, nb)
        ap = arena[0:parts, o + boff:o + boff + n].bitcast(dt)
        if len(shape) == 2:
            ap = ap.rearrange("p (a b) -> p a b", a=shape[0])
        elif len(shape) == 3:
            ap = ap.rearrange("p (a b c) -> p a b c", a=shape[0], b=shape[1])
        return ap

    region("x", 16384); region("hT", 8192); region("ms", 2048); region("rstd", 2048)
    region("identf", 512); region("onesf", 512)
    region("qkk", 24576)
    region("vtok", 16384)
    region("sgT", 16384)
    region("ogT", 16384)
    region("mix", 20480)
    region("PT", 1024); region("on", 4096); region("stats", 1024); region("stmp", 2048); region("aT", 1024)
    region("Sret", 16384); region("Sretb", 8192); region("Sgla", 4096); region("Sglab", 2048)
    region("wring", 3 * 8192)
    for nme, nb in [("ident", 256), ("ones", 256), ("rmask", 2048), ("qdec", 2048), ("kdec", 256), ("gmask", 512),
                    ("ucum", 512), ("neghalf", 2048), ("epsv", 256), ("gains", 256), ("gn", 256), ("ng", 256), ("cw", 1280),
                    ("cb", 512), ("wa2", 1024), ("uhb", 512), ("uhf", 768)]:
        region(nme, nb)

    ident = view("ident", BF16, [128]); ones = view("ones", BF16, [128])
    identf = view("identf", F32, [128]); onesf = view("onesf", F32, [128])
    rmask = view("rmask", F32, [RET_H, 128]); qdec = view("qdec", F32, [RET_H, 128])
    kdec = view("kdec", F32, [2, RET_H]); gmask = view("gmask", F32, [128]); ucum = view("ucum", F32, [128])
    neghalf = view("neghalf", F32, [512]); epsv = view("epsv", F32, [8])
    gains = view("gains", F32, [5, 8]); gnT = view("gn", F32, [16]); ngT = view("ng", F32, [8])
    cw = view("cw", F32, [DEPTH, 3, NBLK_FF]); cb = view("cb", F32, [DEPTH, NBLK_FF])
    wa2 = view("wa2", BF16, [512], parts=32)
    uhb = view("uhb", BF16, [DEPTH, NBLK_FF, 2]); uhf = view("uhf", F32, [DEPTH, NBLK_FF, 2])
    Sret = view("Sret", F32, [RET_H, 2, 512]); Sretb = view("Sretb", BF16, [RET_H, 2, 512])
    Sgla = view("Sgla", F32, [GLA_H, 256]); Sglab = view("Sglab", BF16, [GLA_H, 256])

    psb = [psum[:, b, :] for b in range(8)]
    psb_bf = [psum[:, b, :].bitcast(BF16) for b in range(8)]
    bank_ctr = {"i": 0}

    reserved = set()

    def bank():
        while True:
            b = bank_ctr["i"] % 8
            bank_ctr["i"] += 1
            if b not in reserved:
                return b

    P.dma([(ident, d_ident), (ones, d_ones), (rmask, d_rmask), (qdec, d_qdec), (kdec, d_kdec), (gmask, d_gmask),
           (ucum, d_ucum), (neghalf, d_neghalf), (epsv, d_epsv), (identf, d_identf), (onesf, d_onesf), (gains, d_gains), (gnT, d_gn), (ngT, d_ng), (cw, d_cw), (cb, d_cb)],
          "const")
    P.dma([(wa2[0:17, :], d_wa2)], "wa2c", eng="pool")
    for n, k, m in WEIGHTS:
        if n in ("ret_w_out", "gla_w_out"):
            continue
        P.dma([(Wb[n], W32[n])], "cast_" + n, eng="pool", max_dma_last_dim=4096)
    fi = 0
    for n, gv_, nch in (("ret_w_out", gnT, 16), ("gla_w_out", ngT, 8)):
        for ec in range(nch):
            wi = view("mix", F32, [1024], boff=(fi % 2) * 4096)
            wo = view("mix", BF16, [1024], boff=8192 + (fi % 2) * 2048)
            fi += 1
            P.dma([(wi, W32[n][ec * 128:(ec + 1) * 128, :])], "foldin")
            P.ts("dve", wo, wi, gv_[:, ec:ec + 1], ALU.mult)
            P.dma([(Wb[n][ec * 128:(ec + 1) * 128, :], wo)], "foldout")

    ring = {"i": 0}

    def slab(wname, kc0, nkc, colgroups):
        s = ring["i"] % 3
        ring["i"] += 1
        ncols = sum(c for _, c in colgroups)
        assert nkc * ncols * 2 <= 8192
        v = view("wring", BF16, [nkc, ncols], boff=s * 8192)
        pairs = []
        co = 0
        for c0, cn in colgroups:
            src = Wb[wname][kc0 * 128:(kc0 + nkc) * 128, c0:c0 + cn].rearrange("(k p) n -> p k n", p=128)
            pairs.append((v[:, :, co:co + cn], src))
            co += cn
        P.dma(pairs, "w%d" % s)
        return v

    aT_all = view("aT", BF16, [512], parts=32)
    P.memset("dve", aT_all, 1.0)

    def run_tile(sk, t0, NT, CL, last):
        NCH = NT // CL
        li = 0 if CL == 128 else 1
        xsrc = xTp[t0 // HSEQ] if sk == "p" else xTs
        ydst = yTp[t0 // HSEQ] if sk == "p" else yTs
        tq = t0 % HSEQ if sk == "p" else t0
        pos0 = t0 if sk == "p" else seq + t0
        x = view("x", F32, [8, NT])
        hT = view("hT", BF16, [8, NT])
        ybuf = view("vtok", F32, [8, NT])
        sq = view("sgT", BF16, [8, NT], boff=8192)
        ms = view("ms", F32, [4]); rstd = view("ms", F32, [4], boff=256)
        rbc = view("rstd", F32, [4, 128])
        qT = view("qkk", BF16, [8, NT]); kT = view("qkk", BF16, [8, NT], boff=8192)
        ktok = view("qkk", BF16, [NCH, 1024], boff=16384)
        vtok = view("vtok", BF16, [NCH, 2048])
        sgT = view("sgT", BF16, [16, NT]); ogT = view("ogT", BF16, [16, NT])
        PTv = [view("PT", BF16, [128], boff=i * 256) for i in range(4)]
        onv = [view("on", BF16, [512], boff=i * 1024) for i in range(3)]
        statv = [(view("stats", F32, [6], boff=i * 256), view("stats", F32, [2], boff=i * 256 + 64),
                  view("stats", F32, [1], boff=i * 256 + 128), view("stats", F32, [1], boff=i * 256 + 192))
                 for i in range(4)]
        stmp = [view("stmp", BF16, [NT], boff=i * 1024) for i in range(2)]
        cnt = {"pt": 0, "on": 0, "st": 0, "sv": 0}

        def tsl(ch):
            return slice(ch * CL, (ch + 1) * CL)

        P.dma([(x, xsrc[:, tq:tq + NT].rearrange("(c p) t -> p c t", p=128))], "xin")

        def norm(gidx, out_hT=True):
            TB = min(128, NT)
            NTB = NT // TB
            P.act(sq, x, AF.Square)
            b = bank()
            for tb in range(NTB):
                for c in range(8):
                    P.mm(psb[b][0:TB, tb:tb + 1], sq[:, c, tb * TB:(tb + 1) * TB], ones[:, 0:1],
                         start=(c == 0), stop=(c == 7), sig=(c == 7 and tb == NTB - 1))
            P.ts("dve", ms[0:TB, 0:NTB], psb[b][0:TB, 0:NTB], 1.0 / D, ALU.mult, EPS, ALU.add)
            P.tt("pool", rstd[0:TB, 0:NTB], ms[0:TB, 0:NTB], neghalf[0:TB, 0:NTB], ALU.pow)
            b2 = bank()
            for tb in range(NTB):
                P.ts("dve", rbc[0:TB, tb, :], onesf[0:TB, :], rstd[0:TB, tb:tb + 1], ALU.mult)
                P.mm(psb[b2][:, tb * TB:(tb + 1) * TB], rbc[0:TB, tb, :], identf[0:TB, 0:TB],
                     start=True, stop=True, sig=(tb == NTB - 1))
            for c in range(8):
                dst = hT[:, c, :] if out_hT else ybuf[:, c, :]
                P.stt(dst, x[:, c, :], gains[:, gidx, c:c + 1], psb[b2][:, 0:NT], ALU.mult, ALU.mult)

        def resid_proj(wname, nkc, src, kslabs):
            for cg in range(4):
                banks = [bank(), bank()]
                k0 = 0
                while k0 < nkc:
                    nk = min(kslabs, nkc - k0)
                    sl = slab(wname, k0, nk, [(cg * 256, 256)])
                    for j in range(2):
                        for kk in range(nk):
                            kc = k0 + kk
                            P.mm(psb[banks[j]][:, 0:NT], sl[:, kk, j * 128:(j + 1) * 128], src[:, kc, :],
                                 start=(kc == 0), stop=(kc == nkc - 1),
                                 sig=(kc == nkc - 1) or (kk == nk - 1 and j == 1))
                    k0 += nk
                for j in range(2):
                    blk = cg * 2 + j
                    P.tt("dve", x[:, blk, :], x[:, blk, :], psb[banks[j]][:, 0:NT], ALU.add)

        def retention():
            cs = view("mix", F32, [2, NT])
            cqs = [view("mix", F32, [2, NT], boff=4096 + i * 4096) for i in range(2)]
            rt = view("mix", F32, [4, NT], boff=12288)
            P.dma([(cs[:, 0, :], d_cos[:, pos0:pos0 + NT]), (cs[:, 1, :], d_sin[:, pos0:pos0 + NT])], "cs")

            def rotary(pa, pb, ct, st_, o1, o2):
                P.tt("dve", rt[:, 0, :], pa, ct, ALU.mult)
                P.tt("dve", rt[:, 1, :], pb, st_, ALU.mult)
                P.tt("dve", o1, rt[:, 0, :], rt[:, 1, :], ALU.subtract)
                P.tt("dve", rt[:, 2, :], pa, st_, ALU.mult)
                P.tt("dve", rt[:, 3, :], pb, ct, ALU.mult)
                P.tt("dve", o2, rt[:, 2, :], rt[:, 3, :], ALU.add)

            sg_tok = view("sgT", BF16, [NCH, 2048])

            def mk_groups(h):
                st = {}
                gl = []
                for kind in (0, 1):
                    for ch in range(NCH):
                        def grp(kind=kind, ch=ch):
                            if ch == 0:
                                st[kind] = slab("ret_w_in", 0, 8, [(2048 + kind * 2048 + h * 512, 512)])
                            sl_ = st[kind]
                            b_ = bank()
                            for kc in range(8):
                                P.mm(psb[b_][0:CL, :], hT[:, kc, tsl(ch)], sl_[:, kc, :], start=(kc == 0), stop=(kc == 7))
                            if kind == 0:
                                P.copy("act", vtok[0:CL, ch, h * 512:(h + 1) * 512], psb[b_][0:CL, :])
                            else:
                                P.act(sg_tok[0:CL, ch, h * 512:(h + 1) * 512], psb[b_][0:CL, :], AF.Silu)
                        gl.append(grp)
                return gl

            Q = []
            for h in range(RET_H):
                Q += mk_groups(h)

            def emit_groups(n):
                for _ in range(n):
                    if Q:
                        Q.pop(0)()

            for which in range(2):
                dstT = qT if which == 0 else kT
                for sh in range(2):
                    sl = slab("ret_w_in", 0, 8, [(which * 1024 + sh * 512, 512)])
                    for hh in range(2):
                        h = sh * 2 + hh
                        ba, bb = bank(), bank()
                        for half, bk in ((0, ba), (1, bb)):
                            for kc in range(8):
                                P.mm(psb[bk][:, 0:NT], sl[:, kc, (hh * 2 + half) * 128:(hh * 2 + half + 1) * 128],
                                     hT[:, kc, :], start=(kc == 0), stop=(kc == 7))
                        if which == 0:
                            cq = cqs[h % 2]
                            qd = qdec[:, h, 0:CL].unsqueeze(1).to_broadcast([128, NCH, CL])
                            for t_ in range(2):
                                P.tt("pool", cq[:, t_, :].rearrange("p (a b) -> p a b", a=NCH),
                                     cs[:, t_, :].rearrange("p (a b) -> p a b", a=NCH), qd, ALU.mult)
                            rotary(psb[ba][:, 0:NT], psb[bb][:, 0:NT], cq[:, 0, :], cq[:, 1, :],
                                   dstT[:, 2 * h, :], dstT[:, 2 * h + 1, :])
                        else:
                            rotary(psb[ba][:, 0:NT], psb[bb][:, 0:NT], cs[:, 0, :], cs[:, 1, :],
                                   dstT[:, 2 * h, :], dstT[:, 2 * h + 1, :])
                        emit_groups(1)
            for h in range(RET_H):
                for ch in range(NCH):
                    b = bank()
                    for dc in range(2):
                        P.tr(psb_bf[b][0:CL, dc * 128:(dc + 1) * 128], kT[:, 2 * h + dc, tsl(ch)], ident,
                             sig=(dc == 1))
                    P.act(ktok[0:CL, ch, h * 256:(h + 1) * 256], psb_bf[b][0:CL, 0:256], AF.Copy,
                          scale=kdec[0:CL, li, h:h + 1])
            for h in range(RET_H):
                for ch in range(NCH):
                    bs = bank()
                    for dc in range(2):
                        P.mm(psb[bs][0:CL, 0:CL], kT[:, 2 * h + dc, tsl(ch)], qT[:, 2 * h + dc, tsl(ch)],
                             start=(dc == 0), stop=(dc == 1))
                    PT = PTv[cnt["pt"] % 4]; cnt["pt"] += 1
                    P.tt("dve", PT[0:CL, 0:CL], psb[bs][0:CL, 0:CL], rmask[0:CL, h, 0:CL], ALU.mult)
                    emit_groups(1)
                    bo = bank()
                    P.mm(psb[bo][0:CL, :], PT[0:CL, 0:CL], vtok[0:CL, ch, h * 512:(h + 1) * 512], start=True, stop=False)
                    for dc in range(2):
                        P.mm(psb[bo][0:CL, :], qT[:, 2 * h + dc, tsl(ch)], Sretb[:, h, dc, :], start=False, stop=(dc == 1))
                    bS = [bank(), bank()]
                    for dc in range(2):
                        P.mm(psb[bS[dc]][:, :], ktok[0:CL, ch, h * 256 + dc * 128:h * 256 + (dc + 1) * 128],
                             vtok[0:CL, ch, h * 512:(h + 1) * 512], start=True, stop=True)
                    stats6, mv, ve, rs = statv[cnt["sv"] % 4]; cnt["sv"] += 1
                    P.op("dve", lambda e, o=stats6[0:CL, :], i=psb[bo][0:CL, :]: e.bn_stats(o, i),
                         reads=[psb[bo][0:CL, :]], writes=[stats6[0:CL, :]])
                    P.op("dve", lambda e, o=mv[0:CL, :], i=stats6[0:CL, :]: e.bn_aggr(o, i),
                         reads=[stats6[0:CL, :]], writes=[mv[0:CL, :]])
                    P.ts("dve", ve[0:CL, :], mv[0:CL, 1:2], EPS, ALU.add)
                    P.tt("pool", rs[0:CL, :], ve[0:CL, :], neghalf[0:CL, 0:1], ALU.pow)
                    for dc in range(2):
                        P.stt(Sret[:, h, dc, :], Sret[:, h, dc, :], gL[CL][h], psb[bS[dc]][:, :], ALU.mult, ALU.add)
                        P.copy("act", Sretb[:, h, dc, :], Sret[:, h, dc, :])
                    emit_groups(1)
                    on = onv[cnt["on"] % 3]; cnt["on"] += 1
                    P.ts("dve", on[0:CL, :], psb[bo][0:CL, :], mv[0:CL, 0:1], ALU.subtract, rs[0:CL, :], ALU.mult)
                    P.tt("dve", on[0:CL, :], on[0:CL, :], sg_tok[0:CL, ch, h * 512:(h + 1) * 512], ALU.mult)
                    bt = bank()
                    for eb in range(4):
                        P.tr(psb_bf[bt][:, eb * CL:(eb + 1) * CL], on[0:CL, eb * 128:(eb + 1) * 128],
                             ident[0:CL, 0:CL], sig=(eb == 3))
                    P.copy("act", ogT[:, 4 * h:4 * h + 4, tsl(ch)],
                           psb_bf[bt][:, 0:4 * CL].rearrange("p (a b) -> p a b", a=4))
            emit_groups(len(Q))
            resid_proj("ret_w_out", 16, ogT, 16)

        def ffn(layer):
            actT = view("qkk", BF16, [22, NT])
            ubuf = [[view("sgT", F32, [NT + 2], boff=(r * 2 + gv) * 2304) for gv in range(2)] for r in range(2)]
            cbuf = [[view("sgT", F32, [NT], boff=9216 + gv * 2048), view("on", F32, [NT], boff=gv * 2048)][r]
                    for r in range(2) for gv in range(2)]
            cbuf = [[cbuf[r * 2 + gv] for gv in range(2)] for r in range(2)]
            sgt = [view("sgT", BF16, [NT], boff=13312 + r * 1024) for r in range(2)]
            wn = "ffn_w_up%d" % layer
            it = 0
            pend = None

            def gate_stage(r, gb):
                P.act(sgt[r], cbuf[r][0], AF.Silu)
                P.tt("dve", actT[:, gb, :], cbuf[r][1], sgt[r], ALU.mult)

            for s in range(11):
                sl = slab(wn, 0, 8, [(s * 256, 256), (DFF + s * 256, 256)])
                for jj in range(2):
                    gb = s * 2 + jj
                    r = it % 2; it += 1
                    for gv in range(2):
                        blk = gb + gv * 22
                        b = bank()
                        for kc in range(8):
                            P.mm(psb[b][:, 0:NT], sl[:, kc, gv * 256 + jj * 128:gv * 256 + (jj + 1) * 128], hT[:, kc, :],
                                 start=(kc == 0), stop=(kc == 7))
                        ub = ubuf[r][gv]
                        c = cbuf[r][gv]
                        P.copy("pool", ub[:, 0:2], uhf[:, layer, blk, :])
                        P.copy("act", ub[:, 2:2 + NT], psb[b][:, 0:NT])
                        P.copy("pool", uhf[:, layer, blk, :], ub[:, NT:NT + 2])
                        P.act(c, ub[:, 0:NT], AF.Identity, bias=cb[:, layer, blk:blk + 1], scale=cw[:, layer, 0, blk:blk + 1])
                        P.stt(c, ub[:, 1:NT + 1], cw[:, layer, 1, blk:blk + 1], c, ALU.mult, ALU.add)
                        P.stt(c, ub[:, 2:NT + 2], cw[:, layer, 2, blk:blk + 1], c, ALU.mult, ALU.add)
                    if pend is not None:
                        gate_stage(*pend)
                    pend = (r, gb)
            gate_stage(*pend)
            resid_proj("ffn_w_down%d" % layer, 22, actT, 11)

        def gla():
            vt = view("vtok", BF16, [NCH, 1024])
            sr_tok = view("sgT", BF16, [NCH, 1024])
            kbar = view("qkk", BF16, [NCH, 512], boff=16384)
            sp = view("mix", F32, [NCH, 512])
            Eq = view("mix", F32, [GLA_H, NT], boff=8192)
            Ek = [view("mix", F32, [NT], boff=16384 + i * 2048) for i in range(2)]
            zt = view("on", F32, [512])
            kbt = view("stmp", BF16, [NT])
            junk = view("on", BF16, [256], boff=2048)

            def mk_groups(kind, s2):
                st = {}
                gl = []
                for ch in range(NCH):
                    def grp(ch=ch):
                        if ch == 0:
                            st["s"] = slab("gla_w_in", 0, 8, [(1024 + kind * 1024 + s2 * 512, 512)])
                        b_ = bank()
                        for kc in range(8):
                            P.mm(psb[b_][0:CL, :], hT[:, kc, tsl(ch)], st["s"][:, kc, :], start=(kc == 0), stop=(kc == 7))
                        if kind == 0:
                            P.copy("act", vt[0:CL, ch, s2 * 512:(s2 + 1) * 512], psb[b_][0:CL, :])
                        else:
                            P.act(sr_tok[0:CL, ch, s2 * 512:(s2 + 1) * 512], psb[b_][0:CL, :], AF.Silu)
                    gl.append(grp)
                return gl

            Q = mk_groups(0, 0) + mk_groups(1, 0) + mk_groups(0, 1) + mk_groups(1, 1)

            def emit_groups(n):
                for _ in range(n):
                    if Q:
                        Q.pop(0)()

            sl = slab("gla_w_in", 0, 8, [(3072, 16)])
            b = bank()
            for kc in range(8):
                P.mm(psb[b][0:16, 0:NT], sl[:, kc, 0:16], hT[:, kc, :], start=(kc == 0), stop=(kc == 7))
            P.copy("act", aT_all[0:16, 0:NT], psb[b][0:16, 0:NT])
            for ch in range(NCH):
                b = bank()
                P.mm(psb[b][0:CL, :], aT_all[0:17, tsl(ch)], wa2[0:17, :], start=True, stop=True)
                P.act(zt[0:CL, :], psb[b][0:CL, :], AF.Exp, scale=-1.0)
                P.act(sp[0:CL, ch, :], zt[0:CL, :], AF.Ln, bias=1.0)
            for h in range(GLA_H):
                b = bank()
                for ch in range(NCH):
                    P.mm(psb[b][:, tsl(ch)], sp[0:CL, ch, h * 128:(h + 1) * 128], ucum[0:CL, 0:CL], start=True, stop=True,
                         sig=(ch == NCH - 1))
                P.act(Eq[:, h, :], psb[b][:, 0:NT], AF.Exp)
            slq = slab("gla_w_in", 0, 8, [(0, 512)])
            for h in range(GLA_H):
                bq = bank()
                for kc in range(8):
                    P.mm(psb[bq][:, 0:NT], slq[:, kc, h * 128:(h + 1) * 128], hT[:, kc, :], start=(kc == 0), stop=(kc == 7))
                P.tt("dve", qT[:, h, :], psb[bq][:, 0:NT], Eq[:, h, :], ALU.mult)
                if NCH >= 4:
                    emit_groups(2)
            slk = slab("gla_w_in", 0, 8, [(512, 512)])
            for h in range(GLA_H):
                P.op("dve", lambda e, o=Ek[h % 2], i=Eq[:, h, :]: e.reciprocal(o, i), reads=[Eq[:, h, :]], writes=[Ek[h % 2]])
                bk = bank()
                for kc in range(8):
                    P.mm(psb[bk][:, 0:NT], slk[:, kc, h * 128:(h + 1) * 128], hT[:, kc, :], start=(kc == 0), stop=(kc == 7))
                P.tt("dve", kT[:, h, :], psb[bk][:, 0:NT], Ek[h % 2], ALU.mult)
                for ch in range(NCH):
                    P.ts("dve", kbt[:, tsl(ch)], kT[:, h, tsl(ch)], Eq[:, h, (ch + 1) * CL - 1:(ch + 1) * CL], ALU.mult,
                         GLA_DK ** -0.5, ALU.mult)
                    bt = bank()
                    P.tr(psb_bf[bt][0:CL, 0:128], kbt[:, tsl(ch)], ident)
                    P.copy("act", kbar[0:CL, ch, h * 128:(h + 1) * 128], psb_bf[bt][0:CL, 0:128])
                if NCH >= 4:
                    emit_groups(2)
            emit_groups(len(Q))
            junk = view("on", BF16, [256])
            onb = view("on", BF16, [1024], boff=2048)
            PT4 = view("PT", BF16, [4, 128])
            for ch in range(NCH):
                ba = bank()
                for h in range(GLA_H):
                    P.mm(psb[ba][0:CL, h * CL:(h + 1) * CL], kT[:, h, tsl(ch)], qT[:, h, tsl(ch)], start=True, stop=True,
                         sig=(h == GLA_H - 1))
                P.tt("dve", PT4[0:CL, :, 0:CL], psb[ba][0:CL, 0:4 * CL].rearrange("p (a b) -> p a b", a=4),
                     gmask[0:CL, 0:CL].unsqueeze(1).to_broadcast([CL, 4, CL]), ALU.mult)
                bo = [bank(), bank()]
                obs = []
                for h in range(GLA_H):
                    ob = psb[bo[h // 2]][0:CL, (h % 2) * 256:(h % 2 + 1) * 256]
                    obs.append(ob)
                    P.mm(ob, PT4[0:CL, h, 0:CL], vt[0:CL, ch, h * 256:(h + 1) * 256], start=True, stop=False)
                    P.mm(ob, qT[:, h, tsl(ch)], Sglab[:, h, :], start=False, stop=True)
                bS = [bank(), bank()]
                sbs = []
                for h in range(GLA_H):
                    sb_ = psb[bS[h // 2]][:, (h % 2) * 256:(h % 2 + 1) * 256]
                    sbs.append(sb_)
                    P.mm(sb_, kbar[0:CL, ch, h * 128:(h + 1) * 128], vt[0:CL, ch, h * 256:(h + 1) * 256],
                         start=True, stop=True)
                k_ = cnt["sv"] % 4; cnt["sv"] += 1
                ss4 = view("stats", F32, [4], boff=k_ * 256)
                ms4 = view("stats", F32, [4], boff=k_ * 256 + 64)
                rs4 = view("stats", F32, [4], boff=k_ * 256 + 128)
                for h in range(GLA_H):
                    P.act(junk[0:CL, :], obs[h], AF.Square, accum=ss4[0:CL, h:h + 1])
                P.ts("dve", ms4[0:CL, :], ss4[0:CL, :], 1.0 / GLA_DV, ALU.mult, EPS, ALU.add)
                P.tt("pool", rs4[0:CL, :], ms4[0:CL, :], neghalf[0:CL, 0:4], ALU.pow)
                for h in range(GLA_H):
                    P.stt(Sgla[:, h, :], Sgla[:, h, :], Eq[:, h, (ch + 1) * CL - 1:(ch + 1) * CL], sbs[h],
                          ALU.mult, ALU.add)
                    P.copy("act", Sglab[:, h, :], Sgla[:, h, :])
                for h in range(GLA_H):
                    P.ts("dve", onb[0:CL, h * 256:(h + 1) * 256], obs[h], rs4[0:CL, h:h + 1], ALU.mult)
                P.tt("dve", onb[0:CL, :], onb[0:CL, :], sr_tok[0:CL, ch, :], ALU.mult)
                bt = bank()
                for i8 in range(8):
                    P.tr(psb_bf[bt][:, i8 * CL:(i8 + 1) * CL], onb[0:CL, i8 * 128:(i8 + 1) * 128], ident[0:CL, 0:CL],
                         sig=(i8 == 7))
                P.copy("act", ogT[:, 0:8, tsl(ch)], psb_bf[bt][:, 0:8 * CL].rearrange("p (a b) -> p a b", a=8))
            resid_proj("gla_w_out", 8, ogT, 8)

        norm(0); retention()
        norm(2); ffn(0)
        norm(1); gla()
        norm(3); ffn(1)
        norm(4, out_hT=False)
        P.dma([(ydst[:, tq:tq + NT].rearrange("(c p) t -> p c t", p=128), ybuf)], "yout")

    def seq_end(sk):
        P.dma([(o_ret[sk].rearrange("h (dc p) e -> p h dc e", p=128), Sret)], "so_ret")
        P.dma([(o_gla[sk].rearrange("h p e -> p h e"), Sgla)], "so_gla")
        P.dma([(o_conv[sk], uhf)], "so_conv")

    P.dma([(Sret, st_ret.rearrange("h (dc p) e -> p h dc e", p=128)), (Sgla, st_gla.rearrange("h p e -> p h e")),
           (uhf, cconv)], "stin")
    P.copy("act", Sretb, Sret); P.copy("act", Sglab, Sgla); P.copy("dve", uhb, uhf)
    run_tile("s", 0, DEC_SEQ, DEC_SEQ, True)
    seq_end("s")
    P.memset("pool", Sret, 0.0); P.memset("pool", Sretb, 0.0); P.memset("pool", Sgla, 0.0); P.memset("pool", Sglab, 0.0)
    P.memset("pool", uhb, 0.0); P.memset("pool", uhf, 0.0)
    ntile = seq // 512
    for ti in range(ntile):
        run_tile("p", ti * 512, 512, 128, ti == ntile - 1)
    seq_end("p")

    P.wait_all("sp", ("yout", "so_ret", "so_gla", "so_conv"))

    with contextlib.ExitStack() as es:
        for lname, L in P.lanes.items():
            L["sem"] = es.enter_context(nc.semaphore("s_" + lname))
        block = es.enter_context(nc.Block())
        P.emit(block)
    return nc, cst, P


_CACHE = {}


def _prep_inputs(inp, seq, cst):
    f32 = lambda a: np.ascontiguousarray(np.asarray(a, np.float32))
    shared = {
        "ret_w_in": f32(inp["ret_w_in"][0]), "ret_w_out": f32(inp["ret_w_out"][0]),
        "gla_w_in": f32(inp["gla_w_in"][0]), "gla_w_out": f32(inp["gla_w_out"][0]),
        "ffn_w_up0": f32(inp["ffn_w_up"][0]), "ffn_w_up1": f32(inp["ffn_w_up"][1]),
        "ffn_w_down0": f32(inp["ffn_w_down"][0]), "ffn_w_down1": f32(inp["ffn_w_down"][1]),
    }
    nm, nf = np.asarray(inp["norm_mix"], np.float32), np.asarray(inp["norm_ffn"], np.float32)
    gains = np.stack([_fm(nm[0]), _fm(nm[1]), _fm(nf[0]), _fm(nf[1]), _fm(inp["norm_final"])], axis=1)
    shared["gains"] = np.ascontiguousarray(gains)
    shared["gn"] = _fm(np.asarray(inp["ret_gn_g"], np.float32)[0].reshape(-1))
    shared["ng"] = _fm(np.asarray(inp["gla_norm_g"], np.float32)[0].reshape(-1))
    cwv = np.asarray(inp["ffn_conv_w"], np.float32)
    shared["cw"] = np.ascontiguousarray(cwv.reshape(DEPTH, 3, NBLK_FF, 128).transpose(3, 0, 1, 2))
    cbv = np.asarray(inp["ffn_conv_b"], np.float32)
    shared["cb"] = np.ascontiguousarray(cbv.reshape(DEPTH, NBLK_FF, 128).transpose(2, 0, 1))
    shared["wa2aug"] = np.ascontiguousarray(np.concatenate(
        [np.asarray(inp["gla_w_a2"], np.float32)[0], np.asarray(inp["gla_b_a"], np.float32)[0][None, :]], axis=0))
    for k in ("ident_bf", "ones_bf", "ident_f", "ones_f", "rmask", "qdec", "kdec", "gmask", "ucum", "neghalf", "epsv", "cos", "sin"):
        shared[k] = cst[k]
    xp = np.asarray(inp["x_prompt"], np.float32)
    xs = np.asarray(inp["x_sample"], np.float32)
    cc = np.asarray(inp["cache_conv"], np.float32)
    maps = []
    for b in range(NCORES):
        m = dict(shared)
        hs = seq // (-(-seq // 4096))
        for i in range(seq // hs):
            m["xTp%d" % i] = np.ascontiguousarray(xp[b, i * hs:(i + 1) * hs].T)
        m["xTs"] = np.ascontiguousarray(xs[b].T)
        m["st_ret"] = f32(inp["state_ret"][0, b])
        m["st_gla"] = f32(inp["state_gla"][0, b])
        m["cconv"] = np.ascontiguousarray(cc[:, b].reshape(DEPTH, 2, NBLK_FF, 128).transpose(3, 0, 2, 1))
        maps.append(m)
    return maps


def _run(inp, seq):
    if seq not in _CACHE:
        _CACHE[seq] = build(seq)
    nc, cst, _ = _CACHE[seq]
    maps = _prep_inputs(inp, seq, cst)
    res = run_bass_kernel_spmd(nc, maps, core_ids=list(range(NCORES)))
    R = res.results
    B = NCORES
    hs = seq // (-(-seq // 4096))
    y_p = np.stack([np.concatenate([R[b]["yTp%d" % i].T for i in range(seq // hs)], axis=0)
                    for b in range(B)]).astype(np.float32)
    y_s = np.stack([R[b]["yTs"].T for b in range(B)]).astype(np.float32)
    ret_p = np.stack([R[b]["ret_p"] for b in range(B)])[None].astype(np.float32)
    ret_s = np.stack([R[b]["ret_s"] for b in range(B)])[None].astype(np.float32)
    gla_p = np.stack([R[b]["gla_p"] for b in range(B)])[None].astype(np.float32)
    gla_s = np.stack([R[b]["gla_s"] for b in range(B)])[None].astype(np.float32)

    def conv(k):
        a = np.stack([R[b][k] for b in range(B)])
        return np.ascontiguousarray(a.transpose(2, 0, 4, 3, 1).reshape(DEPTH, B, 2, 2 * DFF)).astype(np.float32)

    return (y_p, y_s, ret_p, ret_s, gla_p, gla_s, conv("conv_p"), conv("conv_s"))


def kernel(**inputs):
    seq = int(np.asarray(inputs["x_prompt"]).shape[1])
    return _run(inputs, seq)
```

```python
import contextlib
import math
import numpy as np
import ml_dtypes
import concourse.bass as bass
import concourse.mybir as mybir
from concourse.bass_utils import run_bass_kernel_spmd

F32 = mybir.dt.float32
BF16 = mybir.dt.bfloat16
U8 = mybir.dt.uint8
AF = mybir.ActivationFunctionType
ALU = mybir.AluOpType
DSZ = {F32: 4, BF16: 2, U8: 1}

D = 1024
DEPTH = 2
RET_H = 4
RET_DK = 256
RET_DV = 512
GLA_H = 4
GLA_DK = 128
GLA_DV = 256
GLA_RANK = 16
GLA_TAU = 16.0
DFF = 2816
NBLK_FF = 2 * DFF // 128
EPS = 1e-6
ROPE_BASE = 10000.0
PAST_LEN = 2048
DEC_SEQ = 32
NCORES = 8

ARENA = 211968


class Prog:
    SBG = 256

    def __init__(self, nc):
        self.nc = nc
        self.q = {e: [] for e in ("pe", "act", "dve", "pool", "sp")}
        self.lanes = {}
        for e in ("pe", "act", "dve", "pool"):
            self.lanes[e] = {"count": 0, "inc": 1, "sem": None}
        self.clock = {e: {} for e in self.q}
        self.snap = {}
        self.gran = {}
        self.maxwait = {}
        self.nops = 0

    def lane(self, name):
        if name not in self.lanes:
            self.lanes[name] = {"count": 0, "inc": 16, "sem": None}
        return name

    def keys(self, ap):
        t = ap.tensor
        name = t.name
        if name not in ("arena", "psum"):
            return [("d", name)]
        esz = DSZ[ap.dtype]
        pairs = list(ap.ap)
        pstep = pairs[0][0]
        off = int(ap.offset)
        inpart = off % pstep if pstep else off
        starts = [inpart]
        free = [(s, c) for (s, c) in pairs[1:] if c > 1 or len(pairs) == 2]
        free = [(s, c) for (s, c) in free if s != 0]
        length = 1
        i = 0
        while i < len(free):
            s, c = free[i]
            inner_ext = sum((cc - 1) * abs(ss) for ss, cc in free[i + 1:]) + 1
            if i == len(free) - 1:
                length = (c - 1) * abs(s) + 1
            elif abs(s) > inner_ext and len(starts) * c <= 128:
                starts = [st + k * s for st in starts for k in range(c)]
            else:
                length = sum((cc - 1) * abs(ss) for ss, cc in free[i:]) + 1
                break
            i += 1
        g = self.SBG if name == "arena" else 2048
        ks = set()
        for st in starts:
            lo = (st * esz) // g
            hi = ((st + length) * esz - 1) // g
            for k in range(lo, hi + 1):
                ks.add((name, k))
        return ks

    def op(self, eng, fn, reads=(), writes=(), sig=True, lane=None, n=1, embed=True):
        self.nops += 1
        mylane = lane if lane is not None else eng
        L = self.lanes[mylane]
        raw = {}
        oth = {}

        def add(d, ls):
            l, s = ls
            if d.get(l, 0) < s:
                d[l] = s

        rkeys = set()
        for ap in reads:
            rkeys |= set(self.keys(ap))
        wkeys = set()
        for ap in writes:
            wkeys |= set(self.keys(ap))
        for k in rkeys:
            g = self.gran.get(k)
            if g is not None and g[0] is not None:
                add(raw, g[0])
        for k in wkeys:
            g = self.gran.get(k)
            if g is not None:
                if g[0] is not None:
                    add(oth, g[0])
                for l, s in g[1].items():
                    add(oth, (l, s))
        deps = {}
        for l, s in raw.items():
            if l == eng and lane is None:
                if eng == "pe":
                    continue
            add(deps, (l, s))
        for l, s in oth.items():
            if l == eng and lane is None and eng == "pe":
                continue
            add(deps, (l, s))
        if lane is not None and L["count"] > 0:
            add(deps, (mylane, L["count"]))
        if sig:
            L["count"] += n
            seq = L["count"]
        else:
            seq = L["count"] + 1
        ck = self.clock[eng]
        waits = []
        for l, s in sorted(deps.items()):
            if ck.get(l, 0) < s:
                waits.append((l, s * self.lanes[l]["inc"]))
                if self.maxwait.get(l, 0) < s:
                    self.maxwait[l] = s
                sn = self.snap.get((l, s))
                if sn:
                    for l2, s2 in sn.items():
                        if ck.get(l2, 0) < s2:
                            ck[l2] = s2
                ck[l] = s
        if sig:
            sn = dict(ck)
            sn[mylane] = seq
            self.snap[(mylane, seq)] = sn
        for k in rkeys:
            g = self.gran.get(k)
            if g is None:
                g = [None, {}]
                self.gran[k] = g
            if g[1].get(mylane, 0) < seq:
                g[1][mylane] = seq
        for k in wkeys:
            self.gran[k] = [(mylane, seq), {}]
        self.q[eng].append((waits, fn, (mylane, L["inc"]) if sig else None, embed))

    def wait_all(self, eng, lanes):
        waits = [(l, self.lanes[l]["count"] * self.lanes[l]["inc"]) for l in lanes if self.lanes[l]["count"] > 0]
        self.q[eng].append((waits, None, None, False))

    def emit(self, block):
        for l, s in self.maxwait.items():
            assert s <= self.lanes[l]["count"], (l, s, self.lanes[l]["count"])

        def runner(name):
            items = self.q[name]
            lanes = self.lanes

            def body(e):
                for waits, fn, sig, embed in items:
                    emb = None
                    if embed and fn is not None and waits:
                        emb = waits[-1]
                        waits = waits[:-1]
                    for l, v in waits:
                        e.wait_ge(lanes[l]["sem"], v)
                    if fn is None:
                        continue
                    r = fn(e)
                    if emb is not None:
                        first = r[0] if isinstance(r, (list, tuple)) else r
                        first._wait_ge(lanes[emb[0]]["sem"], emb[1])
                    if sig is not None:
                        sem = lanes[sig[0]]["sem"]
                        if isinstance(r, (list, tuple)):
                            for ins in r:
                                ins.then_inc(sem, sig[1])
                        else:
                            r.then_inc(sem, sig[1])

            return body

        block.tensor(runner("pe"))
        block.scalar(runner("act"))
        block.vector(runner("dve"))
        block.gpsimd(runner("pool"))
        block.sync(runner("sp"))

    def mm(self, out, lhsT, rhs, start, stop, sig=None):
        self.op("pe", lambda e: e.matmul(out, lhsT, rhs, start=start, stop=stop),
                reads=[lhsT, rhs], writes=[out], sig=(stop if sig is None else sig))

    def tr(self, out, in_, ident, sig=True):
        self.op("pe", lambda e: e.transpose(out, in_, ident), reads=[in_, ident], writes=[out], sig=sig)

    def act(self, out, in_, func, bias=None, scale=None, accum=None):
        reads = [in_]
        kw = {}
        if bias is not None:
            kw["bias"] = bias
            if not isinstance(bias, (int, float)):
                reads.append(bias)
        if scale is not None:
            kw["scale"] = scale
            if not isinstance(scale, (int, float)):
                reads.append(scale)
        writes = [out]
        if accum is not None:
            kw["accum_out"] = accum
            writes.append(accum)
        self.op("act", lambda e: e.activation(out, in_, func, **kw), reads=reads, writes=writes,
                embed=(accum is None))

    def tt(self, eng, out, a, b, op):
        self.op(eng, lambda e: e.tensor_tensor(out, a, b, op), reads=[a, b], writes=[out])

    def ts(self, eng, out, a, s1, op0, s2=None, op1=None):
        reads = [a]
        if not isinstance(s1, (int, float)):
            reads.append(s1)
        if s2 is not None and not isinstance(s2, (int, float)):
            reads.append(s2)
        if op1 is None:
            self.op(eng, lambda e: e.tensor_scalar(out, a, s1, None, op0), reads=reads, writes=[out])
        else:
            self.op(eng, lambda e: e.tensor_scalar(out, a, s1, s2, op0, op1), reads=reads, writes=[out])

    def stt(self, out, in0, scalar, in1, op0, op1):
        reads = [in0, in1]
        if not isinstance(scalar, (int, float)):
            reads.append(scalar)
        self.op("dve", lambda e: e.scalar_tensor_tensor(out, in0, scalar, in1, op0, op1),
                reads=reads, writes=[out])

    def copy(self, eng, out, in_):
        if eng == "act":
            self.op("act", lambda e: e.activation(out, in_, AF.Copy), reads=[in_], writes=[out])
        else:
            self.op(eng, lambda e: e.tensor_copy(out, in_), reads=[in_], writes=[out])

    def memset(self, eng, out, val):
        self.op(eng, lambda e: e.memset(out, val), reads=[], writes=[out])

    def dma(self, pairs, lane, eng="sp", **kw):
        self.lane(lane)
        outs = [p[0] for p in pairs]
        ins = [p[1] for p in pairs]

        def fn(e):
            return [e.dma_start(out=o, in_=i, **kw) for o, i in pairs]

        self.op(eng, fn, reads=ins, writes=outs, lane=lane, n=len(pairs))


def _consts(seq):
    c = {}
    c["ident_bf"] = np.eye(128, dtype=np.float32).astype(ml_dtypes.bfloat16)
    c["ones_bf"] = np.ones((128, 128), dtype=np.float32).astype(ml_dtypes.bfloat16)
    c["ident_f"] = np.eye(128, dtype=np.float32)
    c["ones_f"] = np.ones((128, 128), dtype=np.float32)
    lg = np.log1p(-np.exp2(-5.0 - np.arange(RET_H, dtype=np.float64)))
    j = np.arange(128, dtype=np.float64)
    causalT = (j[None, :] >= j[:, None]).astype(np.float64)
    rmask = np.zeros((128, RET_H, 128), np.float64)
    for h in range(RET_H):
        rmask[:, h, :] = np.exp(-lg[h] * (j[:, None] + 1.0)) * (RET_DK ** -0.5) * causalT
    c["rmask"] = rmask.astype(np.float32)
    qd = np.exp(lg[:, None] * (j[None, :] + 1.0))
    c["qdec"] = np.broadcast_to(qd[None], (128, RET_H, 128)).astype(np.float32).copy()
    kd = np.zeros((128, 2, RET_H), np.float64)
    for li, L in enumerate((128, 32)):
        for h in range(RET_H):
            kd[:, li, h] = np.exp(lg[h] * (L - 1.0 - j)) * (RET_DK ** -0.5)
    c["kdec"] = kd.astype(np.float32)
    c["gL"] = {L: [float(np.exp(lg[h] * L)) for h in range(RET_H)] for L in (128, 32)}
    c["gmask"] = (causalT * (GLA_DK ** -0.5)).astype(np.float32)
    c["ucum"] = ((j[:, None] <= j[None, :]) * (-1.0 / GLA_TAU)).astype(np.float32)
    c["neghalf"] = np.full((128, 512), -0.5, np.float32)
    c["epsv"] = np.full((128, 8), EPS, np.float32)
    inv = (np.float32(ROPE_BASE) ** (-(np.arange(128, dtype=np.float32) / np.float32(128)))).astype(np.float32)
    pos = np.concatenate([np.arange(seq, dtype=np.float32), PAST_LEN + np.arange(DEC_SEQ, dtype=np.float32)])
    ang = (pos[None, :] * inv[:, None]).astype(np.float32)
    c["cos"] = np.cos(ang).astype(np.float32)
    c["sin"] = np.sin(ang).astype(np.float32)
    return c


def _fm(v):
    v = np.asarray(v, np.float32)
    return np.ascontiguousarray(v.reshape(-1, 128).T)


WEIGHTS = [
    ("ret_w_in", D, 6144), ("ret_w_out", 2048, D), ("gla_w_in", D, 3088), ("gla_w_out", D, D),
    ("ffn_w_up0", D, 2 * DFF), ("ffn_w_up1", D, 2 * DFF), ("ffn_w_down0", DFF, D), ("ffn_w_down1", DFF, D),
]


def build(seq, dbg=()):
    assert seq % 512 == 0
    nc = bass.Bass("TRN2", target_bir_lowering=False)
    P = Prog(nc)
    cst = _consts(seq)
    gL = cst["gL"]
    npos = seq + DEC_SEQ

    def din(name, shape, dt=F32):
        return nc.dram_tensor(name, list(shape), dt, kind="ExternalInput").ap()

    def dout(name, shape, dt=F32):
        return nc.dram_tensor(name, list(shape), dt, kind="ExternalOutput").ap()

    NHALF = -(-seq // 4096)
    HSEQ = seq // NHALF
    assert HSEQ * NHALF == seq and HSEQ % 512 == 0
    xTp = [din("xTp%d" % i, [D, HSEQ]) for i in range(NHALF)]; xTs = din("xTs", [D, DEC_SEQ])
    st_ret = din("st_ret", [RET_H, RET_DK, RET_DV]); st_gla = din("st_gla", [GLA_H, GLA_DK, GLA_DV])
    cconv = din("cconv", [128, DEPTH, NBLK_FF, 2])
    W32 = {n: din(n, [k, m]) for n, k, m in WEIGHTS}
    Wb = {n: nc.dram_tensor(n + "_bf", [k, m], BF16, kind="Internal").ap() for n, k, m in WEIGHTS}
    d_gains = din("gains", [128, 5, 8])
    d_gn = din("gn", [128, 16]); d_ng = din("ng", [128, 8])
    d_cw = din("cw", [128, DEPTH, 3, NBLK_FF]); d_cb = din("cb", [128, DEPTH, NBLK_FF])
    d_wa2 = din("wa2aug", [17, 512])
    d_ident = din("ident_bf", [128, 128], BF16); d_ones = din("ones_bf", [128, 128], BF16)
    d_rmask = din("rmask", [128, RET_H, 128]); d_qdec = din("qdec", [128, RET_H, 128])
    d_kdec = din("kdec", [128, 2, RET_H]); d_gmask = din("gmask", [128, 128]); d_ucum = din("ucum", [128, 128])
    d_neghalf = din("neghalf", [128, 512]); d_epsv = din("epsv", [128, 8])
    d_identf = din("ident_f", [128, 128]); d_onesf = din("ones_f", [128, 128])
    d_cos = din("cos", [128, npos]); d_sin = din("sin", [128, npos])

    yTp = [dout("yTp%d" % i, [D, HSEQ]) for i in range(NHALF)]; yTs = dout("yTs", [D, DEC_SEQ])
    o_ret = {"p": dout("ret_p", [RET_H, RET_DK, RET_DV]), "s": dout("ret_s", [RET_H, RET_DK, RET_DV])}
    o_gla = {"p": dout("gla_p", [GLA_H, GLA_DK, GLA_DV]), "s": dout("gla_s", [GLA_H, GLA_DK, GLA_DV])}
    o_conv = {"p": dout("conv_p", [128, DEPTH, NBLK_FF, 2]), "s": dout("conv_s", [128, DEPTH, NBLK_FF, 2])}
    dbg_out = {}

    arena_h = nc.alloc_sbuf_tensor("arena", [128, ARENA], U8)
    arena = arena_h.ap()
    psum_h = nc.alloc_psum_tensor("psum", [128, 8, 512], F32)
    psum = psum_h.ap()

    off = {"_": 0}
    reg = {}

    def region(name, nbytes):
        assert nbytes % 256 == 0, name
        reg[name] = (off["_"], nbytes)
        off["_"] += nbytes
        assert off["_"] <= ARENA, (name, off["_"])

    def view(name, dt, shape, boff=0, parts=128):
        o, nb = reg[name]
        n = int(np.prod(shape)) * DSZ[dt]
        assert boff + n <= nb, (name, boff, n, nb)
        ap = arena[0:parts, o + boff:o + boff + n].bitcast(dt)
        if len(shape) == 2:
            ap = ap.rearrange("p (a b) -> p a b", a=shape[0])
        elif len(shape) == 3:
            ap = ap.rearrange("p (a b c) -> p a b c", a=shape[0], b=shape[1])
        return ap

    region("x", 16384); region("hT", 8192); region("ms", 2048); region("rstd", 2048)
    region("identf", 512); region("onesf", 512)
    region("qkk", 24576)
    region("vtok", 16384)
    region("sgT", 16384)
    region("ogT", 16384)
    region("mix", 20480)
    region("PT", 1024); region("on", 4096); region("stats", 1024); region("stmp", 2048); region("aT", 1024)
    region("Sret", 16384); region("Sretb", 8192); region("Sgla", 4096); region("Sglab", 2048)
    region("wring", 3 * 8192)
    for nme, nb in [("ident", 256), ("ones", 256), ("rmask", 2048), ("qdec", 2048), ("kdec", 256), ("gmask", 512),
                    ("ucum", 512), ("neghalf", 2048), ("epsv", 256), ("gains", 256), ("gn", 256), ("ng", 256), ("cw", 1280),
                    ("cb", 512), ("wa2", 1024), ("uhb", 512), ("uhf", 768)]:
        region(nme, nb)

    ident = view("ident", BF16, [128]); ones = view("ones", BF16, [128])
    identf = view("identf", F32, [128]); onesf = view("onesf", F32, [128])
    rmask = view("rmask", F32, [RET_H, 128]); qdec = view("qdec", F32, [RET_H, 128])
    kdec = view("kdec", F32, [2, RET_H]); gmask = view("gmask", F32, [128]); ucum = view("ucum", F32, [128])
    neghalf = view("neghalf", F32, [512]); epsv = view("epsv", F32, [8])
    gains = view("gains", F32, [5, 8]); gnT = view("gn", F32, [16]); ngT = view("ng", F32, [8])
    cw = view("cw", F32, [DEPTH, 3, NBLK_FF]); cb = view("cb", F32, [DEPTH, NBLK_FF])
    wa2 = view("wa2", BF16, [512], parts=32)
    uhb = view("uhb", BF16, [DEPTH, NBLK_FF, 2]); uhf = view("uhf", F32, [DEPTH, NBLK_FF, 2])
    Sret = view("Sret", F32, [RET_H, 2, 512]); Sretb = view("Sretb", BF16, [RET_H, 2, 512])
    Sgla = view("Sgla", F32, [GLA_H, 256]); Sglab = view("Sglab", BF16, [GLA_H, 256])

    psb = [psum[:, b, :] for b in range(8)]
    psb_bf = [psum[:, b, :].bitcast(BF16) for b in range(8)]
    bank_ctr = {"i": 0}

    reserved = set()

    def bank():
        while True:
            b = bank_ctr["i"] % 8
            bank_ctr["i"] += 1
            if b not in reserved:
                return b

    P.dma([(ident, d_ident), (ones, d_ones), (rmask, d_rmask), (qdec, d_qdec), (kdec, d_kdec), (gmask, d_gmask),
           (ucum, d_ucum), (neghalf, d_neghalf), (epsv, d_epsv), (identf, d_identf), (onesf, d_onesf), (gains, d_gains), (gnT, d_gn), (ngT, d_ng), (cw, d_cw), (cb, d_cb)],
          "const")
    P.dma([(wa2[0:17, :], d_wa2)], "wa2c", eng="pool")
    for n, k, m in WEIGHTS:
        if n in ("ret_w_out", "gla_w_out"):
            continue
        P.dma([(Wb[n], W32[n])], "cast_" + n, eng="pool", max_dma_last_dim=4096)
    fi = 0
    for n, gv_, nch in (("ret_w_out", gnT, 16), ("gla_w_out", ngT, 8)):
        for ec in range(nch):
            wi = view("mix", F32, [1024], boff=(fi % 2) * 4096)
            wo = view("mix", BF16, [1024], boff=8192 + (fi % 2) * 2048)
            fi += 1
            P.dma([(wi, W32[n][ec * 128:(ec + 1) * 128, :])], "foldin")
            P.ts("dve", wo, wi, gv_[:, ec:ec + 1], ALU.mult)
            P.dma([(Wb[n][ec * 128:(ec + 1) * 128, :], wo)], "foldout")

    ring = {"i": 0}

    def slab(wname, kc0, nkc, colgroups):
        s = ring["i"] % 3
        ring["i"] += 1
        ncols = sum(c for _, c in colgroups)
        assert nkc * ncols * 2 <= 8192
        v = view("wring", BF16, [nkc, ncols], boff=s * 8192)
        pairs = []
        co = 0
        for c0, cn in colgroups:
            src = Wb[wname][kc0 * 128:(kc0 + nkc) * 128, c0:c0 + cn].rearrange("(k p) n -> p k n", p=128)
            pairs.append((v[:, :, co:co + cn], src))
            co += cn
        P.dma(pairs, "w%d" % s)
        return v

    aT_all = view("aT", BF16, [512], parts=32)
    P.memset("dve", aT_all, 1.0)

    def run_tile(sk, t0, NT, CL, last, first=True, next_t0=None):
        NCH = NT // CL
        li = 0 if CL == 128 else 1
        xsrc = xTp[t0 // HSEQ] if sk == "p" else xTs
        ydst = yTp[t0 // HSEQ] if sk == "p" else yTs
        tq = t0 % HSEQ if sk == "p" else t0
        pos0 = t0 if sk == "p" else seq + t0
        x = view("x", F32, [8, NT])
        hT = view("hT", BF16, [8, NT])
        ybuf = view("vtok", F32, [8, NT])
        sq = view("sgT", BF16, [8, NT], boff=8192)
        ms = view("ms", F32, [4]); rstd = view("ms", F32, [4], boff=256)
        rbc = view("rstd", F32, [4, 128])
        qT = view("qkk", BF16, [8, NT]); kT = view("qkk", BF16, [8, NT], boff=8192)
        ktok = view("qkk", BF16, [NCH, 1024], boff=16384)
        vtok = view("vtok", BF16, [NCH, 2048])
        sgT = view("sgT", BF16, [16, NT]); ogT = view("ogT", BF16, [16, NT])
        PTv = [view("PT", BF16, [128], boff=i * 256) for i in range(4)]
        onv = [view("on", BF16, [512], boff=i * 1024) for i in range(3)]
        statv = [(view("stats", F32, [6], boff=i * 256), view("stats", F32, [2], boff=i * 256 + 64),
                  view("stats", F32, [1], boff=i * 256 + 128), view("stats", F32, [1], boff=i * 256 + 192))
                 for i in range(4)]
        stmp = [view("stmp", BF16, [NT], boff=i * 1024) for i in range(2)]
        cnt = {"pt": 0, "on": 0, "st": 0, "sv": 0}

        def tsl(ch):
            return slice(ch * CL, (ch + 1) * CL)

        def load_x(tt0):
            src_ = xTp[tt0 // HSEQ] if sk == "p" else xTs
            tq_ = tt0 % HSEQ if sk == "p" else tt0
            for c in range(8):
                P.dma([(x[:, c, :], src_[c * 128:(c + 1) * 128, tq_:tq_ + NT])], "xin%d" % c)

        if first:
            load_x(t0)

        def norm(gidx, out_hT=True, presq=False):
            TB = min(128, NT)
            NTB = NT // TB
            if not presq:
                for c in range(8):
                    P.act(sq[:, c, :], x[:, c, :], AF.Square)
            b = bank()
            for tb in range(NTB):
                for c in range(8):
                    P.mm(psb[b][0:TB, tb:tb + 1], sq[:, c, tb * TB:(tb + 1) * TB], ones[:, 0:1],
                         start=(c == 0), stop=(c == 7), sig=(c == 7 and tb == NTB - 1))
            P.ts("dve", ms[0:TB, 0:NTB], psb[b][0:TB, 0:NTB], 1.0 / D, ALU.mult, EPS, ALU.add)
            P.tt("pool", rstd[0:TB, 0:NTB], ms[0:TB, 0:NTB], neghalf[0:TB, 0:NTB], ALU.pow)
            b2 = bank()
            for tb in range(NTB):
                P.ts("dve", rbc[0:TB, tb, :], onesf[0:TB, :], rstd[0:TB, tb:tb + 1], ALU.mult)
                P.mm(psb[b2][:, tb * TB:(tb + 1) * TB], rbc[0:TB, tb, :], identf[0:TB, 0:TB],
                     start=True, stop=True, sig=(tb == NTB - 1))
            for c in range(8):
                dst = hT[:, c, :] if out_hT else ybuf[:, c, :]
                P.stt(dst, x[:, c, :], gains[:, gidx, c:c + 1], psb[b2][:, 0:NT], ALU.mult, ALU.mult)

        def resid_proj(wname, nkc, src, kslabs):
            for cg in range(4):
                banks = [bank(), bank()]
                k0 = 0
                while k0 < nkc:
                    nk = min(kslabs, nkc - k0)
                    sl = slab(wname, k0, nk, [(cg * 256, 256)])
                    for j in range(2):
                        for kk in range(nk):
                            kc = k0 + kk
                            P.mm(psb[banks[j]][:, 0:NT], sl[:, kk, j * 128:(j + 1) * 128], src[:, kc, :],
                                 start=(kc == 0), stop=(kc == nkc - 1),
                                 sig=(kc == nkc - 1) or (kk == nk - 1 and j == 1))
                    k0 += nk
                for j in range(2):
                    blk = cg * 2 + j
                    P.tt("dve", x[:, blk, :], x[:, blk, :], psb[banks[j]][:, 0:NT], ALU.add)
                    P.act(sq[:, blk, :], x[:, blk, :], AF.Square)

        def retention():
            cs = view("mix", F32, [2, NT])
            cqs = [view("mix", F32, [2, NT], boff=4096 + i * 4096) for i in range(2)]
            rt = view("mix", F32, [4, NT], boff=12288)
            P.dma([(cs[:, 0, :], d_cos[:, pos0:pos0 + NT]), (cs[:, 1, :], d_sin[:, pos0:pos0 + NT])], "cs")

            def rotary(pa, pb, ct, st_, o1, o2):
                P.tt("dve", rt[:, 0, :], pa, ct, ALU.mult)
                P.tt("dve", rt[:, 1, :], pb, st_, ALU.mult)
                P.tt("dve", o1, rt[:, 0, :], rt[:, 1, :], ALU.subtract)
                P.tt("dve", rt[:, 2, :], pa, st_, ALU.mult)
                P.tt("dve", rt[:, 3, :], pb, ct, ALU.mult)
                P.tt("dve", o2, rt[:, 2, :], rt[:, 3, :], ALU.add)

            sg_tok = view("sgT", BF16, [NCH, 2048])

            def mk_groups(h):
                st = {}
                gl = []
                for kind in (0, 1):
                    for ch in range(NCH):
                        def grp(kind=kind, ch=ch):
                            if ch == 0:
                                st[kind] = slab("ret_w_in", 0, 8, [(2048 + kind * 2048 + h * 512, 512)])
                            sl_ = st[kind]
                            b_ = bank()
                            for kc in range(8):
                                P.mm(psb[b_][0:CL, :], hT[:, kc, tsl(ch)], sl_[:, kc, :], start=(kc == 0), stop=(kc == 7))
                            if kind == 0:
                                P.copy("act", vtok[0:CL, ch, h * 512:(h + 1) * 512], psb[b_][0:CL, :])
                            else:
                                P.act(sg_tok[0:CL, ch, h * 512:(h + 1) * 512], psb[b_][0:CL, :], AF.Silu)
                        gl.append(grp)
                return gl

            Q = []
            for h in range(RET_H):
                Q += mk_groups(h)

            def emit_groups(n):
                for _ in range(n):
                    if Q:
                        Q.pop(0)()

            for which in range(2):
                dstT = qT if which == 0 else kT
                for sh in range(2):
                    sl = slab("ret_w_in", 0, 8, [(which * 1024 + sh * 512, 512)])
                    for hh in range(2):
                        h = sh * 2 + hh
                        ba, bb = bank(), bank()
                        for half, bk in ((0, ba), (1, bb)):
                            for kc in range(8):
                                P.mm(psb[bk][:, 0:NT], sl[:, kc, (hh * 2 + half) * 128:(hh * 2 + half + 1) * 128],
                                     hT[:, kc, :], start=(kc == 0), stop=(kc == 7))
                        if which == 0:
                            cq = cqs[h % 2]
                            qd = qdec[:, h, 0:CL].unsqueeze(1).to_broadcast([128, NCH, CL])
                            for t_ in range(2):
                                P.tt("pool", cq[:, t_, :].rearrange("p (a b) -> p a b", a=NCH),
                                     cs[:, t_, :].rearrange("p (a b) -> p a b", a=NCH), qd, ALU.mult)
                            rotary(psb[ba][:, 0:NT], psb[bb][:, 0:NT], cq[:, 0, :], cq[:, 1, :],
                                   dstT[:, 2 * h, :], dstT[:, 2 * h + 1, :])
                        else:
                            rotary(psb[ba][:, 0:NT], psb[bb][:, 0:NT], cs[:, 0, :], cs[:, 1, :],
                                   dstT[:, 2 * h, :], dstT[:, 2 * h + 1, :])
                        emit_groups(1)
            for h in range(RET_H):
                for ch in range(NCH):
                    b = bank()
                    for dc in range(2):
                        P.tr(psb_bf[b][0:CL, dc * 128:(dc + 1) * 128], kT[:, 2 * h + dc, tsl(ch)], ident,
                             sig=(dc == 1))
                    P.act(ktok[0:CL, ch, h * 256:(h + 1) * 256], psb_bf[b][0:CL, 0:256], AF.Copy,
                          scale=kdec[0:CL, li, h:h + 1])
            for h in range(RET_H):
                for ch in range(NCH):
                    bs = bank()
                    for dc in range(2):
                        P.mm(psb[bs][0:CL, 0:CL], kT[:, 2 * h + dc, tsl(ch)], qT[:, 2 * h + dc, tsl(ch)],
                             start=(dc == 0), stop=(dc == 1))
                    PT = PTv[cnt["pt"] % 4]; cnt["pt"] += 1
                    P.tt("dve", PT[0:CL, 0:CL], psb[bs][0:CL, 0:CL], rmask[0:CL, h, 0:CL], ALU.mult)
                    emit_groups(1)
                    bo = bank()
                    P.mm(psb[bo][0:CL, :], PT[0:CL, 0:CL], vtok[0:CL, ch, h * 512:(h + 1) * 512], start=True, stop=False)
                    for dc in range(2):
                        P.mm(psb[bo][0:CL, :], qT[:, 2 * h + dc, tsl(ch)], Sretb[:, h, dc, :], start=False, stop=(dc == 1))
                    bS = [bank(), bank()]
                    for dc in range(2):
                        P.mm(psb[bS[dc]][:, :], ktok[0:CL, ch, h * 256 + dc * 128:h * 256 + (dc + 1) * 128],
                             vtok[0:CL, ch, h * 512:(h + 1) * 512], start=True, stop=True)
                    stats6, mv, ve, rs = statv[cnt["sv"] % 4]; cnt["sv"] += 1
                    P.op("dve", lambda e, o=stats6[0:CL, :], i=psb[bo][0:CL, :]: e.bn_stats(o, i),
                         reads=[psb[bo][0:CL, :]], writes=[stats6[0:CL, :]])
                    P.op("dve", lambda e, o=mv[0:CL, :], i=stats6[0:CL, :]: e.bn_aggr(o, i),
                         reads=[stats6[0:CL, :]], writes=[mv[0:CL, :]])
                    P.ts("dve", ve[0:CL, :], mv[0:CL, 1:2], EPS, ALU.add)
                    P.tt("pool", rs[0:CL, :], ve[0:CL, :], neghalf[0:CL, 0:1], ALU.pow)
                    for dc in range(2):
                        P.stt(Sret[:, h, dc, :], Sret[:, h, dc, :], gL[CL][h], psb[bS[dc]][:, :], ALU.mult, ALU.add)
                        P.copy("act", Sretb[:, h, dc, :], Sret[:, h, dc, :])
                    emit_groups(1)
                    on = onv[cnt["on"] % 3]; cnt["on"] += 1
                    P.ts("dve", on[0:CL, :], psb[bo][0:CL, :], mv[0:CL, 0:1], ALU.subtract, rs[0:CL, :], ALU.mult)
                    P.tt("dve", on[0:CL, :], on[0:CL, :], sg_tok[0:CL, ch, h * 512:(h + 1) * 512], ALU.mult)
                    bt = bank()
                    for eb in range(4):
                        P.tr(psb_bf[bt][:, eb * CL:(eb + 1) * CL], on[0:CL, eb * 128:(eb + 1) * 128],
                             ident[0:CL, 0:CL], sig=(eb == 3))
                    P.copy("act", ogT[:, 4 * h:4 * h + 4, tsl(ch)],
                           psb_bf[bt][:, 0:4 * CL].rearrange("p (a b) -> p a b", a=4))
            emit_groups(len(Q))
            resid_proj("ret_w_out", 16, ogT, 16)

        def ffn(layer):
            actT = view("qkk", BF16, [22, NT])
            ubuf = [[view("sgT", F32, [NT + 2], boff=(r * 2 + gv) * 2304) for gv in range(2)] for r in range(2)]
            cbuf = [[view("sgT", F32, [NT], boff=9216 + gv * 2048), view("on", F32, [NT], boff=gv * 2048)][r]
                    for r in range(2) for gv in range(2)]
            cbuf = [[cbuf[r * 2 + gv] for gv in range(2)] for r in range(2)]
            sgt = [view("sgT", BF16, [NT], boff=13312 + r * 1024) for r in range(2)]
            wn = "ffn_w_up%d" % layer
            it = 0
            pend = None

            def gate_stage(r, gb):
                P.act(sgt[r], cbuf[r][0], AF.Silu)
                P.tt("dve", actT[:, gb, :], cbuf[r][1], sgt[r], ALU.mult)

            for s in range(11):
                sl = slab(wn, 0, 8, [(s * 256, 256), (DFF + s * 256, 256)])
                for jj in range(2):
                    gb = s * 2 + jj
                    r = it % 2; it += 1
                    for gv in range(2):
                        blk = gb + gv * 22
                        b = bank()
                        for kc in range(8):
                            P.mm(psb[b][:, 0:NT], sl[:, kc, gv * 256 + jj * 128:gv * 256 + (jj + 1) * 128], hT[:, kc, :],
                                 start=(kc == 0), stop=(kc == 7))
                        ub = ubuf[r][gv]
                        c = cbuf[r][gv]
                        P.copy("pool", ub[:, 0:2], uhf[:, layer, blk, :])
                        P.copy("act", ub[:, 2:2 + NT], psb[b][:, 0:NT])
                        P.copy("pool", uhf[:, layer, blk, :], ub[:, NT:NT + 2])
                        P.act(c, ub[:, 0:NT], AF.Identity, bias=cb[:, layer, blk:blk + 1], scale=cw[:, layer, 0, blk:blk + 1])
                        P.stt(c, ub[:, 1:NT + 1], cw[:, layer, 1, blk:blk + 1], c, ALU.mult, ALU.add)
                        P.stt(c, ub[:, 2:NT + 2], cw[:, layer, 2, blk:blk + 1], c, ALU.mult, ALU.add)
                    if pend is not None:
                        gate_stage(*pend)
                    pend = (r, gb)
            gate_stage(*pend)
            resid_proj("ffn_w_down%d" % layer, 22, actT, 11)

        def gla():
            vt = view("vtok", BF16, [NCH, 1024])
            sr_tok = view("sgT", BF16, [NCH, 1024])
            kbar = view("qkk", BF16, [NCH, 512], boff=16384)
            sp = view("mix", F32, [NCH, 512])
            Eq = view("mix", F32, [GLA_H, NT], boff=8192)
            Ek = [view("mix", F32, [NT], boff=16384 + i * 2048) for i in range(2)]
            zt = view("on", F32, [512])
            kbt = view("stmp", BF16, [NT])
            junk = view("on", BF16, [256], boff=2048)

            def mk_groups(kind, s2):
                st = {}
                gl = []
                for ch in range(NCH):
                    def grp(ch=ch):
                        if ch == 0:
                            st["s"] = slab("gla_w_in", 0, 8, [(1024 + kind * 1024 + s2 * 512, 512)])
                        b_ = bank()
                        for kc in range(8):
                            P.mm(psb[b_][0:CL, :], hT[:, kc, tsl(ch)], st["s"][:, kc, :], start=(kc == 0), stop=(kc == 7))
                        if kind == 0:
                            P.copy("act", vt[0:CL, ch, s2 * 512:(s2 + 1) * 512], psb[b_][0:CL, :])
                        else:
                            P.act(sr_tok[0:CL, ch, s2 * 512:(s2 + 1) * 512], psb[b_][0:CL, :], AF.Silu)
                    gl.append(grp)
                return gl

            Q = mk_groups(0, 0) + mk_groups(1, 0) + mk_groups(0, 1) + mk_groups(1, 1)

            def emit_groups(n):
                for _ in range(n):
                    if Q:
                        Q.pop(0)()

            sl = slab("gla_w_in", 0, 8, [(3072, 16)])
            b = bank()
            for kc in range(8):
                P.mm(psb[b][0:16, 0:NT], sl[:, kc, 0:16], hT[:, kc, :], start=(kc == 0), stop=(kc == 7))
            P.copy("act", aT_all[0:16, 0:NT], psb[b][0:16, 0:NT])
            if NCH >= 4:
                emit_groups(8)
            for ch in range(NCH):
                b = bank()
                P.mm(psb[b][0:CL, :], aT_all[0:17, tsl(ch)], wa2[0:17, :], start=True, stop=True)
                P.act(zt[0:CL, :], psb[b][0:CL, :], AF.Exp, scale=-1.0)
                P.act(sp[0:CL, ch, :], zt[0:CL, :], AF.Ln, bias=1.0)
            for h in range(GLA_H):
                b = bank()
                for ch in range(NCH):
                    P.mm(psb[b][:, tsl(ch)], sp[0:CL, ch, h * 128:(h + 1) * 128], ucum[0:CL, 0:CL], start=True, stop=True,
                         sig=(ch == NCH - 1))
                P.act(Eq[:, h, :], psb[b][:, 0:NT], AF.Exp)
            slq = slab("gla_w_in", 0, 8, [(0, 512)])
            for h in range(GLA_H):
                bq = bank()
                for kc in range(8):
                    P.mm(psb[bq][:, 0:NT], slq[:, kc, h * 128:(h + 1) * 128], hT[:, kc, :], start=(kc == 0), stop=(kc == 7))
                P.tt("dve", qT[:, h, :], psb[bq][:, 0:NT], Eq[:, h, :], ALU.mult)
                if NCH >= 4:
                    emit_groups(1)
            slk = slab("gla_w_in", 0, 8, [(512, 512)])
            for h in range(GLA_H):
                P.op("dve", lambda e, o=Ek[h % 2], i=Eq[:, h, :]: e.reciprocal(o, i), reads=[Eq[:, h, :]], writes=[Ek[h % 2]])
                bk = bank()
                for kc in range(8):
                    P.mm(psb[bk][:, 0:NT], slk[:, kc, h * 128:(h + 1) * 128], hT[:, kc, :], start=(kc == 0), stop=(kc == 7))
                P.tt("dve", kT[:, h, :], psb[bk][:, 0:NT], Ek[h % 2], ALU.mult)
                for ch in range(NCH):
                    P.ts("dve", kbt[:, tsl(ch)], kT[:, h, tsl(ch)], Eq[:, h, (ch + 1) * CL - 1:(ch + 1) * CL], ALU.mult,
                         GLA_DK ** -0.5, ALU.mult)
                    bt = bank()
                    P.tr(psb_bf[bt][0:CL, 0:128], kbt[:, tsl(ch)], ident)
                    P.copy("act", kbar[0:CL, ch, h * 128:(h + 1) * 128], psb_bf[bt][0:CL, 0:128])
                if NCH >= 4:
                    emit_groups(1)
            emit_groups(len(Q))
            junk = view("on", BF16, [256])
            onb = view("on", BF16, [1024], boff=2048)
            PT4 = view("PT", BF16, [4, 128])
            for ch in range(NCH):
                ba = bank()
                for h in range(GLA_H):
                    P.mm(psb[ba][0:CL, h * CL:(h + 1) * CL], kT[:, h, tsl(ch)], qT[:, h, tsl(ch)], start=True, stop=True,
                         sig=(h == GLA_H - 1))
                P.tt("dve", PT4[0:CL, :, 0:CL], psb[ba][0:CL, 0:4 * CL].rearrange("p (a b) -> p a b", a=4),
                     gmask[0:CL, 0:CL].unsqueeze(1).to_broadcast([CL, 4, CL]), ALU.mult)
                bo = [bank(), bank()]
                obs = []
                for h in range(GLA_H):
                    ob = psb[bo[h // 2]][0:CL, (h % 2) * 256:(h % 2 + 1) * 256]
                    obs.append(ob)
                    P.mm(ob, PT4[0:CL, h, 0:CL], vt[0:CL, ch, h * 256:(h + 1) * 256], start=True, stop=False)
                    P.mm(ob, qT[:, h, tsl(ch)], Sglab[:, h, :], start=False, stop=True)
                bS = [bank(), bank()]
                sbs = []
                for h in range(GLA_H):
                    sb_ = psb[bS[h // 2]][:, (h % 2) * 256:(h % 2 + 1) * 256]
                    sbs.append(sb_)
                    P.mm(sb_, kbar[0:CL, ch, h * 128:(h + 1) * 128], vt[0:CL, ch, h * 256:(h + 1) * 256],
                         start=True, stop=True)
                k_ = cnt["sv"] % 4; cnt["sv"] += 1
                ss4 = view("stats", F32, [4], boff=k_ * 256)
                ms4 = view("stats", F32, [4], boff=k_ * 256 + 64)
                rs4 = view("stats", F32, [4], boff=k_ * 256 + 128)
                for h in range(GLA_H):
                    P.act(junk[0:CL, :], obs[h], AF.Square, accum=ss4[0:CL, h:h + 1])
                P.ts("dve", ms4[0:CL, :], ss4[0:CL, :], 1.0 / GLA_DV, ALU.mult, EPS, ALU.add)
                P.tt("pool", rs4[0:CL, :], ms4[0:CL, :], neghalf[0:CL, 0:4], ALU.pow)
                for h in range(GLA_H):
                    P.stt(Sgla[:, h, :], Sgla[:, h, :], Eq[:, h, (ch + 1) * CL - 1:(ch + 1) * CL], sbs[h],
                          ALU.mult, ALU.add)
                    P.copy("act", Sglab[:, h, :], Sgla[:, h, :])
                for h in range(GLA_H):
                    P.ts("dve", onb[0:CL, h * 256:(h + 1) * 256], obs[h], rs4[0:CL, h:h + 1], ALU.mult)
                P.tt("dve", onb[0:CL, :], onb[0:CL, :], sr_tok[0:CL, ch, :], ALU.mult)
                bt = bank()
                for i8 in range(8):
                    P.tr(psb_bf[bt][:, i8 * CL:(i8 + 1) * CL], onb[0:CL, i8 * 128:(i8 + 1) * 128], ident[0:CL, 0:CL],
                         sig=(i8 == 7))
                P.copy("act", ogT[:, 0:8, tsl(ch)], psb_bf[bt][:, 0:8 * CL].rearrange("p (a b) -> p a b", a=8))
            resid_proj("gla_w_out", 8, ogT, 8)

        norm(0); retention()
        norm(2, presq=True); ffn(0)
        norm(1, presq=True); gla()
        norm(3, presq=True); ffn(1)
        norm(4, out_hT=False, presq=True)
        if next_t0 is not None:
            load_x(next_t0)
        P.dma([(ydst[:, tq:tq + NT].rearrange("(c p) t -> p c t", p=128), ybuf)], "yout")

    def seq_end(sk):
        P.dma([(o_ret[sk].rearrange("h (dc p) e -> p h dc e", p=128), Sret)], "so_ret")
        P.dma([(o_gla[sk].rearrange("h p e -> p h e"), Sgla)], "so_gla")
        P.dma([(o_conv[sk], uhf)], "so_conv")

    P.dma([(Sret, st_ret.rearrange("h (dc p) e -> p h dc e", p=128)), (Sgla, st_gla.rearrange("h p e -> p h e")),
           (uhf, cconv)], "stin")
    P.copy("act", Sretb, Sret); P.copy("act", Sglab, Sgla); P.copy("dve", uhb, uhf)
    run_tile("s", 0, DEC_SEQ, DEC_SEQ, True)
    seq_end("s")
    P.memset("pool", Sret, 0.0); P.memset("pool", Sretb, 0.0); P.memset("pool", Sgla, 0.0); P.memset("pool", Sglab, 0.0)
    P.memset("pool", uhb, 0.0); P.memset("pool", uhf, 0.0)
    ntile = seq // 512
    for ti in range(ntile):
        run_tile("p", ti * 512, 512, 128, ti == ntile - 1, first=(ti == 0),
                 next_t0=((ti + 1) * 512 if ti + 1 < ntile else None))
    seq_end("p")

    P.wait_all("sp", ("yout", "so_ret", "so_gla", "so_conv"))

    with contextlib.ExitStack() as es:
        for lname, L in P.lanes.items():
            L["sem"] = es.enter_context(nc.semaphore("s_" + lname))
        block = es.enter_context(nc.Block())
        P.emit(block)
    return nc, cst, P


_CACHE = {}


def _prep_inputs(inp, seq, cst):
    f32 = lambda a: np.ascontiguousarray(np.asarray(a, np.float32))
    shared = {
        "ret_w_in": f32(inp["ret_w_in"][0]), "ret_w_out": f32(inp["ret_w_out"][0]),
        "gla_w_in": f32(inp["gla_w_in"][0]), "gla_w_out": f32(inp["gla_w_out"][0]),
        "ffn_w_up0": f32(inp["ffn_w_up"][0]), "ffn_w_up1": f32(inp["ffn_w_up"][1]),
        "ffn_w_down0": f32(inp["ffn_w_down"][0]), "ffn_w_down1": f32(inp["ffn_w_down"][1]),
    }
    nm, nf = np.asarray(inp["norm_mix"], np.float32), np.asarray(inp["norm_ffn"], np.float32)
    gains = np.stack([_fm(nm[0]), _fm(nm[1]), _fm(nf[0]), _fm(nf[1]), _fm(inp["norm_final"])], axis=1)
    shared["gains"] = np.ascontiguousarray(gains)
    shared["gn"] = _fm(np.asarray(inp["ret_gn_g"], np.float32)[0].reshape(-1))
    shared["ng"] = _fm(np.asarray(inp["gla_norm_g"], np.float32)[0].reshape(-1))
    cwv = np.asarray(inp["ffn_conv_w"], np.float32)
    shared["cw"] = np.ascontiguousarray(cwv.reshape(DEPTH, 3, NBLK_FF, 128).transpose(3, 0, 1, 2))
    cbv = np.asarray(inp["ffn_conv_b"], np.float32)
    shared["cb"] = np.ascontiguousarray(cbv.reshape(DEPTH, NBLK_FF, 128).transpose(2, 0, 1))
    shared["wa2aug"] = np.ascontiguousarray(np.concatenate(
        [np.asarray(inp["gla_w_a2"], np.float32)[0], np.asarray(inp["gla_b_a"], np.float32)[0][None, :]], axis=0))
    for k in ("ident_bf", "ones_bf", "ident_f", "ones_f", "rmask", "qdec", "kdec", "gmask", "ucum", "neghalf", "epsv", "cos", "sin"):
        shared[k] = cst[k]
    xp = np.asarray(inp["x_prompt"], np.float32)
    xs = np.asarray(inp["x_sample"], np.float32)
    cc = np.asarray(inp["cache_conv"], np.float32)
    maps = []
    for b in range(NCORES):
        m = dict(shared)
        hs = seq // (-(-seq // 4096))
        for i in range(seq // hs):
            m["xTp%d" % i] = np.ascontiguousarray(xp[b, i * hs:(i + 1) * hs].T)
        m["xTs"] = np.ascontiguousarray(xs[b].T)
        m["st_ret"] = f32(inp["state_ret"][0, b])
        m["st_gla"] = f32(inp["state_gla"][0, b])
        m["cconv"] = np.ascontiguousarray(cc[:, b].reshape(DEPTH, 2, NBLK_FF, 128).transpose(3, 0, 2, 1))
        maps.append(m)
    return maps


def _run(inp, seq):
    if seq not in _CACHE:
        _CACHE[seq] = build(seq)
    nc, cst, _ = _CACHE[seq]
    maps = _prep_inputs(inp, seq, cst)
    res = run_bass_kernel_spmd(nc, maps, core_ids=list(range(NCORES)))
    R = res.results
    B = NCORES
    hs = seq // (-(-seq // 4096))
    y_p = np.stack([np.concatenate([R[b]["yTp%d" % i].T for i in range(seq // hs)], axis=0)
                    for b in range(B)]).astype(np.float32)
    y_s = np.stack([R[b]["yTs"].T for b in range(B)]).astype(np.float32)
    ret_p = np.stack([R[b]["ret_p"] for b in range(B)])[None].astype(np.float32)
    ret_s = np.stack([R[b]["ret_s"] for b in range(B)])[None].astype(np.float32)
    gla_p = np.stack([R[b]["gla_p"] for b in range(B)])[None].astype(np.float32)
    gla_s = np.stack([R[b]["gla_s"] for b in range(B)])[None].astype(np.float32)

    def conv(k):
        a = np.stack([R[b][k] for b in range(B)])
        return np.ascontiguousarray(a.transpose(2, 0, 4, 3, 1).reshape(DEPTH, B, 2, 2 * DFF)).astype(np.float32)

    return (y_p, y_s, ret_p, ret_s, gla_p, gla_s, conv("conv_p"), conv("conv_s"))


def kernel(**inputs):
    seq = int(np.asarray(inputs["x_prompt"]).shape[1])
    return _run(inputs, seq)
```

```python
import contextlib
import math
import numpy as np
import ml_dtypes
import concourse.bass as bass
import concourse.mybir as mybir
from concourse.bass_utils import run_bass_kernel_spmd

F32 = mybir.dt.float32
BF16 = mybir.dt.bfloat16
U8 = mybir.dt.uint8
AF = mybir.ActivationFunctionType
ALU = mybir.AluOpType
DSZ = {F32: 4, BF16: 2, U8: 1}

D = 1024
DEPTH = 2
RET_H = 4
RET_DK = 256
RET_DV = 512
GLA_H = 4
GLA_DK = 128
GLA_DV = 256
GLA_RANK = 16
GLA_TAU = 16.0
DFF = 2816
NBLK_FF = 2 * DFF // 128
EPS = 1e-6
ROPE_BASE = 10000.0
PAST_LEN = 2048
DEC_SEQ = 32
NCORES = 8

ARENA = 211968


class Prog:
    SBG = 256

    def __init__(self, nc):
        self.nc = nc
        self.q = {e: [] for e in ("pe", "act", "dve", "pool", "sp")}
        self.lanes = {}
        for e in ("pe", "act", "dve", "pool"):
            self.lanes[e] = {"count": 0, "inc": 1, "sem": None}
        self.clock = {e: {} for e in self.q}
        self.snap = {}
        self.gran = {}
        self.maxwait = {}
        self.nops = 0

    def lane(self, name):
        if name not in self.lanes:
            self.lanes[name] = {"count": 0, "inc": 16, "sem": None}
        return name

    def keys(self, ap):
        t = ap.tensor
        name = t.name
        if name not in ("arena", "psum"):
            return [("d", name)]
        esz = DSZ[ap.dtype]
        pairs = list(ap.ap)
        pstep = pairs[0][0]
        off = int(ap.offset)
        inpart = off % pstep if pstep else off
        starts = [inpart]
        free = [(s, c) for (s, c) in pairs[1:] if c > 1 or len(pairs) == 2]
        free = [(s, c) for (s, c) in free if s != 0]
        length = 1
        i = 0
        while i < len(free):
            s, c = free[i]
            inner_ext = sum((cc - 1) * abs(ss) for ss, cc in free[i + 1:]) + 1
            if i == len(free) - 1:
                length = (c - 1) * abs(s) + 1
            elif abs(s) > inner_ext and len(starts) * c <= 128:
                starts = [st + k * s for st in starts for k in range(c)]
            else:
                length = sum((cc - 1) * abs(ss) for ss, cc in free[i:]) + 1
                break
            i += 1
        g = self.SBG if name == "arena" else 2048
        ks = set()
        for st in starts:
            lo = (st * esz) // g
            hi = ((st + length) * esz - 1) // g
            for k in range(lo, hi + 1):
                ks.add((name, k))
        return ks

    def op(self, eng, fn, reads=(), writes=(), sig=True, lane=None, n=1, embed=True):
        self.nops += 1
        mylane = lane if lane is not None else eng
        L = self.lanes[mylane]
        raw = {}
        oth = {}

        def add(d, ls):
            l, s = ls
            if d.get(l, 0) < s:
                d[l] = s

        rkeys = set()
        for ap in reads:
            rkeys |= set(self.keys(ap))
        wkeys = set()
        for ap in writes:
            wkeys |= set(self.keys(ap))
        for k in rkeys:
            g = self.gran.get(k)
            if g is not None and g[0] is not None:
                add(raw, g[0])
        for k in wkeys:
            g = self.gran.get(k)
            if g is not None:
                if g[0] is not None:
                    add(oth, g[0])
                for l, s in g[1].items():
                    add(oth, (l, s))
        deps = {}
        for l, s in raw.items():
            if l == eng and lane is None:
                if eng == "pe":
                    continue
            add(deps, (l, s))
        for l, s in oth.items():
            if l == eng and lane is None and eng == "pe":
                continue
            add(deps, (l, s))
        if lane is not None and L["count"] > 0:
            add(deps, (mylane, L["count"]))
        if sig:
            L["count"] += n
            seq = L["count"]
        else:
            seq = L["count"] + 1
        ck = self.clock[eng]
        waits = []
        for l, s in sorted(deps.items()):
            if ck.get(l, 0) < s:
                waits.append((l, s * self.lanes[l]["inc"]))
                if self.maxwait.get(l, 0) < s:
                    self.maxwait[l] = s
                sn = self.snap.get((l, s))
                if sn:
                    for l2, s2 in sn.items():
                        if ck.get(l2, 0) < s2:
                            ck[l2] = s2
                ck[l] = s
        if sig:
            sn = dict(ck)
            sn[mylane] = seq
            self.snap[(mylane, seq)] = sn
        for k in rkeys:
            g = self.gran.get(k)
            if g is None:
                g = [None, {}]
                self.gran[k] = g
            if g[1].get(mylane, 0) < seq:
                g[1][mylane] = seq
        for k in wkeys:
            self.gran[k] = [(mylane, seq), {}]
        self.q[eng].append((waits, fn, (mylane, L["inc"]) if sig else None, embed))

    def wait_all(self, eng, lanes):
        waits = [(l, self.lanes[l]["count"] * self.lanes[l]["inc"]) for l in lanes if self.lanes[l]["count"] > 0]
        self.q[eng].append((waits, None, None, False))

    def emit(self, block):
        for l, s in self.maxwait.items():
            assert s <= self.lanes[l]["count"], (l, s, self.lanes[l]["count"])

        def runner(name):
            items = self.q[name]
            lanes = self.lanes

            def body(e):
                for waits, fn, sig, embed in items:
                    emb = None
                    if embed and fn is not None and waits:
                        emb = waits[-1]
                        waits = waits[:-1]
                    for l, v in waits:
                        e.wait_ge(lanes[l]["sem"], v)
                    if fn is None:
                        continue
                    r = fn(e)
                    if emb is not None:
                        first = r[0] if isinstance(r, (list, tuple)) else r
                        first._wait_ge(lanes[emb[0]]["sem"], emb[1])
                    if sig is not None:
                        sem = lanes[sig[0]]["sem"]
                        if isinstance(r, (list, tuple)):
                            for ins in r:
                                ins.then_inc(sem, sig[1])
                        else:
                            r.then_inc(sem, sig[1])

            return body

        block.tensor(runner("pe"))
        block.scalar(runner("act"))
        block.vector(runner("dve"))
        block.gpsimd(runner("pool"))
        block.sync(runner("sp"))

    def mm(self, out, lhsT, rhs, start, stop, sig=None):
        self.op("pe", lambda e: e.matmul(out, lhsT, rhs, start=start, stop=stop),
                reads=[lhsT, rhs], writes=[out], sig=(stop if sig is None else sig))

    def tr(self, out, in_, ident, sig=True):
        self.op("pe", lambda e: e.transpose(out, in_, ident), reads=[in_, ident], writes=[out], sig=sig)

    def act(self, out, in_, func, bias=None, scale=None, accum=None):
        reads = [in_]
        kw = {}
        if bias is not None:
            kw["bias"] = bias
            if not isinstance(bias, (int, float)):
                reads.append(bias)
        if scale is not None:
            kw["scale"] = scale
            if not isinstance(scale, (int, float)):
                reads.append(scale)
        writes = [out]
        if accum is not None:
            kw["accum_out"] = accum
            writes.append(accum)
        self.op("act", lambda e: e.activation(out, in_, func, **kw), reads=reads, writes=writes,
                embed=(accum is None))

    def tt(self, eng, out, a, b, op):
        self.op(eng, lambda e: e.tensor_tensor(out, a, b, op), reads=[a, b], writes=[out])

    def ts(self, eng, out, a, s1, op0, s2=None, op1=None):
        reads = [a]
        if not isinstance(s1, (int, float)):
            reads.append(s1)
        if s2 is not None and not isinstance(s2, (int, float)):
            reads.append(s2)
        if op1 is None:
            self.op(eng, lambda e: e.tensor_scalar(out, a, s1, None, op0), reads=reads, writes=[out])
        else:
            self.op(eng, lambda e: e.tensor_scalar(out, a, s1, s2, op0, op1), reads=reads, writes=[out])

    def stt(self, out, in0, scalar, in1, op0, op1):
        reads = [in0, in1]
        if not isinstance(scalar, (int, float)):
            reads.append(scalar)
        self.op("dve", lambda e: e.scalar_tensor_tensor(out, in0, scalar, in1, op0, op1),
                reads=reads, writes=[out])

    def copy(self, eng, out, in_):
        if eng == "act":
            self.op("act", lambda e: e.activation(out, in_, AF.Copy), reads=[in_], writes=[out])
        else:
            self.op(eng, lambda e: e.tensor_copy(out, in_), reads=[in_], writes=[out])

    def memset(self, eng, out, val):
        self.op(eng, lambda e: e.memset(out, val), reads=[], writes=[out])

    def dma(self, pairs, lane, eng="sp", **kw):
        self.lane(lane)
        outs = [p[0] for p in pairs]
        ins = [p[1] for p in pairs]

        def fn(e):
            return [e.dma_start(out=o, in_=i, **kw) for o, i in pairs]

        self.op(eng, fn, reads=ins, writes=outs, lane=lane, n=len(pairs))


def _consts(seq):
    c = {}
    c["ident_bf"] = np.eye(128, dtype=np.float32).astype(ml_dtypes.bfloat16)
    c["ones_bf"] = np.ones((128, 128), dtype=np.float32).astype(ml_dtypes.bfloat16)
    c["ident_f"] = np.eye(128, dtype=np.float32)
    c["ones_f"] = np.ones((128, 128), dtype=np.float32)
    lg = np.log1p(-np.exp2(-5.0 - np.arange(RET_H, dtype=np.float64)))
    j = np.arange(128, dtype=np.float64)
    causalT = (j[None, :] >= j[:, None]).astype(np.float64)
    rmask = np.zeros((128, RET_H, 128), np.float64)
    for h in range(RET_H):
        rmask[:, h, :] = np.exp(-lg[h] * (j[:, None] + 1.0)) * (RET_DK ** -0.5) * causalT
    c["rmask"] = rmask.astype(np.float32)
    qd = np.exp(lg[:, None] * (j[None, :] + 1.0))
    c["qdec"] = np.broadcast_to(qd[None], (128, RET_H, 128)).astype(np.float32).copy()
    kd = np.zeros((128, 2, RET_H), np.float64)
    for li, L in enumerate((128, 32)):
        for h in range(RET_H):
            kd[:, li, h] = np.exp(lg[h] * (L - 1.0 - j)) * (RET_DK ** -0.5)
    c["kdec"] = kd.astype(np.float32)
    c["gL"] = {L: [float(np.exp(lg[h] * L)) for h in range(RET_H)] for L in (128, 32)}
    c["gmask"] = (causalT * (GLA_DK ** -0.5)).astype(np.float32)
    c["ucum"] = ((j[:, None] <= j[None, :]) * (-1.0 / GLA_TAU)).astype(np.float32)
    c["neghalf"] = np.full((128, 512), -0.5, np.float32)
    c["epsv"] = np.full((128, 8), EPS, np.float32)
    inv = (np.float32(ROPE_BASE) ** (-(np.arange(128, dtype=np.float32) / np.float32(128)))).astype(np.float32)
    pos = np.concatenate([np.arange(seq, dtype=np.float32), PAST_LEN + np.arange(DEC_SEQ, dtype=np.float32)])
    ang = (pos[None, :] * inv[:, None]).astype(np.float32)
    c["cos"] = np.cos(ang).astype(np.float32)
    c["sin"] = np.sin(ang).astype(np.float32)
    return c


def _fm(v):
    v = np.asarray(v, np.float32)
    return np.ascontiguousarray(v.reshape(-1, 128).T)


WEIGHTS = [
    ("ret_w_in", D, 6144), ("ret_w_out", 2048, D), ("gla_w_in", D, 3088), ("gla_w_out", D, D),
    ("ffn_w_up0", D, 2 * DFF), ("ffn_w_up1", D, 2 * DFF), ("ffn_w_down0", DFF, D), ("ffn_w_down1", DFF, D),
]


def build(seq, dbg=()):
    assert seq % 512 == 0
    nc = bass.Bass("TRN2", target_bir_lowering=False)
    P = Prog(nc)
    cst = _consts(seq)
    gL = cst["gL"]
    npos = seq + DEC_SEQ

    def din(name, shape, dt=F32):
        return nc.dram_tensor(name, list(shape), dt, kind="ExternalInput").ap()

    def dout(name, shape, dt=F32):
        return nc.dram_tensor(name, list(shape), dt, kind="ExternalOutput").ap()

    NHALF = -(-seq // 4096)
    HSEQ = seq // NHALF
    assert HSEQ * NHALF == seq and HSEQ % 512 == 0
    xTp = [din("xTp%d" % i, [D, HSEQ]) for i in range(NHALF)]; xTs = din("xTs", [D, DEC_SEQ])
    st_ret = din("st_ret", [RET_H, RET_DK, RET_DV]); st_gla = din("st_gla", [GLA_H, GLA_DK, GLA_DV])
    cconv = din("cconv", [128, DEPTH, NBLK_FF, 2])
    W32 = {n: din(n, [k, m]) for n, k, m in WEIGHTS}
    Wb = {n: nc.dram_tensor(n + "_bf", [k, m], BF16, kind="Internal").ap() for n, k, m in WEIGHTS}
    d_gains = din("gains", [128, 5, 8])
    d_gn = din("gn", [128, 16]); d_ng = din("ng", [128, 8])
    d_cw = din("cw", [128, DEPTH, 3, NBLK_FF]); d_cb = din("cb", [128, DEPTH, NBLK_FF])
    d_wa2 = din("wa2aug", [17, 512])
    d_ident = din("ident_bf", [128, 128], BF16); d_ones = din("ones_bf", [128, 128], BF16)
    d_rmask = din("rmask", [128, RET_H, 128]); d_qdec = din("qdec", [128, RET_H, 128])
    d_kdec = din("kdec", [128, 2, RET_H]); d_gmask = din("gmask", [128, 128]); d_ucum = din("ucum", [128, 128])
    d_neghalf = din("neghalf", [128, 512]); d_epsv = din("epsv", [128, 8])
    d_identf = din("ident_f", [128, 128]); d_onesf = din("ones_f", [128, 128])
    d_cos = din("cos", [128, npos]); d_sin = din("sin", [128, npos])

    yTp = [dout("yTp%d" % i, [D, HSEQ]) for i in range(NHALF)]; yTs = dout("yTs", [D, DEC_SEQ])
    o_ret = {"p": dout("ret_p", [RET_H, RET_DK, RET_DV]), "s": dout("ret_s", [RET_H, RET_DK, RET_DV])}
    o_gla = {"p": dout("gla_p", [GLA_H, GLA_DK, GLA_DV]), "s": dout("gla_s", [GLA_H, GLA_DK, GLA_DV])}
    o_conv = {"p": dout("conv_p", [128, DEPTH, NBLK_FF, 2]), "s": dout("conv_s", [128, DEPTH, NBLK_FF, 2])}
    dbg_out = {}

    arena_h = nc.alloc_sbuf_tensor("arena", [128, ARENA], U8)
    arena = arena_h.ap()
    psum_h = nc.alloc_psum_tensor("psum", [128, 8, 512], F32)
    psum = psum_h.ap()

    off = {"_": 0}
    reg = {}

    def region(name, nbytes):
        assert nbytes % 256 == 0, name
        reg[name] = (off["_"], nbytes)
        off["_"] += nbytes
        assert off["_"] <= ARENA, (name, off["_"])

    def view(name, dt, shape, boff=0, parts=128):
        o, nb = reg[name]
        n = int(np.prod(shape)) * DSZ[dt]
        assert boff + n <= nb, (name, boff, n, nb)
        ap = arena[0:parts, o + boff:o + boff + n].bitcast(dt)
        if len(shape) == 2:
            ap = ap.rearrange("p (a b) -> p a b", a=shape[0])
        elif len(shape) == 3:
            ap = ap.rearrange("p (a b c) -> p a b c", a=shape[0], b=shape[1])
        return ap

    region("x", 16384); region("hT", 8192); region("ms", 2048); region("rstd", 2048)
    region("identf", 512); region("onesf", 512)
    region("qkk", 24576)
    region("vtok", 16384)
    region("sgT", 16384)
    region("ogT", 16384)
    region("mix", 20480)
    region("PT", 1024); region("on", 4096); region("stats", 1024); region("stmp", 2048); region("aT", 1024)
    region("Sret", 16384); region("Sretb", 8192); region("Sgla", 4096); region("Sglab", 2048)
    region("wring", 3 * 8192)
    for nme, nb in [("ident", 256), ("ones", 256), ("rmask", 2048), ("qdec", 2048), ("kdec", 256), ("gmask", 512),
                    ("ucum", 512), ("neghalf", 2048), ("epsv", 256), ("gains", 256), ("gn", 256), ("ng", 256), ("cw", 1280),
                    ("cb", 512), ("wa2", 1024), ("uhb", 512), ("uhf", 768)]:
        region(nme, nb)

    ident = view("ident", BF16, [128]); ones = view("ones", BF16, [128])
    identf = view("identf", F32, [128]); onesf = view("onesf", F32, [128])
    rmask = view("rmask", F32, [RET_H, 128]); qdec = view("qdec", F32, [RET_H, 128])
    kdec = view("kdec", F32, [2, RET_H]); gmask = view("gmask", F32, [128]); ucum = view("ucum", F32, [128])
    neghalf = view("neghalf", F32, [512]); epsv = view("epsv", F32, [8])
    gains = view("gains", F32, [5, 8]); gnT = view("gn", F32, [16]); ngT = view("ng", F32, [8])
    cw = view("cw", F32, [DEPTH, 3, NBLK_FF]); cb = view("cb", F32, [DEPTH, NBLK_FF])
    wa2 = view("wa2", BF16, [512], parts=32)
    uhb = view("uhb", BF16, [DEPTH, NBLK_FF, 2]); uhf = view("uhf", F32, [DEPTH, NBLK_FF, 2])
    Sret = view("Sret", F32, [RET_H, 2, 512]); Sretb = view("Sretb", BF16, [RET_H, 2, 512])
    Sgla = view("Sgla", F32, [GLA_H, 256]); Sglab = view("Sglab", BF16, [GLA_H, 256])

    psb = [psum[:, b, :] for b in range(8)]
    psb_bf = [psum[:, b, :].bitcast(BF16) for b in range(8)]
    bank_ctr = {"i": 0}

    reserved = set()

    def bank():
        while True:
            b = bank_ctr["i"] % 8
            bank_ctr["i"] += 1
            if b not in reserved:
                return b

    P.dma([(ident, d_ident), (ones, d_ones), (rmask, d_rmask), (qdec, d_qdec), (kdec, d_kdec), (gmask, d_gmask),
           (ucum, d_ucum), (neghalf, d_neghalf), (epsv, d_epsv), (identf, d_identf), (onesf, d_onesf), (gains, d_gains), (gnT, d_gn), (ngT, d_ng), (cw, d_cw), (cb, d_cb)],
          "const")
    P.dma([(wa2[0:17, :], d_wa2)], "wa2c", eng="pool")
    for n, k, m in WEIGHTS:
        if n in ("ret_w_out", "gla_w_out"):
            continue
        P.dma([(Wb[n], W32[n])], "cast_" + n, eng="pool", max_dma_last_dim=4096)
    fi = 0
    for n, gv_, nch in (("ret_w_out", gnT, 16), ("gla_w_out", ngT, 8)):
        for ec in range(nch):
            wi = view("mix", F32, [1024], boff=(fi % 2) * 4096)
            wo = view("mix", BF16, [1024], boff=8192 + (fi % 2) * 2048)
            fi += 1
            P.dma([(wi, W32[n][ec * 128:(ec + 1) * 128, :])], "foldin")
            P.ts("dve", wo, wi, gv_[:, ec:ec + 1], ALU.mult)
            P.dma([(Wb[n][ec * 128:(ec + 1) * 128, :], wo)], "foldout")

    ring = {"i": 0}

    def slab(wname, kc0, nkc, colgroups):
        s = ring["i"] % 3
        ring["i"] += 1
        ncols = sum(c for _, c in colgroups)
        assert nkc * ncols * 2 <= 8192
        v = view("wring", BF16, [nkc, ncols], boff=s * 8192)
        pairs = []
        co = 0
        for c0, cn in colgroups:
            src = Wb[wname][kc0 * 128:(kc0 + nkc) * 128, c0:c0 + cn].rearrange("(k p) n -> p k n", p=128)
            pairs.append((v[:, :, co:co + cn], src))
            co += cn
        P.dma(pairs, "w%d" % s)
        return v

    aT_all = view("aT", BF16, [512], parts=32)
    P.memset("dve", aT_all, 1.0)

    def run_tile(sk, t0, NT, CL, last, first=True, next_t0=None):
        NCH = NT // CL
        li = 0 if CL == 128 else 1
        xsrc = xTp[t0 // HSEQ] if sk == "p" else xTs
        ydst = yTp[t0 // HSEQ] if sk == "p" else yTs
        tq = t0 % HSEQ if sk == "p" else t0
        pos0 = t0 if sk == "p" else seq + t0
        x = view("x", F32, [8, NT])
        hT = view("hT", BF16, [8, NT])
        ybuf = view("vtok", F32, [8, NT])
        sq = view("sgT", BF16, [8, NT], boff=8192)
        ms = view("ms", F32, [4]); rstd = view("ms", F32, [4], boff=256)
        rbc = view("rstd", F32, [4, 128])
        qT = view("qkk", BF16, [8, NT]); kT = view("qkk", BF16, [8, NT], boff=8192)
        ktok = view("qkk", BF16, [NCH, 1024], boff=16384)
        vtok = view("vtok", BF16, [NCH, 2048])
        sgT = view("sgT", BF16, [16, NT]); ogT = view("ogT", BF16, [16, NT])
        PTv = [view("PT", BF16, [128], boff=i * 256) for i in range(4)]
        onv = [view("on", BF16, [512], boff=i * 1024) for i in range(3)]
        statv = [(view("stats", F32, [6], boff=i * 256), view("stats", F32, [2], boff=i * 256 + 64),
                  view("stats", F32, [1], boff=i * 256 + 128), view("stats", F32, [1], boff=i * 256 + 192))
                 for i in range(4)]
        stmp = [view("stmp", BF16, [NT], boff=i * 1024) for i in range(2)]
        cnt = {"pt": 0, "on": 0, "st": 0, "sv": 0}

        def tsl(ch):
            return slice(ch * CL, (ch + 1) * CL)

        def load_x(tt0):
            src_ = xTp[tt0 // HSEQ] if sk == "p" else xTs
            tq_ = tt0 % HSEQ if sk == "p" else tt0
            for c in range(8):
                P.dma([(x[:, c, :], src_[c * 128:(c + 1) * 128, tq_:tq_ + NT])], "xin%d" % c)

        if first:
            load_x(t0)

        def norm(gidx, out_hT=True, presq=False):
            TB = min(128, NT)
            NTB = NT // TB
            if not presq:
                for c in range(8):
                    P.act(sq[:, c, :], x[:, c, :], AF.Square)
            b = bank()
            for tb in range(NTB):
                for c in range(8):
                    P.mm(psb[b][0:TB, tb:tb + 1], sq[:, c, tb * TB:(tb + 1) * TB], ones[:, 0:1],
                         start=(c == 0), stop=(c == 7), sig=(c == 7 and tb == NTB - 1))
            P.ts("dve", ms[0:TB, 0:NTB], psb[b][0:TB, 0:NTB], 1.0 / D, ALU.mult, EPS, ALU.add)
            P.tt("pool", rstd[0:TB, 0:NTB], ms[0:TB, 0:NTB], neghalf[0:TB, 0:NTB], ALU.pow)
            b2 = bank()
            for tb in range(NTB):
                P.ts("dve", rbc[0:TB, tb, :], onesf[0:TB, :], rstd[0:TB, tb:tb + 1], ALU.mult)
                P.mm(psb[b2][:, tb * TB:(tb + 1) * TB], rbc[0:TB, tb, :], identf[0:TB, 0:TB],
                     start=True, stop=True, sig=(tb == NTB - 1))
            for c in range(8):
                dst = hT[:, c, :] if out_hT else ybuf[:, c, :]
                P.stt(dst, x[:, c, :], gains[:, gidx, c:c + 1], psb[b2][:, 0:NT], ALU.mult, ALU.mult)

        def resid_proj(wname, nkc, src, kslabs):
            for cg in range(4):
                banks = [bank(), bank()]
                k0 = 0
                while k0 < nkc:
                    nk = min(kslabs, nkc - k0)
                    sl = slab(wname, k0, nk, [(cg * 256, 256)])
                    for j in range(2):
                        for kk in range(nk):
                            kc = k0 + kk
                            P.mm(psb[banks[j]][:, 0:NT], sl[:, kk, j * 128:(j + 1) * 128], src[:, kc, :],
                                 start=(kc == 0), stop=(kc == nkc - 1),
                                 sig=(kc == nkc - 1) or (kk == nk - 1 and j == 1))
                    k0 += nk
                for j in range(2):
                    blk = cg * 2 + j
                    P.tt("dve", x[:, blk, :], x[:, blk, :], psb[banks[j]][:, 0:NT], ALU.add)
                    P.act(sq[:, blk, :], x[:, blk, :], AF.Square)

        def retention():
            cs = view("mix", F32, [2, NT])
            cqs = [view("mix", F32, [2, NT], boff=4096 + i * 4096) for i in range(2)]
            rt = view("mix", F32, [4, NT], boff=12288)
            P.dma([(cs[:, 0, :], d_cos[:, pos0:pos0 + NT]), (cs[:, 1, :], d_sin[:, pos0:pos0 + NT])], "cs")

            def rotary(pa, pb, ct, st_, o1, o2):
                P.tt("dve", rt[:, 0, :], pa, ct, ALU.mult)
                P.tt("dve", rt[:, 1, :], pb, st_, ALU.mult)
                P.tt("dve", o1, rt[:, 0, :], rt[:, 1, :], ALU.subtract)
                P.tt("dve", rt[:, 2, :], pa, st_, ALU.mult)
                P.tt("dve", rt[:, 3, :], pb, ct, ALU.mult)
                P.tt("dve", o2, rt[:, 2, :], rt[:, 3, :], ALU.add)

            sg_tok = view("sgT", BF16, [NCH, 2048])

            def mk_groups(h):
                st = {}
                gl = []
                for kind in (0, 1):
                    for ch in range(NCH):
                        def grp(kind=kind, ch=ch):
                            if ch == 0:
                                st[kind] = slab("ret_w_in", 0, 8, [(2048 + kind * 2048 + h * 512, 512)])
                            sl_ = st[kind]
                            b_ = bank()
                            for kc in range(8):
                                P.mm(psb[b_][0:CL, :], hT[:, kc, tsl(ch)], sl_[:, kc, :], start=(kc == 0), stop=(kc == 7))
                            if kind == 0:
                                P.copy("act", vtok[0:CL, ch, h * 512:(h + 1) * 512], psb[b_][0:CL, :])
                            else:
                                P.act(sg_tok[0:CL, ch, h * 512:(h + 1) * 512], psb[b_][0:CL, :], AF.Silu)
                        gl.append(grp)
                return gl

            Q = []
            for h in range(RET_H):
                Q += mk_groups(h)

            def emit_groups(n):
                for _ in range(n):
                    if Q:
                        Q.pop(0)()

            for which in range(2):
                dstT = qT if which == 0 else kT
                for sh in range(2):
                    sl = slab("ret_w_in", 0, 8, [(which * 1024 + sh * 512, 512)])
                    for hh in range(2):
                        h = sh * 2 + hh
                        ba, bb = bank(), bank()
                        for half, bk in ((0, ba), (1, bb)):
                            for kc in range(8):
                                P.mm(psb[bk][:, 0:NT], sl[:, kc, (hh * 2 + half) * 128:(hh * 2 + half + 1) * 128],
                                     hT[:, kc, :], start=(kc == 0), stop=(kc == 7))
                        if which == 0:
                            cq = cqs[h % 2]
                            qd = qdec[:, h, 0:CL].unsqueeze(1).to_broadcast([128, NCH, CL])
                            for t_ in range(2):
                                P.tt("pool", cq[:, t_, :].rearrange("p (a b) -> p a b", a=NCH),
                                     cs[:, t_, :].rearrange("p (a b) -> p a b", a=NCH), qd, ALU.mult)
                            rotary(psb[ba][:, 0:NT], psb[bb][:, 0:NT], cq[:, 0, :], cq[:, 1, :],
                                   dstT[:, 2 * h, :], dstT[:, 2 * h + 1, :])
                        else:
                            rotary(psb[ba][:, 0:NT], psb[bb][:, 0:NT], cs[:, 0, :], cs[:, 1, :],
                                   dstT[:, 2 * h, :], dstT[:, 2 * h + 1, :])
                        emit_groups(1)
            for h in range(RET_H):
                for ch in range(NCH):
                    b = bank()
                    for dc in range(2):
                        P.tr(psb_bf[b][0:CL, dc * 128:(dc + 1) * 128], kT[:, 2 * h + dc, tsl(ch)], ident,
                             sig=(dc == 1))
                    P.act(ktok[0:CL, ch, h * 256:(h + 1) * 256], psb_bf[b][0:CL, 0:256], AF.Copy,
                          scale=kdec[0:CL, li, h:h + 1])
            for h in range(RET_H):
                for ch in range(NCH):
                    bs = bank()
                    for dc in range(2):
                        P.mm(psb[bs][0:CL, 0:CL], kT[:, 2 * h + dc, tsl(ch)], qT[:, 2 * h + dc, tsl(ch)],
                             start=(dc == 0), stop=(dc == 1))
                    PT = PTv[cnt["pt"] % 4]; cnt["pt"] += 1
                    P.tt("dve", PT[0:CL, 0:CL], psb[bs][0:CL, 0:CL], rmask[0:CL, h, 0:CL], ALU.mult)
                    emit_groups(1)
                    bo = bank()
                    P.mm(psb[bo][0:CL, :], PT[0:CL, 0:CL], vtok[0:CL, ch, h * 512:(h + 1) * 512], start=True, stop=False)
                    for dc in range(2):
                        P.mm(psb[bo][0:CL, :], qT[:, 2 * h + dc, tsl(ch)], Sretb[:, h, dc, :], start=False, stop=(dc == 1))
                    bS = [bank(), bank()]
                    for dc in range(2):
                        P.mm(psb[bS[dc]][:, :], ktok[0:CL, ch, h * 256 + dc * 128:h * 256 + (dc + 1) * 128],
                             vtok[0:CL, ch, h * 512:(h + 1) * 512], start=True, stop=True)
                    stats6, mv, ve, rs = statv[cnt["sv"] % 4]; cnt["sv"] += 1
                    P.op("dve", lambda e, o=stats6[0:CL, :], i=psb[bo][0:CL, :]: e.bn_stats(o, i),
                         reads=[psb[bo][0:CL, :]], writes=[stats6[0:CL, :]])
                    P.op("dve", lambda e, o=mv[0:CL, :], i=stats6[0:CL, :]: e.bn_aggr(o, i),
                         reads=[stats6[0:CL, :]], writes=[mv[0:CL, :]])
                    P.ts("dve", ve[0:CL, :], mv[0:CL, 1:2], EPS, ALU.add)
                    P.tt("pool", rs[0:CL, :], ve[0:CL, :], neghalf[0:CL, 0:1], ALU.pow)
                    for dc in range(2):
                        P.stt(Sret[:, h, dc, :], Sret[:, h, dc, :], gL[CL][h], psb[bS[dc]][:, :], ALU.mult, ALU.add)
                        P.copy("act", Sretb[:, h, dc, :], Sret[:, h, dc, :])
                    emit_groups(1)
                    on = onv[cnt["on"] % 3]; cnt["on"] += 1
                    P.ts("dve", on[0:CL, :], psb[bo][0:CL, :], mv[0:CL, 0:1], ALU.subtract, rs[0:CL, :], ALU.mult)
                    P.tt("dve", on[0:CL, :], on[0:CL, :], sg_tok[0:CL, ch, h * 512:(h + 1) * 512], ALU.mult)
                    bt = bank()
                    for eb in range(4):
                        P.tr(psb_bf[bt][:, eb * CL:(eb + 1) * CL], on[0:CL, eb * 128:(eb + 1) * 128],
                             ident[0:CL, 0:CL], sig=(eb == 3))
                    P.copy("act", ogT[:, 4 * h:4 * h + 4, tsl(ch)],
                           psb_bf[bt][:, 0:4 * CL].rearrange("p (a b) -> p a b", a=4))
            emit_groups(len(Q))
            resid_proj("ret_w_out", 16, ogT, 16)

        def ffn(layer):
            actT = view("qkk", BF16, [22, NT])
            ubuf = [[view("sgT", F32, [NT + 2], boff=(r * 2 + gv) * 2304) for gv in range(2)] for r in range(2)]
            cbuf = [[view("sgT", F32, [NT], boff=9216 + gv * 2048), view("on", F32, [NT], boff=gv * 2048)][r]
                    for r in range(2) for gv in range(2)]
            cbuf = [[cbuf[r * 2 + gv] for gv in range(2)] for r in range(2)]
            sgt = [view("sgT", BF16, [NT], boff=13312 + r * 1024) for r in range(2)]
            wn = "ffn_w_up%d" % layer
            it = 0
            pend = None

            def gate_stage(r, gb):
                P.act(sgt[r], cbuf[r][0], AF.Silu)
                P.tt("dve", actT[:, gb, :], cbuf[r][1], sgt[r], ALU.mult)

            for s in range(11):
                sl = slab(wn, 0, 8, [(s * 256, 256), (DFF + s * 256, 256)])
                for jj in range(2):
                    gb = s * 2 + jj
                    r = it % 2; it += 1
                    for gv in range(2):
                        blk = gb + gv * 22
                        b = bank()
                        for kc in range(8):
                            P.mm(psb[b][:, 0:NT], sl[:, kc, gv * 256 + jj * 128:gv * 256 + (jj + 1) * 128], hT[:, kc, :],
                                 start=(kc == 0), stop=(kc == 7))
                        ub = ubuf[r][gv]
                        c = cbuf[r][gv]
                        P.copy("pool", ub[:, 0:2], uhf[:, layer, blk, :])
                        P.copy("act", ub[:, 2:2 + NT], psb[b][:, 0:NT])
                        P.copy("pool", uhf[:, layer, blk, :], ub[:, NT:NT + 2])
                        P.act(c, ub[:, 0:NT], AF.Identity, bias=cb[:, layer, blk:blk + 1], scale=cw[:, layer, 0, blk:blk + 1])
                        P.stt(c, ub[:, 1:NT + 1], cw[:, layer, 1, blk:blk + 1], c, ALU.mult, ALU.add)
                        P.stt(c, ub[:, 2:NT + 2], cw[:, layer, 2, blk:blk + 1], c, ALU.mult, ALU.add)
                    if pend is not None:
                        gate_stage(*pend)
                    pend = (r, gb)
            gate_stage(*pend)
            resid_proj("ffn_w_down%d" % layer, 22, actT, 11)

        def gla():
            vt = view("vtok", BF16, [NCH, 1024])
            sr_tok = view("sgT", BF16, [NCH, 1024])
            kbar = view("qkk", BF16, [NCH, 512], boff=16384)
            sp = view("mix", F32, [NCH, 512])
            Eq = view("mix", F32, [GLA_H, NT], boff=8192)
            Ek = [view("mix", F32, [NT], boff=16384 + i * 2048) for i in range(2)]
            zt = view("on", F32, [512])
            kbt = view("stmp", BF16, [NT])
            junk = view("on", BF16, [256], boff=2048)

            def mk_groups(kind, s2):
                st = {}
                gl = []
                for ch in range(NCH):
                    def grp(ch=ch):
                        if ch == 0:
                            st["s"] = slab("gla_w_in", 0, 8, [(1024 + kind * 1024 + s2 * 512, 512)])
                        b_ = bank()
                        for kc in range(8):
                            P.mm(psb[b_][0:CL, :], hT[:, kc, tsl(ch)], st["s"][:, kc, :], start=(kc == 0), stop=(kc == 7))
                        if kind == 0:
                            P.copy("act", vt[0:CL, ch, s2 * 512:(s2 + 1) * 512], psb[b_][0:CL, :])
                        else:
                            P.act(sr_tok[0:CL, ch, s2 * 512:(s2 + 1) * 512], psb[b_][0:CL, :], AF.Silu)
                    gl.append(grp)
                return gl

            Q = mk_groups(0, 0) + mk_groups(1, 0) + mk_groups(0, 1) + mk_groups(1, 1)

            def emit_groups(n):
                for _ in range(n):
                    if Q:
                        Q.pop(0)()

            sl = slab("gla_w_in", 0, 8, [(3072, 16)])
            b = bank()
            for kc in range(8):
                P.mm(psb[b][0:16, 0:NT], sl[:, kc, 0:16], hT[:, kc, :], start=(kc == 0), stop=(kc == 7))
            P.copy("act", aT_all[0:16, 0:NT], psb[b][0:16, 0:NT])
            if NCH >= 4:
                emit_groups(8)
            for ch in range(NCH):
                b = bank()
                P.mm(psb[b][0:CL, :], aT_all[0:17, tsl(ch)], wa2[0:17, :], start=True, stop=True)
                P.act(zt[0:CL, :], psb[b][0:CL, :], AF.Exp, scale=-1.0)
                P.act(sp[0:CL, ch, :], zt[0:CL, :], AF.Ln, bias=1.0)
            for h in range(GLA_H):
                b = bank()
                for ch in range(NCH):
                    P.mm(psb[b][:, tsl(ch)], sp[0:CL, ch, h * 128:(h + 1) * 128], ucum[0:CL, 0:CL], start=True, stop=True,
                         sig=(ch == NCH - 1))
                P.act(Eq[:, h, :], psb[b][:, 0:NT], AF.Exp)
            slq = slab("gla_w_in", 0, 8, [(0, 512)])
            for h in range(GLA_H):
                bq = bank()
                for kc in range(8):
                    P.mm(psb[bq][:, 0:NT], slq[:, kc, h * 128:(h + 1) * 128], hT[:, kc, :], start=(kc == 0), stop=(kc == 7))
                P.tt("dve", qT[:, h, :], psb[bq][:, 0:NT], Eq[:, h, :], ALU.mult)
                if NCH >= 4:
                    emit_groups(1)
            slk = slab("gla_w_in", 0, 8, [(512, 512)])
            for h in range(GLA_H):
                P.op("dve", lambda e, o=Ek[h % 2], i=Eq[:, h, :]: e.reciprocal(o, i), reads=[Eq[:, h, :]], writes=[Ek[h % 2]])
                bk = bank()
                for kc in range(8):
                    P.mm(psb[bk][:, 0:NT], slk[:, kc, h * 128:(h + 1) * 128], hT[:, kc, :], start=(kc == 0), stop=(kc == 7))
                P.tt("dve", kT[:, h, :], psb[bk][:, 0:NT], Ek[h % 2], ALU.mult)
                for ch in range(NCH):
                    P.ts("dve", kbt[:, tsl(ch)], kT[:, h, tsl(ch)], Eq[:, h, (ch + 1) * CL - 1:(ch + 1) * CL], ALU.mult,
                         GLA_DK ** -0.5, ALU.mult)
                    bt = bank()
                    P.tr(psb_bf[bt][0:CL, 0:128], kbt[:, tsl(ch)], ident)
                    P.copy("act", kbar[0:CL, ch, h * 128:(h + 1) * 128], psb_bf[bt][0:CL, 0:128])
                if NCH >= 4:
                    emit_groups(1)
            emit_groups(len(Q))
            junk = view("on", BF16, [256])
            onb = view("on", BF16, [1024], boff=2048)
            PT4 = view("PT", BF16, [4, 128])
            stg = {}

            def stage_a(ch):
                ba = bank()
                for h in range(GLA_H):
                    P.mm(psb[ba][0:CL, h * CL:(h + 1) * CL], kT[:, h, tsl(ch)], qT[:, h, tsl(ch)], start=True, stop=True,
                         sig=(h == GLA_H - 1))
                P.tt("dve", PT4[0:CL, :, 0:CL], psb[ba][0:CL, 0:4 * CL].rearrange("p (a b) -> p a b", a=4),
                     gmask[0:CL, 0:CL].unsqueeze(1).to_broadcast([CL, 4, CL]), ALU.mult)

            def stage_b(ch):
                bo = [bank(), bank()]
                obs = []
                for h in range(GLA_H):
                    ob = psb[bo[h // 2]][0:CL, (h % 2) * 256:(h % 2 + 1) * 256]
                    obs.append(ob)
                    P.mm(ob, PT4[0:CL, h, 0:CL], vt[0:CL, ch, h * 256:(h + 1) * 256], start=True, stop=False)
                    P.mm(ob, qT[:, h, tsl(ch)], Sglab[:, h, :], start=False, stop=True)
                bS = [bank(), bank()]
                sbs = []
                for h in range(GLA_H):
                    sb_ = psb[bS[h // 2]][:, (h % 2) * 256:(h % 2 + 1) * 256]
                    sbs.append(sb_)
                    P.mm(sb_, kbar[0:CL, ch, h * 128:(h + 1) * 128], vt[0:CL, ch, h * 256:(h + 1) * 256],
                         start=True, stop=True)
                k_ = cnt["sv"] % 4; cnt["sv"] += 1
                ss4 = view("stats", F32, [4], boff=k_ * 256)
                ms4 = view("stats", F32, [4], boff=k_ * 256 + 64)
                rs4 = view("stats", F32, [4], boff=k_ * 256 + 128)
                for h in range(GLA_H):
                    P.stt(Sgla[:, h, :], Sgla[:, h, :], Eq[:, h, (ch + 1) * CL - 1:(ch + 1) * CL], sbs[h],
                          ALU.mult, ALU.add)
                    P.copy("act", Sglab[:, h, :], Sgla[:, h, :])
                for h in range(GLA_H):
                    P.act(junk[0:CL, :], obs[h], AF.Square, accum=ss4[0:CL, h:h + 1])
                P.ts("dve", ms4[0:CL, :], ss4[0:CL, :], 1.0 / GLA_DV, ALU.mult, EPS, ALU.add)
                P.tt("pool", rs4[0:CL, :], ms4[0:CL, :], neghalf[0:CL, 0:4], ALU.pow)
                stg[ch] = (obs, rs4)

            def stage_c(ch):
                obs, rs4 = stg[ch]
                for h in range(GLA_H):
                    P.ts("dve", onb[0:CL, h * 256:(h + 1) * 256], obs[h], rs4[0:CL, h:h + 1], ALU.mult)
                P.tt("dve", onb[0:CL, :], onb[0:CL, :], sr_tok[0:CL, ch, :], ALU.mult)
                bt = bank()
                for i8 in range(8):
                    P.tr(psb_bf[bt][:, i8 * CL:(i8 + 1) * CL], onb[0:CL, i8 * 128:(i8 + 1) * 128], ident[0:CL, 0:CL],
                         sig=(i8 == 7))
                P.copy("act", ogT[:, 0:8, tsl(ch)], psb_bf[bt][:, 0:8 * CL].rearrange("p (a b) -> p a b", a=8))

            stage_a(0)
            for ch in range(NCH):
                stage_b(ch)
                if ch + 1 < NCH:
                    stage_a(ch + 1)
                stage_c(ch)
            resid_proj("gla_w_out", 8, ogT, 8)

        norm(0); retention()
        norm(2, presq=True); ffn(0)
        norm(1, presq=True); gla()
        norm(3, presq=True); ffn(1)
        norm(4, out_hT=False, presq=True)
        if next_t0 is not None:
            load_x(next_t0)
        P.dma([(ydst[:, tq:tq + NT].rearrange("(c p) t -> p c t", p=128), ybuf)], "yout")

    def seq_end(sk):
        P.dma([(o_ret[sk].rearrange("h (dc p) e -> p h dc e", p=128), Sret)], "so_ret")
        P.dma([(o_gla[sk].rearrange("h p e -> p h e"), Sgla)], "so_gla")
        P.dma([(o_conv[sk], uhf)], "so_conv")

    P.dma([(Sret, st_ret.rearrange("h (dc p) e -> p h dc e", p=128)), (Sgla, st_gla.rearrange("h p e -> p h e")),
           (uhf, cconv)], "stin")
    P.copy("act", Sretb, Sret); P.copy("act", Sglab, Sgla); P.copy("dve", uhb, uhf)
    run_tile("s", 0, DEC_SEQ, DEC_SEQ, True)
    seq_end("s")
    P.memset("pool", Sret, 0.0); P.memset("pool", Sretb, 0.0); P.memset("pool", Sgla, 0.0); P.memset("pool", Sglab, 0.0)
    P.memset("pool", uhb, 0.0); P.memset("pool", uhf, 0.0)
    ntile = seq // 512
    for ti in range(ntile):
        run_tile("p", ti * 512, 512, 128, ti == ntile - 1, first=(ti == 0),
                 next_t0=((ti + 1) * 512 if ti + 1 < ntile else None))
    seq_end("p")

    P.wait_all("sp", ("yout", "so_ret", "so_gla", "so_conv"))

    with contextlib.ExitStack() as es:
        for lname, L in P.lanes.items():
            L["sem"] = es.enter_context(nc.semaphore("s_" + lname))
        block = es.enter_context(nc.Block())
        P.emit(block)
    return nc, cst, P


_CACHE = {}


def _prep_inputs(inp, seq, cst):
    f32 = lambda a: np.ascontiguousarray(np.asarray(a, np.float32))
    shared = {
        "ret_w_in": f32(inp["ret_w_in"][0]), "ret_w_out": f32(inp["ret_w_out"][0]),
        "gla_w_in": f32(inp["gla_w_in"][0]), "gla_w_out": f32(inp["gla_w_out"][0]),
        "ffn_w_up0": f32(inp["ffn_w_up"][0]), "ffn_w_up1": f32(inp["ffn_w_up"][1]),
        "ffn_w_down0": f32(inp["ffn_w_down"][0]), "ffn_w_down1": f32(inp["ffn_w_down"][1]),
    }
    nm, nf = np.asarray(inp["norm_mix"], np.float32), np.asarray(inp["norm_ffn"], np.float32)
    gains = np.stack([_fm(nm[0]), _fm(nm[1]), _fm(nf[0]), _fm(nf[1]), _fm(inp["norm_final"])], axis=1)
    shared["gains"] = np.ascontiguousarray(gains)
    shared["gn"] = _fm(np.asarray(inp["ret_gn_g"], np.float32)[0].reshape(-1))
    shared["ng"] = _fm(np.asarray(inp["gla_norm_g"], np.float32)[0].reshape(-1))
    cwv = np.asarray(inp["ffn_conv_w"], np.float32)
    shared["cw"] = np.ascontiguousarray(cwv.reshape(DEPTH, 3, NBLK_FF, 128).transpose(3, 0, 1, 2))
    cbv = np.asarray(inp["ffn_conv_b"], np.float32)
    shared["cb"] = np.ascontiguousarray(cbv.reshape(DEPTH, NBLK_FF, 128).transpose(2, 0, 1))
    shared["wa2aug"] = np.ascontiguousarray(np.concatenate(
        [np.asarray(inp["gla_w_a2"], np.float32)[0], np.asarray(inp["gla_b_a"], np.float32)[0][None, :]], axis=0))
    for k in ("ident_bf", "ones_bf", "ident_f", "ones_f", "rmask", "qdec", "kdec", "gmask", "ucum", "neghalf", "epsv", "cos", "sin"):
        shared[k] = cst[k]
    xp = np.asarray(inp["x_prompt"], np.float32)
    xs = np.asarray(inp["x_sample"], np.float32)
    cc = np.asarray(inp["cache_conv"], np.float32)
    maps = []
    for b in range(NCORES):
        m = dict(shared)
        hs = seq // (-(-seq // 4096))
        for i in range(seq // hs):
            m["xTp%d" % i] = np.ascontiguousarray(xp[b, i * hs:(i + 1) * hs].T)
        m["xTs"] = np.ascontiguousarray(xs[b].T)
        m["st_ret"] = f32(inp["state_ret"][0, b])
        m["st_gla"] = f32(inp["state_gla"][0, b])
        m["cconv"] = np.ascontiguousarray(cc[:, b].reshape(DEPTH, 2, NBLK_FF, 128).transpose(3, 0, 2, 1))
        maps.append(m)
    return maps


def _run(inp, seq):
    if seq not in _CACHE:
        _CACHE[seq] = build(seq)
    nc, cst, _ = _CACHE[seq]
    maps = _prep_inputs(inp, seq, cst)
    res = run_bass_kernel_spmd(nc, maps, core_ids=list(range(NCORES)))
    R = res.results
    B = NCORES
    hs = seq // (-(-seq // 4096))
    y_p = np.stack([np.concatenate([R[b]["yTp%d" % i].T for i in range(seq // hs)], axis=0)
                    for b in range(B)]).astype(np.float32)
    y_s = np.stack([R[b]["yTs"].T for b in range(B)]).astype(np.float32)
    ret_p = np.stack([R[b]["ret_p"] for b in range(B)])[None].astype(np.float32)
    ret_s = np.stack([R[b]["ret_s"] for b in range(B)])[None].astype(np.float32)
    gla_p = np.stack([R[b]["gla_p"] for b in range(B)])[None].astype(np.float32)
    gla_s = np.stack([R[b]["gla_s"] for b in range(B)])[None].astype(np.float32)

    def conv(k):
        a = np.stack([R[b][k] for b in range(B)])
        return np.ascontiguousarray(a.transpose(2, 0, 4, 3, 1).reshape(DEPTH, B, 2, 2 * DFF)).astype(np.float32)

    return (y_p, y_s, ret_p, ret_s, gla_p, gla_s, conv("conv_p"), conv("conv_s"))


def kernel(**inputs):
    seq = int(np.asarray(inputs["x_prompt"]).shape[1])
    return _run(inputs, seq)
```
